# Optimizing a Trainium2 kernel written in Bass

```python
import jax
import jax.numpy as jnp
from jax import lax
import numpy as np

D_MODEL = 1024
BATCH = 4
SEQ = 8192
DEPTH = 2

HEAD_DIM = 64
ROPE_THETA = 10000.0
NORM_EPS = 1e-6
QUERY_BLOCK = 128
N_MEM = 256

SWA_HEADS = 8
SWA_KV_HEADS = 2
SWA_WINDOW = 128

NSA_HEADS = 8
NSA_KV_GROUPS = 2
NSA_CMP_BLOCK = 32
NSA_CMP_STRIDE = 16
NSA_SEL_BLOCK = 64
NSA_N_SEL = 16
NSA_WINDOW = 512
NSA_FORCE_BONUS = 1e4

MLA_HEADS = 8
MLA_Q_RANK = 384
MLA_KV_RANK = 256
MLA_NOPE_DIM = 64
MLA_ROPE_DIM = 32
MLA_V_DIM = 64

XATTN_HEADS = 4
XATTN_HEAD_DIM = 128

D_FF = -(-8 * D_MODEL // (3 * 256)) * 256

N_BRANCH = 3
IN_SPLITS = (
    SWA_HEADS * HEAD_DIM, SWA_KV_HEADS * HEAD_DIM, SWA_KV_HEADS * HEAD_DIM,
    NSA_HEADS * HEAD_DIM,
    NSA_KV_GROUPS * HEAD_DIM, NSA_KV_GROUPS * HEAD_DIM,
    NSA_KV_GROUPS * HEAD_DIM, NSA_KV_GROUPS * HEAD_DIM,
    NSA_KV_GROUPS * HEAD_DIM, NSA_KV_GROUPS * HEAD_DIM,
    NSA_HEADS * 3,
    MLA_Q_RANK, MLA_KV_RANK, MLA_ROPE_DIM,
    N_BRANCH * D_MODEL,
)
D_IN = sum(IN_SPLITS)

kernel_name = 'hybrid_swa_nsa_mla_block'


def rmsnorm(x, g):
    x32 = x.astype(jnp.float32)
    y = x32 * lax.rsqrt(jnp.mean(x32 * x32, axis=-1, keepdims=True) + NORM_EPS)
    return (y * g.astype(jnp.float32)).astype(x.dtype)


def rope(x, pos):
    half = x.shape[-1] // 2
    inv_freq = ROPE_THETA ** (-jnp.arange(half, dtype=jnp.float32) / half)
    ang = pos.astype(jnp.float32)[..., None] * inv_freq
    cos, sin = jnp.cos(ang)[:, :, None, :], jnp.sin(ang)[:, :, None, :]
    x1, x2 = jnp.split(x.astype(jnp.float32), 2, axis=-1)
    return jnp.concatenate([x1 * cos - x2 * sin, x2 * cos + x1 * sin], axis=-1).astype(x.dtype)


def masked_probs(s, mask, sink=None):
    s = jnp.where(mask, s, -jnp.inf)
    m = jnp.max(s, axis=-1, keepdims=True)
    if sink is not None:
        m = jnp.maximum(m, sink)
    m = jnp.where(jnp.isfinite(m), m, 0.0)
    e = jnp.exp(s - m)
    denom = jnp.sum(e, axis=-1, keepdims=True)
    if sink is not None:
        denom = denom + jnp.exp(sink - m)
    return e / jnp.where(denom > 0, denom, 1.0)


def slice_seq(x, start, size):
    return lax.dynamic_slice_in_dim(x, start, size, axis=1)


def pad_front(x, n):
    return jnp.pad(x, ((0, 0), (n, 0), (0, 0), (0, 0)))


def blocked_map(fn, seq_len):
    out = lax.map(fn, jnp.arange(seq_len // QUERY_BLOCK) * QUERY_BLOCK)
    out = jnp.moveaxis(out, 0, 1)
    return out.reshape(out.shape[0], seq_len, *out.shape[3:])


def window_block(q, k_pad, v_pad, qs, window, sink=None):
    B, _, H, d = q.shape
    G = k_pad.shape[2]
    R = H // G
    L = window + QUERY_BLOCK
    qb = slice_seq(q, qs, QUERY_BLOCK).reshape(B, QUERY_BLOCK, G, R, d)
    kb = slice_seq(k_pad, qs, L)
    vb = slice_seq(v_pad, qs, L)
    t = qs + jnp.arange(QUERY_BLOCK)[:, None]
    kpos = qs - window + jnp.arange(L)[None, :]
    mask = (kpos <= t) & (kpos > t - window) & (kpos >= 0)
    s = jnp.einsum('bqgrd,bkgd->bgrqk', qb, kb).astype(jnp.float32) * (d ** -0.5)
    p = masked_probs(s, mask, sink)
    o = jnp.einsum('bgrqk,bkgd->bqgrd', p.astype(vb.dtype), vb)
    return o.reshape(B, QUERY_BLOCK, H, d)


def swa_mixer(q, k, v, pos, sinks):
    B, S, _ = q.shape
    q = rope(q.reshape(B, S, SWA_HEADS, HEAD_DIM), pos)
    k = pad_front(rope(k.reshape(B, S, SWA_KV_HEADS, HEAD_DIM), pos), SWA_WINDOW)
    v = pad_front(v.reshape(B, S, SWA_KV_HEADS, HEAD_DIM), SWA_WINDOW)
    sink = sinks.astype(jnp.float32).reshape(1, SWA_KV_HEADS, SWA_HEADS // SWA_KV_HEADS, 1, 1)
    o = blocked_map(lambda qs: window_block(q, k, v, qs, SWA_WINDOW, sink), S)
    return o.reshape(B, S, SWA_HEADS * HEAD_DIM)


def nsa_compress(k, pe, w1, w2):
    B, S, G, d = k.shape
    n_cmp = (S - NSA_CMP_BLOCK) // NSA_CMP_STRIDE + 1
    idx = np.arange(n_cmp)[:, None] * NSA_CMP_STRIDE + np.arange(NSA_CMP_BLOCK)[None, :]
    blk = k[:, idx] + pe[None, None, :, None, :]
    blk = jnp.swapaxes(blk, 2, 3).reshape(B, n_cmp, G, NSA_CMP_BLOCK * d)
    return jax.nn.silu(blk @ w1) @ w2


def nsa_mixer(q, kc, vc, ks, vs, kw, vw, g_logits, pos, pe_k, pe_v, wk1, wk2, wv1, wv2):
    B, S, _ = q.shape
    H, G, d = NSA_HEADS, NSA_KV_GROUPS, HEAD_DIM
    R = H // G
    shp = (B, S, G, d)
    q = q.reshape(B, S, H, d)
    q_rot = rope(q, pos)
    k_cmp = nsa_compress(kc.reshape(shp), pe_k, wk1, wk2)
    v_cmp = nsa_compress(vc.reshape(shp), pe_v, wv1, wv2)
    n_cmp = k_cmp.shape[1]
    cmp_start = np.arange(n_cmp) * NSA_CMP_STRIDE
    cmp_end = jnp.asarray(cmp_start + NSA_CMP_BLOCK - 1)
    n_slc = S // NSA_SEL_BLOCK
    sel_start = np.arange(n_slc) * NSA_SEL_BLOCK
    cover = (np.minimum(cmp_start[:, None] + NSA_CMP_BLOCK, sel_start[None, :] + NSA_SEL_BLOCK)
             - np.maximum(cmp_start[:, None], sel_start[None, :]))
    cmp_to_sel = jnp.asarray(np.clip(cover, 0, None) / NSA_CMP_STRIDE, dtype=jnp.float32)
    n_sel = min(NSA_N_SEL, n_slc)
    to_blocks = lambda a: jnp.transpose(a.reshape(B, n_slc, NSA_SEL_BLOCK, G, d), (0, 3, 1, 2, 4))
    k_sel_blk = to_blocks(rope(ks.reshape(shp), pos))
    v_sel_blk = to_blocks(vs.reshape(shp))
    k_win = pad_front(rope(kw.reshape(shp), pos), NSA_WINDOW)
    v_win = pad_front(vw.reshape(shp), NSA_WINDOW)
    gate = jax.nn.sigmoid(g_logits.astype(jnp.float32)).reshape(B, S, H, 3)
    gather = jax.vmap(jax.vmap(lambda blocks, ids: blocks[ids]))
    scale = d ** -0.5
    blk_ids = jnp.arange(n_slc)[None, :]
    m_sel = n_sel * NSA_SEL_BLOCK

    def block(qs):
        t = qs + jnp.arange(QUERY_BLOCK)
        qb = slice_seq(q, qs, QUERY_BLOCK).reshape(B, QUERY_BLOCK, G, R, d)
        s = jnp.einsum('bqgrd,bcgd->bgrqc', qb, k_cmp).astype(jnp.float32) * scale
        p_cmp = masked_probs(s, cmp_end[None, :] <= t[:, None])
        o_cmp = jnp.einsum('bgrqc,bcgd->bqgrd', p_cmp.astype(v_cmp.dtype), v_cmp)
        imp = jnp.einsum('bgqc,cj->bgqj', p_cmp.sum(axis=2), cmp_to_sel)
        cur = (t // NSA_SEL_BLOCK)[:, None]
        valid = blk_ids <= cur
        forced = (blk_ids == 0) | (blk_ids == cur) | (blk_ids == cur - 1)
        score = jnp.where(valid, imp + jnp.where(forced, NSA_FORCE_BONUS, 0.0), -jnp.inf)
        top_val, top_idx = lax.top_k(score, n_sel)
        k_g = gather(k_sel_blk, top_idx).reshape(B, G, QUERY_BLOCK, m_sel, d)
        v_g = gather(v_sel_blk, top_idx).reshape(B, G, QUERY_BLOCK, m_sel, d)
        tok = (top_idx[..., None] * NSA_SEL_BLOCK + jnp.arange(NSA_SEL_BLOCK)).reshape(B, G, QUERY_BLOCK, m_sel)
        ok = (tok <= t[:, None]) & jnp.repeat(jnp.isfinite(top_val), NSA_SEL_BLOCK, axis=-1)
        qrb = slice_seq(q_rot, qs, QUERY_BLOCK).reshape(B, QUERY_BLOCK, G, R, d)
        s = jnp.einsum('bqgrd,bgqmd->bgrqm', qrb, k_g).astype(jnp.float32) * scale
        p_sel = masked_probs(s, ok[:, :, None])
        o_sel = jnp.einsum('bgrqm,bgqmd->bqgrd', p_sel.astype(v_g.dtype), v_g)
        o_win = window_block(q_rot, k_win, v_win, qs, NSA_WINDOW)
        g = slice_seq(gate, qs, QUERY_BLOCK).astype(q.dtype)
        return (g[..., 0:1] * o_cmp.reshape(B, QUERY_BLOCK, H, d)
                + g[..., 1:2] * o_sel.reshape(B, QUERY_BLOCK, H, d)
                + g[..., 2:3] * o_win)

    return blocked_map(block, S).reshape(B, S, H * d)


def mla_mixer(q_lat, kv_lat, k_rot, pos, q_norm, w_q_b, kv_norm, w_kv_b):
    B, S, _ = q_lat.shape
    H = MLA_HEADS
    q = (rmsnorm(q_lat, q_norm) @ w_q_b).reshape(B, S, H, MLA_NOPE_DIM + MLA_ROPE_DIM)
    q_nope, q_pe = q[..., :MLA_NOPE_DIM], rope(q[..., MLA_NOPE_DIM:], pos)
    kv = (rmsnorm(kv_lat, kv_norm) @ w_kv_b).reshape(B, S, H, MLA_NOPE_DIM + MLA_V_DIM)
    k_nope, v = kv[..., :MLA_NOPE_DIM], kv[..., MLA_NOPE_DIM:]
    k_pe = rope(k_rot[:, :, None, :], pos)[:, :, 0]
    scale = (MLA_NOPE_DIM + MLA_ROPE_DIM) ** -0.5
    kpos = jnp.arange(S)[None, :]

    def block(qs):
        t = qs + jnp.arange(QUERY_BLOCK)[:, None]
        qn = slice_seq(q_nope, qs, QUERY_BLOCK)
        qp = slice_seq(q_pe, qs, QUERY_BLOCK)
        s = (jnp.einsum('bqhd,bkhd->bhqk', qn, k_nope)
             + jnp.einsum('bqhd,bkd->bhqk', qp, k_pe)).astype(jnp.float32) * scale
        p = masked_probs(s, kpos <= t)
        return jnp.einsum('bhqk,bkhd->bqhd', p.astype(v.dtype), v)

    return blocked_map(block, S).reshape(B, S, H * MLA_V_DIM)


def cross_attn(hx, hm, w_q, w_kv, w_o):
    B, S, _ = hx.shape
    M = hm.shape[1]
    q = (hx @ w_q).reshape(B, S, XATTN_HEADS, XATTN_HEAD_DIM)
    kv = (hm @ w_kv).reshape(B, M, 2, XATTN_HEADS, XATTN_HEAD_DIM)
    k, v = kv[:, :, 0], kv[:, :, 1]
    s = jnp.einsum('bshd,bmhd->bhsm', q, k).astype(jnp.float32) * (XATTN_HEAD_DIM ** -0.5)
    p = jax.nn.softmax(s, axis=-1)
    o = jnp.einsum('bhsm,bmhd->bshd', p.astype(v.dtype), v)
    return o.reshape(B, S, XATTN_HEADS * XATTN_HEAD_DIM) @ w_o


def setup_inputs(seed: int = 0) -> dict:
    key = jax.random.key(seed)
    ks = iter(jax.random.split(key, 32))
    f32 = jnp.float32
    L = DEPTH

    def dense(shape, fan_in):
        return jax.random.normal(next(ks), shape, f32) * fan_in ** -0.5

    def gain(shape):
        return 1.0 + 0.1 * jax.random.normal(next(ks), shape, f32)

    x = jax.random.normal(next(ks), (BATCH, SEQ, D_MODEL), f32)
    mem = jax.random.normal(next(ks), (BATCH, N_MEM, D_MODEL), f32)
    positions = (jnp.arange(SEQ, dtype=jnp.int32)[None, :]
                 + jax.random.randint(next(ks), (BATCH, 1), 0, 4096, dtype=jnp.int32))
    cmp_in = NSA_CMP_BLOCK * HEAD_DIM
    return {
        'x': x,
        'mem': mem,
        'positions': positions,
        'norm_mix': gain((L, D_MODEL)),
        'w_in': dense((L, D_MODEL, D_IN), D_MODEL),
        'swa_sinks': jax.random.normal(next(ks), (L, SWA_HEADS), f32),
        'nsa_pe_k': 0.1 * jax.random.normal(next(ks), (L, NSA_CMP_BLOCK, HEAD_DIM), f32),
        'nsa_pe_v': 0.1 * jax.random.normal(next(ks), (L, NSA_CMP_BLOCK, HEAD_DIM), f32),
        'nsa_wk1': dense((L, cmp_in, HEAD_DIM), cmp_in),
        'nsa_wk2': dense((L, HEAD_DIM, HEAD_DIM), HEAD_DIM),
        'nsa_wv1': dense((L, cmp_in, HEAD_DIM), cmp_in),
        'nsa_wv2': dense((L, HEAD_DIM, HEAD_DIM), HEAD_DIM),
        'mla_q_norm': gain((L, MLA_Q_RANK)),
        'mla_w_q_b': dense((L, MLA_Q_RANK, MLA_HEADS * (MLA_NOPE_DIM + MLA_ROPE_DIM)), MLA_Q_RANK),
        'mla_kv_norm': gain((L, MLA_KV_RANK)),
        'mla_w_kv_b': dense((L, MLA_KV_RANK, MLA_HEADS * (MLA_NOPE_DIM + MLA_V_DIM)), MLA_KV_RANK),
        'w_br_a': dense((L, SWA_HEADS * HEAD_DIM, D_MODEL), SWA_HEADS * HEAD_DIM),
        'w_br_b': dense((L, NSA_HEADS * HEAD_DIM, D_MODEL), NSA_HEADS * HEAD_DIM),
        'w_br_c': dense((L, MLA_HEADS * MLA_V_DIM, D_MODEL), MLA_HEADS * MLA_V_DIM),
        'w_out': dense((L, D_MODEL, D_MODEL), D_MODEL),
        'norm_xattn': gain((L, D_MODEL)),
        'norm_mem': gain((L, D_MODEL)),
        'w_xq': dense((L, D_MODEL, XATTN_HEADS * XATTN_HEAD_DIM), D_MODEL),
        'w_xkv': dense((L, D_MODEL, 2 * XATTN_HEADS * XATTN_HEAD_DIM), D_MODEL),
        'w_xo': dense((L, XATTN_HEADS * XATTN_HEAD_DIM, D_MODEL), XATTN_HEADS * XATTN_HEAD_DIM),
        'norm_ffn': gain((L, D_MODEL)),
        'w_gate_up': dense((L, D_MODEL, 2 * D_FF), D_MODEL),
        'w_down': dense((L, D_FF, D_MODEL), D_FF),
        'norm_final': gain((D_MODEL,)),
    }


def reference(x, mem, positions, norm_mix, w_in, swa_sinks, nsa_pe_k, nsa_pe_v, nsa_wk1, nsa_wk2,
              nsa_wv1, nsa_wv2, mla_q_norm, mla_w_q_b, mla_kv_norm, mla_w_kv_b, w_br_a, w_br_b,
              w_br_c, w_out, norm_xattn, norm_mem, w_xq, w_xkv, w_xo, norm_ffn, w_gate_up, w_down,
              norm_final):
    B, S, D = x.shape
    offsets = np.cumsum(IN_SPLITS)[:-1].tolist()
    for l in range(DEPTH):
        h = rmsnorm(x, norm_mix[l])
        (a_q, a_k, a_v, b_q, b_kc, b_vc, b_ks, b_vs, b_kw, b_vw, b_g,
         c_qa, c_kv, c_kr, g_br) = jnp.split(h @ w_in[l], offsets, axis=-1)
        o_a = swa_mixer(a_q, a_k, a_v, positions, swa_sinks[l])
        o_b = nsa_mixer(b_q, b_kc, b_vc, b_ks, b_vs, b_kw, b_vw, b_g, positions,
                        nsa_pe_k[l], nsa_pe_v[l], nsa_wk1[l], nsa_wk2[l], nsa_wv1[l], nsa_wv2[l])
        o_c = mla_mixer(c_qa, c_kv, c_kr, positions, mla_q_norm[l], mla_w_q_b[l],
                        mla_kv_norm[l], mla_w_kv_b[l])
        g = jax.nn.sigmoid(g_br.astype(jnp.float32)).reshape(B, S, N_BRANCH, D).astype(x.dtype)
        merged = (g[:, :, 0] * (o_a @ w_br_a[l])
                  + g[:, :, 1] * (o_b @ w_br_b[l])
                  + g[:, :, 2] * (o_c @ w_br_c[l]))
        x = x + merged @ w_out[l]
        x = x + cross_attn(rmsnorm(x, norm_xattn[l]), rmsnorm(mem, norm_mem[l]),
                           w_xq[l], w_xkv[l], w_xo[l])
        h = rmsnorm(x, norm_ffn[l])
        gate_h, up_h = jnp.split(h @ w_gate_up[l], 2, axis=-1)
        x = x + (jax.nn.silu(gate_h) * up_h) @ w_down[l]
    return rmsnorm(x, norm_final)
```

```python
import numpy as np
import ml_dtypes
from contextlib import ExitStack
import concourse.bass as bass
import concourse.mybir as mybir
from concourse.bass_utils import run_bass_kernel_spmd
from concourse.alu_op_type import AluOpType as ALU

AF = mybir.ActivationFunctionType
F32, BF16, I32 = mybir.dt.float32, mybir.dt.bfloat16, mybir.dt.int32
NEG = -30000.0
D = 1024
DFF = 2816
EPS = 1e-6
NBF = ml_dtypes.bfloat16


class Sched:
    CENG = ("pe", "act", "dve", "pool", "sp")

    def __init__(self, nc, n_dma_sems=40):
        self.nc = nc
        self.eng = {"pe": nc.tensor, "act": nc.scalar, "dve": nc.vector,
                    "pool": nc.gpsimd, "sp": nc.sync}
        self.sem = {e: nc.alloc_semaphore("sem_" + e) for e in self.CENG}
        self.cnt = {e: 0 for e in self.CENG}
        self.dsem = [nc.alloc_semaphore("dsem%d" % i) for i in range(n_dma_sems)]
        self.dval = [0] * n_dma_sems
        self.dnext = 0
        self.ops = []
        self.all_tok = {}
        self.nops = 0
        self.last_w = {}
        self.readers = {}
        self.waited = {e: {} for e in self.CENG}
        self.sig_after = {e: [] for e in self.CENG}
        self.op_eng = {}
        self.op_isdma = {}

    def op(self, eng, fn, R=(), W=(), dma=False):
        if dma:
            eng = "pool"
        elif eng == "pool":
            eng = "dve"
        idx = self.nops
        self.nops += 1
        deps = set()
        for r in R:
            if r in self.last_w:
                deps.add(self.last_w[r])
        for w in W:
            if w in self.last_w:
                deps.add(self.last_w[w])
            for rd in self.readers.get(w, ()):
                deps.add(rd)
        deps.discard(idx)
        for r in R:
            self.readers.setdefault(r, []).append(idx)
        for w in W:
            self.last_w[w] = idx
            self.readers[w] = []
        self.op_eng[idx] = eng
        self.op_isdma[idx] = dma
        self.ops.append(dict(idx=idx, eng=eng, fn=fn, deps=deps, dma=dma, sig=False, barrier=False))
        return idx

    def dma(self, fn, R=(), W=(), q="sp"):
        return self.op(q, fn, R, W, dma=True)

    def barrier(self):
        self.ops.append(dict(idx=None, barrier=True))

    def _wait(self, eng, sem, val, key):
        w = self.waited[eng]
        if w.get(key, 0) >= val:
            return
        w[key] = val
        self.eng[eng].wait_ge(sem, val)

    def flush(self):
        ops = self.ops
        self.ops = []
        pend = {o["idx"]: o for o in ops if not o["barrier"]}
        last_on = {}
        for o in ops:
            if o["barrier"]:
                for e, lo in last_on.items():
                    lo["sig"] = True
                continue
            for d in o["deps"]:
                if d in pend and not pend[d]["dma"]:
                    if not (o["eng"] == "pe" and pend[d]["eng"] == "pe" and not o["dma"]):
                        pend[d]["sig"] = True
            if not o["dma"]:
                last_on[o["eng"]] = o
        for e, lo in last_on.items():
            lo["sig"] = True
        for o in ops:
            if o["barrier"]:
                for e in self.CENG:
                    for p in self.CENG:
                        if p != e and self.cnt[p] > 0:
                            self._wait(e, self.sem[p], self.cnt[p], p)
                    for i, v in enumerate(self.dval):
                        if v > 0:
                            self._wait(e, self.dsem[i], v, ("d", i))
                continue
            e = o["eng"]
            E = self.eng[e]
            need = {}
            for d in o["deps"]:
                if self.op_isdma[d]:
                    s_i, v = self.all_tok[d]
                    need[("d", s_i)] = max(need.get(("d", s_i), 0), v)
                else:
                    pe_ = self.op_eng[d]
                    if pe_ == "pe" and e == "pe" and not o["dma"]:
                        continue
                    tok = self.all_tok.get(d)
                    if tok is None or tok[1] is None:
                        v = None
                        for (i2, v2) in self.sig_after[pe_]:
                            if i2 >= d:
                                v = v2
                                break
                        assert v is not None, ("unsignalled dep", d, pe_)
                    else:
                        v = tok[1]
                    need[pe_] = max(need.get(pe_, 0), v)
            for k, v in need.items():
                if isinstance(k, tuple):
                    self._wait(e, self.dsem[k[1]], v, k)
                else:
                    self._wait(e, self.sem[k], v, k)
            if o["dma"]:
                si = self.dnext
                self.dnext = (self.dnext + 1) % len(self.dsem)
                if self.dval[si] > 0:
                    self._wait(e, self.dsem[si], self.dval[si], ("d", si))
                ins = o["fn"](E)
                self.dval[si] += 16
                ins.then_inc(self.dsem[si], 16)
                self.all_tok[o["idx"]] = (si, self.dval[si])
            else:
                ins = o["fn"](E)
                if o["sig"]:
                    self.cnt[e] += 1
                    ins.then_inc(self.sem[e], 1)
                    self.all_tok[o["idx"]] = (e, self.cnt[e])
                    self.sig_after[e].append((o["idx"], self.cnt[e]))
                else:
                    self.all_tok[o["idx"]] = (e, None)

    def finish(self):
        self.barrier()
        self.flush()


def make_consts(T):
    NT = T // 128
    NCT = (T // 16 - 1 + 127) // 128
    n_cmp = (T - 32) // 16 + 1
    c = {}
    c["ident_f"] = np.eye(128, dtype=np.float32)
    c["ident_b"] = np.eye(128, dtype=np.float32).astype(NBF)
    c["ones_f"] = np.ones((128, 128), np.float32)
    k = np.arange(128)[:, None]
    q = np.arange(128)[None, :]
    c["cb"] = np.where(k <= q, 0.0, NEG).astype(NBF)
    c["bb"] = np.where(k > q, 0.0, NEG).astype(NBF)
    mm = np.zeros((128, 4, 4, 128), np.float32)
    for i in range(4):
        for j in range(4):
            if j < i:
                mm[:, i, j, :] = NEG
            elif j == i:
                mm[:, i, j, :] = np.where(k <= q, 0.0, NEG)
    c["mm"] = mm.reshape(128, 4 * 512).astype(NBF)
    cm = np.zeros((128, 17, 128), np.float32)
    for dl in range(17):
        cm[:, dl, :] = np.where(16 * k + 31 - q <= 128 * dl, 0.0, NEG)
    c["cm"] = cm.reshape(128, 17 * 128).astype(NBF)
    ex = np.zeros((128, NT, 128), np.float32)
    for kt in range(NT):
        for half in range(2):
            j = 2 * kt + half
            if j < 128:
                ex[j, kt, half * 64:(half + 1) * 64] = -NEG
    c["ex"] = ex.reshape(128, NT * 128).astype(NBF)
    exh = np.zeros((64, NT, 128), np.float32)
    for kt in range(NT):
        for half in range(2):
            exh[(2 * kt + half) % 64, kt, half * 64:(half + 1) * 64] = -NEG
    c["exh"] = exh.reshape(64, NT * 128).astype(NBF)
    n_slc = T // 64
    cs = np.arange(n_cmp) * 16
    ss = np.arange(n_slc) * 64
    cover = np.minimum(cs[:, None] + 32, ss[None, :] + 64) - np.maximum(cs[:, None], ss[None, :])
    c2s = np.clip(cover, 0, None) / 16.0
    ms = np.zeros((NCT * 128, 129), np.float32)
    ms[:n_cmp, :n_slc] = c2s
    ms[:, 128] = 1.0
    c["msel"] = ms.reshape(NCT, 128, 129).transpose(1, 0, 2).reshape(128, NCT * 129).astype(NBF)
    fb = np.zeros((128, 256), np.float32)
    for qq in range(128):
        hi = 1 if qq >= 64 else 0
        for dl in (hi, hi - 1):
            fb[qq, 128 + dl] = 1e4
    c["fb"] = fb
    rp = np.zeros((128, 4), np.float32)
    p = np.arange(128)
    fa = (10000.0 ** (-(np.arange(32, dtype=np.float32)) / 32)).astype(np.float32)
    rp[:, 0] = fa[p % 32]
    rp[:, 1] = np.where((p % 64) < 32, -1.0, 1.0)
    fm = (10000.0 ** (-(np.arange(16, dtype=np.float32)) / 16)).astype(np.float32)
    rp[:, 2] = fm[p % 16]
    rp[:, 3] = np.where((p % 32) < 16, -1.0, 1.0)
    c["ropep"] = rp
    return c


IN_SPLITS = (512, 128, 128, 512, 128, 128, 128, 128, 128, 128, 24, 384, 256, 32, 3072)
OFF = np.concatenate([[0], np.cumsum(IN_SPLITS)]).tolist()
(O_AQ, O_AK, O_AV, O_BQ, O_BKC, O_BVC, O_BKS, O_BVS, O_BKW, O_BVW, O_BG, O_CQA, O_CKV, O_CKR, O_GBR) = OFF[:15]


def perm_half(cols, hd):
    cols = np.asarray(cols)
    n = len(cols) // hd
    out = []
    for h in range(n):
        blk = cols[h * hd:(h + 1) * hd]
        out.append(np.concatenate([blk[hd // 2:], blk[:hd // 2]]))
    return np.concatenate(out)


def fm_groups():
    g = []
    r = lambda a, n: list(range(a, a + n))
    for i in range(4):
        g.append(("aq%d" % i, r(O_AQ + 128 * i, 128)))
        g.append(("aq%dP" % i, perm_half(r(O_AQ + 128 * i, 128), 64)))
    g.append(("ak", r(O_AK, 128)))
    g.append(("akP", perm_half(r(O_AK, 128), 64)))
    for i in range(4):
        g.append(("bq%d" % i, r(O_BQ + 128 * i, 128)))
        g.append(("bq%dP" % i, perm_half(r(O_BQ + 128 * i, 128), 64)))
    g.append(("bks", r(O_BKS, 128)))
    g.append(("bksP", perm_half(r(O_BKS, 128), 64)))
    g.append(("bkw", r(O_BKW, 128)))
    g.append(("bkwP", perm_half(r(O_BKW, 128), 64)))
    g.append(("bkc", r(O_BKC, 128)))
    g.append(("bvc", r(O_BVC, 128)))
    for i in range(3):
        g.append(("cqa%d" % i, r(O_CQA + 128 * i, 128)))
    for i in range(2):
        g.append(("ckv%d" % i, r(O_CKV + 128 * i, 128)))
    kr = r(O_CKR, 32)
    g.append(("ckr", kr * 4))
    g.append(("ckrP", list(perm_half(kr, 32)) * 4))
    bg = r(O_BG, 24)
    g.append(("bg", bg + bg[:8] + r(O_BG, 24) * 4))
    return g


FMG = fm_groups()
FMI = {n: i for i, (n, _) in enumerate(FMG)}
NFM = len(FMG)


def layer_weights(inp, l):
    w = {}
    win = np.asarray(inp["w_in"][l], np.float32)
    cols = np.concatenate([np.asarray(c) for _, c in FMG])
    for _, c in FMG:
        assert len(c) == 128 or True
    w["wfm"] = np.ascontiguousarray(np.concatenate(
        [win[:, np.asarray(c)[:128]] for _, c in FMG], axis=1))
    w["wtm"] = np.ascontiguousarray(np.concatenate(
        [win[:, O_AV:O_AV + 128], win[:, O_BVS:O_BVS + 128], win[:, O_BVW:O_BVW + 128]], axis=1))
    w["wg"] = np.ascontiguousarray(win[:, O_GBR:O_GBR + 3072])
    w["nmix"] = np.ascontiguousarray(np.asarray(inp["norm_mix"][l], np.float32).reshape(8, 128).T)
    w["nx"] = np.ascontiguousarray(np.asarray(inp["norm_xattn"][l], np.float32).reshape(8, 128).T)
    w["nmem"] = np.ascontiguousarray(np.asarray(inp["norm_mem"][l], np.float32).reshape(8, 128).T)
    w["nffn"] = np.ascontiguousarray(np.asarray(inp["norm_ffn"][l], np.float32).reshape(8, 128).T)
    w["sinks"] = np.asarray(inp["swa_sinks"][l], np.float32).reshape(1, 8)
    w["pek"] = np.ascontiguousarray(np.asarray(inp["nsa_pe_k"][l], np.float32).T)
    w["pev"] = np.ascontiguousarray(np.asarray(inp["nsa_pe_v"][l], np.float32).T)
    w["wk1"] = np.ascontiguousarray(np.asarray(inp["nsa_wk1"][l], np.float32).reshape(32, 64, 64).transpose(1, 0, 2))
    w["wv1"] = np.ascontiguousarray(np.asarray(inp["nsa_wv1"][l], np.float32).reshape(32, 64, 64).transpose(1, 0, 2))
    w["wk2"] = np.asarray(inp["nsa_wk2"][l], np.float32)
    w["wv2"] = np.asarray(inp["nsa_wv2"][l], np.float32)
    w["qn"] = np.ascontiguousarray(np.asarray(inp["mla_q_norm"][l], np.float32).reshape(3, 128).T)
    w["kvn"] = np.ascontiguousarray(np.asarray(inp["mla_kv_norm"][l], np.float32).reshape(2, 128).T)
    wq = np.asarray(inp["mla_w_q_b"][l], np.float32).reshape(384, 8, 96)
    wqp = wq.copy()
    pi = perm_half(np.arange(64, 96), 32)
    wqp[:, :, 64:96] = wq[:, :, pi]
    w["wq"] = np.ascontiguousarray(wq.reshape(3, 128, 8 * 96).transpose(1, 0, 2))
    w["wqp"] = np.ascontiguousarray(wqp.reshape(3, 128, 8 * 96).transpose(1, 0, 2))
    wkv = np.asarray(inp["mla_w_kv_b"][l], np.float32).reshape(256, 8, 128)
    w["wkvk"] = np.ascontiguousarray(wkv[:, :, :64].reshape(2, 128, 512).transpose(1, 0, 2))
    w["wkvv"] = np.ascontiguousarray(wkv[:, :, 64:].reshape(2, 128, 512).transpose(1, 0, 2))
    for nm, key in (("wbra", "w_br_a"), ("wbrb", "w_br_b"), ("wbrc", "w_br_c")):
        w[nm] = np.ascontiguousarray(np.asarray(inp[key][l], np.float32).reshape(8, 64, 1024).transpose(1, 0, 2))
    w["wout"] = np.ascontiguousarray(np.asarray(inp["w_out"][l], np.float32).reshape(8, 128, 1024).transpose(1, 0, 2))
    w["wxq"] = np.ascontiguousarray(np.asarray(inp["w_xq"][l], np.float32).reshape(8, 128, 512).transpose(1, 0, 2))
    w["wxkv"] = np.ascontiguousarray(np.asarray(inp["w_xkv"][l], np.float32).reshape(8, 128, 1024).transpose(1, 0, 2))
    w["wxo"] = np.ascontiguousarray(np.asarray(inp["w_xo"][l], np.float32).reshape(4, 128, 1024).transpose(1, 0, 2))
    w["wgu"] = np.ascontiguousarray(np.asarray(inp["w_gate_up"][l], np.float32).reshape(8, 128, 2 * DFF).transpose(1, 0, 2))
    w["wdn"] = np.ascontiguousarray(np.asarray(inp["w_down"][l], np.float32).reshape(22, 128, 1024).transpose(1, 0, 2))
    return w


WSHAPES = None


def build(T, L, wshapes, cshapes, stop_after=None, debug=False):
    NT = T // 128
    NS = T // 512
    NCT = (T // 16 - 1 + 127) // 128
    n_cmp = (T - 32) // 16 + 1
    nc = bass.Bass("TRN2", target_bir_lowering=False)
    S = Sched(nc)

    def din(name, shape, dt=F32):
        return nc.dram_tensor(name, list(shape), dt, kind="ExternalInput").ap()

    def dscr(name, shape, dt):
        return nc.dram_tensor(name, list(shape), dt, kind=("ExternalOutput" if debug else "Internal")).ap()

    x_in = din("x", [T, D])
    mem_in = din("mem", [256, D])
    pos_in = din("pos", [1, T], I32)
    nfin_in = din("nfin", [1, D])
    C = {k: din("c_" + k, v[0], BF16 if v[1] == "bf16" else F32) for k, v in cshapes.items()}
    Wt = [{k: din("w%d_%s" % (l, k), shp) for k, shp in wshapes.items()} for l in range(L)]
    y_out = nc.dram_tensor("y", [T, D], F32, kind="ExternalOutput").ap()

    xres = dscr("xres", [T, D], F32)
    rcA, rsA, rcM, rsM = (dscr(n, [128, T], F32) for n in ("rcA", "rsA", "rcM", "rsM"))
    QA = dscr("QA", [64, 8, T], BF16)
    KA = dscr("KA", [64, 2, T], BF16)
    VA = dscr("VA", [T, 2, 128], BF16)
    QBU = dscr("QBU", [64, 8, T], BF16)
    QBR = dscr("QBR", [64, 8, T], BF16)
    KC = dscr("KC", [64, 2, T], BF16)
    VC = dscr("VC", [64, 2, T], BF16)
    KBS = dscr("KBS", [64, 2, T], BF16)
    VBS = dscr("VBS", [T, 2, 128], BF16)
    KBW = dscr("KBW", [64, 2, T], BF16)
    VBW = dscr("VBW", [T, 2, 128], BF16)
    GB = dscr("GB", [8, 3, T], F32)
    QLAT = dscr("QLAT", [128, 3, T], BF16)
    CKV = dscr("CKV", [128, 2, T], BF16)
    KPE = dscr("KPE", [32, T], BF16)
    OA = dscr("OA", [64, 8, T], BF16)
    OBP = dscr("OBP", [64, 8, T], F32)
    OB = dscr("OB", [64, 8, T], BF16)
    OC = dscr("OC", [64, 8, T], BF16)
    ACTD = dscr("ACTD", [128, 22, T], BF16)
    KCMP = dscr("KCMP", [64, 2, NCT * 128], BF16)
    VCMP = dscr("VCMP", [128, NCT, 2, 128], BF16)

    PS = [nc.alloc_psum_tensor("ps%d" % i, [128, 512], F32).ap() for i in range(8)]
    psr = [0]

    def ps_next():
        i = psr[0]
        psr[0] = (i + 1) % 8
        return i

    def PR(i):
        return ("ps", i)

    def sb(st, name, shape, dt):
        return st.enter_context(nc.sbuf_tensor(name, list(shape), dt)).ap() if False else nc_alloc(st, name, shape, dt)

    acnt = [0]

    def nc_alloc(st, name, shape, dt):
        acnt[0] += 1
        g = nc.sbuf_tensor("sb%d_%s" % (acnt[0], name), list(shape), dt)
        t = st.enter_context(g)
        return t.ap() if hasattr(t, "ap") and callable(t.ap) else t

    uid = [0]

    def u():
        uid[0] += 1
        return uid[0]

    def wload(dst, src, reg, pieces=1, axis=None):
        if pieces == 1:
            S.dma(lambda e: e.dma_start(out=dst, in_=src), R=(), W=(reg,), q="pool")
        else:
            n = dst.shape[1]
            step = (n + pieces - 1) // pieces
            for a in range(0, n, step):
                b = min(n, a + step)
                S.dma(lambda e, a=a, b=b: e.dma_start(out=dst[:, a:b], in_=src[:, a:b]), R=(), W=((reg, a),), q="pool")

    with ExitStack() as st:
        TWO_PI = 2.0 * np.pi
        c1 = float(np.float32(6.28125))
        c2 = float(np.float32(np.float32(TWO_PI - 6.28125).view(np.uint32) & np.uint32(0xFFFFF000)).view(np.float32)) if False else None
        r2 = TWO_PI - 6.28125
        c2 = float(np.array(np.array(r2, np.float32).view(np.uint32) & np.uint32(0xFFFFF000), np.uint32).view(np.float32))
        c3 = float(np.float32(r2 - c2))
        MAGIC = 12582912.0
        PIS = 3.1415925
        CH = min(T, 2048)
        ropep = nc_alloc(st, "ropep", [128, 4], F32)
        S.dma(lambda e: e.dma_start(out=ropep, in_=C["ropep"]), W=("ropep",))
        posi = nc_alloc(st, "posi", [128, CH], I32)
        posf = nc_alloc(st, "posf", [128, CH], F32)
        ang = nc_alloc(st, "ang", [128, CH], F32)
        kk = nc_alloc(st, "kk", [128, CH], F32)
        rr = nc_alloc(st, "rr", [128, CH], F32)
        sn = nc_alloc(st, "sn", [128, CH], F32)
        cs_ = nc_alloc(st, "cs", [128, CH], F32)
        xt = [nc_alloc(st, "xcp%d" % i, [128, 4, D], F32) for i in range(2)]
        xv = x_in.rearrange("(s j p) c -> s p j c", p=128, j=4)
        xrv = xres.rearrange("(s j p) c -> s p j c", p=128, j=4)
        for s in range(NS):
            b = xt[s % 2]
            S.dma(lambda e, b=b, s=s: e.dma_start(out=b, in_=xv[s]), W=(("xcp", s % 2),))
            S.dma(lambda e, b=b, s=s: e.dma_start(out=xrv[s], in_=b), R=(("xcp", s % 2),), W=(("xres", s),), q="pool")
        for c0 in range(0, T, CH):
            S.dma(lambda e, c0=c0: e.dma_start(out=posi, in_=pos_in[:, c0:c0 + CH].broadcast_to([128, CH])), W=("posi",))
            S.op("dve", lambda e: e.tensor_copy(out=posf, in_=posi), R=("posi",), W=("posf",))
            for (fc, sc, dc, ds) in ((0, 1, rcA, rsA), (2, 3, rcM, rsM)):
                S.op("dve", lambda e, fc=fc: e.tensor_scalar(out=ang, in0=posf, scalar1=ropep[:, fc:fc + 1], scalar2=None, op0=ALU.mult),
                     R=("posf", "ropep"), W=("ang",))
                S.op("dve", lambda e: e.tensor_scalar(out=kk, in0=ang, scalar1=float(1.0 / TWO_PI), scalar2=MAGIC, op0=ALU.mult, op1=ALU.add),
                     R=("ang",), W=("kk",))
                S.op("dve", lambda e: e.tensor_scalar(out=kk, in0=kk, scalar1=MAGIC, scalar2=None, op0=ALU.subtract),
                     R=("kk",), W=("kk",))
                S.op("dve", lambda e: e.scalar_tensor_tensor(out=rr, in0=kk, scalar=-c1, in1=ang, op0=ALU.mult, op1=ALU.add),
                     R=("kk", "ang"), W=("rr",))
                S.op("dve", lambda e: e.scalar_tensor_tensor(out=rr, in0=kk, scalar=-c2, in1=rr, op0=ALU.mult, op1=ALU.add),
                     R=("kk", "rr"), W=("rr",))
                S.op("dve", lambda e: e.scalar_tensor_tensor(out=rr, in0=kk, scalar=-c3, in1=rr, op0=ALU.mult, op1=ALU.add),
                     R=("kk", "rr"), W=("rr",))
                S.op("dve", lambda e: e.tensor_scalar(out=rr, in0=rr, scalar1=-PIS, scalar2=PIS, op0=ALU.max, op1=ALU.min),
                     R=("rr",), W=("rr",))
                S.op("act", lambda e: e.activation(out=sn, in_=rr, func=AF.Sin), R=("rr",), W=("sn",))
                S.op("dve", lambda e, sc=sc: e.tensor_scalar(out=sn, in0=sn, scalar1=ropep[:, sc:sc + 1], scalar2=None, op0=ALU.mult),
                     R=("sn", "ropep"), W=("sn",))
                S.dma(lambda e, ds=ds, c0=c0: e.dma_start(out=ds[:, c0:c0 + CH], in_=sn), R=("sn",), W=(("rope", u()),), q="pool")
                S.op("dve", lambda e: e.scalar_tensor_tensor(out=cs_, in0=rr, scalar=-1.0, in1=rr, op0=ALU.mult, op1=ALU.max), R=("rr",), W=("cs",))
                S.op("dve", lambda e: e.tensor_scalar(out=cs_, in0=cs_, scalar1=-1.0, scalar2=float(np.pi / 2), op0=ALU.mult, op1=ALU.add),
                     R=("cs",), W=("cs",))
                S.op("act", lambda e: e.activation(out=cs_, in_=cs_, func=AF.Sin), R=("cs",), W=("cs",))
                S.dma(lambda e, dc=dc, c0=c0: e.dma_start(out=dc[:, c0:c0 + CH], in_=cs_), R=("cs",), W=(("rope", u()),), q="pool")
        S.finish()

    def norm_T(xt, xreg, hT, hreg, gcol, gcreg, tmp, ntok_tiles=4):
        for j in range(ntok_tiles):
            S.op("act", lambda e, j=j: e.activation(out=tmp["sq"], in_=xt[:, j, :], func=AF.Square, accum_out=tmp["ss"][:, j:j + 1]),
                 R=(xreg,), W=("nt_sq", ("nt_ss", j)))
        ssr = tuple(("nt_ss", j) for j in range(ntok_tiles))
        S.op("act", lambda e: e.activation(out=tmp["ss"][:, 0:ntok_tiles], in_=tmp["ss"][:, 0:ntok_tiles], func=AF.Sqrt, scale=1.0 / D, bias=EPS),
             R=ssr, W=ssr)
        S.op("dve", lambda e: e.reciprocal(out=tmp["ss"][:, 0:ntok_tiles], in_=tmp["ss"][:, 0:ntok_tiles]), R=ssr, W=ssr)
        for j in range(ntok_tiles):
            S.op("dve", lambda e, j=j: e.tensor_scalar(out=tmp["hs"][:, j, :], in0=xt[:, j, :], scalar1=tmp["ss"][:, j:j + 1], scalar2=None, op0=ALU.mult),
                 R=(xreg, ("nt_ss", j)), W=(("nt_hs", j),))
        for k in range(8):
            b = ps_next()
            for j in range(ntok_tiles):
                S.op("pe", lambda e, k=k, j=j, b=b: e.transpose(out=PS[b][:, j * 128:(j + 1) * 128], in_=tmp["hs"][:, j, k * 128:(k + 1) * 128], identity=tmp["ident"]),
                     R=(("nt_hs", j), "ident_f"), W=(PR(b),))
            n = ntok_tiles * 128
            S.op("dve" if k % 2 == 0 else "act",
                 (lambda e, k=k, b=b, n=n: e.tensor_scalar(out=hT[:, k, 0:n], in0=PS[b][:, 0:n], scalar1=gcol[:, k:k + 1], scalar2=None, op0=ALU.mult))
                 if k % 2 == 0 else
                 (lambda e, k=k, b=b, n=n: e.activation(out=hT[:, k, 0:n], in_=PS[b][:, 0:n], func=AF.Copy, scale=gcol[:, k:k + 1])),
                 R=(PR(b), gcreg), W=((hreg, k),))

    def norm_tmp(st):
        t = {}
        t["sq"] = nc_alloc(st, "nt_sq", [128, D], F32)
        t["ss"] = nc_alloc(st, "nt_ss", [128, 4], F32)
        t["hs"] = nc_alloc(st, "nt_hs", [128, 4, D], F32)
        t["ident"] = nc_alloc(st, "ident_f", [128, 128], F32)
        S.dma(lambda e: e.dma_start(out=t["ident"], in_=C["ident_f"]), W=("ident_f",))
        return t

    xrv = xres.rearrange("(s j p) c -> s p j c", p=128, j=4)
    if stop_after == "P0":
        S.finish()
        return nc

    for l in range(L):
        W = Wt[l]
        with ExitStack() as st:
            S.barrier()
            tmp = norm_tmp(st)
            wfm = nc_alloc(st, "wfm", [128, 8, NFM * 128], BF16)
            wtm = nc_alloc(st, "wtm", [128, 8, 384], BF16)
            for k in range(8):
                S.dma(lambda e, k=k: e.dma_start(out=wfm[:, k, :], in_=W["wfm"][k * 128:(k + 1) * 128, :]), W=(("wfm", k),), q="pool")
                S.dma(lambda e, k=k: e.dma_start(out=wtm[:, k, :], in_=W["wtm"][k * 128:(k + 1) * 128, :]), W=(("wtm", k),), q="pool")
            wfr = tuple(("wfm", k) for k in range(8))
            wtr = tuple(("wtm", k) for k in range(8))
            gmix = nc_alloc(st, "gmix", [128, 8], F32)
            S.dma(lambda e: e.dma_start(out=gmix, in_=W["nmix"]), W=("gmix",))
            qn = nc_alloc(st, "qn", [128, 3], F32)
            kvn = nc_alloc(st, "kvn", [128, 2], F32)
            S.dma(lambda e: e.dma_start(out=qn, in_=W["qn"]), W=("qn",))
            S.dma(lambda e: e.dma_start(out=kvn, in_=W["kvn"]), W=("kvn",))
            ones_f = nc_alloc(st, "ones_f", [128, 128], F32)
            S.dma(lambda e: e.dma_start(out=ones_f, in_=C["ones_f"]), W=("ones_f",))
            xts = [nc_alloc(st, "p1x%d" % i, [128, 4, D], F32) for i in range(2)]
            hT = nc_alloc(st, "p1hT", [128, 8, 512], BF16)
            tab = [nc_alloc(st, "p1tab%d" % i, [128, 512], F32) for i in range(4)]
            t1 = [nc_alloc(st, "p1t1_%d" % i, [128, 512], F32) for i in range(2)]
            t2 = [nc_alloc(st, "p1t2_%d" % i, [128, 512], F32) for i in range(2)]
            ob = [nc_alloc(st, "p1ob%d" % i, [128, 512], BF16) for i in range(4)]
            lat = [nc_alloc(st, "p1lat%d" % i, [128, 512], F32) for i in range(3)]
            sqb = nc_alloc(st, "p1sqb", [128, 512], F32)
            rstd = nc_alloc(st, "p1rstd", [128, 512], F32)
            gsb = nc_alloc(st, "p1gsb", [128, 512], F32)
            vst = [nc_alloc(st, "p1vst%d" % i, [128, 6, 128], BF16) for i in range(2)]
            for i in range(2):
                S.op("pool", lambda e, i=i: e.memset(vst[i], 1.0), W=(("vst", i),))
            obc = [0]

            def next_ob():
                i = obc[0]
                obc[0] = (i + 1) % 4
                return i

            def proj_fm(gname):
                gi = FMI[gname]
                b = ps_next()
                for k in range(8):
                    S.op("pe", lambda e, k=k, b=b, gi=gi: e.matmul(PS[b], lhsT=wfm[:, k, gi * 128:(gi + 1) * 128], rhs=hT[:, k, :], start=(k == 0), stop=(k == 7)),
                         R=(("wfm", k), ("hT", k)), W=(PR(b),))
                return b

            tc = [0]
            for s in range(NS):
                xt = xts[s % 2]
                xr = ("p1x", s % 2)
                S.dma(lambda e, xt=xt, s=s: e.dma_start(out=xt, in_=xrv[s]), R=(("xres", s),), W=(xr,))
                for i, tsrc in enumerate((rcA, rsA, rcM, rsM)):
                    S.dma(lambda e, i=i, tsrc=tsrc, s=s: e.dma_start(out=tab[i], in_=tsrc[:, s * 512:(s + 1) * 512]), W=(("tab", i),))
                norm_T(xt, xr, hT, "hT", gmix, "gmix", tmp)
                tsl = slice(s * 512, (s + 1) * 512)

                def rope_group(nm, dsts, ci=0, si=1, rows=None):
                    ba = proj_fm(nm)
                    bp = proj_fm(nm + "P") if not nm.startswith("ckr") else proj_fm("ckrP")
                    i = tc[0] % 2
                    tc[0] += 1
                    o = next_ob()
                    S.op("dve", lambda e, ba=ba, i=i: e.tensor_tensor(out=t1[i], in0=PS[ba], in1=tab[ci], op=ALU.mult),
                         R=(PR(ba), ("tab", ci)), W=(("t1", i),))
                    S.op("dve", lambda e, bp=bp, i=i: e.tensor_tensor(out=t2[i], in0=PS[bp], in1=tab[si], op=ALU.mult),
                         R=(PR(bp), ("tab", si)), W=(("t2", i),))
                    S.op("pool", lambda e, i=i, o=o: e.tensor_tensor(out=ob[o], in0=t1[i], in1=t2[i], op=ALU.add),
                         R=(("t1", i), ("t2", i)), W=(("ob", o),))
                    emit_out(dsts, o)

                def emit_out(dsts, o):
                    if len(dsts) == 1 and dsts[0][1] is None:
                        dst = dsts[0][0]
                        S.dma(lambda e, dst=dst, o=o: e.dma_start(out=dst, in_=ob[o]), R=(("ob", o),), W=(("p1out", u()),))
                    else:
                        for (dst, psl) in dsts:
                            S.dma(lambda e, dst=dst, psl=psl, o=o: e.dma_start(out=dst, in_=ob[o][psl, :]), R=(("ob", o),), W=(("p1out", u()),))

                def plain_group(nm, dsts):
                    b = proj_fm(nm)
                    o = next_ob()
                    S.op("act", lambda e, b=b, o=o: e.copy(out=ob[o], in_=PS[b]), R=(PR(b),), W=(("ob", o),))
                    emit_out(dsts, o)

                lo, hi = slice(0, 64), slice(64, 128)

                def both(Dt, i):
                    return [(Dt[:, 2 * i, tsl], lo), (Dt[:, 2 * i + 1, tsl], hi)]

                for i in range(4):
                    rope_group("aq%d" % i, both(QA, i))
                rope_group("ak", both(KA, 0))
                for i in range(4):
                    rope_group("bq%d" % i, both(QBR, i))
                    plain_group("bq%d" % i, both(QBU, i))
                rope_group("bks", both(KBS, 0))
                rope_group("bkw", both(KBW, 0))
                plain_group("bkc", both(KC, 0))
                plain_group("bvc", both(VC, 0))
                rope_group("ckr", [(KPE[:, tsl], slice(64, 96))], ci=2, si=3)
                b = proj_fm("bg")
                S.op("act", lambda e, b=b: e.activation(out=gsb[0:24, :], in_=PS[b][0:24, :], func=AF.Sigmoid), R=(PR(b),), W=("gsb",))
                S.dma(lambda e, tsl=tsl: e.dma_start(out=GB.rearrange("h j t -> (h j) t")[:, tsl], in_=gsb[0:24, :]), R=("gsb",), W=(("p1out", u()),), q="pool")

                def latent(names, gcol, gcreg, dstT, nfeat):
                    n = len(names)
                    bsum = ps_next()
                    for i, nm in enumerate(names):
                        b = proj_fm(nm)
                        S.op("act", lambda e, b=b, i=i: e.copy(out=lat[i], in_=PS[b]), R=(PR(b),), W=(("lat", i),))
                        S.op("act", lambda e, b=b: e.activation(out=sqb, in_=PS[b], func=AF.Square), R=(PR(b),), W=("sqb",))
                        S.op("pe", lambda e, i=i, bsum=bsum: e.matmul(PS[bsum], lhsT=ones_f, rhs=sqb, start=(i == 0), stop=(i == n - 1)),
                             R=("ones_f", "sqb"), W=(PR(bsum),))
                    S.op("act", lambda e, bsum=bsum: e.activation(out=rstd, in_=PS[bsum], func=AF.Sqrt, scale=1.0 / nfeat, bias=EPS),
                         R=(PR(bsum),), W=("rstd",))
                    S.op("dve", lambda e: e.reciprocal(out=rstd, in_=rstd), R=("rstd",), W=("rstd",))
                    for i in range(n):
                        o = next_ob()
                        S.op("dve", lambda e, i=i, o=o: e.scalar_tensor_tensor(out=ob[o], in0=lat[i], scalar=gcol[:, i:i + 1], in1=rstd, op0=ALU.mult, op1=ALU.mult),
                             R=(("lat", i), gcreg, "rstd"), W=(("ob", o),))
                        S.dma(lambda e, i=i, o=o, tsl=tsl: e.dma_start(out=dstT[:, i, tsl], in_=ob[o]), R=(("ob", o),), W=(("p1out", u()),), q="pool")

                latent(["cqa0", "cqa1", "cqa2"], qn, "qn", QLAT, 384)
                latent(["ckv0", "ckv1"], kvn, "kvn", CKV, 256)
                vi = s % 2
                for j in range(4):
                    b = ps_next()
                    for k in range(8):
                        S.op("pe", lambda e, k=k, b=b, j=j: e.matmul(PS[b][:, 0:384], lhsT=hT[:, k, j * 128:(j + 1) * 128], rhs=wtm[:, k, :], start=(k == 0), stop=(k == 7)),
                             R=(("hT", k), ("wtm", k)), W=(PR(b),))
                    S.op("act", lambda e, b=b, vi=vi: e.copy(out=vst[vi][:, :, 0:64], in_=PS[b][:, 0:384].rearrange("p (a c) -> p a c", c=64)),
                         R=(PR(b),), W=(("vst", vi),))
                    tok = slice(s * 512 + j * 128, s * 512 + (j + 1) * 128)
                    for m, dst in enumerate((VA, VBS, VBW)):
                        S.dma(lambda e, m=m, dst=dst, tok=tok, vi=vi: e.dma_start(out=dst[tok, :, :], in_=vst[vi][:, 2 * m:2 * m + 2, :]),
                              R=(("vst", vi),), W=(("p1out", u()),), q="pool")
            S.flush()
        if stop_after == "P1":
            break

        def bc4(ap):
            return ap.unsqueeze(1).broadcast_to([ap.shape[0], 4, 128])

        def load_consts_att(st, names):
            d = {}
            for nm in names:
                shp = cshapes[nm][0]
                dt = BF16 if cshapes[nm][1] == "bf16" else F32
                d[nm] = nc_alloc(st, "c_" + nm, shp, dt)
                S.dma(lambda e, nm=nm: e.dma_start(out=d[nm], in_=C[nm]), W=("c_" + nm,))
            return d

        class Att:
            def __init__(self, st, sbanks, nE=None):
                self.sb = sbanks
                self.sc = 0
                self.ec = 0
                self.depth = max(1, len(sbanks) - 1)
                nE = nE or (self.depth + 3)
                self.E = [nc_alloc(st, "attE%d" % i, [128, 512], BF16) for i in range(nE)]
                self.pend = []

            def tile(self, kT, kreg, q, qreg, masks, v, vreg, ob, first, last, scale, extra=None):
                b = self.sb[self.sc % len(self.sb)]
                self.sc += 1
                n = len(masks)
                kr = list(kreg) if isinstance(kreg, list) else [kreg]
                qr_ = list(qreg) if isinstance(qreg, list) else [qreg]
                S.op("pe", lambda e: e.matmul(PS[b], lhsT=kT, rhs=q, start=True, stop=(n == 0)), R=tuple(kr + qr_), W=(PR(b),))
                for i, (ml, mr, mregs) in enumerate(masks):
                    S.op("pe", lambda e, ml=ml, mr=mr, i=i: e.matmul(PS[b], lhsT=ml, rhs=mr, start=False, stop=(i == n - 1)), R=tuple(mregs), W=(PR(b),))
                ei = self.ec % len(self.E)
                self.ec += 1
                Et = self.E[ei]
                S.op("act", lambda e: e.activation(out=Et, in_=PS[b], func=AF.Exp, scale=scale), R=(PR(b),), W=(("E", ei),))

                def pv():
                    S.op("pe", lambda e: e.matmul(PS[ob], lhsT=v, rhs=Et, start=first, stop=last), R=(vreg, ("E", ei)), W=(PR(ob),))
                    if extra is not None:
                        extra(Et, ("E", ei))
                self.pend.append(pv)
                while len(self.pend) > self.depth:
                    self.pend.pop(0)()

            def drain(self):
                while self.pend:
                    self.pend.pop(0)()

        def fin_den(ob, den, dreg, sink=None, gate=None, greg=None):
            S.op("dve", lambda e: e.tensor_scalar(out=den, in0=PS[ob][64:128, :], scalar1=1e-30, scalar2=None, op0=ALU.max), R=(PR(ob),), W=(dreg,))
            if sink is not None:
                S.op("dve", lambda e: e.tensor_tensor(out=den.rearrange("p (h q) -> p h q", h=4), in0=den.rearrange("p (h q) -> p h q", h=4),
                                                      in1=sink.unsqueeze(2).broadcast_to([64, 4, 128]), op=ALU.add), R=(dreg, "esink"), W=(dreg,))
            S.op("dve", lambda e: e.reciprocal(out=den, in_=den), R=(dreg,), W=(dreg,))
            if gate is not None:
                S.op("pool", lambda e: e.tensor_tensor(out=den, in0=den, in1=gate, op=ALU.mult), R=(dreg, greg), W=(dreg,))

        qv = lambda Q, qb: Q[:, :, qb * 128:(qb + 1) * 128]

        def window_pass(tag, Qd, Kd, Vd, wt, sink, gate_j, part_in, Od):
            with ExitStack() as st:
                S.barrier()
                cc = load_consts_att(st, ["ident_b", "cb", "bb"])
                att = Att(st, [0, 1, 2, 3])
                kA = nc_alloc(st, tag + "k", [64, 2, T], BF16)
                vA = nc_alloc(st, tag + "v", [128, NT, 2, 128], BF16)
                for g in range(2):
                    S.dma(lambda e, g=g: e.dma_start(out=kA[:, g, :], in_=Kd[:, g, :]), W=((tag + "k", g),))
                Vr = Vd.rearrange("(n p) g c -> p n g c", p=128)
                for n0 in range(0, NT, 8):
                    S.dma(lambda e, n0=n0: e.dma_start(out=vA[:, n0:min(NT, n0 + 8)], in_=Vr[:, n0:min(NT, n0 + 8)]), W=((tag + "v", n0),))
                if sink:
                    es = nc_alloc(st, "esink", [64, 8], F32)
                    S.dma(lambda e: e.dma_start(out=es, in_=W["sinks"].broadcast_to([64, 8])), W=("esink",))
                    S.op("act", lambda e: e.activation(out=es, in_=es, func=AF.Exp), R=("esink",), W=("esink",))
                qts = [nc_alloc(st, tag + "q%d" % i, [64, 8, 128], BF16) for i in range(3)]
                dens = [nc_alloc(st, tag + "den%d" % i, [64, 512], F32) for i in range(2)]
                outs = [nc_alloc(st, tag + "out%d" % i, [64, 512], BF16) for i in range(2)]
                if gate_j is not None:
                    gts = [nc_alloc(st, tag + "gt%d" % i, [64, 4, 128], F32) for i in range(2)]
                    pts = [nc_alloc(st, tag + "pt%d" % i, [64, 4, 128], F32) for i in range(2)]
                    tmps = [nc_alloc(st, tag + "tmp%d" % i, [64, 512], F32) for i in range(2)]
                uc = 0
                for qb in range(NT):
                    qt = qts[qb % 3]
                    qr = (tag + "q", qb % 3)
                    S.dma(lambda e, qt=qt, qb=qb: e.dma_start(out=qt, in_=qv(Qd, qb)), W=(qr,))
                    for g in range(2):
                        i2 = uc % 2
                        ob = 4 + i2
                        uc += 1
                        qsl = slice(qb * 128, (qb + 1) * 128)
                        if gate_j is not None:
                            gt, pt = gts[i2], pts[i2]
                            S.dma(lambda e, gt=gt, g=g, qsl=qsl: e.dma_start(out=gt, in_=GB[4 * g:4 * g + 4, gate_j, qsl].unsqueeze(0).broadcast_to([64, 4, 128])), W=((tag + "gt", i2),))
                            S.dma(lambda e, pt=pt, g=g, qsl=qsl: e.dma_start(out=pt, in_=part_in[:, 4 * g:4 * g + 4, qsl]), W=((tag + "pt", i2),))
                        kts = [kt for kt in range(qb - wt, qb + 1) if kt >= 0]
                        for ti, kt in enumerate(kts):
                            masks = []
                            if kt == qb - wt:
                                masks.append((cc["ident_b"], bc4(cc["bb"]), ("c_ident_b", "c_bb")))
                            if kt == qb:
                                masks.append((cc["ident_b"], bc4(cc["cb"]), ("c_ident_b", "c_cb")))
                            att.tile(kA[:, g, kt * 128:(kt + 1) * 128], (tag + "k", g), qt[:, 4 * g:4 * g + 4, :], qr, masks,
                                     vA[:, kt, g, :], (tag + "v", (kt // 8) * 8), ob, ti == 0, ti == len(kts) - 1, 0.125)
                        att.drain()
                        den, dreg = dens[i2], (tag + "den", i2)
                        out, oreg = outs[i2], (tag + "out", i2)
                        if gate_j is None:
                            fin_den(ob, den, dreg, sink=(es[:, 4 * g:4 * g + 4] if sink else None))
                            S.op("dve", lambda e, ob=ob, den=den, out=out: e.tensor_tensor(out=out, in0=PS[ob][0:64, :], in1=den, op=ALU.mult),
                                 R=(PR(ob), dreg), W=(oreg,))
                        else:
                            fin_den(ob, den, dreg, gate=gt.rearrange("p h q -> p (h q)"), greg=(tag + "gt", i2))
                            tmp = tmps[i2]
                            S.op("dve", lambda e, ob=ob, den=den, tmp=tmp: e.tensor_tensor(out=tmp, in0=PS[ob][0:64, :], in1=den, op=ALU.mult),
                                 R=(PR(ob), dreg), W=((tag + "tmp", i2),))
                            S.op("pool", lambda e, tmp=tmp, pt=pt, out=out: e.tensor_tensor(out=out, in0=tmp, in1=pt.rearrange("p h q -> p (h q)"), op=ALU.add),
                                 R=((tag + "tmp", i2), (tag + "pt", i2)), W=(oreg,))
                        S.dma(lambda e, out=out, g=g, qsl=qsl: e.dma_start(out=Od[:, 4 * g:4 * g + 4, qsl], in_=out.rearrange("p (h q) -> p h q", h=4)),
                              R=(oreg,), W=(("wout", u()),), q="pool")
                S.flush()

        window_pass("pa", QA, KA, VA, 1, True, None, None, OA)
        if stop_after == "PA":
            break

        with ExitStack() as st:
            S.barrier()
            kc = nc_alloc(st, "kc", [64, 4, T], BF16)
            for g in range(2):
                S.dma(lambda e, g=g: e.dma_start(out=kc[:, g, :], in_=KC[:, g, :]), W=(("kc", g),))
                S.dma(lambda e, g=g: e.dma_start(out=kc[:, 2 + g, :], in_=VC[:, g, :]), W=(("kc", 2 + g),))
            w1 = nc_alloc(st, "w1", [64, 2, 32 * 64], BF16)
            S.dma(lambda e: e.dma_start(out=w1[:, 0, :], in_=W["wk1"].rearrange("d l o -> d (l o)")), W=("w1",), q="pool")
            S.dma(lambda e: e.dma_start(out=w1[:, 1, :], in_=W["wv1"].rearrange("d l o -> d (l o)")), W=("w1b",), q="pool")
            w2 = nc_alloc(st, "w2", [64, 2, 64], BF16)
            S.dma(lambda e: e.dma_start(out=w2[:, 0, :], in_=W["wk2"]), W=("w2",), q="pool")
            S.dma(lambda e: e.dma_start(out=w2[:, 1, :], in_=W["wv2"]), W=("w2b",), q="pool")
            pe = nc_alloc(st, "pe", [64, 2, 32], BF16)
            S.dma(lambda e: e.dma_start(out=pe[:, 0, :], in_=W["pek"]), W=("pe",), q="pool")
            S.dma(lambda e: e.dma_start(out=pe[:, 1, :], in_=W["pev"]), W=("peb",), q="pool")
            NC_ = NCT * 128
            hid = nc_alloc(st, "hid", [64, NC_], BF16)
            kcmp = nc_alloc(st, "kcmp", [64, 2, NC_], BF16)
            vcmp = nc_alloc(st, "vcmp", [128, NCT, 2, 128], BF16)
            S.op("pool", lambda e: e.memset(vcmp, 1.0), W=("vcmp",))
            S.op("pool", lambda e: e.memset(hid, 0.0), W=("hid",))
            for kv in range(2):
                for g in range(2):
                    b = 6
                    src = kc[:, 2 * kv + g, :].rearrange("p (c r) -> p c r", r=16)
                    for li in range(32):
                        rhs = src[:, (li // 16):(li // 16) + n_cmp, li % 16]
                        S.op("pe", lambda e, l=li, rhs=rhs, kv=kv: e.matmul(PS[b][0:64, 0:n_cmp], lhsT=w1[:, kv, l * 64:(l + 1) * 64], rhs=rhs, start=(l == 0), stop=False),
                             R=(("kc", 2 * kv + g), "w1", "w1b"), W=(PR(b),))
                    for li in range(32):
                        S.op("pe", lambda e, l=li, kv=kv: e.matmul(PS[b][0:64, 0:n_cmp], lhsT=w1[:, kv, l * 64:(l + 1) * 64], rhs=pe[:, kv, l:l + 1].broadcast_to([64, n_cmp]), start=False, stop=(l == 31)),
                             R=("pe", "peb", "w1", "w1b"), W=(PR(b),))
                    S.op("act", lambda e: e.activation(out=hid[:, 0:n_cmp], in_=PS[b][0:64, 0:n_cmp], func=AF.Silu), R=(PR(b),), W=("hid",))
                    if kv == 0:
                        b2 = 7
                        S.op("pe", lambda e: e.matmul(PS[b2][0:64, 0:NC_], lhsT=w2[:, 0, :], rhs=hid, start=True, stop=True), R=("hid", "w2"), W=(PR(b2),))
                        S.op("act", lambda e, g=g: e.copy(out=kcmp[:, g, :], in_=PS[b2][0:64, 0:NC_]), R=(PR(b2),), W=(("kcmp", g),))
                    else:
                        b2 = 7
                        for ct in range(NCT):
                            S.op("pe", lambda e, ct=ct: e.matmul(PS[b2][:, ct * 64:(ct + 1) * 64], lhsT=hid[:, ct * 128:(ct + 1) * 128], rhs=w2[:, 1, :], start=(ct == 0), stop=(ct == NCT - 1)),
                                 R=("hid", "w2b"), W=(PR(b2),))
                        S.op("act", lambda e, g=g: e.copy(out=vcmp[:, :, g, 0:64], in_=PS[b2][:, 0:NCT * 64].rearrange("p (c d) -> p c d", d=64)), R=(PR(b2),), W=("vcmp",))
            S.dma(lambda e: e.dma_start(out=KCMP, in_=kcmp), R=(("kcmp", 0), ("kcmp", 1)), W=("KCMPd",))
            S.dma(lambda e: e.dma_start(out=VCMP, in_=vcmp), R=("vcmp",), W=("VCMPd",))
            S.flush()
        with ExitStack() as st:
            S.barrier()
            cc = load_consts_att(st, ["ident_b", "ident_f", "cb", "cm", "msel", "fb"])
            att = Att(st, [0, 1, 2])
            NC_ = NCT * 128
            kcmp = nc_alloc(st, "kcmp2", [64, 2, NC_], BF16)
            vcmp = nc_alloc(st, "vcmp2", [128, NCT, 2, 128], BF16)
            S.dma(lambda e: e.dma_start(out=kcmp, in_=KCMP), R=("KCMPd",), W=(("kcmp", 0), ("kcmp", 1)))
            S.dma(lambda e: e.dma_start(out=vcmp, in_=VCMP), R=("VCMPd",), W=("vcmp",))
            kS = nc_alloc(st, "pbk", [128, 2, T], BF16)
            for g in range(2):
                S.dma(lambda e, g=g: e.dma_start(out=kS[64:128, g, :], in_=C["exh"]), W=(("exh", g),))
            QS = [[nc_alloc(st, "QS%d_%d" % (i, hf), [128, 512], BF16) for hf in range(2)] for i in range(2)]
            vS = nc_alloc(st, "pbv", [128, NT, 2, 128], BF16)
            for g in range(2):
                S.dma(lambda e, g=g: e.dma_start(out=kS[0:64, g, :], in_=KBS[:, g, :]), W=(("pbk", g),))
            Vr = VBS.rearrange("(n p) g c -> p n g c", p=128)
            for n0 in range(0, NT, 8):
                S.dma(lambda e, n0=n0: e.dma_start(out=vS[:, n0:min(NT, n0 + 8)], in_=Vr[:, n0:min(NT, n0 + 8)]), W=(("pbv", n0),))
            qus = [nc_alloc(st, "pbqu%d" % i, [64, 8, 128], BF16) for i in range(3)]
            qrs = [nc_alloc(st, "pbqr%d" % i, [64, 8, 128], BF16) for i in range(3)]
            gts = [nc_alloc(st, "pbgt%d" % i, [64, 2, 4, 128], F32) for i in range(2)]
            dens = [nc_alloc(st, "pbden%d" % i, [64, 512], F32) for i in range(2)]
            rcmp = [nc_alloc(st, "pbrc%d" % i, [64, 512], F32) for i in range(2)]
            tmps = [nc_alloc(st, "pbtmp%d" % i, [64, 512], F32) for i in range(2)]
            outs = [nc_alloc(st, "pbout%d" % i, [64, 512], F32) for i in range(2)]
            rd4 = nc_alloc(st, "rd4", [128, 4], F32)
            scr = nc_alloc(st, "scr", [128, 128], F32)
            scr2 = nc_alloc(st, "scr2", [128, 128], F32)
            m8 = nc_alloc(st, "m8", [128, 16], F32)
            selms = [nc_alloc(st, "selm%d" % i, [128, 128], F32) for i in range(2)]
            selT = [nc_alloc(st, "selT%d" % i, [128, 128], BF16) for i in range(2)]
            units = [(qb, g) for qb in range(NT) for g in range(2)]

            def loadq(qb):
                i = qb % 3
                S.dma(lambda e: e.dma_start(out=qus[i], in_=qv(QBU, qb)), W=(("pbqu", i),))
                S.dma(lambda e: e.dma_start(out=qrs[i], in_=qv(QBR, qb)), W=(("pbqr", i),))

            def cmp_job(ui):
                qb, g = units[ui]
                i2 = ui % 2
                qsl = slice(qb * 128, (qb + 1) * 128)
                gt = gts[i2]
                for jj in range(2):
                    S.dma(lambda e, jj=jj: e.dma_start(out=gt[:, jj, :, :], in_=GB[4 * g:4 * g + 4, jj, qsl].unsqueeze(0).broadcast_to([64, 4, 128])), W=(("pbgt", i2, jj),))
                nct = min(NCT, (8 * qb + 7 + 127) // 128)
                ob = 4
                for ct in range(nct):
                    dl = qb - 16 * ct
                    masks = []
                    if dl <= 16:
                        masks.append((cc["ident_b"], bc4(cc["cm"][:, dl * 128:(dl + 1) * 128]), ("c_ident_b", "c_cm")))

                    def extra(Et, ereg, ct=ct):
                        for h in range(4):
                            bi = 5 + h // 2
                            co = (h % 2) * 129
                            S.op("pe", lambda e, h=h, bi=bi, co=co: e.matmul(PS[bi][:, co:co + 129], lhsT=Et[:, h * 128:(h + 1) * 128], rhs=cc["msel"][:, ct * 129:(ct + 1) * 129],
                                                                          start=(ct == 0 and h % 2 == 0), stop=(ct == nct - 1), skip_group_check=True),
                                 R=(ereg, "c_msel"), W=(PR(bi),))
                    att.tile(kcmp[:, g, ct * 128:(ct + 1) * 128], ("kcmp", g), qus[qb % 3][:, 4 * g:4 * g + 4, :], ("pbqu", qb % 3), masks,
                             vcmp[:, ct, g, :], "vcmp", ob, ct == 0, ct == nct - 1, 0.125, extra=extra)
                att.drain()
                den, dreg = dens[i2], ("pbden", i2)
                fin_den(ob, den, dreg, gate=gt[:, 0, :, :].rearrange("p h q -> p (h q)"), greg=("pbgt", i2, 0))
                S.op("dve", lambda e: e.tensor_tensor(out=rcmp[i2], in0=PS[ob][0:64, :], in1=den, op=ALU.mult), R=(PR(ob), dreg), W=(("pbrc", i2),))
                for h in range(4):
                    bi = 5 + h // 2
                    co = (h % 2) * 129 + 128
                    S.op("dve", lambda e, h=h, bi=bi, co=co: e.tensor_scalar(out=rd4[:, h:h + 1], in0=PS[bi][:, co:co + 1], scalar1=1e-30, scalar2=None, op0=ALU.max),
                         R=(PR(bi),), W=("rd4",))
                S.op("dve", lambda e: e.reciprocal(out=rd4, in_=rd4), R=("rd4",), W=("rd4",))
                for h in range(4):
                    bi = 5 + h // 2
                    co = (h % 2) * 129
                    if h == 0:
                        S.op("dve", lambda e, bi=bi, co=co: e.tensor_scalar(out=scr, in0=PS[bi][:, co:co + 128], scalar1=rd4[:, 0:1], scalar2=None, op0=ALU.mult),
                             R=(PR(bi), "rd4"), W=("scr",))
                    else:
                        S.op("dve", lambda e, h=h, bi=bi, co=co: e.scalar_tensor_tensor(out=scr, in0=PS[bi][:, co:co + 128], scalar=rd4[:, h:h + 1], in1=scr, op0=ALU.mult, op1=ALU.add),
                             R=(PR(bi), "rd4", "scr"), W=("scr",))
                S.op("dve", lambda e: e.tensor_tensor(out=scr, in0=scr, in1=cc["fb"][:, 128 - 2 * qb:256 - 2 * qb], op=ALU.add), R=("scr", "c_fb"), W=("scr",))
                S.op("dve", lambda e: e.tensor_scalar(out=scr[:, 0:1], in0=scr[:, 0:1], scalar1=1e4, scalar2=None, op0=ALU.add), R=("scr",), W=("scr",))
                S.op("dve", lambda e: e.max(out=m8[:, 0:8], in_=scr), R=("scr",), W=("m8",))
                S.op("dve", lambda e: e.match_replace(out=scr2, in_to_replace=m8[:, 0:8], in_values=scr, imm_value=-1e30), R=("scr", "m8"), W=("scr2",))
                S.op("dve", lambda e: e.max(out=m8[:, 8:16], in_=scr2), R=("scr2",), W=("m8b",))
                S.op("dve", lambda e: e.tensor_scalar(out=selms[i2], in0=scr, scalar1=m8[:, 15:16], scalar2=1.0, op0=ALU.is_ge, op1=ALU.subtract), R=("scr", "m8b"), W=(("selm", i2),))

            def cmp_job2(ui):
                qb, g = units[ui]
                i2 = ui % 2
                S.op("pe", lambda e: e.transpose(out=PS[7][:, 0:128], in_=selms[i2], identity=cc["ident_f"]), R=(("selm", i2), "c_ident_f"), W=(PR(7),))
                nh = 2 if qb >= 32 else 1
                for hf in range(nh):
                    S.op("dve", lambda e, hf=hf: e.tensor_copy(out=QS[i2][hf][64:128, :].rearrange("p (h q) -> p h q", h=4),
                                                              in_=PS[7][hf * 64:(hf + 1) * 64, 0:128].unsqueeze(1).broadcast_to([64, 4, 128])),
                         R=(PR(7),), W=(("QS", i2, hf, "m"),))
                    S.op("dve", lambda e, hf=hf: e.tensor_copy(out=QS[i2][hf][0:64, :].rearrange("p (h q) -> p h q", h=4), in_=qrs[qb % 3][:, 4 * g:4 * g + 4, :]),
                         R=(("pbqr", qb % 3),), W=(("QS", i2, hf, "q"),))

            def sel_job(ui, nxt=None):
                qb, g = units[ui]
                i2 = ui % 2
                ob = 3
                qsl = slice(qb * 128, (qb + 1) * 128)
                for kt in range(qb + 1):
                    hf = kt // 32
                    masks = []
                    if kt == qb:
                        masks.append((cc["ident_b"], bc4(cc["cb"]), ("c_ident_b", "c_cb")))
                    att.tile(kS[:, g, kt * 128:(kt + 1) * 128], [("pbk", g), ("exh", g)], QS[i2][hf], [("QS", i2, hf, "m"), ("QS", i2, hf, "q")], masks,
                             vS[:, kt, g, :], ("pbv", (kt // 8) * 8), ob, kt == 0, kt == qb, 0.125)
                att.drain()
                if nxt is not None:
                    cmp_job2(nxt)
                den, dreg = dens[i2], ("pbden", i2)
                fin_den(ob, den, dreg, gate=gts[i2][:, 1, :, :].rearrange("p h q -> p (h q)"), greg=("pbgt", i2, 1))
                S.op("dve", lambda e: e.tensor_tensor(out=tmps[i2], in0=PS[ob][0:64, :], in1=den, op=ALU.mult), R=(PR(ob), dreg), W=(("pbtmp", i2),))
                S.op("dve", lambda e: e.tensor_tensor(out=outs[i2], in0=tmps[i2], in1=rcmp[i2], op=ALU.add), R=(("pbtmp", i2), ("pbrc", i2)), W=(("pbout", i2),))
                S.dma(lambda e: e.dma_start(out=OBP[:, 4 * g:4 * g + 4, qsl], in_=outs[i2].rearrange("p (h q) -> p h q", h=4)), R=(("pbout", i2),), W=(("wout", u()),))

            loadq(0)
            if NT > 1:
                loadq(1)
            cmp_job(0)
            cmp_job2(0)
            for ui in range(len(units)):
                qb, g = units[ui]
                if g == 0 and qb + 2 < NT:
                    loadq(qb + 2)
                if ui + 1 < len(units):
                    cmp_job(ui + 1)
                sel_job(ui, (ui + 1) if ui + 1 < len(units) else None)
            S.flush()
        if stop_after == "PB1":
            break

        window_pass("pw", QBR, KBW, VBW, 4, False, 2, OBP, OB)
        if stop_after == "PB2":
            break

        with ExitStack() as st:
            S.barrier()
            cc = load_consts_att(st, ["ident_b", "mm"])
            att = Att(st, [0, 1, 2, 3])
            qlat = nc_alloc(st, "qlat", [128, 3, T], BF16)
            ckv = nc_alloc(st, "ckv", [128, 2, T], BF16)
            for c in range(3):
                S.dma(lambda e, c=c: e.dma_start(out=qlat[:, c, :], in_=QLAT[:, c, :]), W=(("qlat", c),))
            for c in range(2):
                S.dma(lambda e, c=c: e.dma_start(out=ckv[:, c, :], in_=CKV[:, c, :]), W=(("ckv", c),))
            wq = nc_alloc(st, "wq", [128, 3, 768], BF16)
            wqp = nc_alloc(st, "wqp", [128, 3, 768], BF16)
            wkk = nc_alloc(st, "wkk", [128, 2, 512], BF16)
            wkv_ = nc_alloc(st, "wkv", [128, 2, 512], BF16)
            S.dma(lambda e: e.dma_start(out=wq, in_=W["wq"]), W=("wq",), q="pool")
            S.dma(lambda e: e.dma_start(out=wqp, in_=W["wqp"]), W=("wqp",), q="pool")
            S.dma(lambda e: e.dma_start(out=wkk, in_=W["wkvk"]), W=("wkk",), q="pool")
            S.dma(lambda e: e.dma_start(out=wkv_, in_=W["wkvv"]), W=("wkv",), q="pool")
            KH = nc_alloc(st, "KH", [96, T], BF16)
            VH = nc_alloc(st, "VH", [128, NT, 128], BF16)
            S.op("pool", lambda e: e.memset(VH, 1.0), W=("VH",))
            S.dma(lambda e: e.dma_start(out=KH[64:96, :], in_=KPE), W=("KHpe",))
            QH = [nc_alloc(st, "QH%d" % i, [96, 512], BF16) for i in range(2)]
            tabc = [nc_alloc(st, "mtc%d" % i, [96, 512], F32) for i in range(2)]
            tabs = [nc_alloc(st, "mts%d" % i, [96, 512], F32) for i in range(2)]
            t1 = nc_alloc(st, "mt1", [96, 512], F32)
            t2 = nc_alloc(st, "mt2", [96, 512], F32)
            dens = [nc_alloc(st, "mden%d" % i, [64, 512], F32) for i in range(2)]
            outs = [nc_alloc(st, "mout%d" % i, [64, 512], BF16) for i in range(2)]
            ucl = [0]

            def mla_head(h):
                for s in range(NS):
                    b = 6 + (s % 2)
                    for c in range(2):
                        S.op("pe", lambda e, c=c, b=b, s=s: e.matmul(PS[b][0:64, :], lhsT=wkk[:, c, h * 64:(h + 1) * 64], rhs=ckv[:, c, s * 512:(s + 1) * 512], start=(c == 0), stop=(c == 1)),
                             R=("wkk", ("ckv", c)), W=(PR(b),))
                    S.op("dve", lambda e, b=b, s=s: e.tensor_copy(out=KH[0:64, s * 512:(s + 1) * 512], in_=PS[b][0:64, :]), R=(PR(b),), W=("KHn",))
                for i4 in range(NT // 4):
                    b = 6 + (i4 % 2)
                    for t4 in range(4):
                        tt = i4 * 4 + t4
                        for c in range(2):
                            S.op("pe", lambda e, c=c, b=b, tt=tt, t4=t4: e.matmul(PS[b][:, t4 * 64:(t4 + 1) * 64], lhsT=ckv[:, c, tt * 128:(tt + 1) * 128], rhs=wkv_[:, c, h * 64:(h + 1) * 64],
                                                                             start=(t4 == 0 and c == 0), stop=(t4 == 3 and c == 1), skip_group_check=True),
                                 R=("wkv", ("ckv", c)), W=(PR(b),))
                    S.op("act", lambda e, b=b, i4=i4: e.copy(out=VH[:, i4 * 4:(i4 + 1) * 4, 0:64], in_=PS[b][:, 0:256].rearrange("p (a d) -> p a d", d=64)), R=(PR(b),), W=("VH",))
                for qs in range(NS):
                    i2 = ucl[0] % 2
                    ucl[0] += 1
                    ob = 4 + i2
                    tsl = slice(qs * 512, (qs + 1) * 512)
                    S.dma(lambda e, i2=i2, tsl=tsl: e.dma_start(out=tabc[i2][64:96, :], in_=rcM[64:96, tsl]), W=(("mtc", i2),))
                    S.dma(lambda e, i2=i2, tsl=tsl: e.dma_start(out=tabs[i2][64:96, :], in_=rsM[64:96, tsl]), W=(("mts", i2),))
                    ba, bb_ = 6, 7
                    for c in range(3):
                        S.op("pe", lambda e, c=c, tsl=tsl: e.matmul(PS[ba][0:96, :], lhsT=wq[:, c, h * 96:(h + 1) * 96], rhs=qlat[:, c, tsl], start=(c == 0), stop=(c == 2)),
                             R=("wq", ("qlat", c)), W=(PR(ba),))
                    for c in range(3):
                        S.op("pe", lambda e, c=c, tsl=tsl: e.matmul(PS[bb_][0:96, :], lhsT=wqp[:, c, h * 96:(h + 1) * 96], rhs=qlat[:, c, tsl], start=(c == 0), stop=(c == 2)),
                             R=("wqp", ("qlat", c)), W=(PR(bb_),))
                    qh, qreg = QH[i2], ("QH", i2)
                    S.op("act", lambda e, qh=qh: e.copy(out=qh[0:64, :], in_=PS[ba][0:64, :]), R=(PR(ba),), W=((qreg, "n"),))
                    S.op("dve", lambda e, i2=i2: e.tensor_tensor(out=t1[64:96, :], in0=PS[ba][64:96, :], in1=tabc[i2][64:96, :], op=ALU.mult), R=(PR(ba), ("mtc", i2)), W=("mt1",))
                    S.op("dve", lambda e, i2=i2: e.tensor_tensor(out=t2[64:96, :], in0=PS[bb_][64:96, :], in1=tabs[i2][64:96, :], op=ALU.mult), R=(PR(bb_), ("mts", i2)), W=("mt2",))
                    S.op("pool", lambda e, qh=qh: e.tensor_tensor(out=qh[64:96, :], in0=t1[64:96, :], in1=t2[64:96, :], op=ALU.add), R=("mt1", "mt2"), W=((qreg, "r"),))
                    nk = 4 * qs + 4
                    for kt in range(nk):
                        masks = []
                        if kt >= 4 * qs:
                            i = kt - 4 * qs
                            masks.append((cc["ident_b"], cc["mm"][:, i * 512:(i + 1) * 512], ("c_ident_b", "c_mm")))
                        att.tile(KH[:, kt * 128:(kt + 1) * 128], ["KHn", "KHpe"], qh, [(qreg, "n"), (qreg, "r")], masks, VH[:, kt, :], "VH", ob, kt == 0, kt == nk - 1, float(96 ** -0.5))
                    att.drain()
                    den, dreg = dens[i2], ("mden", i2)
                    fin_den(ob, den, dreg)
                    out = outs[i2]
                    S.op("dve", lambda e, ob=ob, den=den, out=out: e.tensor_tensor(out=out, in0=PS[ob][0:64, :], in1=den, op=ALU.mult), R=(PR(ob), dreg), W=(("mout", i2),))
                    S.dma(lambda e, out=out, tsl=tsl: e.dma_start(out=OC[:, h, tsl], in_=out), R=(("mout", i2),), W=(("wout", u()),), q="pool")
            for h_ in range(8):
                mla_head(h_)
            S.flush()
        if stop_after == "PC":
            break

        with ExitStack() as st:
            S.barrier()
            tmp = norm_tmp(st)
            wg = nc_alloc(st, "wg", [128, 8, 3072], BF16)
            for k in range(8):
                S.dma(lambda e, k=k: e.dma_start(out=wg[:, k, :], in_=W["wg"][k * 128:(k + 1) * 128, :]), W=(("wg", k),))
            wbr = nc_alloc(st, "wbr", [64, 3, 8, 1024], BF16)
            for xi, nm in enumerate(("wbra", "wbrb", "wbrc")):
                S.dma(lambda e, xi=xi, nm=nm: e.dma_start(out=wbr[:, xi, :, :], in_=W[nm]), W=(("wbr", xi),))
            wout = nc_alloc(st, "wout", [128, 8, 1024], BF16)
            S.dma(lambda e: e.dma_start(out=wout, in_=W["wout"]), W=("wout",))
            gmix = nc_alloc(st, "gmix", [128, 8], F32)
            S.dma(lambda e: e.dma_start(out=gmix, in_=W["nmix"]), W=("gmix",))
            xt1 = nc_alloc(st, "pmx", [128, 4, D], F32)
            xts = [xt1, xt1]
            hT = nc_alloc(st, "pmhT", [128, 8, 512], BF16)
            oin1 = [nc_alloc(st, "pmo_%d" % xi, [64, 8, 512], BF16) for xi in range(3)]
            oin = [oin1, oin1]
            gsb = [nc_alloc(st, "pmg%d" % i, [128, 512], F32) for i in range(2)]
            tmpm = nc_alloc(st, "pmt", [128, 512], F32)
            macc = nc_alloc(st, "pmacc", [128, 512], F32)
            mT = nc_alloc(st, "pmmT", [128, 8, 512], BF16)
            gc = 0
            for s in range(NS):
                xt, xr = xts[s % 2], ("pmx", 0)
                tsl = slice(s * 512, (s + 1) * 512)
                S.dma(lambda e, xt=xt, s=s: e.dma_start(out=xt, in_=xrv[s]), R=(("xres", s),), W=(xr,))
                for xi, Od in enumerate((OA, OB, OC)):
                    S.dma(lambda e, xi=xi, Od=Od, tsl=tsl, s=s: e.dma_start(out=oin[s % 2][xi], in_=Od[:, :, tsl]), W=(("pmo", 0, xi),))
                norm_T(xt, xr, hT, "pmhT", gmix, "gmix", tmp)
                for cg in range(8):
                    for xi in range(3):
                        bp = ps_next()
                        for h in range(8):
                            S.op("pe", lambda e, bp=bp, xi=xi, cg=cg, s=s, h=h: e.matmul(PS[bp], lhsT=wbr[:, xi, h, cg * 128:(cg + 1) * 128],
                                                                                  rhs=oin[s % 2][xi][:, h, :], start=(h == 0), stop=(h == 7)),
                                 R=(("wbr", xi), ("pmo", 0, xi)), W=(PR(bp),))
                        bg = ps_next()
                        for k in range(8):
                            S.op("pe", lambda e, bg=bg, xi=xi, k=k, cg=cg: e.matmul(PS[bg], lhsT=wg[:, k, xi * 1024 + cg * 128: xi * 1024 + (cg + 1) * 128], rhs=hT[:, k, :], start=(k == 0), stop=(k == 7)),
                                 R=(("wg", k), ("pmhT", k)), W=(PR(bg),))
                        gi = gc % 2
                        gc += 1
                        S.op("act", lambda e, bg=bg, gi=gi: e.activation(out=gsb[gi], in_=PS[bg], func=AF.Sigmoid), R=(PR(bg),), W=(("pmg", gi),))
                        if xi == 0:
                            S.op("dve", lambda e, bp=bp, gi=gi: e.tensor_tensor(out=macc, in0=PS[bp], in1=gsb[gi], op=ALU.mult), R=(PR(bp), ("pmg", gi)), W=("pmacc",))
                        else:
                            S.op("dve", lambda e, bp=bp, gi=gi: e.tensor_tensor(out=tmpm, in0=PS[bp], in1=gsb[gi], op=ALU.mult), R=(PR(bp), ("pmg", gi)), W=("pmt",))
                            if xi == 1:
                                S.op("dve", lambda e: e.tensor_tensor(out=macc, in0=macc, in1=tmpm, op=ALU.add), R=("pmacc", "pmt"), W=("pmacc",))
                            else:
                                S.op("dve", lambda e, cg=cg: e.tensor_tensor(out=mT[:, cg, :], in0=macc, in1=tmpm, op=ALU.add), R=("pmacc", "pmt"), W=(("pmmT", cg),))
                for j in range(4):
                    for half in range(2):
                        b = ps_next()
                        for cg in range(8):
                            S.op("pe", lambda e, b=b, cg=cg, j=j, half=half: e.matmul(PS[b], lhsT=mT[:, cg, j * 128:(j + 1) * 128], rhs=wout[:, cg, half * 512:(half + 1) * 512], start=(cg == 0), stop=(cg == 7)),
                                 R=(("pmmT", cg), "wout"), W=(PR(b),))
                        S.op("dve", lambda e, b=b, j=j, half=half, xt=xt: e.tensor_tensor(out=xt[:, j, half * 512:(half + 1) * 512], in0=PS[b], in1=xt[:, j, half * 512:(half + 1) * 512], op=ALU.add),
                             R=(PR(b), xr), W=(xr,))
                S.dma(lambda e, xt=xt, s=s: e.dma_start(out=xrv[s], in_=xt), R=(xr,), W=(("xres", s),))
            S.flush()
        if stop_after == "PM":
            break

        with ExitStack() as st:
            S.barrier()
            tmp = norm_tmp(st)
            wxq = nc_alloc(st, "wxq", [128, 8, 512], BF16)
            wxkv = nc_alloc(st, "wxkv", [128, 8, 1024], BF16)
            wxo = nc_alloc(st, "wxo", [128, 4, 1024], BF16)
            S.dma(lambda e: e.dma_start(out=wxq, in_=W["wxq"]), W=("wxq",))
            S.dma(lambda e: e.dma_start(out=wxkv, in_=W["wxkv"]), W=("wxkv",))
            S.dma(lambda e: e.dma_start(out=wxo, in_=W["wxo"]), W=("wxo",))
            gx = nc_alloc(st, "gx", [128, 8], F32)
            gm = nc_alloc(st, "gm", [128, 8], F32)
            S.dma(lambda e: e.dma_start(out=gx, in_=W["nx"]), W=("gx",))
            S.dma(lambda e: e.dma_start(out=gm, in_=W["nmem"]), W=("gm",))
            ones_b = nc_alloc(st, "ones_b", [128, 128], BF16)
            S.dma(lambda e: e.dma_start(out=ones_b, in_=C["ones_f"]), W=("ones_b",))
            xts = [nc_alloc(st, "pxx%d" % i, [128, 4, D], F32) for i in range(2)]
            hT = nc_alloc(st, "pxhT", [128, 8, 512], BF16)
            KM = nc_alloc(st, "KM", [128, 4, 256], BF16)
            VM = nc_alloc(st, "VM", [128, 2, 512], BF16)
            memt = xts[1]
            S.dma(lambda e: e.dma_start(out=memt[:, 0:2, :], in_=mem_in.rearrange("(j p) c -> p j c", p=128)), W=(("pxx", 1),))
            norm_T(memt, ("pxx", 1), hT, "pxhT", gm, "gm", tmp, ntok_tiles=2)
            for h in range(4):
                b = ps_next()
                for k in range(8):
                    S.op("pe", lambda e, b=b, k=k, h=h: e.matmul(PS[b][:, 0:256], lhsT=wxkv[:, k, h * 128:(h + 1) * 128], rhs=hT[:, k, 0:256], start=(k == 0), stop=(k == 7)),
                         R=("wxkv", ("pxhT", k)), W=(PR(b),))
                S.op("act", lambda e, b=b, h=h: e.copy(out=KM[:, h, :], in_=PS[b][:, 0:256]), R=(PR(b),), W=("KM",))
            for mt in range(2):
                b = ps_next()
                for k in range(8):
                    S.op("pe", lambda e, b=b, k=k, mt=mt: e.matmul(PS[b], lhsT=hT[:, k, mt * 128:(mt + 1) * 128], rhs=wxkv[:, k, 512:1024], start=(k == 0), stop=(k == 7)),
                         R=("wxkv", ("pxhT", k)), W=(PR(b),))
                S.op("act", lambda e, b=b, mt=mt: e.copy(out=VM[:, mt, :], in_=PS[b]), R=(PR(b),), W=("VM",))
            qx = [nc_alloc(st, "qx%d" % i, [128, 512], BF16) for i in range(2)]
            Ex = [nc_alloc(st, "Ex%d" % i, [128, 512], BF16) for i in range(4)]
            denx = nc_alloc(st, "denx", [128, 512], F32)
            oxT = nc_alloc(st, "oxT", [128, 4, 512], BF16)
            ec = 0
            for s in range(NS):
                xt, xr = xts[s % 2], ("pxx", s % 2)
                S.dma(lambda e, xt=xt, s=s: e.dma_start(out=xt, in_=xrv[s]), R=(("xres", s),), W=(xr,))
                norm_T(xt, xr, hT, "pxhT", gx, "gx", tmp)
                for h in range(4):
                    bq = ps_next()
                    for k in range(8):
                        S.op("pe", lambda e, bq=bq, k=k, h=h: e.matmul(PS[bq], lhsT=wxq[:, k, h * 128:(h + 1) * 128], rhs=hT[:, k, :], start=(k == 0), stop=(k == 7)),
                             R=("wxq", ("pxhT", k)), W=(PR(bq),))
                    qi = h % 2
                    S.op("act", lambda e, bq=bq, qi=qi: e.copy(out=qx[qi], in_=PS[bq]), R=(PR(bq),), W=(("qx", qi),))
                    bo = ps_next()
                    bd = ps_next()
                    for mt in range(2):
                        bs = ps_next()
                        S.op("pe", lambda e, bs=bs, h=h, mt=mt, qi=qi: e.matmul(PS[bs], lhsT=KM[:, h, mt * 128:(mt + 1) * 128], rhs=qx[qi], start=True, stop=True),
                             R=("KM", ("qx", qi)), W=(PR(bs),))
                        ei = ec % 4
                        ec += 1
                        S.op("act", lambda e, bs=bs, ei=ei: e.activation(out=Ex[ei], in_=PS[bs], func=AF.Exp, scale=float(128 ** -0.5)), R=(PR(bs),), W=(("Ex", ei),))
                        S.op("pe", lambda e, bo=bo, h=h, mt=mt, ei=ei: e.matmul(PS[bo], lhsT=VM[:, mt, h * 128:(h + 1) * 128], rhs=Ex[ei], start=(mt == 0), stop=(mt == 1)),
                             R=("VM", ("Ex", ei)), W=(PR(bo),))
                        S.op("pe", lambda e, bd=bd, mt=mt, ei=ei: e.matmul(PS[bd], lhsT=ones_b, rhs=Ex[ei], start=(mt == 0), stop=(mt == 1)),
                             R=("ones_b", ("Ex", ei)), W=(PR(bd),))
                    S.op("dve", lambda e, bd=bd: e.reciprocal(out=denx, in_=PS[bd]), R=(PR(bd),), W=("denx",))
                    S.op("dve", lambda e, bo=bo, h=h: e.tensor_tensor(out=oxT[:, h, :], in0=PS[bo], in1=denx, op=ALU.mult), R=(PR(bo), "denx"), W=(("oxT", h),))
                for j in range(4):
                    for half in range(2):
                        b = ps_next()
                        for h in range(4):
                            S.op("pe", lambda e, b=b, h=h, j=j, half=half: e.matmul(PS[b], lhsT=oxT[:, h, j * 128:(j + 1) * 128], rhs=wxo[:, h, half * 512:(half + 1) * 512], start=(h == 0), stop=(h == 3)),
                                 R=(("oxT", h), "wxo"), W=(PR(b),))
                        S.op("dve", lambda e, b=b, j=j, half=half, xt=xt: e.tensor_tensor(out=xt[:, j, half * 512:(half + 1) * 512], in0=PS[b], in1=xt[:, j, half * 512:(half + 1) * 512], op=ALU.add),
                             R=(PR(b), xr), W=(xr,))
                S.dma(lambda e, xt=xt, s=s: e.dma_start(out=xrv[s], in_=xt), R=(xr,), W=(("xres", s),))
            S.flush()
        if stop_after == "PX":
            break

        with ExitStack() as st:
            S.barrier()
            tmp = norm_tmp(st)
            wgu = nc_alloc(st, "wgu", [128, 8, 2 * DFF], BF16)
            for k in range(8):
                S.dma(lambda e, k=k: e.dma_start(out=wgu[:, k, :], in_=W["wgu"][:, k, :]), W=(("wgu", k),))
            gf = nc_alloc(st, "gf", [128, 8], F32)
            S.dma(lambda e: e.dma_start(out=gf, in_=W["nffn"]), W=("gf",))
            xts = [nc_alloc(st, "pfx%d" % i, [128, 4, D], F32) for i in range(2)]
            hT = nc_alloc(st, "pfhT", [128, 8, 512], BF16)
            sg = [nc_alloc(st, "pfsg%d" % i, [128, 512], F32) for i in range(2)]
            actT1 = nc_alloc(st, "pfact", [128, 22, 512], BF16)
            actT = [actT1, actT1]
            for s in range(NS):
                xt, xr = xts[s % 2], ("pfx", s % 2)
                tsl = slice(s * 512, (s + 1) * 512)
                S.dma(lambda e, xt=xt, s=s: e.dma_start(out=xt, in_=xrv[s]), R=(("xres", s),), W=(xr,))
                norm_T(xt, xr, hT, "pfhT", gf, "gf", tmp)
                at_, ar = actT[s % 2], ("pfact", 0)
                for f in range(22):
                    bg = ps_next()
                    bu = ps_next()
                    for k in range(8):
                        S.op("pe", lambda e, bg=bg, k=k, f=f: e.matmul(PS[bg], lhsT=wgu[:, k, f * 128:(f + 1) * 128], rhs=hT[:, k, :], start=(k == 0), stop=(k == 7)),
                             R=(("wgu", k), ("pfhT", k)), W=(PR(bg),))
                    for k in range(8):
                        S.op("pe", lambda e, bu=bu, k=k, f=f: e.matmul(PS[bu], lhsT=wgu[:, k, DFF + f * 128:DFF + (f + 1) * 128], rhs=hT[:, k, :], start=(k == 0), stop=(k == 7)),
                             R=(("wgu", k), ("pfhT", k)), W=(PR(bu),))
                    si = f % 2
                    S.op("act", lambda e, bg=bg, si=si: e.activation(out=sg[si], in_=PS[bg], func=AF.Silu), R=(PR(bg),), W=(("pfsg", si),))
                    S.op("dve", lambda e, bu=bu, si=si, f=f, at_=at_: e.tensor_tensor(out=at_[:, f, :], in0=PS[bu], in1=sg[si], op=ALU.mult), R=(PR(bu), ("pfsg", si)), W=((ar, f),))
                S.dma(lambda e, at_=at_, tsl=tsl: e.dma_start(out=ACTD[:, :, tsl], in_=at_), R=tuple((ar, f) for f in range(22)), W=(("actd", s),))
            S.flush()
        with ExitStack() as st:
            S.barrier()
            wdn = nc_alloc(st, "wdn", [128, 22, 1024], BF16)
            S.dma(lambda e: e.dma_start(out=wdn, in_=W["wdn"]), W=("wdn",))
            xts = [nc_alloc(st, "pgx%d" % i, [128, 4, D], F32) for i in range(2)]
            actT = [nc_alloc(st, "pgact%d" % i, [128, 22, 512], BF16) for i in range(2)]
            last = (l == L - 1)
            if last:
                gfin = nc_alloc(st, "gfin", [128, D], F32)
                S.dma(lambda e: e.dma_start(out=gfin, in_=nfin_in.broadcast_to([128, D])), W=("gfin",))
                sq = nc_alloc(st, "fsq", [128, D], F32)
                ss = nc_alloc(st, "fss", [128, 4], F32)
            yv = y_out.rearrange("(s j p) c -> s p j c", p=128, j=4)
            for s in range(NS):
                xt, xr = xts[s % 2], ("pgx", s % 2)
                at_, ar = actT[s % 2], ("pgact", s % 2)
                tsl = slice(s * 512, (s + 1) * 512)
                S.dma(lambda e, xt=xt, s=s: e.dma_start(out=xt, in_=xrv[s]), R=(("xres", s),), W=(xr,))
                S.dma(lambda e, at_=at_, tsl=tsl: e.dma_start(out=at_, in_=ACTD[:, :, tsl]), R=(("actd", s),), W=(ar,))
                for j in range(4):
                    for half in range(2):
                        b = ps_next()
                        for f in range(22):
                            S.op("pe", lambda e, b=b, f=f, j=j, half=half, at_=at_: e.matmul(PS[b], lhsT=at_[:, f, j * 128:(j + 1) * 128], rhs=wdn[:, f, half * 512:(half + 1) * 512], start=(f == 0), stop=(f == 21)),
                                 R=(ar, "wdn"), W=(PR(b),))
                        S.op("dve", lambda e, b=b, j=j, half=half, xt=xt: e.tensor_tensor(out=xt[:, j, half * 512:(half + 1) * 512], in0=PS[b], in1=xt[:, j, half * 512:(half + 1) * 512], op=ALU.add),
                             R=(PR(b), xr), W=(xr,))
                if not last:
                    S.dma(lambda e, xt=xt, s=s: e.dma_start(out=xrv[s], in_=xt), R=(xr,), W=(("xres", s),))
                else:
                    for j in range(4):
                        S.op("act", lambda e, j=j, xt=xt: e.activation(out=sq, in_=xt[:, j, :], func=AF.Square, accum_out=ss[:, j:j + 1]), R=(xr,), W=("fsq", ("fss", j)))
                    ssr = tuple(("fss", j) for j in range(4))
                    S.op("act", lambda e: e.activation(out=ss, in_=ss, func=AF.Sqrt, scale=1.0 / D, bias=EPS), R=ssr, W=ssr)
                    S.op("dve", lambda e: e.reciprocal(out=ss, in_=ss), R=ssr, W=ssr)
                    for j in range(4):
                        S.op("dve", lambda e, j=j, xt=xt: e.scalar_tensor_tensor(out=xt[:, j, :], in0=xt[:, j, :], scalar=ss[:, j:j + 1], in1=gfin, op0=ALU.mult, op1=ALU.mult),
                             R=(xr, ("fss", j), "gfin"), W=(xr,))
                    S.dma(lambda e, xt=xt, s=s: e.dma_start(out=yv[s], in_=xt), R=(xr,), W=(("y", s),))
            S.flush()
    S.finish()
    return nc


def prep_inputs(inp, T, L, b):
    consts = make_consts(T)
    m = {}
    m["x"] = np.ascontiguousarray(np.asarray(inp["x"][b, :T], np.float32))
    m["mem"] = np.ascontiguousarray(np.asarray(inp["mem"][b], np.float32))
    m["pos"] = np.ascontiguousarray(np.asarray(inp["positions"][b, :T], np.int32).reshape(1, T))
    m["nfin"] = np.ascontiguousarray(np.asarray(inp["norm_final"], np.float32).reshape(1, D))
    for k, v in consts.items():
        m["c_" + k] = v
    return m, consts


T_FULL, L_FULL, B_FULL = 8192, 2, 4


def kernel(**inputs):
    inp = {k: np.asarray(v) for k, v in inputs.items()}
    T, L, B = T_FULL, L_FULL, B_FULL
    ws = [layer_weights(inp, l) for l in range(L)]
    in_maps = []
    consts = None
    for b in range(B):
        m, consts = prep_inputs(inp, T, L, b)
        for l in range(L):
            for k, v in ws[l].items():
                m["w%d_%s" % (l, k)] = v
        in_maps.append(m)
    wshapes = {k: v.shape for k, v in ws[0].items()}
    cshapes = {k: (v.shape, "bf16" if v.dtype == NBF else "f32") for k, v in consts.items()}
    nc = build(T, L, wshapes, cshapes)
    res = run_bass_kernel_spmd(nc, in_maps, core_ids=list(range(B)))
    out = np.stack([np.asarray(r["y"], dtype=np.float32) for r in res.results], axis=0)
    return out
```

```python
import numpy as np
import ml_dtypes
from contextlib import ExitStack
import concourse.bass as bass
import concourse.mybir as mybir
from concourse.bass_utils import run_bass_kernel_spmd
from concourse.alu_op_type import AluOpType as ALU

AF = mybir.ActivationFunctionType
F32, BF16, I32 = mybir.dt.float32, mybir.dt.bfloat16, mybir.dt.int32
NEG = -30000.0
D = 1024
DFF = 2816
EPS = 1e-6
NBF = ml_dtypes.bfloat16


class Sched:
    CENG = ("pe", "act", "dve", "pool", "sp")

    def __init__(self, nc, n_dma_sems=40):
        self.nc = nc
        self.eng = {"pe": nc.tensor, "act": nc.scalar, "dve": nc.vector,
                    "pool": nc.gpsimd, "sp": nc.sync}
        self.sem = {e: nc.alloc_semaphore("sem_" + e) for e in self.CENG}
        self.cnt = {e: 0 for e in self.CENG}
        self.dsem = [nc.alloc_semaphore("dsem%d" % i) for i in range(n_dma_sems)]
        self.dval = [0] * n_dma_sems
        self.dnext = 0
        self.ops = []
        self.all_tok = {}
        self.nops = 0
        self.last_w = {}
        self.readers = {}
        self.waited = {e: {} for e in self.CENG}
        self.sig_after = {e: [] for e in self.CENG}
        self.op_eng = {}
        self.op_isdma = {}

    def op(self, eng, fn, R=(), W=(), dma=False):
        if dma:
            eng = "pool"
        elif eng == "pool":
            eng = "dve"
        idx = self.nops
        self.nops += 1
        deps = set()
        for r in R:
            if r in self.last_w:
                deps.add(self.last_w[r])
        for w in W:
            if w in self.last_w:
                deps.add(self.last_w[w])
            for rd in self.readers.get(w, ()):
                deps.add(rd)
        deps.discard(idx)
        for r in R:
            self.readers.setdefault(r, []).append(idx)
        for w in W:
            self.last_w[w] = idx
            self.readers[w] = []
        self.op_eng[idx] = eng
        self.op_isdma[idx] = dma
        self.ops.append(dict(idx=idx, eng=eng, fn=fn, deps=deps, dma=dma, sig=False, barrier=False))
        return idx

    def dma(self, fn, R=(), W=(), q="sp"):
        return self.op(q, fn, R, W, dma=True)

    def barrier(self):
        self.ops.append(dict(idx=None, barrier=True))

    def _wait(self, eng, sem, val, key):
        w = self.waited[eng]
        if w.get(key, 0) >= val:
            return
        w[key] = val
        self.eng[eng].wait_ge(sem, val)

    def flush(self):
        ops = self.ops
        self.ops = []
        pend = {o["idx"]: o for o in ops if not o["barrier"]}
        last_on = {}
        for o in ops:
            if o["barrier"]:
                for e, lo in last_on.items():
                    lo["sig"] = True
                continue
            for d in o["deps"]:
                if d in pend and not pend[d]["dma"]:
                    if not (o["eng"] == "pe" and pend[d]["eng"] == "pe" and not o["dma"]):
                        pend[d]["sig"] = True
            if not o["dma"]:
                last_on[o["eng"]] = o
        for e, lo in last_on.items():
            lo["sig"] = True
        for o in ops:
            if o["barrier"]:
                for e in self.CENG:
                    for p in self.CENG:
                        if p != e and self.cnt[p] > 0:
                            self._wait(e, self.sem[p], self.cnt[p], p)
                    for i, v in enumerate(self.dval):
                        if v > 0:
                            self._wait(e, self.dsem[i], v, ("d", i))
                continue
            e = o["eng"]
            E = self.eng[e]
            need = {}
            for d in o["deps"]:
                if self.op_isdma[d]:
                    s_i, v = self.all_tok[d]
                    need[("d", s_i)] = max(need.get(("d", s_i), 0), v)
                else:
                    pe_ = self.op_eng[d]
                    if pe_ == "pe" and e == "pe" and not o["dma"]:
                        continue
                    tok = self.all_tok.get(d)
                    if tok is None or tok[1] is None:
                        v = None
                        for (i2, v2) in self.sig_after[pe_]:
                            if i2 >= d:
                                v = v2
                                break
                        assert v is not None, ("unsignalled dep", d, pe_)
                    else:
                        v = tok[1]
                    need[pe_] = max(need.get(pe_, 0), v)
            for k, v in need.items():
                if isinstance(k, tuple):
                    self._wait(e, self.dsem[k[1]], v, k)
                else:
                    self._wait(e, self.sem[k], v, k)
            if o["dma"]:
                si = self.dnext
                self.dnext = (self.dnext + 1) % len(self.dsem)
                if self.dval[si] > 0:
                    self._wait(e, self.dsem[si], self.dval[si], ("d", si))
                ins = o["fn"](E)
                self.dval[si] += 16
                ins.then_inc(self.dsem[si], 16)
                self.all_tok[o["idx"]] = (si, self.dval[si])
            else:
                ins = o["fn"](E)
                if o["sig"]:
                    self.cnt[e] += 1
                    ins.then_inc(self.sem[e], 1)
                    self.all_tok[o["idx"]] = (e, self.cnt[e])
                    self.sig_after[e].append((o["idx"], self.cnt[e]))
                else:
                    self.all_tok[o["idx"]] = (e, None)

    def finish(self):
        self.barrier()
        self.flush()


def make_consts(T):
    NT = T // 128
    NCT = (T // 16 - 1 + 127) // 128
    n_cmp = (T - 32) // 16 + 1
    c = {}
    c["ident_f"] = np.eye(128, dtype=np.float32)
    c["ident_b"] = np.eye(128, dtype=np.float32).astype(NBF)
    c["ones_f"] = np.ones((128, 128), np.float32)
    k = np.arange(128)[:, None]
    q = np.arange(128)[None, :]
    c["cb"] = np.where(k <= q, 0.0, NEG).astype(NBF)
    c["bb"] = np.where(k > q, 0.0, NEG).astype(NBF)
    mm = np.zeros((128, 4, 4, 128), np.float32)
    for i in range(4):
        for j in range(4):
            if j < i:
                mm[:, i, j, :] = NEG
            elif j == i:
                mm[:, i, j, :] = np.where(k <= q, 0.0, NEG)
    c["mm"] = mm.reshape(128, 4 * 512).astype(NBF)
    cm = np.zeros((128, 17, 128), np.float32)
    for dl in range(17):
        cm[:, dl, :] = np.where(16 * k + 31 - q <= 128 * dl, 0.0, NEG)
    c["cm"] = cm.reshape(128, 17 * 128).astype(NBF)
    ex = np.zeros((128, NT, 128), np.float32)
    for kt in range(NT):
        for half in range(2):
            j = 2 * kt + half
            if j < 128:
                ex[j, kt, half * 64:(half + 1) * 64] = -NEG
    c["ex"] = ex.reshape(128, NT * 128).astype(NBF)
    exh = np.zeros((64, NT, 128), np.float32)
    for kt in range(NT):
        for half in range(2):
            exh[(2 * kt + half) % 64, kt, half * 64:(half + 1) * 64] = -NEG
    c["exh"] = exh.reshape(64, NT * 128).astype(NBF)
    n_slc = T // 64
    cs = np.arange(n_cmp) * 16
    ss = np.arange(n_slc) * 64
    cover = np.minimum(cs[:, None] + 32, ss[None, :] + 64) - np.maximum(cs[:, None], ss[None, :])
    c2s = np.clip(cover, 0, None) / 16.0
    ms = np.zeros((NCT * 128, 129), np.float32)
    ms[:n_cmp, :n_slc] = c2s
    ms[:, 128] = 1.0
    c["msel"] = ms.reshape(NCT, 128, 129).transpose(1, 0, 2).reshape(128, NCT * 129).astype(NBF)
    fb = np.zeros((128, 256), np.float32)
    for qq in range(128):
        hi = 1 if qq >= 64 else 0
        for dl in (hi, hi - 1):
            fb[qq, 128 + dl] = 1e4
    c["fb"] = fb
    rp = np.zeros((128, 4), np.float32)
    p = np.arange(128)
    fa = (10000.0 ** (-(np.arange(32, dtype=np.float32)) / 32)).astype(np.float32)
    rp[:, 0] = fa[p % 32]
    rp[:, 1] = np.where((p % 64) < 32, -1.0, 1.0)
    fm = (10000.0 ** (-(np.arange(16, dtype=np.float32)) / 16)).astype(np.float32)
    rp[:, 2] = fm[p % 16]
    rp[:, 3] = np.where((p % 32) < 16, -1.0, 1.0)
    c["ropep"] = rp
    return c


IN_SPLITS = (512, 128, 128, 512, 128, 128, 128, 128, 128, 128, 24, 384, 256, 32, 3072)
OFF = np.concatenate([[0], np.cumsum(IN_SPLITS)]).tolist()
(O_AQ, O_AK, O_AV, O_BQ, O_BKC, O_BVC, O_BKS, O_BVS, O_BKW, O_BVW, O_BG, O_CQA, O_CKV, O_CKR, O_GBR) = OFF[:15]


def perm_half(cols, hd):
    cols = np.asarray(cols)
    n = len(cols) // hd
    out = []
    for h in range(n):
        blk = cols[h * hd:(h + 1) * hd]
        out.append(np.concatenate([blk[hd // 2:], blk[:hd // 2]]))
    return np.concatenate(out)


def fm_groups():
    g = []
    r = lambda a, n: list(range(a, a + n))
    for i in range(4):
        g.append(("aq%d" % i, r(O_AQ + 128 * i, 128)))
        g.append(("aq%dP" % i, perm_half(r(O_AQ + 128 * i, 128), 64)))
    g.append(("ak", r(O_AK, 128)))
    g.append(("akP", perm_half(r(O_AK, 128), 64)))
    for i in range(4):
        g.append(("bq%d" % i, r(O_BQ + 128 * i, 128)))
        g.append(("bq%dP" % i, perm_half(r(O_BQ + 128 * i, 128), 64)))
    g.append(("bks", r(O_BKS, 128)))
    g.append(("bksP", perm_half(r(O_BKS, 128), 64)))
    g.append(("bkw", r(O_BKW, 128)))
    g.append(("bkwP", perm_half(r(O_BKW, 128), 64)))
    g.append(("bkc", r(O_BKC, 128)))
    g.append(("bvc", r(O_BVC, 128)))
    for i in range(3):
        g.append(("cqa%d" % i, r(O_CQA + 128 * i, 128)))
    for i in range(2):
        g.append(("ckv%d" % i, r(O_CKV + 128 * i, 128)))
    kr = r(O_CKR, 32)
    g.append(("ckr", kr * 4))
    g.append(("ckrP", list(perm_half(kr, 32)) * 4))
    bg = r(O_BG, 24)
    g.append(("bg", bg + bg[:8] + r(O_BG, 24) * 4))
    return g


FMG = fm_groups()
FMI = {n: i for i, (n, _) in enumerate(FMG)}
NFM = len(FMG)


def layer_weights(inp, l):
    w = {}
    win = np.asarray(inp["w_in"][l], np.float32)
    cols = np.concatenate([np.asarray(c) for _, c in FMG])
    for _, c in FMG:
        assert len(c) == 128 or True
    w["wfm"] = np.ascontiguousarray(np.concatenate(
        [win[:, np.asarray(c)[:128]] for _, c in FMG], axis=1))
    w["wtm"] = np.ascontiguousarray(np.concatenate(
        [win[:, O_AV:O_AV + 128], win[:, O_BVS:O_BVS + 128], win[:, O_BVW:O_BVW + 128]], axis=1))
    w["wg"] = np.ascontiguousarray(win[:, O_GBR:O_GBR + 3072])
    w["nmix"] = np.ascontiguousarray(np.asarray(inp["norm_mix"][l], np.float32).reshape(8, 128).T)
    w["nx"] = np.ascontiguousarray(np.asarray(inp["norm_xattn"][l], np.float32).reshape(8, 128).T)
    w["nmem"] = np.ascontiguousarray(np.asarray(inp["norm_mem"][l], np.float32).reshape(8, 128).T)
    w["nffn"] = np.ascontiguousarray(np.asarray(inp["norm_ffn"][l], np.float32).reshape(8, 128).T)
    w["sinks"] = np.asarray(inp["swa_sinks"][l], np.float32).reshape(1, 8)
    w["pek"] = np.ascontiguousarray(np.asarray(inp["nsa_pe_k"][l], np.float32).T)
    w["pev"] = np.ascontiguousarray(np.asarray(inp["nsa_pe_v"][l], np.float32).T)
    w["wk1"] = np.ascontiguousarray(np.asarray(inp["nsa_wk1"][l], np.float32).reshape(32, 64, 64).transpose(1, 0, 2))
    w["wv1"] = np.ascontiguousarray(np.asarray(inp["nsa_wv1"][l], np.float32).reshape(32, 64, 64).transpose(1, 0, 2))
    w["wk2"] = np.asarray(inp["nsa_wk2"][l], np.float32)
    w["wv2"] = np.asarray(inp["nsa_wv2"][l], np.float32)
    w["qn"] = np.ascontiguousarray(np.asarray(inp["mla_q_norm"][l], np.float32).reshape(3, 128).T)
    w["kvn"] = np.ascontiguousarray(np.asarray(inp["mla_kv_norm"][l], np.float32).reshape(2, 128).T)
    wq = np.asarray(inp["mla_w_q_b"][l], np.float32).reshape(384, 8, 96)
    wqp = wq.copy()
    pi = perm_half(np.arange(64, 96), 32)
    wqp[:, :, 64:96] = wq[:, :, pi]
    w["wq"] = np.ascontiguousarray(wq.reshape(3, 128, 8 * 96).transpose(1, 0, 2))
    w["wqp"] = np.ascontiguousarray(wqp.reshape(3, 128, 8 * 96).transpose(1, 0, 2))
    wkv = np.asarray(inp["mla_w_kv_b"][l], np.float32).reshape(256, 8, 128)
    w["wkvk"] = np.ascontiguousarray(wkv[:, :, :64].reshape(2, 128, 512).transpose(1, 0, 2))
    w["wkvv"] = np.ascontiguousarray(wkv[:, :, 64:].reshape(2, 128, 512).transpose(1, 0, 2))
    for nm, key in (("wbra", "w_br_a"), ("wbrb", "w_br_b"), ("wbrc", "w_br_c")):
        w[nm] = np.ascontiguousarray(np.asarray(inp[key][l], np.float32).reshape(8, 64, 1024).transpose(1, 0, 2))
    w["wout"] = np.ascontiguousarray(np.asarray(inp["w_out"][l], np.float32).reshape(8, 128, 1024).transpose(1, 0, 2))
    w["wxq"] = np.ascontiguousarray(np.asarray(inp["w_xq"][l], np.float32).reshape(8, 128, 512).transpose(1, 0, 2))
    w["wxkv"] = np.ascontiguousarray(np.asarray(inp["w_xkv"][l], np.float32).reshape(8, 128, 1024).transpose(1, 0, 2))
    w["wxo"] = np.ascontiguousarray(np.asarray(inp["w_xo"][l], np.float32).reshape(4, 128, 1024).transpose(1, 0, 2))
    w["wgu"] = np.ascontiguousarray(np.asarray(inp["w_gate_up"][l], np.float32).reshape(8, 128, 2 * DFF).transpose(1, 0, 2))
    w["wdn"] = np.ascontiguousarray(np.asarray(inp["w_down"][l], np.float32).reshape(22, 128, 1024).transpose(1, 0, 2))
    return w


WSHAPES = None


def build(T, L, wshapes, cshapes, stop_after=None, debug=False):
    NT = T // 128
    NS = T // 512
    NCT = (T // 16 - 1 + 127) // 128
    n_cmp = (T - 32) // 16 + 1
    nc = bass.Bass("TRN2", target_bir_lowering=False)
    S = Sched(nc)

    def din(name, shape, dt=F32):
        return nc.dram_tensor(name, list(shape), dt, kind="ExternalInput").ap()

    def dscr(name, shape, dt):
        return nc.dram_tensor(name, list(shape), dt, kind=("ExternalOutput" if debug else "Internal")).ap()

    x_in = din("x", [T, D])
    mem_in = din("mem", [256, D])
    pos_in = din("pos", [1, T], I32)
    nfin_in = din("nfin", [1, D])
    C = {k: din("c_" + k, v[0], BF16 if v[1] == "bf16" else F32) for k, v in cshapes.items()}
    Wt = [{k: din("w%d_%s" % (l, k), shp) for k, shp in wshapes.items()} for l in range(L)]
    y_out = nc.dram_tensor("y", [T, D], F32, kind="ExternalOutput").ap()

    xres = dscr("xres", [T, D], F32)
    rcA, rsA, rcM, rsM = (dscr(n, [128, T], F32) for n in ("rcA", "rsA", "rcM", "rsM"))
    QA = dscr("QA", [64, 8, T], BF16)
    KA = dscr("KA", [64, 2, T], BF16)
    VA = dscr("VA", [T, 2, 128], BF16)
    QBU = dscr("QBU", [64, 8, T], BF16)
    QBR = dscr("QBR", [64, 8, T], BF16)
    KC = dscr("KC", [64, 2, T], BF16)
    VC = dscr("VC", [64, 2, T], BF16)
    KBS = dscr("KBS", [64, 2, T], BF16)
    VBS = dscr("VBS", [T, 2, 128], BF16)
    KBW = dscr("KBW", [64, 2, T], BF16)
    VBW = dscr("VBW", [T, 2, 128], BF16)
    GB = dscr("GB", [8, 3, T], F32)
    QLAT = dscr("QLAT", [128, 3, T], BF16)
    CKV = dscr("CKV", [128, 2, T], BF16)
    KPE = dscr("KPE", [32, T], BF16)
    OA = dscr("OA", [64, 8, T], BF16)
    OBP = dscr("OBP", [64, 8, T], F32)
    OB = dscr("OB", [64, 8, T], BF16)
    OC = dscr("OC", [64, 8, T], BF16)
    ACTD = dscr("ACTD", [128, 22, T], BF16)
    KCMP = dscr("KCMP", [64, 2, NCT * 128], BF16)
    VCMP = dscr("VCMP", [128, NCT, 2, 128], BF16)

    PS = [nc.alloc_psum_tensor("ps%d" % i, [128, 512], F32).ap() for i in range(8)]
    psr = [0]

    def ps_next():
        i = psr[0]
        psr[0] = (i + 1) % 8
        return i

    def PR(i):
        return ("ps", i)

    def sb(st, name, shape, dt):
        return st.enter_context(nc.sbuf_tensor(name, list(shape), dt)).ap() if False else nc_alloc(st, name, shape, dt)

    acnt = [0]

    def nc_alloc(st, name, shape, dt):
        acnt[0] += 1
        g = nc.sbuf_tensor("sb%d_%s" % (acnt[0], name), list(shape), dt)
        t = st.enter_context(g)
        return t.ap() if hasattr(t, "ap") and callable(t.ap) else t

    uid = [0]

    def u():
        uid[0] += 1
        return uid[0]

    def wload(dst, src, reg, pieces=1, axis=None):
        if pieces == 1:
            S.dma(lambda e: e.dma_start(out=dst, in_=src), R=(), W=(reg,), q="pool")
        else:
            n = dst.shape[1]
            step = (n + pieces - 1) // pieces
            for a in range(0, n, step):
                b = min(n, a + step)
                S.dma(lambda e, a=a, b=b: e.dma_start(out=dst[:, a:b], in_=src[:, a:b]), R=(), W=((reg, a),), q="pool")

    with ExitStack() as st:
        TWO_PI = 2.0 * np.pi
        c1 = float(np.float32(6.28125))
        c2 = float(np.float32(np.float32(TWO_PI - 6.28125).view(np.uint32) & np.uint32(0xFFFFF000)).view(np.float32)) if False else None
        r2 = TWO_PI - 6.28125
        c2 = float(np.array(np.array(r2, np.float32).view(np.uint32) & np.uint32(0xFFFFF000), np.uint32).view(np.float32))
        c3 = float(np.float32(r2 - c2))
        MAGIC = 12582912.0
        PIS = 3.1415925
        CH = min(T, 2048)
        ropep = nc_alloc(st, "ropep", [128, 4], F32)
        S.dma(lambda e: e.dma_start(out=ropep, in_=C["ropep"]), W=("ropep",))
        posi = nc_alloc(st, "posi", [128, CH], I32)
        posf = nc_alloc(st, "posf", [128, CH], F32)
        ang = nc_alloc(st, "ang", [128, CH], F32)
        kk = nc_alloc(st, "kk", [128, CH], F32)
        rr = nc_alloc(st, "rr", [128, CH], F32)
        sn = nc_alloc(st, "sn", [128, CH], F32)
        cs_ = nc_alloc(st, "cs", [128, CH], F32)
        xt = [nc_alloc(st, "xcp%d" % i, [128, 4, D], F32) for i in range(2)]
        xv = x_in.rearrange("(s j p) c -> s p j c", p=128, j=4)
        xrv = xres.rearrange("(s j p) c -> s p j c", p=128, j=4)
        for s in range(NS):
            b = xt[s % 2]
            S.dma(lambda e, b=b, s=s: e.dma_start(out=b, in_=xv[s]), W=(("xcp", s % 2),))
            S.dma(lambda e, b=b, s=s: e.dma_start(out=xrv[s], in_=b), R=(("xcp", s % 2),), W=(("xres", s),), q="pool")
        for c0 in range(0, T, CH):
            S.dma(lambda e, c0=c0: e.dma_start(out=posi, in_=pos_in[:, c0:c0 + CH].broadcast_to([128, CH])), W=("posi",))
            S.op("dve", lambda e: e.tensor_copy(out=posf, in_=posi), R=("posi",), W=("posf",))
            for (fc, sc, dc, ds) in ((0, 1, rcA, rsA), (2, 3, rcM, rsM)):
                S.op("dve", lambda e, fc=fc: e.tensor_scalar(out=ang, in0=posf, scalar1=ropep[:, fc:fc + 1], scalar2=None, op0=ALU.mult),
                     R=("posf", "ropep"), W=("ang",))
                S.op("dve", lambda e: e.tensor_scalar(out=kk, in0=ang, scalar1=float(1.0 / TWO_PI), scalar2=MAGIC, op0=ALU.mult, op1=ALU.add),
                     R=("ang",), W=("kk",))
                S.op("dve", lambda e: e.tensor_scalar(out=kk, in0=kk, scalar1=MAGIC, scalar2=None, op0=ALU.subtract),
                     R=("kk",), W=("kk",))
                S.op("dve", lambda e: e.scalar_tensor_tensor(out=rr, in0=kk, scalar=-c1, in1=ang, op0=ALU.mult, op1=ALU.add),
                     R=("kk", "ang"), W=("rr",))
                S.op("dve", lambda e: e.scalar_tensor_tensor(out=rr, in0=kk, scalar=-c2, in1=rr, op0=ALU.mult, op1=ALU.add),
                     R=("kk", "rr"), W=("rr",))
                S.op("dve", lambda e: e.scalar_tensor_tensor(out=rr, in0=kk, scalar=-c3, in1=rr, op0=ALU.mult, op1=ALU.add),
                     R=("kk", "rr"), W=("rr",))
                S.op("dve", lambda e: e.tensor_scalar(out=rr, in0=rr, scalar1=-PIS, scalar2=PIS, op0=ALU.max, op1=ALU.min),
                     R=("rr",), W=("rr",))
                S.op("act", lambda e: e.activation(out=sn, in_=rr, func=AF.Sin), R=("rr",), W=("sn",))
                S.op("dve", lambda e, sc=sc: e.tensor_scalar(out=sn, in0=sn, scalar1=ropep[:, sc:sc + 1], scalar2=None, op0=ALU.mult),
                     R=("sn", "ropep"), W=("sn",))
                S.dma(lambda e, ds=ds, c0=c0: e.dma_start(out=ds[:, c0:c0 + CH], in_=sn), R=("sn",), W=(("rope", u()),), q="pool")
                S.op("dve", lambda e: e.scalar_tensor_tensor(out=cs_, in0=rr, scalar=-1.0, in1=rr, op0=ALU.mult, op1=ALU.max), R=("rr",), W=("cs",))
                S.op("dve", lambda e: e.tensor_scalar(out=cs_, in0=cs_, scalar1=-1.0, scalar2=float(np.pi / 2), op0=ALU.mult, op1=ALU.add),
                     R=("cs",), W=("cs",))
                S.op("act", lambda e: e.activation(out=cs_, in_=cs_, func=AF.Sin), R=("cs",), W=("cs",))
                S.dma(lambda e, dc=dc, c0=c0: e.dma_start(out=dc[:, c0:c0 + CH], in_=cs_), R=("cs",), W=(("rope", u()),), q="pool")
        S.finish()

    def norm_A(xt, xreg, tmp, ntok_tiles=4):
        for j in range(ntok_tiles):
            S.op("act", lambda e, j=j: e.activation(out=tmp["sq"], in_=xt[:, j, :], func=AF.Square, accum_out=tmp["ss"][:, j:j + 1]),
                 R=(xreg,), W=("nt_sq", ("nt_ss", j)))
        ssr = tuple(("nt_ss", j) for j in range(ntok_tiles))
        S.op("act", lambda e: e.activation(out=tmp["ss"][:, 0:ntok_tiles], in_=tmp["ss"][:, 0:ntok_tiles], func=AF.Sqrt, scale=1.0 / D, bias=EPS),
             R=ssr, W=ssr)
        S.op("dve", lambda e: e.reciprocal(out=tmp["ss"][:, 0:ntok_tiles], in_=tmp["ss"][:, 0:ntok_tiles]), R=ssr, W=ssr)
        for j in range(ntok_tiles):
            S.op("dve", lambda e, j=j: e.tensor_scalar(out=tmp["hs"][:, j, :], in0=xt[:, j, :], scalar1=tmp["ss"][:, j:j + 1], scalar2=None, op0=ALU.mult),
                 R=(xreg, ("nt_ss", j)), W=(("nt_hs", j),))

    def norm_B(hT, hreg, gcol, gcreg, tmp, ntok_tiles=4):
        for k in range(8):
            b = ps_next()
            for j in range(ntok_tiles):
                S.op("pe", lambda e, k=k, j=j, b=b: e.transpose(out=PS[b][:, j * 128:(j + 1) * 128], in_=tmp["hs"][:, j, k * 128:(k + 1) * 128], identity=tmp["ident"]),
                     R=(("nt_hs", j), "ident_f"), W=(PR(b),))
            n = ntok_tiles * 128
            S.op("dve" if k % 2 == 0 else "act",
                 (lambda e, k=k, b=b, n=n: e.tensor_scalar(out=hT[:, k, 0:n], in0=PS[b][:, 0:n], scalar1=gcol[:, k:k + 1], scalar2=None, op0=ALU.mult))
                 if k % 2 == 0 else
                 (lambda e, k=k, b=b, n=n: e.activation(out=hT[:, k, 0:n], in_=PS[b][:, 0:n], func=AF.Copy, scale=gcol[:, k:k + 1])),
                 R=(PR(b), gcreg), W=((hreg, k),))

    def norm_T(xt, xreg, hT, hreg, gcol, gcreg, tmp, ntok_tiles=4):
        norm_A(xt, xreg, tmp, ntok_tiles)
        norm_B(hT, hreg, gcol, gcreg, tmp, ntok_tiles)

    def norm_tmp(st):
        t = {}
        t["sq"] = nc_alloc(st, "nt_sq", [128, D], F32)
        t["ss"] = nc_alloc(st, "nt_ss", [128, 4], F32)
        t["hs"] = nc_alloc(st, "nt_hs", [128, 4, D], F32)
        t["ident"] = nc_alloc(st, "ident_f", [128, 128], F32)
        S.dma(lambda e: e.dma_start(out=t["ident"], in_=C["ident_f"]), W=("ident_f",))
        return t

    xrv = xres.rearrange("(s j p) c -> s p j c", p=128, j=4)
    if stop_after == "P0":
        S.finish()
        return nc

    for l in range(L):
        W = Wt[l]
        with ExitStack() as st:
            S.barrier()
            tmp = norm_tmp(st)
            wfm = nc_alloc(st, "wfm", [128, 8, NFM * 128], BF16)
            wtm = nc_alloc(st, "wtm", [128, 8, 384], BF16)
            for k in range(8):
                S.dma(lambda e, k=k: e.dma_start(out=wfm[:, k, :], in_=W["wfm"][k * 128:(k + 1) * 128, :]), W=(("wfm", k),), q="pool")
                S.dma(lambda e, k=k: e.dma_start(out=wtm[:, k, :], in_=W["wtm"][k * 128:(k + 1) * 128, :]), W=(("wtm", k),), q="pool")
            wfr = tuple(("wfm", k) for k in range(8))
            wtr = tuple(("wtm", k) for k in range(8))
            gmix = nc_alloc(st, "gmix", [128, 8], F32)
            S.dma(lambda e: e.dma_start(out=gmix, in_=W["nmix"]), W=("gmix",))
            qn = nc_alloc(st, "qn", [128, 3], F32)
            kvn = nc_alloc(st, "kvn", [128, 2], F32)
            S.dma(lambda e: e.dma_start(out=qn, in_=W["qn"]), W=("qn",))
            S.dma(lambda e: e.dma_start(out=kvn, in_=W["kvn"]), W=("kvn",))
            ones_f = nc_alloc(st, "ones_f", [128, 128], F32)
            S.dma(lambda e: e.dma_start(out=ones_f, in_=C["ones_f"]), W=("ones_f",))
            xts = [nc_alloc(st, "p1x%d" % i, [128, 4, D], F32) for i in range(2)]
            hT = nc_alloc(st, "p1hT", [128, 8, 512], BF16)
            tab = [nc_alloc(st, "p1tab%d" % i, [128, 512], F32) for i in range(4)]
            t1 = [nc_alloc(st, "p1t1_%d" % i, [128, 512], F32) for i in range(2)]
            t2 = [nc_alloc(st, "p1t2_%d" % i, [128, 512], F32) for i in range(2)]
            ob = [nc_alloc(st, "p1ob%d" % i, [128, 512], BF16) for i in range(4)]
            lat = [nc_alloc(st, "p1lat%d" % i, [128, 512], F32) for i in range(3)]
            sqb = nc_alloc(st, "p1sqb", [128, 512], F32)
            rstd = nc_alloc(st, "p1rstd", [128, 512], F32)
            gsb = nc_alloc(st, "p1gsb", [128, 512], F32)
            vst = [nc_alloc(st, "p1vst%d" % i, [128, 6, 128], BF16) for i in range(2)]
            for i in range(2):
                S.op("pool", lambda e, i=i: e.memset(vst[i], 1.0), W=(("vst", i),))
            obc = [0]

            def next_ob():
                i = obc[0]
                obc[0] = (i + 1) % 4
                return i

            def proj_fm(gname):
                gi = FMI[gname]
                b = ps_next()
                for k in range(8):
                    S.op("pe", lambda e, k=k, b=b, gi=gi: e.matmul(PS[b], lhsT=wfm[:, k, gi * 128:(gi + 1) * 128], rhs=hT[:, k, :], start=(k == 0), stop=(k == 7)),
                         R=(("wfm", k), ("hT", k)), W=(PR(b),))
                return b

            tc = [0]
            def p1_load(s):
                S.dma(lambda e: e.dma_start(out=xts[s % 2], in_=xrv[s]), R=(("xres", s),), W=(("p1x", s % 2),))

            p1_load(0)
            norm_A(xts[0], ("p1x", 0), tmp)
            for s in range(NS):
                xt = xts[s % 2]
                xr = ("p1x", s % 2)
                if s + 1 < NS:
                    p1_load(s + 1)
                for i, tsrc in enumerate((rcA, rsA, rcM, rsM)):
                    S.dma(lambda e, i=i, tsrc=tsrc, s=s: e.dma_start(out=tab[i], in_=tsrc[:, s * 512:(s + 1) * 512]), W=(("tab", i),))
                norm_B(hT, "hT", gmix, "gmix", tmp)
                if s + 1 < NS:
                    norm_A(xts[(s + 1) % 2], ("p1x", (s + 1) % 2), tmp)
                tsl = slice(s * 512, (s + 1) * 512)

                def rope_group(nm, dsts, ci=0, si=1, rows=None):
                    ba = proj_fm(nm)
                    bp = proj_fm(nm + "P") if not nm.startswith("ckr") else proj_fm("ckrP")
                    i = tc[0] % 2
                    tc[0] += 1
                    o = next_ob()
                    S.op("dve", lambda e, ba=ba, i=i: e.tensor_tensor(out=t1[i], in0=PS[ba], in1=tab[ci], op=ALU.mult),
                         R=(PR(ba), ("tab", ci)), W=(("t1", i),))
                    S.op("dve", lambda e, bp=bp, i=i: e.tensor_tensor(out=t2[i], in0=PS[bp], in1=tab[si], op=ALU.mult),
                         R=(PR(bp), ("tab", si)), W=(("t2", i),))
                    S.op("pool", lambda e, i=i, o=o: e.tensor_tensor(out=ob[o], in0=t1[i], in1=t2[i], op=ALU.add),
                         R=(("t1", i), ("t2", i)), W=(("ob", o),))
                    emit_out(dsts, o)

                def emit_out(dsts, o):
                    if len(dsts) == 1 and dsts[0][1] is None:
                        dst = dsts[0][0]
                        S.dma(lambda e, dst=dst, o=o: e.dma_start(out=dst, in_=ob[o]), R=(("ob", o),), W=(("p1out", u()),))
                    else:
                        for (dst, psl) in dsts:
                            S.dma(lambda e, dst=dst, psl=psl, o=o: e.dma_start(out=dst, in_=ob[o][psl, :]), R=(("ob", o),), W=(("p1out", u()),))

                def plain_group(nm, dsts):
                    b = proj_fm(nm)
                    o = next_ob()
                    S.op("act", lambda e, b=b, o=o: e.copy(out=ob[o], in_=PS[b]), R=(PR(b),), W=(("ob", o),))
                    emit_out(dsts, o)

                lo, hi = slice(0, 64), slice(64, 128)

                def both(Dt, i):
                    return [(Dt[:, 2 * i, tsl], lo), (Dt[:, 2 * i + 1, tsl], hi)]

                for i in range(4):
                    rope_group("aq%d" % i, both(QA, i))
                rope_group("ak", both(KA, 0))
                for i in range(4):
                    rope_group("bq%d" % i, both(QBR, i))
                    plain_group("bq%d" % i, both(QBU, i))
                rope_group("bks", both(KBS, 0))
                rope_group("bkw", both(KBW, 0))
                plain_group("bkc", both(KC, 0))
                plain_group("bvc", both(VC, 0))
                rope_group("ckr", [(KPE[:, tsl], slice(64, 96))], ci=2, si=3)
                b = proj_fm("bg")
                S.op("act", lambda e, b=b: e.activation(out=gsb[0:24, :], in_=PS[b][0:24, :], func=AF.Sigmoid), R=(PR(b),), W=("gsb",))
                S.dma(lambda e, tsl=tsl: e.dma_start(out=GB.rearrange("h j t -> (h j) t")[:, tsl], in_=gsb[0:24, :]), R=("gsb",), W=(("p1out", u()),), q="pool")

                def latent(names, gcol, gcreg, dstT, nfeat):
                    n = len(names)
                    bsum = ps_next()
                    for i, nm in enumerate(names):
                        b = proj_fm(nm)
                        S.op("act", lambda e, b=b, i=i: e.copy(out=lat[i], in_=PS[b]), R=(PR(b),), W=(("lat", i),))
                        S.op("act", lambda e, b=b: e.activation(out=sqb, in_=PS[b], func=AF.Square), R=(PR(b),), W=("sqb",))
                        S.op("pe", lambda e, i=i, bsum=bsum: e.matmul(PS[bsum], lhsT=ones_f, rhs=sqb, start=(i == 0), stop=(i == n - 1)),
                             R=("ones_f", "sqb"), W=(PR(bsum),))
                    S.op("act", lambda e, bsum=bsum: e.activation(out=rstd, in_=PS[bsum], func=AF.Sqrt, scale=1.0 / nfeat, bias=EPS),
                         R=(PR(bsum),), W=("rstd",))
                    S.op("dve", lambda e: e.reciprocal(out=rstd, in_=rstd), R=("rstd",), W=("rstd",))
                    for i in range(n):
                        o = next_ob()
                        S.op("dve", lambda e, i=i, o=o: e.scalar_tensor_tensor(out=ob[o], in0=lat[i], scalar=gcol[:, i:i + 1], in1=rstd, op0=ALU.mult, op1=ALU.mult),
                             R=(("lat", i), gcreg, "rstd"), W=(("ob", o),))
                        S.dma(lambda e, i=i, o=o, tsl=tsl: e.dma_start(out=dstT[:, i, tsl], in_=ob[o]), R=(("ob", o),), W=(("p1out", u()),), q="pool")

                latent(["cqa0", "cqa1", "cqa2"], qn, "qn", QLAT, 384)
                latent(["ckv0", "ckv1"], kvn, "kvn", CKV, 256)
                vi = s % 2
                for j in range(4):
                    b = ps_next()
                    for k in range(8):
                        S.op("pe", lambda e, k=k, b=b, j=j: e.matmul(PS[b][:, 0:384], lhsT=hT[:, k, j * 128:(j + 1) * 128], rhs=wtm[:, k, :], start=(k == 0), stop=(k == 7)),
                             R=(("hT", k), ("wtm", k)), W=(PR(b),))
                    S.op("act", lambda e, b=b, vi=vi: e.copy(out=vst[vi][:, :, 0:64], in_=PS[b][:, 0:384].rearrange("p (a c) -> p a c", c=64)),
                         R=(PR(b),), W=(("vst", vi),))
                    tok = slice(s * 512 + j * 128, s * 512 + (j + 1) * 128)
                    for m, dst in enumerate((VA, VBS, VBW)):
                        S.dma(lambda e, m=m, dst=dst, tok=tok, vi=vi: e.dma_start(out=dst[tok, :, :], in_=vst[vi][:, 2 * m:2 * m + 2, :]),
                              R=(("vst", vi),), W=(("p1out", u()),), q="pool")
            S.flush()
        if stop_after == "P1":
            break

        def bc4(ap):
            return ap.unsqueeze(1).broadcast_to([ap.shape[0], 4, 128])

        def load_consts_att(st, names):
            d = {}
            for nm in names:
                shp = cshapes[nm][0]
                dt = BF16 if cshapes[nm][1] == "bf16" else F32
                d[nm] = nc_alloc(st, "c_" + nm, shp, dt)
                S.dma(lambda e, nm=nm: e.dma_start(out=d[nm], in_=C[nm]), W=("c_" + nm,))
            return d

        class Att:
            def __init__(self, st, sbanks, nE=None):
                self.sb = sbanks
                self.sc = 0
                self.ec = 0
                self.depth = max(1, len(sbanks) - 1)
                nE = nE or (self.depth + 3)
                self.E = [nc_alloc(st, "attE%d" % i, [128, 512], BF16) for i in range(nE)]
                self.pend = []

            def tile(self, kT, kreg, q, qreg, masks, v, vreg, ob, first, last, scale, extra=None):
                b = self.sb[self.sc % len(self.sb)]
                self.sc += 1
                n = len(masks)
                kr = list(kreg) if isinstance(kreg, list) else [kreg]
                qr_ = list(qreg) if isinstance(qreg, list) else [qreg]
                S.op("pe", lambda e: e.matmul(PS[b], lhsT=kT, rhs=q, start=True, stop=(n == 0)), R=tuple(kr + qr_), W=(PR(b),))
                for i, (ml, mr, mregs) in enumerate(masks):
                    S.op("pe", lambda e, ml=ml, mr=mr, i=i: e.matmul(PS[b], lhsT=ml, rhs=mr, start=False, stop=(i == n - 1)), R=tuple(mregs), W=(PR(b),))
                ei = self.ec % len(self.E)
                self.ec += 1
                Et = self.E[ei]
                S.op("act", lambda e: e.activation(out=Et, in_=PS[b], func=AF.Exp, scale=scale), R=(PR(b),), W=(("E", ei),))

                def pv():
                    S.op("pe", lambda e: e.matmul(PS[ob], lhsT=v, rhs=Et, start=first, stop=last), R=(vreg, ("E", ei)), W=(PR(ob),))
                    if extra is not None:
                        extra(Et, ("E", ei))
                self.pend.append(pv)
                while len(self.pend) > self.depth:
                    self.pend.pop(0)()

            def drain(self):
                while self.pend:
                    self.pend.pop(0)()

        def fin_den(ob, den, dreg, sink=None, gate=None, greg=None):
            S.op("dve", lambda e: e.tensor_scalar(out=den, in0=PS[ob][64:128, :], scalar1=1e-30, scalar2=None, op0=ALU.max), R=(PR(ob),), W=(dreg,))
            if sink is not None:
                S.op("dve", lambda e: e.tensor_tensor(out=den.rearrange("p (h q) -> p h q", h=4), in0=den.rearrange("p (h q) -> p h q", h=4),
                                                      in1=sink.unsqueeze(2).broadcast_to([64, 4, 128]), op=ALU.add), R=(dreg, "esink"), W=(dreg,))
            S.op("dve", lambda e: e.reciprocal(out=den, in_=den), R=(dreg,), W=(dreg,))
            if gate is not None:
                S.op("pool", lambda e: e.tensor_tensor(out=den, in0=den, in1=gate, op=ALU.mult), R=(dreg, greg), W=(dreg,))

        qv = lambda Q, qb: Q[:, :, qb * 128:(qb + 1) * 128]

        def window_pass(tag, Qd, Kd, Vd, wt, sink, gate_j, part_in, Od):
            with ExitStack() as st:
                S.barrier()
                cc = load_consts_att(st, ["ident_b", "cb", "bb"])
                att = Att(st, [0, 1, 2, 3])
                kA = nc_alloc(st, tag + "k", [64, 2, T], BF16)
                vA = nc_alloc(st, tag + "v", [128, NT, 2, 128], BF16)
                for g in range(2):
                    S.dma(lambda e, g=g: e.dma_start(out=kA[:, g, :], in_=Kd[:, g, :]), W=((tag + "k", g),))
                Vr = Vd.rearrange("(n p) g c -> p n g c", p=128)
                for n0 in range(0, NT, 8):
                    S.dma(lambda e, n0=n0: e.dma_start(out=vA[:, n0:min(NT, n0 + 8)], in_=Vr[:, n0:min(NT, n0 + 8)]), W=((tag + "v", n0),))
                if sink:
                    es = nc_alloc(st, "esink", [64, 8], F32)
                    S.dma(lambda e: e.dma_start(out=es, in_=W["sinks"].broadcast_to([64, 8])), W=("esink",))
                    S.op("act", lambda e: e.activation(out=es, in_=es, func=AF.Exp), R=("esink",), W=("esink",))
                qts = [nc_alloc(st, tag + "q%d" % i, [64, 8, 128], BF16) for i in range(3)]
                dens = [nc_alloc(st, tag + "den%d" % i, [64, 512], F32) for i in range(2)]
                outs = [nc_alloc(st, tag + "out%d" % i, [64, 512], BF16) for i in range(2)]
                if gate_j is not None:
                    gts = [nc_alloc(st, tag + "gt%d" % i, [64, 4, 128], F32) for i in range(2)]
                    pts = [nc_alloc(st, tag + "pt%d" % i, [64, 4, 128], F32) for i in range(2)]
                    tmps = [nc_alloc(st, tag + "tmp%d" % i, [64, 512], F32) for i in range(2)]
                uc = 0
                for qb in range(NT):
                    qt = qts[qb % 3]
                    qr = (tag + "q", qb % 3)
                    S.dma(lambda e, qt=qt, qb=qb: e.dma_start(out=qt, in_=qv(Qd, qb)), W=(qr,))
                    for g in range(2):
                        i2 = uc % 2
                        ob = 4 + i2
                        uc += 1
                        qsl = slice(qb * 128, (qb + 1) * 128)
                        if gate_j is not None:
                            gt, pt = gts[i2], pts[i2]
                            S.dma(lambda e, gt=gt, g=g, qsl=qsl: e.dma_start(out=gt, in_=GB[4 * g:4 * g + 4, gate_j, qsl].unsqueeze(0).broadcast_to([64, 4, 128])), W=((tag + "gt", i2),))
                            S.dma(lambda e, pt=pt, g=g, qsl=qsl: e.dma_start(out=pt, in_=part_in[:, 4 * g:4 * g + 4, qsl]), W=((tag + "pt", i2),))
                        kts = [kt for kt in range(qb - wt, qb + 1) if kt >= 0]
                        for ti, kt in enumerate(kts):
                            masks = []
                            if kt == qb - wt:
                                masks.append((cc["ident_b"], bc4(cc["bb"]), ("c_ident_b", "c_bb")))
                            if kt == qb:
                                masks.append((cc["ident_b"], bc4(cc["cb"]), ("c_ident_b", "c_cb")))
                            att.tile(kA[:, g, kt * 128:(kt + 1) * 128], (tag + "k", g), qt[:, 4 * g:4 * g + 4, :], qr, masks,
                                     vA[:, kt, g, :], (tag + "v", (kt // 8) * 8), ob, ti == 0, ti == len(kts) - 1, 0.125)
                        att.drain()
                        den, dreg = dens[i2], (tag + "den", i2)
                        out, oreg = outs[i2], (tag + "out", i2)
                        if gate_j is None:
                            fin_den(ob, den, dreg, sink=(es[:, 4 * g:4 * g + 4] if sink else None))
                            S.op("dve", lambda e, ob=ob, den=den, out=out: e.tensor_tensor(out=out, in0=PS[ob][0:64, :], in1=den, op=ALU.mult),
                                 R=(PR(ob), dreg), W=(oreg,))
                        else:
                            fin_den(ob, den, dreg, gate=gt.rearrange("p h q -> p (h q)"), greg=(tag + "gt", i2))
                            tmp = tmps[i2]
                            S.op("dve", lambda e, ob=ob, den=den, tmp=tmp: e.tensor_tensor(out=tmp, in0=PS[ob][0:64, :], in1=den, op=ALU.mult),
                                 R=(PR(ob), dreg), W=((tag + "tmp", i2),))
                            S.op("pool", lambda e, tmp=tmp, pt=pt, out=out: e.tensor_tensor(out=out, in0=tmp, in1=pt.rearrange("p h q -> p (h q)"), op=ALU.add),
                                 R=((tag + "tmp", i2), (tag + "pt", i2)), W=(oreg,))
                        S.dma(lambda e, out=out, g=g, qsl=qsl: e.dma_start(out=Od[:, 4 * g:4 * g + 4, qsl], in_=out.rearrange("p (h q) -> p h q", h=4)),
                              R=(oreg,), W=(("wout", u()),), q="pool")
                S.flush()

        window_pass("pa", QA, KA, VA, 1, True, None, None, OA)
        if stop_after == "PA":
            break

        with ExitStack() as st:
            S.barrier()
            kc = nc_alloc(st, "kc", [64, 4, T], BF16)
            for g in range(2):
                S.dma(lambda e, g=g: e.dma_start(out=kc[:, g, :], in_=KC[:, g, :]), W=(("kc", g),))
                S.dma(lambda e, g=g: e.dma_start(out=kc[:, 2 + g, :], in_=VC[:, g, :]), W=(("kc", 2 + g),))
            w1 = nc_alloc(st, "w1", [64, 2, 32 * 64], BF16)
            S.dma(lambda e: e.dma_start(out=w1[:, 0, :], in_=W["wk1"].rearrange("d l o -> d (l o)")), W=("w1",), q="pool")
            S.dma(lambda e: e.dma_start(out=w1[:, 1, :], in_=W["wv1"].rearrange("d l o -> d (l o)")), W=("w1b",), q="pool")
            w2 = nc_alloc(st, "w2", [64, 2, 64], BF16)
            S.dma(lambda e: e.dma_start(out=w2[:, 0, :], in_=W["wk2"]), W=("w2",), q="pool")
            S.dma(lambda e: e.dma_start(out=w2[:, 1, :], in_=W["wv2"]), W=("w2b",), q="pool")
            pe = nc_alloc(st, "pe", [64, 2, 32], BF16)
            S.dma(lambda e: e.dma_start(out=pe[:, 0, :], in_=W["pek"]), W=("pe",), q="pool")
            S.dma(lambda e: e.dma_start(out=pe[:, 1, :], in_=W["pev"]), W=("peb",), q="pool")
            NC_ = NCT * 128
            hid = nc_alloc(st, "hid", [64, NC_], BF16)
            kcmp = nc_alloc(st, "kcmp", [64, 2, NC_], BF16)
            vcmp = nc_alloc(st, "vcmp", [128, NCT, 2, 128], BF16)
            S.op("pool", lambda e: e.memset(vcmp, 1.0), W=("vcmp",))
            S.op("pool", lambda e: e.memset(hid, 0.0), W=("hid",))
            for kv in range(2):
                for g in range(2):
                    b = 6
                    src = kc[:, 2 * kv + g, :].rearrange("p (c r) -> p c r", r=16)
                    for li in range(32):
                        rhs = src[:, (li // 16):(li // 16) + n_cmp, li % 16]
                        S.op("pe", lambda e, l=li, rhs=rhs, kv=kv: e.matmul(PS[b][0:64, 0:n_cmp], lhsT=w1[:, kv, l * 64:(l + 1) * 64], rhs=rhs, start=(l == 0), stop=False),
                             R=(("kc", 2 * kv + g), "w1", "w1b"), W=(PR(b),))
                    for li in range(32):
                        S.op("pe", lambda e, l=li, kv=kv: e.matmul(PS[b][0:64, 0:n_cmp], lhsT=w1[:, kv, l * 64:(l + 1) * 64], rhs=pe[:, kv, l:l + 1].broadcast_to([64, n_cmp]), start=False, stop=(l == 31)),
                             R=("pe", "peb", "w1", "w1b"), W=(PR(b),))
                    S.op("act", lambda e: e.activation(out=hid[:, 0:n_cmp], in_=PS[b][0:64, 0:n_cmp], func=AF.Silu), R=(PR(b),), W=("hid",))
                    if kv == 0:
                        b2 = 7
                        S.op("pe", lambda e: e.matmul(PS[b2][0:64, 0:NC_], lhsT=w2[:, 0, :], rhs=hid, start=True, stop=True), R=("hid", "w2"), W=(PR(b2),))
                        S.op("act", lambda e, g=g: e.copy(out=kcmp[:, g, :], in_=PS[b2][0:64, 0:NC_]), R=(PR(b2),), W=(("kcmp", g),))
                    else:
                        b2 = 7
                        for ct in range(NCT):
                            S.op("pe", lambda e, ct=ct: e.matmul(PS[b2][:, ct * 64:(ct + 1) * 64], lhsT=hid[:, ct * 128:(ct + 1) * 128], rhs=w2[:, 1, :], start=(ct == 0), stop=(ct == NCT - 1)),
                                 R=("hid", "w2b"), W=(PR(b2),))
                        S.op("act", lambda e, g=g: e.copy(out=vcmp[:, :, g, 0:64], in_=PS[b2][:, 0:NCT * 64].rearrange("p (c d) -> p c d", d=64)), R=(PR(b2),), W=("vcmp",))
            S.dma(lambda e: e.dma_start(out=KCMP, in_=kcmp), R=(("kcmp", 0), ("kcmp", 1)), W=("KCMPd",))
            S.dma(lambda e: e.dma_start(out=VCMP, in_=vcmp), R=("vcmp",), W=("VCMPd",))
            S.flush()
        with ExitStack() as st:
            S.barrier()
            cc = load_consts_att(st, ["ident_b", "ident_f", "cb", "cm", "msel", "fb"])
            att = Att(st, [0, 1, 2])
            NC_ = NCT * 128
            kcmp = nc_alloc(st, "kcmp2", [64, 2, NC_], BF16)
            vcmp = nc_alloc(st, "vcmp2", [128, NCT, 2, 128], BF16)
            S.dma(lambda e: e.dma_start(out=kcmp, in_=KCMP), R=("KCMPd",), W=(("kcmp", 0), ("kcmp", 1)))
            S.dma(lambda e: e.dma_start(out=vcmp, in_=VCMP), R=("VCMPd",), W=("vcmp",))
            kS = nc_alloc(st, "pbk", [128, 2, T], BF16)
            for g in range(2):
                S.dma(lambda e, g=g: e.dma_start(out=kS[64:128, g, :], in_=C["exh"]), W=(("exh", g),))
            QS = [[nc_alloc(st, "QS%d_%d" % (i, hf), [128, 512], BF16) for hf in range(2)] for i in range(2)]
            vS = nc_alloc(st, "pbv", [128, NT, 2, 128], BF16)
            for g in range(2):
                S.dma(lambda e, g=g: e.dma_start(out=kS[0:64, g, :], in_=KBS[:, g, :]), W=(("pbk", g),))
            Vr = VBS.rearrange("(n p) g c -> p n g c", p=128)
            for n0 in range(0, NT, 8):
                S.dma(lambda e, n0=n0: e.dma_start(out=vS[:, n0:min(NT, n0 + 8)], in_=Vr[:, n0:min(NT, n0 + 8)]), W=(("pbv", n0),))
            qus = [nc_alloc(st, "pbqu%d" % i, [64, 8, 128], BF16) for i in range(3)]
            qrs = [nc_alloc(st, "pbqr%d" % i, [64, 8, 128], BF16) for i in range(3)]
            gts = [nc_alloc(st, "pbgt%d" % i, [64, 2, 4, 128], F32) for i in range(2)]
            dens = [nc_alloc(st, "pbden%d" % i, [64, 512], F32) for i in range(2)]
            rcmp = [nc_alloc(st, "pbrc%d" % i, [64, 512], F32) for i in range(2)]
            tmps = [nc_alloc(st, "pbtmp%d" % i, [64, 512], F32) for i in range(2)]
            outs = [nc_alloc(st, "pbout%d" % i, [64, 512], F32) for i in range(2)]
            rd4 = nc_alloc(st, "rd4", [128, 4], F32)
            scr = nc_alloc(st, "scr", [128, 128], F32)
            scr2 = nc_alloc(st, "scr2", [128, 128], F32)
            m8 = nc_alloc(st, "m8", [128, 16], F32)
            selms = [nc_alloc(st, "selm%d" % i, [128, 128], F32) for i in range(2)]
            selT = [nc_alloc(st, "selT%d" % i, [128, 128], BF16) for i in range(2)]
            units = [(qb, g) for qb in range(NT) for g in range(2)]

            def loadq(qb):
                i = qb % 3
                S.dma(lambda e: e.dma_start(out=qus[i], in_=qv(QBU, qb)), W=(("pbqu", i),))
                S.dma(lambda e: e.dma_start(out=qrs[i], in_=qv(QBR, qb)), W=(("pbqr", i),))

            def cmp_job(ui):
                qb, g = units[ui]
                i2 = ui % 2
                qsl = slice(qb * 128, (qb + 1) * 128)
                gt = gts[i2]
                for jj in range(2):
                    S.dma(lambda e, jj=jj: e.dma_start(out=gt[:, jj, :, :], in_=GB[4 * g:4 * g + 4, jj, qsl].unsqueeze(0).broadcast_to([64, 4, 128])), W=(("pbgt", i2, jj),))
                nct = min(NCT, (8 * qb + 7 + 127) // 128)
                ob = 4
                for ct in range(nct):
                    dl = qb - 16 * ct
                    masks = []
                    if dl <= 16:
                        masks.append((cc["ident_b"], bc4(cc["cm"][:, dl * 128:(dl + 1) * 128]), ("c_ident_b", "c_cm")))

                    def extra(Et, ereg, ct=ct):
                        for h in range(4):
                            bi = 5 + h // 2
                            co = (h % 2) * 129
                            S.op("pe", lambda e, h=h, bi=bi, co=co: e.matmul(PS[bi][:, co:co + 129], lhsT=Et[:, h * 128:(h + 1) * 128], rhs=cc["msel"][:, ct * 129:(ct + 1) * 129],
                                                                          start=(ct == 0 and h % 2 == 0), stop=(ct == nct - 1), skip_group_check=True),
                                 R=(ereg, "c_msel"), W=(PR(bi),))
                    att.tile(kcmp[:, g, ct * 128:(ct + 1) * 128], ("kcmp", g), qus[qb % 3][:, 4 * g:4 * g + 4, :], ("pbqu", qb % 3), masks,
                             vcmp[:, ct, g, :], "vcmp", ob, ct == 0, ct == nct - 1, 0.125, extra=extra)
                att.drain()
                den, dreg = dens[i2], ("pbden", i2)
                fin_den(ob, den, dreg, gate=gt[:, 0, :, :].rearrange("p h q -> p (h q)"), greg=("pbgt", i2, 0))
                S.op("dve", lambda e: e.tensor_tensor(out=rcmp[i2], in0=PS[ob][0:64, :], in1=den, op=ALU.mult), R=(PR(ob), dreg), W=(("pbrc", i2),))
                for h in range(4):
                    bi = 5 + h // 2
                    co = (h % 2) * 129 + 128
                    S.op("dve", lambda e, h=h, bi=bi, co=co: e.tensor_scalar(out=rd4[:, h:h + 1], in0=PS[bi][:, co:co + 1], scalar1=1e-30, scalar2=None, op0=ALU.max),
                         R=(PR(bi),), W=("rd4",))
                S.op("dve", lambda e: e.reciprocal(out=rd4, in_=rd4), R=("rd4",), W=("rd4",))
                for h in range(4):
                    bi = 5 + h // 2
                    co = (h % 2) * 129
                    if h == 0:
                        S.op("dve", lambda e, bi=bi, co=co: e.tensor_scalar(out=scr, in0=PS[bi][:, co:co + 128], scalar1=rd4[:, 0:1], scalar2=None, op0=ALU.mult),
                             R=(PR(bi), "rd4"), W=("scr",))
                    else:
                        S.op("dve", lambda e, h=h, bi=bi, co=co: e.scalar_tensor_tensor(out=scr, in0=PS[bi][:, co:co + 128], scalar=rd4[:, h:h + 1], in1=scr, op0=ALU.mult, op1=ALU.add),
                             R=(PR(bi), "rd4", "scr"), W=("scr",))
                S.op("dve", lambda e: e.tensor_tensor(out=scr, in0=scr, in1=cc["fb"][:, 128 - 2 * qb:256 - 2 * qb], op=ALU.add), R=("scr", "c_fb"), W=("scr",))
                S.op("dve", lambda e: e.tensor_scalar(out=scr[:, 0:1], in0=scr[:, 0:1], scalar1=1e4, scalar2=None, op0=ALU.add), R=("scr",), W=("scr",))
                S.op("dve", lambda e: e.max(out=m8[:, 0:8], in_=scr), R=("scr",), W=("m8",))
                S.op("dve", lambda e: e.match_replace(out=scr2, in_to_replace=m8[:, 0:8], in_values=scr, imm_value=-1e30), R=("scr", "m8"), W=("scr2",))
                S.op("dve", lambda e: e.max(out=m8[:, 8:16], in_=scr2), R=("scr2",), W=("m8b",))
                S.op("dve", lambda e: e.tensor_scalar(out=selms[i2], in0=scr, scalar1=m8[:, 15:16], scalar2=1.0, op0=ALU.is_ge, op1=ALU.subtract), R=("scr", "m8b"), W=(("selm", i2),))

            def cmp_job2(ui):
                qb, g = units[ui]
                i2 = ui % 2
                S.op("pe", lambda e: e.transpose(out=PS[7][:, 0:128], in_=selms[i2], identity=cc["ident_f"]), R=(("selm", i2), "c_ident_f"), W=(PR(7),))
                nh = 2 if qb >= 32 else 1
                for hf in range(nh):
                    S.op("dve", lambda e, hf=hf: e.tensor_copy(out=QS[i2][hf][64:128, :].rearrange("p (h q) -> p h q", h=4),
                                                              in_=PS[7][hf * 64:(hf + 1) * 64, 0:128].unsqueeze(1).broadcast_to([64, 4, 128])),
                         R=(PR(7),), W=(("QS", i2, hf, "m"),))
                    S.op("dve", lambda e, hf=hf: e.tensor_copy(out=QS[i2][hf][0:64, :].rearrange("p (h q) -> p h q", h=4), in_=qrs[qb % 3][:, 4 * g:4 * g + 4, :]),
                         R=(("pbqr", qb % 3),), W=(("QS", i2, hf, "q"),))

            def sel_job(ui, nxt=None):
                qb, g = units[ui]
                i2 = ui % 2
                ob = 3
                qsl = slice(qb * 128, (qb + 1) * 128)
                for kt in range(qb + 1):
                    hf = kt // 32
                    masks = []
                    if kt == qb:
                        masks.append((cc["ident_b"], bc4(cc["cb"]), ("c_ident_b", "c_cb")))
                    att.tile(kS[:, g, kt * 128:(kt + 1) * 128], [("pbk", g), ("exh", g)], QS[i2][hf], [("QS", i2, hf, "m"), ("QS", i2, hf, "q")], masks,
                             vS[:, kt, g, :], ("pbv", (kt // 8) * 8), ob, kt == 0, kt == qb, 0.125)
                att.drain()
                if nxt is not None:
                    cmp_job2(nxt)
                den, dreg = dens[i2], ("pbden", i2)
                fin_den(ob, den, dreg, gate=gts[i2][:, 1, :, :].rearrange("p h q -> p (h q)"), greg=("pbgt", i2, 1))
                S.op("dve", lambda e: e.tensor_tensor(out=tmps[i2], in0=PS[ob][0:64, :], in1=den, op=ALU.mult), R=(PR(ob), dreg), W=(("pbtmp", i2),))
                S.op("dve", lambda e: e.tensor_tensor(out=outs[i2], in0=tmps[i2], in1=rcmp[i2], op=ALU.add), R=(("pbtmp", i2), ("pbrc", i2)), W=(("pbout", i2),))
                S.dma(lambda e: e.dma_start(out=OBP[:, 4 * g:4 * g + 4, qsl], in_=outs[i2].rearrange("p (h q) -> p h q", h=4)), R=(("pbout", i2),), W=(("wout", u()),))

            loadq(0)
            if NT > 1:
                loadq(1)
            cmp_job(0)
            cmp_job2(0)
            for ui in range(len(units)):
                qb, g = units[ui]
                if g == 0 and qb + 2 < NT:
                    loadq(qb + 2)
                if ui + 1 < len(units):
                    cmp_job(ui + 1)
                sel_job(ui, (ui + 1) if ui + 1 < len(units) else None)
            S.flush()
        if stop_after == "PB1":
            break

        window_pass("pw", QBR, KBW, VBW, 4, False, 2, OBP, OB)
        if stop_after == "PB2":
            break

        with ExitStack() as st:
            S.barrier()
            cc = load_consts_att(st, ["ident_b", "mm"])
            att = Att(st, [0, 1, 2, 3])
            qlat = nc_alloc(st, "qlat", [128, 3, T], BF16)
            ckv = nc_alloc(st, "ckv", [128, 2, T], BF16)
            for c in range(3):
                S.dma(lambda e, c=c: e.dma_start(out=qlat[:, c, :], in_=QLAT[:, c, :]), W=(("qlat", c),))
            for c in range(2):
                S.dma(lambda e, c=c: e.dma_start(out=ckv[:, c, :], in_=CKV[:, c, :]), W=(("ckv", c),))
            wq = nc_alloc(st, "wq", [128, 3, 768], BF16)
            wqp = nc_alloc(st, "wqp", [128, 3, 768], BF16)
            wkk = nc_alloc(st, "wkk", [128, 2, 512], BF16)
            wkv_ = nc_alloc(st, "wkv", [128, 2, 512], BF16)
            S.dma(lambda e: e.dma_start(out=wq, in_=W["wq"]), W=("wq",), q="pool")
            S.dma(lambda e: e.dma_start(out=wqp, in_=W["wqp"]), W=("wqp",), q="pool")
            S.dma(lambda e: e.dma_start(out=wkk, in_=W["wkvk"]), W=("wkk",), q="pool")
            S.dma(lambda e: e.dma_start(out=wkv_, in_=W["wkvv"]), W=("wkv",), q="pool")
            KH = nc_alloc(st, "KH", [96, T], BF16)
            VH = nc_alloc(st, "VH", [128, NT, 128], BF16)
            S.op("pool", lambda e: e.memset(VH, 1.0), W=("VH",))
            S.dma(lambda e: e.dma_start(out=KH[64:96, :], in_=KPE), W=("KHpe",))
            QH = [nc_alloc(st, "QH%d" % i, [96, 512], BF16) for i in range(2)]
            tabc = [nc_alloc(st, "mtc%d" % i, [96, 512], F32) for i in range(2)]
            tabs = [nc_alloc(st, "mts%d" % i, [96, 512], F32) for i in range(2)]
            t1 = nc_alloc(st, "mt1", [96, 512], F32)
            t2 = nc_alloc(st, "mt2", [96, 512], F32)
            dens = [nc_alloc(st, "mden%d" % i, [64, 512], F32) for i in range(2)]
            outs = [nc_alloc(st, "mout%d" % i, [64, 512], BF16) for i in range(2)]
            ucl = [0]

            def mla_head(h):
                for s in range(NS):
                    b = 6 + (s % 2)
                    for c in range(2):
                        S.op("pe", lambda e, c=c, b=b, s=s: e.matmul(PS[b][0:64, :], lhsT=wkk[:, c, h * 64:(h + 1) * 64], rhs=ckv[:, c, s * 512:(s + 1) * 512], start=(c == 0), stop=(c == 1)),
                             R=("wkk", ("ckv", c)), W=(PR(b),))
                    S.op("dve", lambda e, b=b, s=s: e.tensor_copy(out=KH[0:64, s * 512:(s + 1) * 512], in_=PS[b][0:64, :]), R=(PR(b),), W=("KHn",))
                for i4 in range(NT // 4):
                    b = 6 + (i4 % 2)
                    for t4 in range(4):
                        tt = i4 * 4 + t4
                        for c in range(2):
                            S.op("pe", lambda e, c=c, b=b, tt=tt, t4=t4: e.matmul(PS[b][:, t4 * 64:(t4 + 1) * 64], lhsT=ckv[:, c, tt * 128:(tt + 1) * 128], rhs=wkv_[:, c, h * 64:(h + 1) * 64],
                                                                             start=(t4 == 0 and c == 0), stop=(t4 == 3 and c == 1), skip_group_check=True),
                                 R=("wkv", ("ckv", c)), W=(PR(b),))
                    S.op("act", lambda e, b=b, i4=i4: e.copy(out=VH[:, i4 * 4:(i4 + 1) * 4, 0:64], in_=PS[b][:, 0:256].rearrange("p (a d) -> p a d", d=64)), R=(PR(b),), W=("VH",))
                for qs in range(NS):
                    i2 = ucl[0] % 2
                    ucl[0] += 1
                    ob = 4 + i2
                    tsl = slice(qs * 512, (qs + 1) * 512)
                    S.dma(lambda e, i2=i2, tsl=tsl: e.dma_start(out=tabc[i2][64:96, :], in_=rcM[64:96, tsl]), W=(("mtc", i2),))
                    S.dma(lambda e, i2=i2, tsl=tsl: e.dma_start(out=tabs[i2][64:96, :], in_=rsM[64:96, tsl]), W=(("mts", i2),))
                    ba, bb_ = 6, 7
                    for c in range(3):
                        S.op("pe", lambda e, c=c, tsl=tsl: e.matmul(PS[ba][0:96, :], lhsT=wq[:, c, h * 96:(h + 1) * 96], rhs=qlat[:, c, tsl], start=(c == 0), stop=(c == 2)),
                             R=("wq", ("qlat", c)), W=(PR(ba),))
                    for c in range(3):
                        S.op("pe", lambda e, c=c, tsl=tsl: e.matmul(PS[bb_][0:96, :], lhsT=wqp[:, c, h * 96:(h + 1) * 96], rhs=qlat[:, c, tsl], start=(c == 0), stop=(c == 2)),
                             R=("wqp", ("qlat", c)), W=(PR(bb_),))
                    qh, qreg = QH[i2], ("QH", i2)
                    S.op("act", lambda e, qh=qh: e.copy(out=qh[0:64, :], in_=PS[ba][0:64, :]), R=(PR(ba),), W=((qreg, "n"),))
                    S.op("dve", lambda e, i2=i2: e.tensor_tensor(out=t1[64:96, :], in0=PS[ba][64:96, :], in1=tabc[i2][64:96, :], op=ALU.mult), R=(PR(ba), ("mtc", i2)), W=("mt1",))
                    S.op("dve", lambda e, i2=i2: e.tensor_tensor(out=t2[64:96, :], in0=PS[bb_][64:96, :], in1=tabs[i2][64:96, :], op=ALU.mult), R=(PR(bb_), ("mts", i2)), W=("mt2",))
                    S.op("pool", lambda e, qh=qh: e.tensor_tensor(out=qh[64:96, :], in0=t1[64:96, :], in1=t2[64:96, :], op=ALU.add), R=("mt1", "mt2"), W=((qreg, "r"),))
                    nk = 4 * qs + 4
                    for kt in range(nk):
                        masks = []
                        if kt >= 4 * qs:
                            i = kt - 4 * qs
                            masks.append((cc["ident_b"], cc["mm"][:, i * 512:(i + 1) * 512], ("c_ident_b", "c_mm")))
                        att.tile(KH[:, kt * 128:(kt + 1) * 128], ["KHn", "KHpe"], qh, [(qreg, "n"), (qreg, "r")], masks, VH[:, kt, :], "VH", ob, kt == 0, kt == nk - 1, float(96 ** -0.5))
                    att.drain()
                    den, dreg = dens[i2], ("mden", i2)
                    fin_den(ob, den, dreg)
                    out = outs[i2]
                    S.op("dve", lambda e, ob=ob, den=den, out=out: e.tensor_tensor(out=out, in0=PS[ob][0:64, :], in1=den, op=ALU.mult), R=(PR(ob), dreg), W=(("mout", i2),))
                    S.dma(lambda e, out=out, tsl=tsl: e.dma_start(out=OC[:, h, tsl], in_=out), R=(("mout", i2),), W=(("wout", u()),), q="pool")
            for h_ in range(8):
                mla_head(h_)
            S.flush()
        if stop_after == "PC":
            break

        with ExitStack() as st:
            S.barrier()
            tmp = norm_tmp(st)
            wg = nc_alloc(st, "wg", [128, 8, 3072], BF16)
            for k in range(8):
                S.dma(lambda e, k=k: e.dma_start(out=wg[:, k, :], in_=W["wg"][k * 128:(k + 1) * 128, :]), W=(("wg", k),))
            wbr = nc_alloc(st, "wbr", [64, 3, 8, 1024], BF16)
            for xi, nm in enumerate(("wbra", "wbrb", "wbrc")):
                S.dma(lambda e, xi=xi, nm=nm: e.dma_start(out=wbr[:, xi, :, :], in_=W[nm]), W=(("wbr", xi),))
            wout = nc_alloc(st, "wout", [128, 8, 1024], BF16)
            S.dma(lambda e: e.dma_start(out=wout, in_=W["wout"]), W=("wout",))
            gmix = nc_alloc(st, "gmix", [128, 8], F32)
            S.dma(lambda e: e.dma_start(out=gmix, in_=W["nmix"]), W=("gmix",))
            xt1 = nc_alloc(st, "pmx", [128, 4, D], F32)
            xts = [xt1, xt1]
            hT = nc_alloc(st, "pmhT", [128, 8, 512], BF16)
            oin1 = [nc_alloc(st, "pmo_%d" % xi, [64, 8, 512], BF16) for xi in range(3)]
            oin = [oin1, oin1]
            gsb = [nc_alloc(st, "pmg%d" % i, [128, 512], F32) for i in range(2)]
            tmpm = nc_alloc(st, "pmt", [128, 512], F32)
            macc = nc_alloc(st, "pmacc", [128, 512], F32)
            mT = nc_alloc(st, "pmmT", [128, 8, 512], BF16)
            gc = 0
            for s in range(NS):
                xt, xr = xts[s % 2], ("pmx", 0)
                tsl = slice(s * 512, (s + 1) * 512)
                S.dma(lambda e, xt=xt, s=s: e.dma_start(out=xt, in_=xrv[s]), R=(("xres", s),), W=(xr,))
                for xi, Od in enumerate((OA, OB, OC)):
                    S.dma(lambda e, xi=xi, Od=Od, tsl=tsl, s=s: e.dma_start(out=oin[s % 2][xi], in_=Od[:, :, tsl]), W=(("pmo", 0, xi),))
                norm_T(xt, xr, hT, "pmhT", gmix, "gmix", tmp)
                for cg in range(8):
                    for xi in range(3):
                        bp = ps_next()
                        for h in range(8):
                            S.op("pe", lambda e, bp=bp, xi=xi, cg=cg, s=s, h=h: e.matmul(PS[bp], lhsT=wbr[:, xi, h, cg * 128:(cg + 1) * 128],
                                                                                  rhs=oin[s % 2][xi][:, h, :], start=(h == 0), stop=(h == 7)),
                                 R=(("wbr", xi), ("pmo", 0, xi)), W=(PR(bp),))
                        bg = ps_next()
                        for k in range(8):
                            S.op("pe", lambda e, bg=bg, xi=xi, k=k, cg=cg: e.matmul(PS[bg], lhsT=wg[:, k, xi * 1024 + cg * 128: xi * 1024 + (cg + 1) * 128], rhs=hT[:, k, :], start=(k == 0), stop=(k == 7)),
                                 R=(("wg", k), ("pmhT", k)), W=(PR(bg),))
                        gi = gc % 2
                        gc += 1
                        S.op("act", lambda e, bg=bg, gi=gi: e.activation(out=gsb[gi], in_=PS[bg], func=AF.Sigmoid), R=(PR(bg),), W=(("pmg", gi),))
                        if xi == 0:
                            S.op("dve", lambda e, bp=bp, gi=gi: e.tensor_tensor(out=macc, in0=PS[bp], in1=gsb[gi], op=ALU.mult), R=(PR(bp), ("pmg", gi)), W=("pmacc",))
                        else:
                            S.op("dve", lambda e, bp=bp, gi=gi: e.tensor_tensor(out=tmpm, in0=PS[bp], in1=gsb[gi], op=ALU.mult), R=(PR(bp), ("pmg", gi)), W=("pmt",))
                            if xi == 1:
                                S.op("dve", lambda e: e.tensor_tensor(out=macc, in0=macc, in1=tmpm, op=ALU.add), R=("pmacc", "pmt"), W=("pmacc",))
                            else:
                                S.op("dve", lambda e, cg=cg: e.tensor_tensor(out=mT[:, cg, :], in0=macc, in1=tmpm, op=ALU.add), R=("pmacc", "pmt"), W=(("pmmT", cg),))
                for j in range(4):
                    for half in range(2):
                        b = ps_next()
                        for cg in range(8):
                            S.op("pe", lambda e, b=b, cg=cg, j=j, half=half: e.matmul(PS[b], lhsT=mT[:, cg, j * 128:(j + 1) * 128], rhs=wout[:, cg, half * 512:(half + 1) * 512], start=(cg == 0), stop=(cg == 7)),
                                 R=(("pmmT", cg), "wout"), W=(PR(b),))
                        S.op("dve", lambda e, b=b, j=j, half=half, xt=xt: e.tensor_tensor(out=xt[:, j, half * 512:(half + 1) * 512], in0=PS[b], in1=xt[:, j, half * 512:(half + 1) * 512], op=ALU.add),
                             R=(PR(b), xr), W=(xr,))
                S.dma(lambda e, xt=xt, s=s: e.dma_start(out=xrv[s], in_=xt), R=(xr,), W=(("xres", s),))
            S.flush()
        if stop_after == "PM":
            break

        with ExitStack() as st:
            S.barrier()
            tmp = norm_tmp(st)
            wxq = nc_alloc(st, "wxq", [128, 8, 512], BF16)
            wxkv = nc_alloc(st, "wxkv", [128, 8, 1024], BF16)
            wxo = nc_alloc(st, "wxo", [128, 4, 1024], BF16)
            S.dma(lambda e: e.dma_start(out=wxq, in_=W["wxq"]), W=("wxq",))
            S.dma(lambda e: e.dma_start(out=wxkv, in_=W["wxkv"]), W=("wxkv",))
            S.dma(lambda e: e.dma_start(out=wxo, in_=W["wxo"]), W=("wxo",))
            gx = nc_alloc(st, "gx", [128, 8], F32)
            gm = nc_alloc(st, "gm", [128, 8], F32)
            S.dma(lambda e: e.dma_start(out=gx, in_=W["nx"]), W=("gx",))
            S.dma(lambda e: e.dma_start(out=gm, in_=W["nmem"]), W=("gm",))
            ones_b = nc_alloc(st, "ones_b", [128, 128], BF16)
            S.dma(lambda e: e.dma_start(out=ones_b, in_=C["ones_f"]), W=("ones_b",))
            xts = [nc_alloc(st, "pxx%d" % i, [128, 4, D], F32) for i in range(2)]
            hT = nc_alloc(st, "pxhT", [128, 8, 512], BF16)
            KM = nc_alloc(st, "KM", [128, 4, 256], BF16)
            VM = nc_alloc(st, "VM", [128, 2, 512], BF16)
            memt = xts[1]
            S.dma(lambda e: e.dma_start(out=memt[:, 0:2, :], in_=mem_in.rearrange("(j p) c -> p j c", p=128)), W=(("pxx", 1),))
            norm_T(memt, ("pxx", 1), hT, "pxhT", gm, "gm", tmp, ntok_tiles=2)
            for h in range(4):
                b = ps_next()
                for k in range(8):
                    S.op("pe", lambda e, b=b, k=k, h=h: e.matmul(PS[b][:, 0:256], lhsT=wxkv[:, k, h * 128:(h + 1) * 128], rhs=hT[:, k, 0:256], start=(k == 0), stop=(k == 7)),
                         R=("wxkv", ("pxhT", k)), W=(PR(b),))
                S.op("act", lambda e, b=b, h=h: e.copy(out=KM[:, h, :], in_=PS[b][:, 0:256]), R=(PR(b),), W=("KM",))
            for mt in range(2):
                b = ps_next()
                for k in range(8):
                    S.op("pe", lambda e, b=b, k=k, mt=mt: e.matmul(PS[b], lhsT=hT[:, k, mt * 128:(mt + 1) * 128], rhs=wxkv[:, k, 512:1024], start=(k == 0), stop=(k == 7)),
                         R=("wxkv", ("pxhT", k)), W=(PR(b),))
                S.op("act", lambda e, b=b, mt=mt: e.copy(out=VM[:, mt, :], in_=PS[b]), R=(PR(b),), W=("VM",))
            qx = [nc_alloc(st, "qx%d" % i, [128, 512], BF16) for i in range(2)]
            Ex = [nc_alloc(st, "Ex%d" % i, [128, 512], BF16) for i in range(4)]
            denx = nc_alloc(st, "denx", [128, 512], F32)
            oxT = nc_alloc(st, "oxT", [128, 4, 512], BF16)
            ec = 0
            def px_load(s):
                S.dma(lambda e: e.dma_start(out=xts[s % 2], in_=xrv[s]), R=(("xres", s),), W=(("pxx", s % 2),))

            px_load(0)
            norm_A(xts[0], ("pxx", 0), tmp)
            for s in range(NS):
                xt, xr = xts[s % 2], ("pxx", s % 2)
                if s + 1 < NS:
                    px_load(s + 1)
                norm_B(hT, "pxhT", gx, "gx", tmp)
                if s + 1 < NS:
                    norm_A(xts[(s + 1) % 2], ("pxx", (s + 1) % 2), tmp)
                for h in range(4):
                    bq = ps_next()
                    for k in range(8):
                        S.op("pe", lambda e, bq=bq, k=k, h=h: e.matmul(PS[bq], lhsT=wxq[:, k, h * 128:(h + 1) * 128], rhs=hT[:, k, :], start=(k == 0), stop=(k == 7)),
                             R=("wxq", ("pxhT", k)), W=(PR(bq),))
                    qi = h % 2
                    S.op("act", lambda e, bq=bq, qi=qi: e.copy(out=qx[qi], in_=PS[bq]), R=(PR(bq),), W=(("qx", qi),))
                    bo = ps_next()
                    bd = ps_next()
                    for mt in range(2):
                        bs = ps_next()
                        S.op("pe", lambda e, bs=bs, h=h, mt=mt, qi=qi: e.matmul(PS[bs], lhsT=KM[:, h, mt * 128:(mt + 1) * 128], rhs=qx[qi], start=True, stop=True),
                             R=("KM", ("qx", qi)), W=(PR(bs),))
                        ei = ec % 4
                        ec += 1
                        S.op("act", lambda e, bs=bs, ei=ei: e.activation(out=Ex[ei], in_=PS[bs], func=AF.Exp, scale=float(128 ** -0.5)), R=(PR(bs),), W=(("Ex", ei),))
                        S.op("pe", lambda e, bo=bo, h=h, mt=mt, ei=ei: e.matmul(PS[bo], lhsT=VM[:, mt, h * 128:(h + 1) * 128], rhs=Ex[ei], start=(mt == 0), stop=(mt == 1)),
                             R=("VM", ("Ex", ei)), W=(PR(bo),))
                        S.op("pe", lambda e, bd=bd, mt=mt, ei=ei: e.matmul(PS[bd], lhsT=ones_b, rhs=Ex[ei], start=(mt == 0), stop=(mt == 1)),
                             R=("ones_b", ("Ex", ei)), W=(PR(bd),))
                    S.op("dve", lambda e, bd=bd: e.reciprocal(out=denx, in_=PS[bd]), R=(PR(bd),), W=("denx",))
                    S.op("dve", lambda e, bo=bo, h=h: e.tensor_tensor(out=oxT[:, h, :], in0=PS[bo], in1=denx, op=ALU.mult), R=(PR(bo), "denx"), W=(("oxT", h),))
                for j in range(4):
                    for half in range(2):
                        b = ps_next()
                        for h in range(4):
                            S.op("pe", lambda e, b=b, h=h, j=j, half=half: e.matmul(PS[b], lhsT=oxT[:, h, j * 128:(j + 1) * 128], rhs=wxo[:, h, half * 512:(half + 1) * 512], start=(h == 0), stop=(h == 3)),
                                 R=(("oxT", h), "wxo"), W=(PR(b),))
                        S.op("dve", lambda e, b=b, j=j, half=half, xt=xt: e.tensor_tensor(out=xt[:, j, half * 512:(half + 1) * 512], in0=PS[b], in1=xt[:, j, half * 512:(half + 1) * 512], op=ALU.add),
                             R=(PR(b), xr), W=(xr,))
                S.dma(lambda e, xt=xt, s=s: e.dma_start(out=xrv[s], in_=xt), R=(xr,), W=(("xres", s),))
            S.flush()
        if stop_after == "PX":
            break

        with ExitStack() as st:
            S.barrier()
            tmp = norm_tmp(st)
            wgu = nc_alloc(st, "wgu", [128, 8, 2 * DFF], BF16)
            for k in range(8):
                S.dma(lambda e, k=k: e.dma_start(out=wgu[:, k, :], in_=W["wgu"][:, k, :]), W=(("wgu", k),))
            gf = nc_alloc(st, "gf", [128, 8], F32)
            S.dma(lambda e: e.dma_start(out=gf, in_=W["nffn"]), W=("gf",))
            xts = [nc_alloc(st, "pfx%d" % i, [128, 4, D], F32) for i in range(2)]
            hT = nc_alloc(st, "pfhT", [128, 8, 512], BF16)
            sg = [nc_alloc(st, "pfsg%d" % i, [128, 512], F32) for i in range(2)]
            actT1 = nc_alloc(st, "pfact", [128, 22, 512], BF16)
            actT = [actT1, actT1]
            def pf_load(s):
                S.dma(lambda e: e.dma_start(out=xts[s % 2], in_=xrv[s]), R=(("xres", s),), W=(("pfx", s % 2),))

            pf_load(0)
            norm_A(xts[0], ("pfx", 0), tmp)
            for s in range(NS):
                xt, xr = xts[s % 2], ("pfx", s % 2)
                tsl = slice(s * 512, (s + 1) * 512)
                if s + 1 < NS:
                    pf_load(s + 1)
                norm_B(hT, "pfhT", gf, "gf", tmp)
                if s + 1 < NS:
                    norm_A(xts[(s + 1) % 2], ("pfx", (s + 1) % 2), tmp)
                at_, ar = actT[s % 2], ("pfact", 0)
                for f in range(22):
                    bg = ps_next()
                    bu = ps_next()
                    for k in range(8):
                        S.op("pe", lambda e, bg=bg, k=k, f=f: e.matmul(PS[bg], lhsT=wgu[:, k, f * 128:(f + 1) * 128], rhs=hT[:, k, :], start=(k == 0), stop=(k == 7)),
                             R=(("wgu", k), ("pfhT", k)), W=(PR(bg),))
                    for k in range(8):
                        S.op("pe", lambda e, bu=bu, k=k, f=f: e.matmul(PS[bu], lhsT=wgu[:, k, DFF + f * 128:DFF + (f + 1) * 128], rhs=hT[:, k, :], start=(k == 0), stop=(k == 7)),
                             R=(("wgu", k), ("pfhT", k)), W=(PR(bu),))
                    si = f % 2
                    S.op("act", lambda e, bg=bg, si=si: e.activation(out=sg[si], in_=PS[bg], func=AF.Silu), R=(PR(bg),), W=(("pfsg", si),))
                    S.op("dve", lambda e, bu=bu, si=si, f=f, at_=at_: e.tensor_tensor(out=at_[:, f, :], in0=PS[bu], in1=sg[si], op=ALU.mult), R=(PR(bu), ("pfsg", si)), W=((ar, f),))
                S.dma(lambda e, at_=at_, tsl=tsl: e.dma_start(out=ACTD[:, :, tsl], in_=at_), R=tuple((ar, f) for f in range(22)), W=(("actd", s),))
            S.flush()
        with ExitStack() as st:
            S.barrier()
            wdn = nc_alloc(st, "wdn", [128, 22, 1024], BF16)
            S.dma(lambda e: e.dma_start(out=wdn, in_=W["wdn"]), W=("wdn",))
            xts = [nc_alloc(st, "pgx%d" % i, [128, 4, D], F32) for i in range(2)]
            actT = [nc_alloc(st, "pgact%d" % i, [128, 22, 512], BF16) for i in range(2)]
            last = (l == L - 1)
            if last:
                gfin = nc_alloc(st, "gfin", [128, D], F32)
                S.dma(lambda e: e.dma_start(out=gfin, in_=nfin_in.broadcast_to([128, D])), W=("gfin",))
                sq = nc_alloc(st, "fsq", [128, D], F32)
                ss = nc_alloc(st, "fss", [128, 4], F32)
            yv = y_out.rearrange("(s j p) c -> s p j c", p=128, j=4)

            def pg_load(s):
                S.dma(lambda e: e.dma_start(out=xts[s % 2], in_=xrv[s]), R=(("xres", s),), W=(("pgx", s % 2),))
                S.dma(lambda e: e.dma_start(out=actT[s % 2], in_=ACTD[:, :, s * 512:(s + 1) * 512]), R=(("actd", s),), W=(("pgact", s % 2),))

            for s in range(NS):
                xt, xr = xts[s % 2], ("pgx", s % 2)
                at_, ar = actT[s % 2], ("pgact", s % 2)
                tsl = slice(s * 512, (s + 1) * 512)
                if s == 0:
                    pg_load(0)
                if s + 1 < NS:
                    pg_load(s + 1)
                for j in range(4):
                    for half in range(2):
                        b = ps_next()
                        for f in range(22):
                            S.op("pe", lambda e, b=b, f=f, j=j, half=half, at_=at_: e.matmul(PS[b], lhsT=at_[:, f, j * 128:(j + 1) * 128], rhs=wdn[:, f, half * 512:(half + 1) * 512], start=(f == 0), stop=(f == 21)),
                                 R=(ar, "wdn"), W=(PR(b),))
                        S.op("dve", lambda e, b=b, j=j, half=half, xt=xt: e.tensor_tensor(out=xt[:, j, half * 512:(half + 1) * 512], in0=PS[b], in1=xt[:, j, half * 512:(half + 1) * 512], op=ALU.add),
                             R=(PR(b), xr), W=(xr,))
                if not last:
                    S.dma(lambda e, xt=xt, s=s: e.dma_start(out=xrv[s], in_=xt), R=(xr,), W=(("xres", s),))
                else:
                    for j in range(4):
                        S.op("act", lambda e, j=j, xt=xt: e.activation(out=sq, in_=xt[:, j, :], func=AF.Square, accum_out=ss[:, j:j + 1]), R=(xr,), W=("fsq", ("fss", j)))
                    ssr = tuple(("fss", j) for j in range(4))
                    S.op("act", lambda e: e.activation(out=ss, in_=ss, func=AF.Sqrt, scale=1.0 / D, bias=EPS), R=ssr, W=ssr)
                    S.op("dve", lambda e: e.reciprocal(out=ss, in_=ss), R=ssr, W=ssr)
                    for j in range(4):
                        S.op("dve", lambda e, j=j, xt=xt: e.scalar_tensor_tensor(out=xt[:, j, :], in0=xt[:, j, :], scalar=ss[:, j:j + 1], in1=gfin, op0=ALU.mult, op1=ALU.mult),
                             R=(xr, ("fss", j), "gfin"), W=(xr,))
                    S.dma(lambda e, xt=xt, s=s: e.dma_start(out=yv[s], in_=xt), R=(xr,), W=(("y", s),))
            S.flush()
    S.finish()
    return nc


def prep_inputs(inp, T, L, b):
    consts = make_consts(T)
    m = {}
    m["x"] = np.ascontiguousarray(np.asarray(inp["x"][b, :T], np.float32))
    m["mem"] = np.ascontiguousarray(np.asarray(inp["mem"][b], np.float32))
    m["pos"] = np.ascontiguousarray(np.asarray(inp["positions"][b, :T], np.int32).reshape(1, T))
    m["nfin"] = np.ascontiguousarray(np.asarray(inp["norm_final"], np.float32).reshape(1, D))
    for k, v in consts.items():
        m["c_" + k] = v
    return m, consts


T_FULL, L_FULL, B_FULL = 8192, 2, 4


def kernel(**inputs):
    inp = {k: np.asarray(v) for k, v in inputs.items()}
    T, L, B = T_FULL, L_FULL, B_FULL
    ws = [layer_weights(inp, l) for l in range(L)]
    in_maps = []
    consts = None
    for b in range(B):
        m, consts = prep_inputs(inp, T, L, b)
        for l in range(L):
            for k, v in ws[l].items():
                m["w%d_%s" % (l, k)] = v
        in_maps.append(m)
    wshapes = {k: v.shape for k, v in ws[0].items()}
    cshapes = {k: (v.shape, "bf16" if v.dtype == NBF else "f32") for k, v in consts.items()}
    nc = build(T, L, wshapes, cshapes)
    res = run_bass_kernel_spmd(nc, in_maps, core_ids=list(range(B)))
    out = np.stack([np.asarray(r["y"], dtype=np.float32) for r in res.results], axis=0)
    return out
```

```python
import numpy as np
import ml_dtypes
from contextlib import ExitStack
import concourse.bass as bass
import concourse.mybir as mybir
from concourse.bass_utils import run_bass_kernel_spmd
from concourse.alu_op_type import AluOpType as ALU

AF = mybir.ActivationFunctionType
F32, BF16, I32 = mybir.dt.float32, mybir.dt.bfloat16, mybir.dt.int32
NEG = -30000.0
D = 1024
DFF = 2816
EPS = 1e-6
NBF = ml_dtypes.bfloat16


class Sched:
    CENG = ("pe", "act", "dve", "pool", "sp")

    def __init__(self, nc, n_dma_sems=40):
        self.nc = nc
        self.eng = {"pe": nc.tensor, "act": nc.scalar, "dve": nc.vector,
                    "pool": nc.gpsimd, "sp": nc.sync}
        self.sem = {e: nc.alloc_semaphore("sem_" + e) for e in self.CENG}
        self.cnt = {e: 0 for e in self.CENG}
        self.dsem = [nc.alloc_semaphore("dsem%d" % i) for i in range(n_dma_sems)]
        self.dval = [0] * n_dma_sems
        self.dnext = 0
        self.ops = []
        self.all_tok = {}
        self.nops = 0
        self.last_w = {}
        self.readers = {}
        self.waited = {e: {} for e in self.CENG}
        self.sig_after = {e: [] for e in self.CENG}
        self.op_eng = {}
        self.op_isdma = {}

    def op(self, eng, fn, R=(), W=(), dma=False):
        if dma:
            eng = "pool"
        elif eng == "pool":
            eng = "dve"
        idx = self.nops
        self.nops += 1
        deps = set()
        for r in R:
            if r in self.last_w:
                deps.add(self.last_w[r])
        for w in W:
            if w in self.last_w:
                deps.add(self.last_w[w])
            for rd in self.readers.get(w, ()):
                deps.add(rd)
        deps.discard(idx)
        for r in R:
            self.readers.setdefault(r, []).append(idx)
        for w in W:
            self.last_w[w] = idx
            self.readers[w] = []
        self.op_eng[idx] = eng
        self.op_isdma[idx] = dma
        self.ops.append(dict(idx=idx, eng=eng, fn=fn, deps=deps, dma=dma, sig=False, barrier=False))
        return idx

    def dma(self, fn, R=(), W=(), q="sp"):
        return self.op(q, fn, R, W, dma=True)

    def barrier(self):
        self.ops.append(dict(idx=None, barrier=True))

    def _wait(self, eng, sem, val, key):
        w = self.waited[eng]
        if w.get(key, 0) >= val:
            return
        w[key] = val
        self.eng[eng].wait_ge(sem, val)

    def flush(self):
        ops = self.ops
        self.ops = []
        pend = {o["idx"]: o for o in ops if not o["barrier"]}
        last_on = {}
        for o in ops:
            if o["barrier"]:
                for e, lo in last_on.items():
                    lo["sig"] = True
                continue
            for d in o["deps"]:
                if d in pend and not pend[d]["dma"]:
                    if not (o["eng"] == "pe" and pend[d]["eng"] == "pe" and not o["dma"]):
                        pend[d]["sig"] = True
            if not o["dma"]:
                last_on[o["eng"]] = o
        for e, lo in last_on.items():
            lo["sig"] = True
        for o in ops:
            if o["barrier"]:
                for e in self.CENG:
                    for p in self.CENG:
                        if p != e and self.cnt[p] > 0:
                            self._wait(e, self.sem[p], self.cnt[p], p)
                    for i, v in enumerate(self.dval):
                        if v > 0:
                            self._wait(e, self.dsem[i], v, ("d", i))
                continue
            e = o["eng"]
            E = self.eng[e]
            need = {}
            for d in o["deps"]:
                if self.op_isdma[d]:
                    s_i, v = self.all_tok[d]
                    need[("d", s_i)] = max(need.get(("d", s_i), 0), v)
                else:
                    pe_ = self.op_eng[d]
                    if pe_ == "pe" and e == "pe" and not o["dma"]:
                        continue
                    tok = self.all_tok.get(d)
                    if tok is None or tok[1] is None:
                        v = None
                        for (i2, v2) in self.sig_after[pe_]:
                            if i2 >= d:
                                v = v2
                                break
                        assert v is not None, ("unsignalled dep", d, pe_)
                    else:
                        v = tok[1]
                    need[pe_] = max(need.get(pe_, 0), v)
            for k, v in need.items():
                if isinstance(k, tuple):
                    self._wait(e, self.dsem[k[1]], v, k)
                else:
                    self._wait(e, self.sem[k], v, k)
            if o["dma"]:
                si = self.dnext
                self.dnext = (self.dnext + 1) % len(self.dsem)
                if self.dval[si] > 0:
                    self._wait(e, self.dsem[si], self.dval[si], ("d", si))
                ins = o["fn"](E)
                self.dval[si] += 16
                ins.then_inc(self.dsem[si], 16)
                self.all_tok[o["idx"]] = (si, self.dval[si])
            else:
                ins = o["fn"](E)
                if o["sig"]:
                    self.cnt[e] += 1
                    ins.then_inc(self.sem[e], 1)
                    self.all_tok[o["idx"]] = (e, self.cnt[e])
                    self.sig_after[e].append((o["idx"], self.cnt[e]))
                else:
                    self.all_tok[o["idx"]] = (e, None)

    def finish(self):
        self.barrier()
        self.flush()


def make_consts(T):
    NT = T // 128
    NCT = (T // 16 - 1 + 127) // 128
    n_cmp = (T - 32) // 16 + 1
    c = {}
    c["ident_f"] = np.eye(128, dtype=np.float32)
    c["ident_b"] = np.eye(128, dtype=np.float32).astype(NBF)
    c["ones_f"] = np.ones((128, 128), np.float32)
    k = np.arange(128)[:, None]
    q = np.arange(128)[None, :]
    c["cb"] = np.where(k <= q, 0.0, NEG).astype(NBF)
    c["bb"] = np.where(k > q, 0.0, NEG).astype(NBF)
    mm = np.zeros((128, 4, 4, 128), np.float32)
    for i in range(4):
        for j in range(4):
            if j < i:
                mm[:, i, j, :] = NEG
            elif j == i:
                mm[:, i, j, :] = np.where(k <= q, 0.0, NEG)
    c["mm"] = mm.reshape(128, 4 * 512).astype(NBF)
    cm = np.zeros((128, 17, 128), np.float32)
    for dl in range(17):
        cm[:, dl, :] = np.where(16 * k + 31 - q <= 128 * dl, 0.0, NEG)
    c["cm"] = cm.reshape(128, 17 * 128).astype(NBF)
    ex = np.zeros((128, NT, 128), np.float32)
    for kt in range(NT):
        for half in range(2):
            j = 2 * kt + half
            if j < 128:
                ex[j, kt, half * 64:(half + 1) * 64] = -NEG
    c["ex"] = ex.reshape(128, NT * 128).astype(NBF)
    exh = np.zeros((64, NT, 128), np.float32)
    for kt in range(NT):
        for half in range(2):
            exh[(2 * kt + half) % 64, kt, half * 64:(half + 1) * 64] = -NEG
    c["exh"] = exh.reshape(64, NT * 128).astype(NBF)
    n_slc = T // 64
    cs = np.arange(n_cmp) * 16
    ss = np.arange(n_slc) * 64
    cover = np.minimum(cs[:, None] + 32, ss[None, :] + 64) - np.maximum(cs[:, None], ss[None, :])
    c2s = np.clip(cover, 0, None) / 16.0
    ms = np.zeros((NCT * 128, 129), np.float32)
    ms[:n_cmp, :n_slc] = c2s
    ms[:, 128] = 1.0
    c["msel"] = ms.reshape(NCT, 128, 129).transpose(1, 0, 2).reshape(128, NCT * 129).astype(NBF)
    fb = np.zeros((128, 256), np.float32)
    for qq in range(128):
        hi = 1 if qq >= 64 else 0
        for dl in (hi, hi - 1):
            fb[qq, 128 + dl] = 1e4
    c["fb"] = fb
    rp = np.zeros((128, 4), np.float32)
    p = np.arange(128)
    fa = (10000.0 ** (-(np.arange(32, dtype=np.float32)) / 32)).astype(np.float32)
    rp[:, 0] = fa[p % 32]
    rp[:, 1] = np.where((p % 64) < 32, -1.0, 1.0)
    fm = (10000.0 ** (-(np.arange(16, dtype=np.float32)) / 16)).astype(np.float32)
    rp[:, 2] = fm[p % 16]
    rp[:, 3] = np.where((p % 32) < 16, -1.0, 1.0)
    c["ropep"] = rp
    return c


IN_SPLITS = (512, 128, 128, 512, 128, 128, 128, 128, 128, 128, 24, 384, 256, 32, 3072)
OFF = np.concatenate([[0], np.cumsum(IN_SPLITS)]).tolist()
(O_AQ, O_AK, O_AV, O_BQ, O_BKC, O_BVC, O_BKS, O_BVS, O_BKW, O_BVW, O_BG, O_CQA, O_CKV, O_CKR, O_GBR) = OFF[:15]


def perm_half(cols, hd):
    cols = np.asarray(cols)
    n = len(cols) // hd
    out = []
    for h in range(n):
        blk = cols[h * hd:(h + 1) * hd]
        out.append(np.concatenate([blk[hd // 2:], blk[:hd // 2]]))
    return np.concatenate(out)


def fm_groups():
    g = []
    r = lambda a, n: list(range(a, a + n))
    for i in range(4):
        g.append(("aq%d" % i, r(O_AQ + 128 * i, 128)))
        g.append(("aq%dP" % i, perm_half(r(O_AQ + 128 * i, 128), 64)))
    g.append(("ak", r(O_AK, 128)))
    g.append(("akP", perm_half(r(O_AK, 128), 64)))
    for i in range(4):
        g.append(("bq%d" % i, r(O_BQ + 128 * i, 128)))
        g.append(("bq%dP" % i, perm_half(r(O_BQ + 128 * i, 128), 64)))
    g.append(("bks", r(O_BKS, 128)))
    g.append(("bksP", perm_half(r(O_BKS, 128), 64)))
    g.append(("bkw", r(O_BKW, 128)))
    g.append(("bkwP", perm_half(r(O_BKW, 128), 64)))
    g.append(("bkc", r(O_BKC, 128)))
    g.append(("bvc", r(O_BVC, 128)))
    for i in range(3):
        g.append(("cqa%d" % i, r(O_CQA + 128 * i, 128)))
    for i in range(2):
        g.append(("ckv%d" % i, r(O_CKV + 128 * i, 128)))
    kr = r(O_CKR, 32)
    g.append(("ckr", kr * 4))
    g.append(("ckrP", list(perm_half(kr, 32)) * 4))
    bg = r(O_BG, 24)
    g.append(("bg", bg + bg[:8] + r(O_BG, 24) * 4))
    return g


FMG = fm_groups()
FMI = {n: i for i, (n, _) in enumerate(FMG)}
NFM = len(FMG)


def layer_weights(inp, l):
    w = {}
    win = np.asarray(inp["w_in"][l], np.float32)
    cols = np.concatenate([np.asarray(c) for _, c in FMG])
    for _, c in FMG:
        assert len(c) == 128 or True
    w["wfm"] = np.ascontiguousarray(np.concatenate(
        [win[:, np.asarray(c)[:128]] for _, c in FMG], axis=1))
    w["wtm"] = np.ascontiguousarray(np.concatenate(
        [win[:, O_AV:O_AV + 128], win[:, O_BVS:O_BVS + 128], win[:, O_BVW:O_BVW + 128]], axis=1))
    w["wg"] = np.ascontiguousarray(win[:, O_GBR:O_GBR + 3072])
    w["nmix"] = np.ascontiguousarray(np.asarray(inp["norm_mix"][l], np.float32).reshape(8, 128).T)
    w["nx"] = np.ascontiguousarray(np.asarray(inp["norm_xattn"][l], np.float32).reshape(8, 128).T)
    w["nmem"] = np.ascontiguousarray(np.asarray(inp["norm_mem"][l], np.float32).reshape(8, 128).T)
    w["nffn"] = np.ascontiguousarray(np.asarray(inp["norm_ffn"][l], np.float32).reshape(8, 128).T)
    w["sinks"] = np.asarray(inp["swa_sinks"][l], np.float32).reshape(1, 8)
    w["pek"] = np.ascontiguousarray(np.asarray(inp["nsa_pe_k"][l], np.float32).T)
    w["pev"] = np.ascontiguousarray(np.asarray(inp["nsa_pe_v"][l], np.float32).T)
    w["wk1"] = np.ascontiguousarray(np.asarray(inp["nsa_wk1"][l], np.float32).reshape(32, 64, 64).transpose(1, 0, 2))
    w["wv1"] = np.ascontiguousarray(np.asarray(inp["nsa_wv1"][l], np.float32).reshape(32, 64, 64).transpose(1, 0, 2))
    w["wk2"] = np.asarray(inp["nsa_wk2"][l], np.float32)
    w["wv2"] = np.asarray(inp["nsa_wv2"][l], np.float32)
    w["qn"] = np.ascontiguousarray(np.asarray(inp["mla_q_norm"][l], np.float32).reshape(3, 128).T)
    w["kvn"] = np.ascontiguousarray(np.asarray(inp["mla_kv_norm"][l], np.float32).reshape(2, 128).T)
    wq = np.asarray(inp["mla_w_q_b"][l], np.float32).reshape(384, 8, 96)
    wqp = wq.copy()
    pi = perm_half(np.arange(64, 96), 32)
    wqp[:, :, 64:96] = wq[:, :, pi]
    w["wq"] = np.ascontiguousarray(wq.reshape(3, 128, 8 * 96).transpose(1, 0, 2))
    w["wqp"] = np.ascontiguousarray(wqp.reshape(3, 128, 8 * 96).transpose(1, 0, 2))
    wkv = np.asarray(inp["mla_w_kv_b"][l], np.float32).reshape(256, 8, 128)
    w["wkvk"] = np.ascontiguousarray(wkv[:, :, :64].reshape(2, 128, 512).transpose(1, 0, 2))
    w["wkvv"] = np.ascontiguousarray(wkv[:, :, 64:].reshape(2, 128, 512).transpose(1, 0, 2))
    for nm, key in (("wbra", "w_br_a"), ("wbrb", "w_br_b"), ("wbrc", "w_br_c")):
        w[nm] = np.ascontiguousarray(np.asarray(inp[key][l], np.float32).reshape(8, 64, 1024).transpose(1, 0, 2))
    w["wout"] = np.ascontiguousarray(np.asarray(inp["w_out"][l], np.float32).reshape(8, 128, 1024).transpose(1, 0, 2))
    w["wxq"] = np.ascontiguousarray(np.asarray(inp["w_xq"][l], np.float32).reshape(8, 128, 512).transpose(1, 0, 2))
    w["wxkv"] = np.ascontiguousarray(np.asarray(inp["w_xkv"][l], np.float32).reshape(8, 128, 1024).transpose(1, 0, 2))
    w["wxo"] = np.ascontiguousarray(np.asarray(inp["w_xo"][l], np.float32).reshape(4, 128, 1024).transpose(1, 0, 2))
    w["wgu"] = np.ascontiguousarray(np.asarray(inp["w_gate_up"][l], np.float32).reshape(8, 128, 2 * DFF).transpose(1, 0, 2))
    w["wdn"] = np.ascontiguousarray(np.asarray(inp["w_down"][l], np.float32).reshape(22, 128, 1024).transpose(1, 0, 2))
    return w


WSHAPES = None


def build(T, L, wshapes, cshapes, stop_after=None, debug=False):
    NT = T // 128
    NS = T // 512
    NCT = (T // 16 - 1 + 127) // 128
    n_cmp = (T - 32) // 16 + 1
    nc = bass.Bass("TRN2", target_bir_lowering=False)
    S = Sched(nc)

    def din(name, shape, dt=F32):
        return nc.dram_tensor(name, list(shape), dt, kind="ExternalInput").ap()

    def dscr(name, shape, dt):
        return nc.dram_tensor(name, list(shape), dt, kind=("ExternalOutput" if debug else "Internal")).ap()

    x_in = din("x", [T, D])
    mem_in = din("mem", [256, D])
    pos_in = din("pos", [1, T], I32)
    nfin_in = din("nfin", [1, D])
    C = {k: din("c_" + k, v[0], BF16 if v[1] == "bf16" else F32) for k, v in cshapes.items()}
    Wt = [{k: din("w%d_%s" % (l, k), shp) for k, shp in wshapes.items()} for l in range(L)]
    y_out = nc.dram_tensor("y", [T, D], F32, kind="ExternalOutput").ap()

    xres = dscr("xres", [T, D], F32)
    rcA, rsA, rcM, rsM = (dscr(n, [128, T], F32) for n in ("rcA", "rsA", "rcM", "rsM"))
    QA = dscr("QA", [64, 8, T], BF16)
    KA = dscr("KA", [64, 2, T], BF16)
    VA = dscr("VA", [T, 2, 128], BF16)
    QBU = dscr("QBU", [64, 8, T], BF16)
    QBR = dscr("QBR", [64, 8, T], BF16)
    KC = dscr("KC", [64, 2, T], BF16)
    VC = dscr("VC", [64, 2, T], BF16)
    KBS = dscr("KBS", [64, 2, T], BF16)
    VBS = dscr("VBS", [T, 2, 128], BF16)
    KBW = dscr("KBW", [64, 2, T], BF16)
    VBW = dscr("VBW", [T, 2, 128], BF16)
    GB = dscr("GB", [8, 3, T], F32)
    QLAT = dscr("QLAT", [128, 3, T], BF16)
    CKV = dscr("CKV", [128, 2, T], BF16)
    KPE = dscr("KPE", [32, T], BF16)
    OA = dscr("OA", [64, 8, T], BF16)
    OBP = dscr("OBP", [64, 8, T], F32)
    OB = dscr("OB", [64, 8, T], BF16)
    OC = dscr("OC", [64, 8, T], BF16)
    ACTD = dscr("ACTD", [128, 22, T], BF16)
    KCMP = dscr("KCMP", [64, 2, NCT * 128], BF16)
    VCMP = dscr("VCMP", [128, NCT, 2, 128], BF16)

    PS = [nc.alloc_psum_tensor("ps%d" % i, [128, 512], F32).ap() for i in range(8)]
    psr = [0]

    def ps_next():
        i = psr[0]
        psr[0] = (i + 1) % 8
        return i

    def PR(i):
        return ("ps", i)

    def sb(st, name, shape, dt):
        return st.enter_context(nc.sbuf_tensor(name, list(shape), dt)).ap() if False else nc_alloc(st, name, shape, dt)

    acnt = [0]

    def nc_alloc(st, name, shape, dt):
        acnt[0] += 1
        g = nc.sbuf_tensor("sb%d_%s" % (acnt[0], name), list(shape), dt)
        t = st.enter_context(g)
        return t.ap() if hasattr(t, "ap") and callable(t.ap) else t

    uid = [0]

    def u():
        uid[0] += 1
        return uid[0]

    def wload(dst, src, reg, pieces=1, axis=None):
        if pieces == 1:
            S.dma(lambda e: e.dma_start(out=dst, in_=src), R=(), W=(reg,), q="pool")
        else:
            n = dst.shape[1]
            step = (n + pieces - 1) // pieces
            for a in range(0, n, step):
                b = min(n, a + step)
                S.dma(lambda e, a=a, b=b: e.dma_start(out=dst[:, a:b], in_=src[:, a:b]), R=(), W=((reg, a),), q="pool")

    with ExitStack() as st:
        TWO_PI = 2.0 * np.pi
        c1 = float(np.float32(6.28125))
        c2 = float(np.float32(np.float32(TWO_PI - 6.28125).view(np.uint32) & np.uint32(0xFFFFF000)).view(np.float32)) if False else None
        r2 = TWO_PI - 6.28125
        c2 = float(np.array(np.array(r2, np.float32).view(np.uint32) & np.uint32(0xFFFFF000), np.uint32).view(np.float32))
        c3 = float(np.float32(r2 - c2))
        MAGIC = 12582912.0
        PIS = 3.1415925
        CH = min(T, 2048)
        ropep = nc_alloc(st, "ropep", [128, 4], F32)
        S.dma(lambda e: e.dma_start(out=ropep, in_=C["ropep"]), W=("ropep",))
        posi = nc_alloc(st, "posi", [128, CH], I32)
        posf = nc_alloc(st, "posf", [128, CH], F32)
        ang = nc_alloc(st, "ang", [128, CH], F32)
        kk = nc_alloc(st, "kk", [128, CH], F32)
        rr = nc_alloc(st, "rr", [128, CH], F32)
        sn = nc_alloc(st, "sn", [128, CH], F32)
        cs_ = nc_alloc(st, "cs", [128, CH], F32)
        xt = [nc_alloc(st, "xcp%d" % i, [128, 4, D], F32) for i in range(2)]
        xv = x_in.rearrange("(s j p) c -> s p j c", p=128, j=4)
        xrv = xres.rearrange("(s j p) c -> s p j c", p=128, j=4)
        for s in range(NS):
            b = xt[s % 2]
            S.dma(lambda e, b=b, s=s: e.dma_start(out=b, in_=xv[s]), W=(("xcp", s % 2),))
            S.dma(lambda e, b=b, s=s: e.dma_start(out=xrv[s], in_=b), R=(("xcp", s % 2),), W=(("xres", s),), q="pool")
        for c0 in range(0, T, CH):
            S.dma(lambda e, c0=c0: e.dma_start(out=posi, in_=pos_in[:, c0:c0 + CH].broadcast_to([128, CH])), W=("posi",))
            S.op("dve", lambda e: e.tensor_copy(out=posf, in_=posi), R=("posi",), W=("posf",))
            for (fc, sc, dc, ds) in ((0, 1, rcA, rsA), (2, 3, rcM, rsM)):
                S.op("dve", lambda e, fc=fc: e.tensor_scalar(out=ang, in0=posf, scalar1=ropep[:, fc:fc + 1], scalar2=None, op0=ALU.mult),
                     R=("posf", "ropep"), W=("ang",))
                S.op("dve", lambda e: e.tensor_scalar(out=kk, in0=ang, scalar1=float(1.0 / TWO_PI), scalar2=MAGIC, op0=ALU.mult, op1=ALU.add),
                     R=("ang",), W=("kk",))
                S.op("dve", lambda e: e.tensor_scalar(out=kk, in0=kk, scalar1=MAGIC, scalar2=None, op0=ALU.subtract),
                     R=("kk",), W=("kk",))
                S.op("dve", lambda e: e.scalar_tensor_tensor(out=rr, in0=kk, scalar=-c1, in1=ang, op0=ALU.mult, op1=ALU.add),
                     R=("kk", "ang"), W=("rr",))
                S.op("dve", lambda e: e.scalar_tensor_tensor(out=rr, in0=kk, scalar=-c2, in1=rr, op0=ALU.mult, op1=ALU.add),
                     R=("kk", "rr"), W=("rr",))
                S.op("dve", lambda e: e.scalar_tensor_tensor(out=rr, in0=kk, scalar=-c3, in1=rr, op0=ALU.mult, op1=ALU.add),
                     R=("kk", "rr"), W=("rr",))
                S.op("dve", lambda e: e.tensor_scalar(out=rr, in0=rr, scalar1=-PIS, scalar2=PIS, op0=ALU.max, op1=ALU.min),
                     R=("rr",), W=("rr",))
                S.op("act", lambda e: e.activation(out=sn, in_=rr, func=AF.Sin), R=("rr",), W=("sn",))
                S.op("dve", lambda e, sc=sc: e.tensor_scalar(out=sn, in0=sn, scalar1=ropep[:, sc:sc + 1], scalar2=None, op0=ALU.mult),
                     R=("sn", "ropep"), W=("sn",))
                S.dma(lambda e, ds=ds, c0=c0: e.dma_start(out=ds[:, c0:c0 + CH], in_=sn), R=("sn",), W=(("rope", u()),), q="pool")
                S.op("dve", lambda e: e.scalar_tensor_tensor(out=cs_, in0=rr, scalar=-1.0, in1=rr, op0=ALU.mult, op1=ALU.max), R=("rr",), W=("cs",))
                S.op("dve", lambda e: e.tensor_scalar(out=cs_, in0=cs_, scalar1=-1.0, scalar2=float(np.pi / 2), op0=ALU.mult, op1=ALU.add),
                     R=("cs",), W=("cs",))
                S.op("act", lambda e: e.activation(out=cs_, in_=cs_, func=AF.Sin), R=("cs",), W=("cs",))
                S.dma(lambda e, dc=dc, c0=c0: e.dma_start(out=dc[:, c0:c0 + CH], in_=cs_), R=("cs",), W=(("rope", u()),), q="pool")
        S.finish()

    def norm_A(xt, xreg, tmp, ntok_tiles=4):
        for j in range(ntok_tiles):
            S.op("act", lambda e, j=j: e.activation(out=tmp["sq"], in_=xt[:, j, :], func=AF.Square, accum_out=tmp["ss"][:, j:j + 1]),
                 R=(xreg,), W=("nt_sq", ("nt_ss", j)))
        ssr = tuple(("nt_ss", j) for j in range(ntok_tiles))
        S.op("act", lambda e: e.activation(out=tmp["ss"][:, 0:ntok_tiles], in_=tmp["ss"][:, 0:ntok_tiles], func=AF.Sqrt, scale=1.0 / D, bias=EPS),
             R=ssr, W=ssr)
        S.op("dve", lambda e: e.reciprocal(out=tmp["ss"][:, 0:ntok_tiles], in_=tmp["ss"][:, 0:ntok_tiles]), R=ssr, W=ssr)
        for j in range(ntok_tiles):
            S.op("dve", lambda e, j=j: e.tensor_scalar(out=tmp["hs"][:, j, :], in0=xt[:, j, :], scalar1=tmp["ss"][:, j:j + 1], scalar2=None, op0=ALU.mult),
                 R=(xreg, ("nt_ss", j)), W=(("nt_hs", j),))

    def norm_B(hT, hreg, gcol, gcreg, tmp, ntok_tiles=4):
        for k in range(8):
            b = ps_next()
            for j in range(ntok_tiles):
                S.op("pe", lambda e, k=k, j=j, b=b: e.transpose(out=PS[b][:, j * 128:(j + 1) * 128], in_=tmp["hs"][:, j, k * 128:(k + 1) * 128], identity=tmp["ident"]),
                     R=(("nt_hs", j), "ident_f"), W=(PR(b),))
            n = ntok_tiles * 128
            S.op("dve" if k % 2 == 0 else "act",
                 (lambda e, k=k, b=b, n=n: e.tensor_scalar(out=hT[:, k, 0:n], in0=PS[b][:, 0:n], scalar1=gcol[:, k:k + 1], scalar2=None, op0=ALU.mult))
                 if k % 2 == 0 else
                 (lambda e, k=k, b=b, n=n: e.activation(out=hT[:, k, 0:n], in_=PS[b][:, 0:n], func=AF.Copy, scale=gcol[:, k:k + 1])),
                 R=(PR(b), gcreg), W=((hreg, k),))

    def norm_T(xt, xreg, hT, hreg, gcol, gcreg, tmp, ntok_tiles=4):
        norm_A(xt, xreg, tmp, ntok_tiles)
        norm_B(hT, hreg, gcol, gcreg, tmp, ntok_tiles)

    def norm_tmp(st):
        t = {}
        t["sq"] = nc_alloc(st, "nt_sq", [128, D], F32)
        t["ss"] = nc_alloc(st, "nt_ss", [128, 4], F32)
        t["hs"] = nc_alloc(st, "nt_hs", [128, 4, D], F32)
        t["ident"] = nc_alloc(st, "ident_f", [128, 128], F32)
        S.dma(lambda e: e.dma_start(out=t["ident"], in_=C["ident_f"]), W=("ident_f",))
        return t

    xrv = xres.rearrange("(s j p) c -> s p j c", p=128, j=4)
    if stop_after == "P0":
        S.finish()
        return nc

    for l in range(L):
        W = Wt[l]
        with ExitStack() as st:
            S.barrier()
            tmp = norm_tmp(st)
            wfm = nc_alloc(st, "wfm", [128, 8, NFM * 128], BF16)
            wtm = nc_alloc(st, "wtm", [128, 8, 384], BF16)
            for k in range(8):
                S.dma(lambda e, k=k: e.dma_start(out=wfm[:, k, :], in_=W["wfm"][k * 128:(k + 1) * 128, :]), W=(("wfm", k),), q="pool")
                S.dma(lambda e, k=k: e.dma_start(out=wtm[:, k, :], in_=W["wtm"][k * 128:(k + 1) * 128, :]), W=(("wtm", k),), q="pool")
            wfr = tuple(("wfm", k) for k in range(8))
            wtr = tuple(("wtm", k) for k in range(8))
            gmix = nc_alloc(st, "gmix", [128, 8], F32)
            S.dma(lambda e: e.dma_start(out=gmix, in_=W["nmix"]), W=("gmix",))
            qn = nc_alloc(st, "qn", [128, 3], F32)
            kvn = nc_alloc(st, "kvn", [128, 2], F32)
            S.dma(lambda e: e.dma_start(out=qn, in_=W["qn"]), W=("qn",))
            S.dma(lambda e: e.dma_start(out=kvn, in_=W["kvn"]), W=("kvn",))
            ones_f = nc_alloc(st, "ones_f", [128, 128], F32)
            S.dma(lambda e: e.dma_start(out=ones_f, in_=C["ones_f"]), W=("ones_f",))
            xts = [nc_alloc(st, "p1x%d" % i, [128, 4, D], F32) for i in range(2)]
            hT = nc_alloc(st, "p1hT", [128, 8, 512], BF16)
            tab = [nc_alloc(st, "p1tab%d" % i, [128, 512], F32) for i in range(4)]
            t1 = [nc_alloc(st, "p1t1_%d" % i, [128, 512], F32) for i in range(2)]
            t2 = [nc_alloc(st, "p1t2_%d" % i, [128, 512], F32) for i in range(2)]
            ob = [nc_alloc(st, "p1ob%d" % i, [128, 512], BF16) for i in range(4)]
            lat = [nc_alloc(st, "p1lat%d" % i, [128, 512], F32) for i in range(3)]
            sqb = nc_alloc(st, "p1sqb", [128, 512], F32)
            rstd = nc_alloc(st, "p1rstd", [128, 512], F32)
            gsb = nc_alloc(st, "p1gsb", [128, 512], F32)
            vst = [nc_alloc(st, "p1vst%d" % i, [128, 6, 128], BF16) for i in range(2)]
            for i in range(2):
                S.op("pool", lambda e, i=i: e.memset(vst[i], 1.0), W=(("vst", i),))
            obc = [0]

            def next_ob():
                i = obc[0]
                obc[0] = (i + 1) % 4
                return i

            def proj_fm(gname):
                gi = FMI[gname]
                b = ps_next()
                for k in range(8):
                    S.op("pe", lambda e, k=k, b=b, gi=gi: e.matmul(PS[b], lhsT=wfm[:, k, gi * 128:(gi + 1) * 128], rhs=hT[:, k, :], start=(k == 0), stop=(k == 7)),
                         R=(("wfm", k), ("hT", k)), W=(PR(b),))
                return b

            tc = [0]
            def p1_load(s):
                S.dma(lambda e: e.dma_start(out=xts[s % 2], in_=xrv[s]), R=(("xres", s),), W=(("p1x", s % 2),))

            p1_load(0)
            norm_A(xts[0], ("p1x", 0), tmp)
            for s in range(NS):
                xt = xts[s % 2]
                xr = ("p1x", s % 2)
                if s + 1 < NS:
                    p1_load(s + 1)
                for i, tsrc in enumerate((rcA, rsA, rcM, rsM)):
                    S.dma(lambda e, i=i, tsrc=tsrc, s=s: e.dma_start(out=tab[i], in_=tsrc[:, s * 512:(s + 1) * 512]), W=(("tab", i),))
                norm_B(hT, "hT", gmix, "gmix", tmp)
                if s + 1 < NS:
                    norm_A(xts[(s + 1) % 2], ("p1x", (s + 1) % 2), tmp)
                tsl = slice(s * 512, (s + 1) * 512)

                def rope_group(nm, dsts, ci=0, si=1, rows=None):
                    ba = proj_fm(nm)
                    bp = proj_fm(nm + "P") if not nm.startswith("ckr") else proj_fm("ckrP")
                    i = tc[0] % 2
                    tc[0] += 1
                    o = next_ob()
                    S.op("dve", lambda e, ba=ba, i=i: e.tensor_tensor(out=t1[i], in0=PS[ba], in1=tab[ci], op=ALU.mult),
                         R=(PR(ba), ("tab", ci)), W=(("t1", i),))
                    S.op("dve", lambda e, bp=bp, i=i: e.tensor_tensor(out=t2[i], in0=PS[bp], in1=tab[si], op=ALU.mult),
                         R=(PR(bp), ("tab", si)), W=(("t2", i),))
                    S.op("pool", lambda e, i=i, o=o: e.tensor_tensor(out=ob[o], in0=t1[i], in1=t2[i], op=ALU.add),
                         R=(("t1", i), ("t2", i)), W=(("ob", o),))
                    emit_out(dsts, o)

                def emit_out(dsts, o):
                    if len(dsts) == 1 and dsts[0][1] is None:
                        dst = dsts[0][0]
                        S.dma(lambda e, dst=dst, o=o: e.dma_start(out=dst, in_=ob[o]), R=(("ob", o),), W=(("p1out", u()),))
                    else:
                        for (dst, psl) in dsts:
                            S.dma(lambda e, dst=dst, psl=psl, o=o: e.dma_start(out=dst, in_=ob[o][psl, :]), R=(("ob", o),), W=(("p1out", u()),))

                def plain_group(nm, dsts):
                    b = proj_fm(nm)
                    o = next_ob()
                    S.op("act", lambda e, b=b, o=o: e.copy(out=ob[o], in_=PS[b]), R=(PR(b),), W=(("ob", o),))
                    emit_out(dsts, o)

                lo, hi = slice(0, 64), slice(64, 128)

                def both(Dt, i):
                    return [(Dt[:, 2 * i, tsl], lo), (Dt[:, 2 * i + 1, tsl], hi)]

                for i in range(4):
                    rope_group("aq%d" % i, both(QA, i))
                rope_group("ak", both(KA, 0))
                for i in range(4):
                    rope_group("bq%d" % i, both(QBR, i))
                    plain_group("bq%d" % i, both(QBU, i))
                rope_group("bks", both(KBS, 0))
                rope_group("bkw", both(KBW, 0))
                plain_group("bkc", both(KC, 0))
                plain_group("bvc", both(VC, 0))
                rope_group("ckr", [(KPE[:, tsl], slice(64, 96))], ci=2, si=3)
                b = proj_fm("bg")
                S.op("act", lambda e, b=b: e.activation(out=gsb[0:24, :], in_=PS[b][0:24, :], func=AF.Sigmoid), R=(PR(b),), W=("gsb",))
                S.dma(lambda e, tsl=tsl: e.dma_start(out=GB.rearrange("h j t -> (h j) t")[:, tsl], in_=gsb[0:24, :]), R=("gsb",), W=(("p1out", u()),), q="pool")

                def latent(names, gcol, gcreg, dstT, nfeat):
                    n = len(names)
                    bsum = ps_next()
                    for i, nm in enumerate(names):
                        b = proj_fm(nm)
                        S.op("act", lambda e, b=b, i=i: e.copy(out=lat[i], in_=PS[b]), R=(PR(b),), W=(("lat", i),))
                        S.op("act", lambda e, b=b: e.activation(out=sqb, in_=PS[b], func=AF.Square), R=(PR(b),), W=("sqb",))
                        S.op("pe", lambda e, i=i, bsum=bsum: e.matmul(PS[bsum], lhsT=ones_f, rhs=sqb, start=(i == 0), stop=(i == n - 1)),
                             R=("ones_f", "sqb"), W=(PR(bsum),))
                    S.op("act", lambda e, bsum=bsum: e.activation(out=rstd, in_=PS[bsum], func=AF.Sqrt, scale=1.0 / nfeat, bias=EPS),
                         R=(PR(bsum),), W=("rstd",))
                    S.op("dve", lambda e: e.reciprocal(out=rstd, in_=rstd), R=("rstd",), W=("rstd",))
                    for i in range(n):
                        o = next_ob()
                        S.op("dve", lambda e, i=i, o=o: e.scalar_tensor_tensor(out=ob[o], in0=lat[i], scalar=gcol[:, i:i + 1], in1=rstd, op0=ALU.mult, op1=ALU.mult),
                             R=(("lat", i), gcreg, "rstd"), W=(("ob", o),))
                        S.dma(lambda e, i=i, o=o, tsl=tsl: e.dma_start(out=dstT[:, i, tsl], in_=ob[o]), R=(("ob", o),), W=(("p1out", u()),), q="pool")

                latent(["cqa0", "cqa1", "cqa2"], qn, "qn", QLAT, 384)
                latent(["ckv0", "ckv1"], kvn, "kvn", CKV, 256)
                vi = s % 2
                for j in range(4):
                    b = ps_next()
                    for k in range(8):
                        S.op("pe", lambda e, k=k, b=b, j=j: e.matmul(PS[b][:, 0:384], lhsT=hT[:, k, j * 128:(j + 1) * 128], rhs=wtm[:, k, :], start=(k == 0), stop=(k == 7)),
                             R=(("hT", k), ("wtm", k)), W=(PR(b),))
                    S.op("act", lambda e, b=b, vi=vi: e.copy(out=vst[vi][:, :, 0:64], in_=PS[b][:, 0:384].rearrange("p (a c) -> p a c", c=64)),
                         R=(PR(b),), W=(("vst", vi),))
                    tok = slice(s * 512 + j * 128, s * 512 + (j + 1) * 128)
                    for m, dst in enumerate((VA, VBS, VBW)):
                        S.dma(lambda e, m=m, dst=dst, tok=tok, vi=vi: e.dma_start(out=dst[tok, :, :], in_=vst[vi][:, 2 * m:2 * m + 2, :]),
                              R=(("vst", vi),), W=(("p1out", u()),), q="pool")
            S.flush()
        if stop_after == "P1":
            break

        def bc4(ap):
            return ap.unsqueeze(1).broadcast_to([ap.shape[0], 4, 128])

        def load_consts_att(st, names):
            d = {}
            for nm in names:
                shp = cshapes[nm][0]
                dt = BF16 if cshapes[nm][1] == "bf16" else F32
                d[nm] = nc_alloc(st, "c_" + nm, shp, dt)
                S.dma(lambda e, nm=nm: e.dma_start(out=d[nm], in_=C[nm]), W=("c_" + nm,))
            return d

        class Att:
            def __init__(self, st, sbanks, nE=None):
                self.sb = sbanks
                self.sc = 0
                self.ec = 0
                self.depth = max(1, len(sbanks) - 1)
                nE = nE or (self.depth + 3)
                self.E = [nc_alloc(st, "attE%d" % i, [128, 512], BF16) for i in range(nE)]
                self.pend = []

            def tile(self, kT, kreg, q, qreg, masks, v, vreg, ob, first, last, scale, extra=None):
                b = self.sb[self.sc % len(self.sb)]
                self.sc += 1
                n = len(masks)
                kr = list(kreg) if isinstance(kreg, list) else [kreg]
                qr_ = list(qreg) if isinstance(qreg, list) else [qreg]
                S.op("pe", lambda e: e.matmul(PS[b], lhsT=kT, rhs=q, start=True, stop=(n == 0)), R=tuple(kr + qr_), W=(PR(b),))
                for i, (ml, mr, mregs) in enumerate(masks):
                    S.op("pe", lambda e, ml=ml, mr=mr, i=i: e.matmul(PS[b], lhsT=ml, rhs=mr, start=False, stop=(i == n - 1)), R=tuple(mregs), W=(PR(b),))
                ei = self.ec % len(self.E)
                self.ec += 1
                Et = self.E[ei]
                S.op("act", lambda e: e.activation(out=Et, in_=PS[b], func=AF.Exp, scale=scale), R=(PR(b),), W=(("E", ei),))

                def pv():
                    S.op("pe", lambda e: e.matmul(PS[ob], lhsT=v, rhs=Et, start=first, stop=last), R=(vreg, ("E", ei)), W=(PR(ob),))
                    if extra is not None:
                        extra(Et, ("E", ei))
                self.pend.append(pv)
                while len(self.pend) > self.depth:
                    self.pend.pop(0)()

            def drain(self):
                while self.pend:
                    self.pend.pop(0)()

        def fin_den(ob, den, dreg, sink=None, gate=None, greg=None):
            S.op("dve", lambda e: e.tensor_scalar(out=den, in0=PS[ob][64:128, :], scalar1=1e-30, scalar2=None, op0=ALU.max), R=(PR(ob),), W=(dreg,))
            if sink is not None:
                S.op("dve", lambda e: e.tensor_tensor(out=den.rearrange("p (h q) -> p h q", h=4), in0=den.rearrange("p (h q) -> p h q", h=4),
                                                      in1=sink.unsqueeze(2).broadcast_to([64, 4, 128]), op=ALU.add), R=(dreg, "esink"), W=(dreg,))
            S.op("dve", lambda e: e.reciprocal(out=den, in_=den), R=(dreg,), W=(dreg,))
            if gate is not None:
                S.op("pool", lambda e: e.tensor_tensor(out=den, in0=den, in1=gate, op=ALU.mult), R=(dreg, greg), W=(dreg,))

        qv = lambda Q, qb: Q[:, :, qb * 128:(qb + 1) * 128]

        def window_pass(tag, Qd, Kd, Vd, wt, sink, gate_j, part_in, Od):
            with ExitStack() as st:
                S.barrier()
                cc = load_consts_att(st, ["ident_b", "cb", "bb"])
                att = Att(st, [0, 1, 2, 3])
                kA = nc_alloc(st, tag + "k", [64, 2, T], BF16)
                vA = nc_alloc(st, tag + "v", [128, NT, 2, 128], BF16)
                for g in range(2):
                    S.dma(lambda e, g=g: e.dma_start(out=kA[:, g, :], in_=Kd[:, g, :]), W=((tag + "k", g),))
                Vr = Vd.rearrange("(n p) g c -> p n g c", p=128)
                for n0 in range(0, NT, 8):
                    S.dma(lambda e, n0=n0: e.dma_start(out=vA[:, n0:min(NT, n0 + 8)], in_=Vr[:, n0:min(NT, n0 + 8)]), W=((tag + "v", n0),))
                if sink:
                    es = nc_alloc(st, "esink", [64, 8], F32)
                    S.dma(lambda e: e.dma_start(out=es, in_=W["sinks"].broadcast_to([64, 8])), W=("esink",))
                    S.op("act", lambda e: e.activation(out=es, in_=es, func=AF.Exp), R=("esink",), W=("esink",))
                qts = [nc_alloc(st, tag + "q%d" % i, [64, 8, 128], BF16) for i in range(3)]
                dens = [nc_alloc(st, tag + "den%d" % i, [64, 512], F32) for i in range(2)]
                outs = [nc_alloc(st, tag + "out%d" % i, [64, 512], BF16) for i in range(2)]
                if gate_j is not None:
                    gts = [nc_alloc(st, tag + "gt%d" % i, [64, 4, 128], F32) for i in range(2)]
                    pts = [nc_alloc(st, tag + "pt%d" % i, [64, 4, 128], F32) for i in range(2)]
                    tmps = [nc_alloc(st, tag + "tmp%d" % i, [64, 512], F32) for i in range(2)]
                units = [(qb, g) for qb in range(NT) for g in range(2)]

                def wp_q(qb):
                    S.dma(lambda e: e.dma_start(out=qts[qb % 3], in_=qv(Qd, qb)), W=((tag + "q", qb % 3),))

                def wp_loads(ui):
                    if gate_j is None:
                        return
                    qb, g = units[ui]
                    i2 = ui % 2
                    qsl = slice(qb * 128, (qb + 1) * 128)
                    S.dma(lambda e: e.dma_start(out=gts[i2], in_=GB[4 * g:4 * g + 4, gate_j, qsl].unsqueeze(0).broadcast_to([64, 4, 128])), W=((tag + "gt", i2),))
                    S.dma(lambda e: e.dma_start(out=pts[i2], in_=part_in[:, 4 * g:4 * g + 4, qsl]), W=((tag + "pt", i2),))

                def act_recip(den, dreg):
                    S.op("act", lambda e: e.activation(out=den, in_=den, func=AF.Ln), R=(dreg,), W=(dreg,))
                    S.op("act", lambda e: e.activation(out=den, in_=den, func=AF.Exp, scale=-1.0), R=(dreg,), W=(dreg,))

                wp_q(0)
                if NT > 1:
                    wp_q(1)
                wp_loads(0)
                for ui, (qb, g) in enumerate(units):
                    if g == 0 and qb + 2 < NT:
                        wp_q(qb + 2)
                    if ui + 1 < len(units):
                        wp_loads(ui + 1)
                    qt = qts[qb % 3]
                    qr = (tag + "q", qb % 3)
                    i2 = ui % 2
                    ob = 4 + i2
                    qsl = slice(qb * 128, (qb + 1) * 128)
                    kts = [kt for kt in range(qb - wt, qb + 1) if kt >= 0]
                    for ti, kt in enumerate(kts):
                        masks = []
                        if kt == qb - wt:
                            masks.append((cc["ident_b"], bc4(cc["bb"]), ("c_ident_b", "c_bb")))
                        if kt == qb:
                            masks.append((cc["ident_b"], bc4(cc["cb"]), ("c_ident_b", "c_cb")))
                        att.tile(kA[:, g, kt * 128:(kt + 1) * 128], (tag + "k", g), qt[:, 4 * g:4 * g + 4, :], qr, masks,
                                 vA[:, kt, g, :], (tag + "v", (kt // 8) * 8), ob, ti == 0, ti == len(kts) - 1, 0.125)
                    att.drain()
                    den, dreg = dens[i2], (tag + "den", i2)
                    out, oreg = outs[i2], (tag + "out", i2)
                    S.op("dve", lambda e, ob=ob, den=den: e.tensor_copy(out=den, in_=PS[ob][64:128, :]), R=(PR(ob),), W=(dreg,))
                    if sink:
                        S.op("dve", lambda e, den=den, g=g: e.tensor_tensor(out=den.rearrange("p (h q) -> p h q", h=4), in0=den.rearrange("p (h q) -> p h q", h=4),
                                                                         in1=es[:, 4 * g:4 * g + 4].unsqueeze(2).broadcast_to([64, 4, 128]), op=ALU.add), R=(dreg, "esink"), W=(dreg,))
                    act_recip(den, dreg)
                    if gate_j is None:
                        S.op("dve", lambda e, ob=ob, den=den, out=out: e.tensor_tensor(out=out, in0=PS[ob][0:64, :], in1=den, op=ALU.mult),
                             R=(PR(ob), dreg), W=(oreg,))
                    else:
                        gt, pt, tmp = gts[i2], pts[i2], tmps[i2]
                        S.op("dve", lambda e, den=den, gt=gt: e.tensor_tensor(out=den, in0=den, in1=gt.rearrange("p h q -> p (h q)"), op=ALU.mult), R=(dreg, (tag + "gt", i2)), W=(dreg,))
                        S.op("dve", lambda e, ob=ob, den=den, tmp=tmp: e.tensor_tensor(out=tmp, in0=PS[ob][0:64, :], in1=den, op=ALU.mult),
                             R=(PR(ob), dreg), W=((tag + "tmp", i2),))
                        S.op("dve", lambda e, tmp=tmp, pt=pt, out=out: e.tensor_tensor(out=out, in0=tmp, in1=pt.rearrange("p h q -> p (h q)"), op=ALU.add),
                             R=((tag + "tmp", i2), (tag + "pt", i2)), W=(oreg,))
                    S.dma(lambda e, out=out, g=g, qsl=qsl: e.dma_start(out=Od[:, 4 * g:4 * g + 4, qsl], in_=out.rearrange("p (h q) -> p h q", h=4)),
                          R=(oreg,), W=(("wout", u()),))
                S.flush()

        window_pass("pa", QA, KA, VA, 1, True, None, None, OA)
        if stop_after == "PA":
            break

        with ExitStack() as st:
            S.barrier()
            kc = nc_alloc(st, "kc", [64, 4, T], BF16)
            for g in range(2):
                S.dma(lambda e, g=g: e.dma_start(out=kc[:, g, :], in_=KC[:, g, :]), W=(("kc", g),))
                S.dma(lambda e, g=g: e.dma_start(out=kc[:, 2 + g, :], in_=VC[:, g, :]), W=(("kc", 2 + g),))
            w1 = nc_alloc(st, "w1", [64, 2, 32 * 64], BF16)
            S.dma(lambda e: e.dma_start(out=w1[:, 0, :], in_=W["wk1"].rearrange("d l o -> d (l o)")), W=("w1",), q="pool")
            S.dma(lambda e: e.dma_start(out=w1[:, 1, :], in_=W["wv1"].rearrange("d l o -> d (l o)")), W=("w1b",), q="pool")
            w2 = nc_alloc(st, "w2", [64, 2, 64], BF16)
            S.dma(lambda e: e.dma_start(out=w2[:, 0, :], in_=W["wk2"]), W=("w2",), q="pool")
            S.dma(lambda e: e.dma_start(out=w2[:, 1, :], in_=W["wv2"]), W=("w2b",), q="pool")
            pe = nc_alloc(st, "pe", [64, 2, 32], BF16)
            S.dma(lambda e: e.dma_start(out=pe[:, 0, :], in_=W["pek"]), W=("pe",), q="pool")
            S.dma(lambda e: e.dma_start(out=pe[:, 1, :], in_=W["pev"]), W=("peb",), q="pool")
            NC_ = NCT * 128
            hid = nc_alloc(st, "hid", [64, NC_], BF16)
            kcmp = nc_alloc(st, "kcmp", [64, 2, NC_], BF16)
            vcmp = nc_alloc(st, "vcmp", [128, NCT, 2, 128], BF16)
            S.op("pool", lambda e: e.memset(vcmp, 1.0), W=("vcmp",))
            S.op("pool", lambda e: e.memset(hid, 0.0), W=("hid",))
            for kv in range(2):
                for g in range(2):
                    b = 6
                    src = kc[:, 2 * kv + g, :].rearrange("p (c r) -> p c r", r=16)
                    for li in range(32):
                        rhs = src[:, (li // 16):(li // 16) + n_cmp, li % 16]
                        S.op("pe", lambda e, l=li, rhs=rhs, kv=kv: e.matmul(PS[b][0:64, 0:n_cmp], lhsT=w1[:, kv, l * 64:(l + 1) * 64], rhs=rhs, start=(l == 0), stop=False),
                             R=(("kc", 2 * kv + g), "w1", "w1b"), W=(PR(b),))
                    for li in range(32):
                        S.op("pe", lambda e, l=li, kv=kv: e.matmul(PS[b][0:64, 0:n_cmp], lhsT=w1[:, kv, l * 64:(l + 1) * 64], rhs=pe[:, kv, l:l + 1].broadcast_to([64, n_cmp]), start=False, stop=(l == 31)),
                             R=("pe", "peb", "w1", "w1b"), W=(PR(b),))
                    S.op("act", lambda e: e.activation(out=hid[:, 0:n_cmp], in_=PS[b][0:64, 0:n_cmp], func=AF.Silu), R=(PR(b),), W=("hid",))
                    if kv == 0:
                        b2 = 7
                        S.op("pe", lambda e: e.matmul(PS[b2][0:64, 0:NC_], lhsT=w2[:, 0, :], rhs=hid, start=True, stop=True), R=("hid", "w2"), W=(PR(b2),))
                        S.op("act", lambda e, g=g: e.copy(out=kcmp[:, g, :], in_=PS[b2][0:64, 0:NC_]), R=(PR(b2),), W=(("kcmp", g),))
                    else:
                        b2 = 7
                        for ct in range(NCT):
                            S.op("pe", lambda e, ct=ct: e.matmul(PS[b2][:, ct * 64:(ct + 1) * 64], lhsT=hid[:, ct * 128:(ct + 1) * 128], rhs=w2[:, 1, :], start=(ct == 0), stop=(ct == NCT - 1)),
                                 R=("hid", "w2b"), W=(PR(b2),))
                        S.op("act", lambda e, g=g: e.copy(out=vcmp[:, :, g, 0:64], in_=PS[b2][:, 0:NCT * 64].rearrange("p (c d) -> p c d", d=64)), R=(PR(b2),), W=("vcmp",))
            S.dma(lambda e: e.dma_start(out=KCMP, in_=kcmp), R=(("kcmp", 0), ("kcmp", 1)), W=("KCMPd",))
            S.dma(lambda e: e.dma_start(out=VCMP, in_=vcmp), R=("vcmp",), W=("VCMPd",))
            S.flush()
        with ExitStack() as st:
            S.barrier()
            cc = load_consts_att(st, ["ident_b", "ident_f", "cb", "cm", "msel", "fb"])
            att = Att(st, [0, 1, 2])
            NC_ = NCT * 128
            kcmp = nc_alloc(st, "kcmp2", [64, 2, NC_], BF16)
            vcmp = nc_alloc(st, "vcmp2", [128, NCT, 2, 128], BF16)
            S.dma(lambda e: e.dma_start(out=kcmp, in_=KCMP), R=("KCMPd",), W=(("kcmp", 0), ("kcmp", 1)))
            S.dma(lambda e: e.dma_start(out=vcmp, in_=VCMP), R=("VCMPd",), W=("vcmp",))
            kS = nc_alloc(st, "pbk", [128, 2, T], BF16)
            for g in range(2):
                S.dma(lambda e, g=g: e.dma_start(out=kS[64:128, g, :], in_=C["exh"]), W=(("exh", g),))
            QS = [[nc_alloc(st, "QS%d_%d" % (i, hf), [128, 512], BF16) for hf in range(2)] for i in range(2)]
            vS = nc_alloc(st, "pbv", [128, NT, 2, 128], BF16)
            for g in range(2):
                S.dma(lambda e, g=g: e.dma_start(out=kS[0:64, g, :], in_=KBS[:, g, :]), W=(("pbk", g),))
            Vr = VBS.rearrange("(n p) g c -> p n g c", p=128)
            for n0 in range(0, NT, 8):
                S.dma(lambda e, n0=n0: e.dma_start(out=vS[:, n0:min(NT, n0 + 8)], in_=Vr[:, n0:min(NT, n0 + 8)]), W=(("pbv", n0),))
            qus = [nc_alloc(st, "pbqu%d" % i, [64, 8, 128], BF16) for i in range(3)]
            qrs = [nc_alloc(st, "pbqr%d" % i, [64, 8, 128], BF16) for i in range(3)]
            gts = [nc_alloc(st, "pbgt%d" % i, [64, 2, 4, 128], F32) for i in range(2)]
            dens = [nc_alloc(st, "pbden%d" % i, [64, 512], F32) for i in range(2)]
            rcmp = [nc_alloc(st, "pbrc%d" % i, [64, 512], F32) for i in range(2)]
            tmps = [nc_alloc(st, "pbtmp%d" % i, [64, 512], F32) for i in range(2)]
            outs = [nc_alloc(st, "pbout%d" % i, [64, 512], F32) for i in range(2)]
            rd4 = nc_alloc(st, "rd4", [128, 4], F32)
            scr = nc_alloc(st, "scr", [128, 128], F32)
            scr2 = nc_alloc(st, "scr2", [128, 128], F32)
            m8 = nc_alloc(st, "m8", [128, 16], F32)
            selms = [nc_alloc(st, "selm%d" % i, [128, 128], F32) for i in range(2)]
            selT = [nc_alloc(st, "selT%d" % i, [128, 128], BF16) for i in range(2)]
            units = [(qb, g) for qb in range(NT) for g in range(2)]

            def loadq(qb):
                i = qb % 3
                S.dma(lambda e: e.dma_start(out=qus[i], in_=qv(QBU, qb)), W=(("pbqu", i),))
                S.dma(lambda e: e.dma_start(out=qrs[i], in_=qv(QBR, qb)), W=(("pbqr", i),))

            def cmp_job(ui):
                qb, g = units[ui]
                i2 = ui % 2
                qsl = slice(qb * 128, (qb + 1) * 128)
                gt = gts[i2]
                for jj in range(2):
                    S.dma(lambda e, jj=jj: e.dma_start(out=gt[:, jj, :, :], in_=GB[4 * g:4 * g + 4, jj, qsl].unsqueeze(0).broadcast_to([64, 4, 128])), W=(("pbgt", i2, jj),))
                nct = min(NCT, (8 * qb + 7 + 127) // 128)
                ob = 4
                for ct in range(nct):
                    dl = qb - 16 * ct
                    masks = []
                    if dl <= 16:
                        masks.append((cc["ident_b"], bc4(cc["cm"][:, dl * 128:(dl + 1) * 128]), ("c_ident_b", "c_cm")))

                    def extra(Et, ereg, ct=ct):
                        for h in range(4):
                            bi = 5 + h // 2
                            co = (h % 2) * 129
                            S.op("pe", lambda e, h=h, bi=bi, co=co: e.matmul(PS[bi][:, co:co + 129], lhsT=Et[:, h * 128:(h + 1) * 128], rhs=cc["msel"][:, ct * 129:(ct + 1) * 129],
                                                                          start=(ct == 0 and h % 2 == 0), stop=(ct == nct - 1), skip_group_check=True),
                                 R=(ereg, "c_msel"), W=(PR(bi),))
                    att.tile(kcmp[:, g, ct * 128:(ct + 1) * 128], ("kcmp", g), qus[qb % 3][:, 4 * g:4 * g + 4, :], ("pbqu", qb % 3), masks,
                             vcmp[:, ct, g, :], "vcmp", ob, ct == 0, ct == nct - 1, 0.125, extra=extra)
                att.drain()
                den, dreg = dens[i2], ("pbden", i2)
                fin_den(ob, den, dreg, gate=gt[:, 0, :, :].rearrange("p h q -> p (h q)"), greg=("pbgt", i2, 0))
                S.op("dve", lambda e: e.tensor_tensor(out=rcmp[i2], in0=PS[ob][0:64, :], in1=den, op=ALU.mult), R=(PR(ob), dreg), W=(("pbrc", i2),))
                for h in range(4):
                    bi = 5 + h // 2
                    co = (h % 2) * 129 + 128
                    S.op("dve", lambda e, h=h, bi=bi, co=co: e.tensor_scalar(out=rd4[:, h:h + 1], in0=PS[bi][:, co:co + 1], scalar1=1e-30, scalar2=None, op0=ALU.max),
                         R=(PR(bi),), W=("rd4",))
                S.op("dve", lambda e: e.reciprocal(out=rd4, in_=rd4), R=("rd4",), W=("rd4",))
                for h in range(4):
                    bi = 5 + h // 2
                    co = (h % 2) * 129
                    if h == 0:
                        S.op("dve", lambda e, bi=bi, co=co: e.tensor_scalar(out=scr, in0=PS[bi][:, co:co + 128], scalar1=rd4[:, 0:1], scalar2=None, op0=ALU.mult),
                             R=(PR(bi), "rd4"), W=("scr",))
                    else:
                        S.op("dve", lambda e, h=h, bi=bi, co=co: e.scalar_tensor_tensor(out=scr, in0=PS[bi][:, co:co + 128], scalar=rd4[:, h:h + 1], in1=scr, op0=ALU.mult, op1=ALU.add),
                             R=(PR(bi), "rd4", "scr"), W=("scr",))
                S.op("dve", lambda e: e.tensor_tensor(out=scr, in0=scr, in1=cc["fb"][:, 128 - 2 * qb:256 - 2 * qb], op=ALU.add), R=("scr", "c_fb"), W=("scr",))
                S.op("dve", lambda e: e.tensor_scalar(out=scr[:, 0:1], in0=scr[:, 0:1], scalar1=1e4, scalar2=None, op0=ALU.add), R=("scr",), W=("scr",))
                S.op("dve", lambda e: e.max(out=m8[:, 0:8], in_=scr), R=("scr",), W=("m8",))
                S.op("dve", lambda e: e.match_replace(out=scr2, in_to_replace=m8[:, 0:8], in_values=scr, imm_value=-1e30), R=("scr", "m8"), W=("scr2",))
                S.op("dve", lambda e: e.max(out=m8[:, 8:16], in_=scr2), R=("scr2",), W=("m8b",))
                S.op("dve", lambda e: e.tensor_scalar(out=selms[i2], in0=scr, scalar1=m8[:, 15:16], scalar2=1.0, op0=ALU.is_ge, op1=ALU.subtract), R=("scr", "m8b"), W=(("selm", i2),))

            def cmp_job2(ui):
                qb, g = units[ui]
                i2 = ui % 2
                S.op("pe", lambda e: e.transpose(out=PS[7][:, 0:128], in_=selms[i2], identity=cc["ident_f"]), R=(("selm", i2), "c_ident_f"), W=(PR(7),))
                nh = 2 if qb >= 32 else 1
                for hf in range(nh):
                    S.op("dve", lambda e, hf=hf: e.tensor_copy(out=QS[i2][hf][64:128, :].rearrange("p (h q) -> p h q", h=4),
                                                              in_=PS[7][hf * 64:(hf + 1) * 64, 0:128].unsqueeze(1).broadcast_to([64, 4, 128])),
                         R=(PR(7),), W=(("QS", i2, hf, "m"),))
                    S.op("dve", lambda e, hf=hf: e.tensor_copy(out=QS[i2][hf][0:64, :].rearrange("p (h q) -> p h q", h=4), in_=qrs[qb % 3][:, 4 * g:4 * g + 4, :]),
                         R=(("pbqr", qb % 3),), W=(("QS", i2, hf, "q"),))

            def sel_job(ui, nxt=None):
                qb, g = units[ui]
                i2 = ui % 2
                ob = 3
                qsl = slice(qb * 128, (qb + 1) * 128)
                for kt in range(qb + 1):
                    hf = kt // 32
                    masks = []
                    if kt == qb:
                        masks.append((cc["ident_b"], bc4(cc["cb"]), ("c_ident_b", "c_cb")))
                    att.tile(kS[:, g, kt * 128:(kt + 1) * 128], [("pbk", g), ("exh", g)], QS[i2][hf], [("QS", i2, hf, "m"), ("QS", i2, hf, "q")], masks,
                             vS[:, kt, g, :], ("pbv", (kt // 8) * 8), ob, kt == 0, kt == qb, 0.125)
                att.drain()
                if nxt is not None:
                    cmp_job2(nxt)
                den, dreg = dens[i2], ("pbden", i2)
                fin_den(ob, den, dreg, gate=gts[i2][:, 1, :, :].rearrange("p h q -> p (h q)"), greg=("pbgt", i2, 1))
                S.op("dve", lambda e: e.tensor_tensor(out=tmps[i2], in0=PS[ob][0:64, :], in1=den, op=ALU.mult), R=(PR(ob), dreg), W=(("pbtmp", i2),))
                S.op("dve", lambda e: e.tensor_tensor(out=outs[i2], in0=tmps[i2], in1=rcmp[i2], op=ALU.add), R=(("pbtmp", i2), ("pbrc", i2)), W=(("pbout", i2),))
                S.dma(lambda e: e.dma_start(out=OBP[:, 4 * g:4 * g + 4, qsl], in_=outs[i2].rearrange("p (h q) -> p h q", h=4)), R=(("pbout", i2),), W=(("wout", u()),))

            loadq(0)
            if NT > 1:
                loadq(1)
            cmp_job(0)
            cmp_job2(0)
            for ui in range(len(units)):
                qb, g = units[ui]
                if g == 0 and qb + 2 < NT:
                    loadq(qb + 2)
                if ui + 1 < len(units):
                    cmp_job(ui + 1)
                sel_job(ui, (ui + 1) if ui + 1 < len(units) else None)
            S.flush()
        if stop_after == "PB1":
            break

        window_pass("pw", QBR, KBW, VBW, 4, False, 2, OBP, OB)
        if stop_after == "PB2":
            break

        with ExitStack() as st:
            S.barrier()
            cc = load_consts_att(st, ["ident_b", "mm"])
            att = Att(st, [0, 1, 2, 3])
            qlat = nc_alloc(st, "qlat", [128, 3, T], BF16)
            ckv = nc_alloc(st, "ckv", [128, 2, T], BF16)
            for c in range(3):
                S.dma(lambda e, c=c: e.dma_start(out=qlat[:, c, :], in_=QLAT[:, c, :]), W=(("qlat", c),))
            for c in range(2):
                S.dma(lambda e, c=c: e.dma_start(out=ckv[:, c, :], in_=CKV[:, c, :]), W=(("ckv", c),))
            wq = nc_alloc(st, "wq", [128, 3, 768], BF16)
            wqp = nc_alloc(st, "wqp", [128, 3, 768], BF16)
            wkk = nc_alloc(st, "wkk", [128, 2, 512], BF16)
            wkv_ = nc_alloc(st, "wkv", [128, 2, 512], BF16)
            S.dma(lambda e: e.dma_start(out=wq, in_=W["wq"]), W=("wq",), q="pool")
            S.dma(lambda e: e.dma_start(out=wqp, in_=W["wqp"]), W=("wqp",), q="pool")
            S.dma(lambda e: e.dma_start(out=wkk, in_=W["wkvk"]), W=("wkk",), q="pool")
            S.dma(lambda e: e.dma_start(out=wkv_, in_=W["wkvv"]), W=("wkv",), q="pool")
            KH = nc_alloc(st, "KH", [96, T], BF16)
            VH = nc_alloc(st, "VH", [128, NT, 128], BF16)
            S.op("pool", lambda e: e.memset(VH, 1.0), W=("VH",))
            S.dma(lambda e: e.dma_start(out=KH[64:96, :], in_=KPE), W=("KHpe",))
            QH = [nc_alloc(st, "QH%d" % i, [96, 512], BF16) for i in range(2)]
            tabc = [nc_alloc(st, "mtc%d" % i, [96, 512], F32) for i in range(2)]
            tabs = [nc_alloc(st, "mts%d" % i, [96, 512], F32) for i in range(2)]
            t1 = nc_alloc(st, "mt1", [96, 512], F32)
            t2 = nc_alloc(st, "mt2", [96, 512], F32)
            dens = [nc_alloc(st, "mden%d" % i, [64, 512], F32) for i in range(2)]
            outs = [nc_alloc(st, "mout%d" % i, [64, 512], BF16) for i in range(2)]
            ucl = [0]

            def mla_head(h):
                for s in range(NS):
                    b = 6 + (s % 2)
                    for c in range(2):
                        S.op("pe", lambda e, c=c, b=b, s=s: e.matmul(PS[b][0:64, :], lhsT=wkk[:, c, h * 64:(h + 1) * 64], rhs=ckv[:, c, s * 512:(s + 1) * 512], start=(c == 0), stop=(c == 1)),
                             R=("wkk", ("ckv", c)), W=(PR(b),))
                    S.op("dve", lambda e, b=b, s=s: e.tensor_copy(out=KH[0:64, s * 512:(s + 1) * 512], in_=PS[b][0:64, :]), R=(PR(b),), W=("KHn",))
                for i4 in range(NT // 4):
                    b = 6 + (i4 % 2)
                    for t4 in range(4):
                        tt = i4 * 4 + t4
                        for c in range(2):
                            S.op("pe", lambda e, c=c, b=b, tt=tt, t4=t4: e.matmul(PS[b][:, t4 * 64:(t4 + 1) * 64], lhsT=ckv[:, c, tt * 128:(tt + 1) * 128], rhs=wkv_[:, c, h * 64:(h + 1) * 64],
                                                                             start=(t4 == 0 and c == 0), stop=(t4 == 3 and c == 1), skip_group_check=True),
                                 R=("wkv", ("ckv", c)), W=(PR(b),))
                    S.op("act", lambda e, b=b, i4=i4: e.copy(out=VH[:, i4 * 4:(i4 + 1) * 4, 0:64], in_=PS[b][:, 0:256].rearrange("p (a d) -> p a d", d=64)), R=(PR(b),), W=("VH",))
                for qs in range(NS):
                    i2 = ucl[0] % 2
                    ucl[0] += 1
                    ob = 4 + i2
                    tsl = slice(qs * 512, (qs + 1) * 512)
                    S.dma(lambda e, i2=i2, tsl=tsl: e.dma_start(out=tabc[i2][64:96, :], in_=rcM[64:96, tsl]), W=(("mtc", i2),))
                    S.dma(lambda e, i2=i2, tsl=tsl: e.dma_start(out=tabs[i2][64:96, :], in_=rsM[64:96, tsl]), W=(("mts", i2),))
                    ba, bb_ = 6, 7
                    for c in range(3):
                        S.op("pe", lambda e, c=c, tsl=tsl: e.matmul(PS[ba][0:96, :], lhsT=wq[:, c, h * 96:(h + 1) * 96], rhs=qlat[:, c, tsl], start=(c == 0), stop=(c == 2)),
                             R=("wq", ("qlat", c)), W=(PR(ba),))
                    for c in range(3):
                        S.op("pe", lambda e, c=c, tsl=tsl: e.matmul(PS[bb_][0:96, :], lhsT=wqp[:, c, h * 96:(h + 1) * 96], rhs=qlat[:, c, tsl], start=(c == 0), stop=(c == 2)),
                             R=("wqp", ("qlat", c)), W=(PR(bb_),))
                    qh, qreg = QH[i2], ("QH", i2)
                    S.op("act", lambda e, qh=qh: e.copy(out=qh[0:64, :], in_=PS[ba][0:64, :]), R=(PR(ba),), W=((qreg, "n"),))
                    S.op("dve", lambda e, i2=i2: e.tensor_tensor(out=t1[64:96, :], in0=PS[ba][64:96, :], in1=tabc[i2][64:96, :], op=ALU.mult), R=(PR(ba), ("mtc", i2)), W=("mt1",))
                    S.op("dve", lambda e, i2=i2: e.tensor_tensor(out=t2[64:96, :], in0=PS[bb_][64:96, :], in1=tabs[i2][64:96, :], op=ALU.mult), R=(PR(bb_), ("mts", i2)), W=("mt2",))
                    S.op("pool", lambda e, qh=qh: e.tensor_tensor(out=qh[64:96, :], in0=t1[64:96, :], in1=t2[64:96, :], op=ALU.add), R=("mt1", "mt2"), W=((qreg, "r"),))
                    nk = 4 * qs + 4
                    for kt in range(nk):
                        masks = []
                        if kt >= 4 * qs:
                            i = kt - 4 * qs
                            masks.append((cc["ident_b"], cc["mm"][:, i * 512:(i + 1) * 512], ("c_ident_b", "c_mm")))
                        att.tile(KH[:, kt * 128:(kt + 1) * 128], ["KHn", "KHpe"], qh, [(qreg, "n"), (qreg, "r")], masks, VH[:, kt, :], "VH", ob, kt == 0, kt == nk - 1, float(96 ** -0.5))
                    att.drain()
                    den, dreg = dens[i2], ("mden", i2)
                    fin_den(ob, den, dreg)
                    out = outs[i2]
                    S.op("dve", lambda e, ob=ob, den=den, out=out: e.tensor_tensor(out=out, in0=PS[ob][0:64, :], in1=den, op=ALU.mult), R=(PR(ob), dreg), W=(("mout", i2),))
                    S.dma(lambda e, out=out, tsl=tsl: e.dma_start(out=OC[:, h, tsl], in_=out), R=(("mout", i2),), W=(("wout", u()),), q="pool")
            for h_ in range(8):
                mla_head(h_)
            S.flush()
        if stop_after == "PC":
            break

        with ExitStack() as st:
            S.barrier()
            tmp = norm_tmp(st)
            wg = nc_alloc(st, "wg", [128, 8, 3072], BF16)
            for k in range(8):
                S.dma(lambda e, k=k: e.dma_start(out=wg[:, k, :], in_=W["wg"][k * 128:(k + 1) * 128, :]), W=(("wg", k),))
            wbr = nc_alloc(st, "wbr", [64, 3, 8, 1024], BF16)
            for xi, nm in enumerate(("wbra", "wbrb", "wbrc")):
                S.dma(lambda e, xi=xi, nm=nm: e.dma_start(out=wbr[:, xi, :, :], in_=W[nm]), W=(("wbr", xi),))
            wout = nc_alloc(st, "wout", [128, 8, 1024], BF16)
            S.dma(lambda e: e.dma_start(out=wout, in_=W["wout"]), W=("wout",))
            gmix = nc_alloc(st, "gmix", [128, 8], F32)
            S.dma(lambda e: e.dma_start(out=gmix, in_=W["nmix"]), W=("gmix",))
            xt1 = nc_alloc(st, "pmx", [128, 4, D], F32)
            xts = [xt1, xt1]
            hT = nc_alloc(st, "pmhT", [128, 8, 512], BF16)
            oin1 = [nc_alloc(st, "pmo_%d" % xi, [64, 8, 512], BF16) for xi in range(3)]
            oin = [oin1, oin1]
            gsb = [nc_alloc(st, "pmg%d" % i, [128, 512], F32) for i in range(2)]
            tmpm = nc_alloc(st, "pmt", [128, 512], F32)
            macc = nc_alloc(st, "pmacc", [128, 512], F32)
            mT = nc_alloc(st, "pmmT", [128, 8, 512], BF16)
            gc = 0
            for s in range(NS):
                xt, xr = xts[s % 2], ("pmx", 0)
                tsl = slice(s * 512, (s + 1) * 512)
                S.dma(lambda e, xt=xt, s=s: e.dma_start(out=xt, in_=xrv[s]), R=(("xres", s),), W=(xr,))
                for xi, Od in enumerate((OA, OB, OC)):
                    S.dma(lambda e, xi=xi, Od=Od, tsl=tsl, s=s: e.dma_start(out=oin[s % 2][xi], in_=Od[:, :, tsl]), W=(("pmo", 0, xi),))
                norm_T(xt, xr, hT, "pmhT", gmix, "gmix", tmp)
                for cg in range(8):
                    for xi in range(3):
                        bp = ps_next()
                        for h in range(8):
                            S.op("pe", lambda e, bp=bp, xi=xi, cg=cg, s=s, h=h: e.matmul(PS[bp], lhsT=wbr[:, xi, h, cg * 128:(cg + 1) * 128],
                                                                                  rhs=oin[s % 2][xi][:, h, :], start=(h == 0), stop=(h == 7)),
                                 R=(("wbr", xi), ("pmo", 0, xi)), W=(PR(bp),))
                        bg = ps_next()
                        for k in range(8):
                            S.op("pe", lambda e, bg=bg, xi=xi, k=k, cg=cg: e.matmul(PS[bg], lhsT=wg[:, k, xi * 1024 + cg * 128: xi * 1024 + (cg + 1) * 128], rhs=hT[:, k, :], start=(k == 0), stop=(k == 7)),
                                 R=(("wg", k), ("pmhT", k)), W=(PR(bg),))
                        gi = gc % 2
                        gc += 1
                        S.op("act", lambda e, bg=bg, gi=gi: e.activation(out=gsb[gi], in_=PS[bg], func=AF.Sigmoid), R=(PR(bg),), W=(("pmg", gi),))
                        if xi == 0:
                            S.op("dve", lambda e, bp=bp, gi=gi: e.tensor_tensor(out=macc, in0=PS[bp], in1=gsb[gi], op=ALU.mult), R=(PR(bp), ("pmg", gi)), W=("pmacc",))
                        else:
                            S.op("dve", lambda e, bp=bp, gi=gi: e.tensor_tensor(out=tmpm, in0=PS[bp], in1=gsb[gi], op=ALU.mult), R=(PR(bp), ("pmg", gi)), W=("pmt",))
                            if xi == 1:
                                S.op("dve", lambda e: e.tensor_tensor(out=macc, in0=macc, in1=tmpm, op=ALU.add), R=("pmacc", "pmt"), W=("pmacc",))
                            else:
                                S.op("dve", lambda e, cg=cg: e.tensor_tensor(out=mT[:, cg, :], in0=macc, in1=tmpm, op=ALU.add), R=("pmacc", "pmt"), W=(("pmmT", cg),))
                for j in range(4):
                    for half in range(2):
                        b = ps_next()
                        for cg in range(8):
                            S.op("pe", lambda e, b=b, cg=cg, j=j, half=half: e.matmul(PS[b], lhsT=mT[:, cg, j * 128:(j + 1) * 128], rhs=wout[:, cg, half * 512:(half + 1) * 512], start=(cg == 0), stop=(cg == 7)),
                                 R=(("pmmT", cg), "wout"), W=(PR(b),))
                        S.op("dve", lambda e, b=b, j=j, half=half, xt=xt: e.tensor_tensor(out=xt[:, j, half * 512:(half + 1) * 512], in0=PS[b], in1=xt[:, j, half * 512:(half + 1) * 512], op=ALU.add),
                             R=(PR(b), xr), W=(xr,))
                S.dma(lambda e, xt=xt, s=s: e.dma_start(out=xrv[s], in_=xt), R=(xr,), W=(("xres", s),))
            S.flush()
        if stop_after == "PM":
            break

        with ExitStack() as st:
            S.barrier()
            tmp = norm_tmp(st)
            wxq = nc_alloc(st, "wxq", [128, 8, 512], BF16)
            wxkv = nc_alloc(st, "wxkv", [128, 8, 1024], BF16)
            wxo = nc_alloc(st, "wxo", [128, 4, 1024], BF16)
            S.dma(lambda e: e.dma_start(out=wxq, in_=W["wxq"]), W=("wxq",))
            S.dma(lambda e: e.dma_start(out=wxkv, in_=W["wxkv"]), W=("wxkv",))
            S.dma(lambda e: e.dma_start(out=wxo, in_=W["wxo"]), W=("wxo",))
            gx = nc_alloc(st, "gx", [128, 8], F32)
            gm = nc_alloc(st, "gm", [128, 8], F32)
            S.dma(lambda e: e.dma_start(out=gx, in_=W["nx"]), W=("gx",))
            S.dma(lambda e: e.dma_start(out=gm, in_=W["nmem"]), W=("gm",))
            ones_b = nc_alloc(st, "ones_b", [128, 128], BF16)
            S.dma(lambda e: e.dma_start(out=ones_b, in_=C["ones_f"]), W=("ones_b",))
            xts = [nc_alloc(st, "pxx%d" % i, [128, 4, D], F32) for i in range(2)]
            hT = nc_alloc(st, "pxhT", [128, 8, 512], BF16)
            KM = nc_alloc(st, "KM", [128, 4, 256], BF16)
            VM = nc_alloc(st, "VM", [128, 2, 512], BF16)
            memt = xts[1]
            S.dma(lambda e: e.dma_start(out=memt[:, 0:2, :], in_=mem_in.rearrange("(j p) c -> p j c", p=128)), W=(("pxx", 1),))
            norm_T(memt, ("pxx", 1), hT, "pxhT", gm, "gm", tmp, ntok_tiles=2)
            for h in range(4):
                b = ps_next()
                for k in range(8):
                    S.op("pe", lambda e, b=b, k=k, h=h: e.matmul(PS[b][:, 0:256], lhsT=wxkv[:, k, h * 128:(h + 1) * 128], rhs=hT[:, k, 0:256], start=(k == 0), stop=(k == 7)),
                         R=("wxkv", ("pxhT", k)), W=(PR(b),))
                S.op("act", lambda e, b=b, h=h: e.copy(out=KM[:, h, :], in_=PS[b][:, 0:256]), R=(PR(b),), W=("KM",))
            for mt in range(2):
                b = ps_next()
                for k in range(8):
                    S.op("pe", lambda e, b=b, k=k, mt=mt: e.matmul(PS[b], lhsT=hT[:, k, mt * 128:(mt + 1) * 128], rhs=wxkv[:, k, 512:1024], start=(k == 0), stop=(k == 7)),
                         R=("wxkv", ("pxhT", k)), W=(PR(b),))
                S.op("act", lambda e, b=b, mt=mt: e.copy(out=VM[:, mt, :], in_=PS[b]), R=(PR(b),), W=("VM",))
            qx = [nc_alloc(st, "qx%d" % i, [128, 512], BF16) for i in range(2)]
            Ex = [nc_alloc(st, "Ex%d" % i, [128, 512], BF16) for i in range(4)]
            denx = nc_alloc(st, "denx", [128, 512], F32)
            oxT = nc_alloc(st, "oxT", [128, 4, 512], BF16)
            ec = 0
            def px_load(s):
                S.dma(lambda e: e.dma_start(out=xts[s % 2], in_=xrv[s]), R=(("xres", s),), W=(("pxx", s % 2),))

            px_load(0)
            norm_A(xts[0], ("pxx", 0), tmp)
            for s in range(NS):
                xt, xr = xts[s % 2], ("pxx", s % 2)
                if s + 1 < NS:
                    px_load(s + 1)
                norm_B(hT, "pxhT", gx, "gx", tmp)
                if s + 1 < NS:
                    norm_A(xts[(s + 1) % 2], ("pxx", (s + 1) % 2), tmp)
                for h in range(4):
                    bq = ps_next()
                    for k in range(8):
                        S.op("pe", lambda e, bq=bq, k=k, h=h: e.matmul(PS[bq], lhsT=wxq[:, k, h * 128:(h + 1) * 128], rhs=hT[:, k, :], start=(k == 0), stop=(k == 7)),
                             R=("wxq", ("pxhT", k)), W=(PR(bq),))
                    qi = h % 2
                    S.op("act", lambda e, bq=bq, qi=qi: e.copy(out=qx[qi], in_=PS[bq]), R=(PR(bq),), W=(("qx", qi),))
                    bo = ps_next()
                    bd = ps_next()
                    for mt in range(2):
                        bs = ps_next()
                        S.op("pe", lambda e, bs=bs, h=h, mt=mt, qi=qi: e.matmul(PS[bs], lhsT=KM[:, h, mt * 128:(mt + 1) * 128], rhs=qx[qi], start=True, stop=True),
                             R=("KM", ("qx", qi)), W=(PR(bs),))
                        ei = ec % 4
                        ec += 1
                        S.op("act", lambda e, bs=bs, ei=ei: e.activation(out=Ex[ei], in_=PS[bs], func=AF.Exp, scale=float(128 ** -0.5)), R=(PR(bs),), W=(("Ex", ei),))
                        S.op("pe", lambda e, bo=bo, h=h, mt=mt, ei=ei: e.matmul(PS[bo], lhsT=VM[:, mt, h * 128:(h + 1) * 128], rhs=Ex[ei], start=(mt == 0), stop=(mt == 1)),
                             R=("VM", ("Ex", ei)), W=(PR(bo),))
                        S.op("pe", lambda e, bd=bd, mt=mt, ei=ei: e.matmul(PS[bd], lhsT=ones_b, rhs=Ex[ei], start=(mt == 0), stop=(mt == 1)),
                             R=("ones_b", ("Ex", ei)), W=(PR(bd),))
                    S.op("dve", lambda e, bd=bd: e.reciprocal(out=denx, in_=PS[bd]), R=(PR(bd),), W=("denx",))
                    S.op("dve", lambda e, bo=bo, h=h: e.tensor_tensor(out=oxT[:, h, :], in0=PS[bo], in1=denx, op=ALU.mult), R=(PR(bo), "denx"), W=(("oxT", h),))
                for j in range(4):
                    for half in range(2):
                        b = ps_next()
                        for h in range(4):
                            S.op("pe", lambda e, b=b, h=h, j=j, half=half: e.matmul(PS[b], lhsT=oxT[:, h, j * 128:(j + 1) * 128], rhs=wxo[:, h, half * 512:(half + 1) * 512], start=(h == 0), stop=(h == 3)),
                                 R=(("oxT", h), "wxo"), W=(PR(b),))
                        S.op("dve", lambda e, b=b, j=j, half=half, xt=xt: e.tensor_tensor(out=xt[:, j, half * 512:(half + 1) * 512], in0=PS[b], in1=xt[:, j, half * 512:(half + 1) * 512], op=ALU.add),
                             R=(PR(b), xr), W=(xr,))
                S.dma(lambda e, xt=xt, s=s: e.dma_start(out=xrv[s], in_=xt), R=(xr,), W=(("xres", s),))
            S.flush()
        if stop_after == "PX":
            break

        with ExitStack() as st:
            S.barrier()
            tmp = norm_tmp(st)
            wgu = nc_alloc(st, "wgu", [128, 8, 2 * DFF], BF16)
            for k in range(8):
                S.dma(lambda e, k=k: e.dma_start(out=wgu[:, k, :], in_=W["wgu"][:, k, :]), W=(("wgu", k),))
            gf = nc_alloc(st, "gf", [128, 8], F32)
            S.dma(lambda e: e.dma_start(out=gf, in_=W["nffn"]), W=("gf",))
            xts = [nc_alloc(st, "pfx%d" % i, [128, 4, D], F32) for i in range(2)]
            hT = nc_alloc(st, "pfhT", [128, 8, 512], BF16)
            sg = [nc_alloc(st, "pfsg%d" % i, [128, 512], F32) for i in range(2)]
            actT1 = nc_alloc(st, "pfact", [128, 22, 512], BF16)
            actT = [actT1, actT1]
            def pf_load(s):
                S.dma(lambda e: e.dma_start(out=xts[s % 2], in_=xrv[s]), R=(("xres", s),), W=(("pfx", s % 2),))

            pf_load(0)
            norm_A(xts[0], ("pfx", 0), tmp)
            for s in range(NS):
                xt, xr = xts[s % 2], ("pfx", s % 2)
                tsl = slice(s * 512, (s + 1) * 512)
                if s + 1 < NS:
                    pf_load(s + 1)
                norm_B(hT, "pfhT", gf, "gf", tmp)
                if s + 1 < NS:
                    norm_A(xts[(s + 1) % 2], ("pfx", (s + 1) % 2), tmp)
                at_, ar = actT[s % 2], ("pfact", 0)
                for f in range(22):
                    bg = ps_next()
                    bu = ps_next()
                    for k in range(8):
                        S.op("pe", lambda e, bg=bg, k=k, f=f: e.matmul(PS[bg], lhsT=wgu[:, k, f * 128:(f + 1) * 128], rhs=hT[:, k, :], start=(k == 0), stop=(k == 7)),
                             R=(("wgu", k), ("pfhT", k)), W=(PR(bg),))
                    for k in range(8):
                        S.op("pe", lambda e, bu=bu, k=k, f=f: e.matmul(PS[bu], lhsT=wgu[:, k, DFF + f * 128:DFF + (f + 1) * 128], rhs=hT[:, k, :], start=(k == 0), stop=(k == 7)),
                             R=(("wgu", k), ("pfhT", k)), W=(PR(bu),))
                    si = f % 2
                    S.op("act", lambda e, bg=bg, si=si: e.activation(out=sg[si], in_=PS[bg], func=AF.Silu), R=(PR(bg),), W=(("pfsg", si),))
                    S.op("dve", lambda e, bu=bu, si=si, f=f, at_=at_: e.tensor_tensor(out=at_[:, f, :], in0=PS[bu], in1=sg[si], op=ALU.mult), R=(PR(bu), ("pfsg", si)), W=((ar, f),))
                S.dma(lambda e, at_=at_, tsl=tsl: e.dma_start(out=ACTD[:, :, tsl], in_=at_), R=tuple((ar, f) for f in range(22)), W=(("actd", s),))
            S.flush()
        with ExitStack() as st:
            S.barrier()
            wdn = nc_alloc(st, "wdn", [128, 22, 1024], BF16)
            S.dma(lambda e: e.dma_start(out=wdn, in_=W["wdn"]), W=("wdn",))
            xts = [nc_alloc(st, "pgx%d" % i, [128, 4, D], F32) for i in range(2)]
            actT = [nc_alloc(st, "pgact%d" % i, [128, 22, 512], BF16) for i in range(2)]
            last = (l == L - 1)
            if last:
                gfin = nc_alloc(st, "gfin", [128, D], F32)
                S.dma(lambda e: e.dma_start(out=gfin, in_=nfin_in.broadcast_to([128, D])), W=("gfin",))
                sq = nc_alloc(st, "fsq", [128, D], F32)
                ss = nc_alloc(st, "fss", [128, 4], F32)
            yv = y_out.rearrange("(s j p) c -> s p j c", p=128, j=4)

            def pg_load(s):
                S.dma(lambda e: e.dma_start(out=xts[s % 2], in_=xrv[s]), R=(("xres", s),), W=(("pgx", s % 2),))
                S.dma(lambda e: e.dma_start(out=actT[s % 2], in_=ACTD[:, :, s * 512:(s + 1) * 512]), R=(("actd", s),), W=(("pgact", s % 2),))

            for s in range(NS):
                xt, xr = xts[s % 2], ("pgx", s % 2)
                at_, ar = actT[s % 2], ("pgact", s % 2)
                tsl = slice(s * 512, (s + 1) * 512)
                if s == 0:
                    pg_load(0)
                if s + 1 < NS:
                    pg_load(s + 1)
                for j in range(4):
                    for half in range(2):
                        b = ps_next()
                        for f in range(22):
                            S.op("pe", lambda e, b=b, f=f, j=j, half=half, at_=at_: e.matmul(PS[b], lhsT=at_[:, f, j * 128:(j + 1) * 128], rhs=wdn[:, f, half * 512:(half + 1) * 512], start=(f == 0), stop=(f == 21)),
                                 R=(ar, "wdn"), W=(PR(b),))
                        S.op("dve", lambda e, b=b, j=j, half=half, xt=xt: e.tensor_tensor(out=xt[:, j, half * 512:(half + 1) * 512], in0=PS[b], in1=xt[:, j, half * 512:(half + 1) * 512], op=ALU.add),
                             R=(PR(b), xr), W=(xr,))
                if not last:
                    S.dma(lambda e, xt=xt, s=s: e.dma_start(out=xrv[s], in_=xt), R=(xr,), W=(("xres", s),))
                else:
                    for j in range(4):
                        S.op("act", lambda e, j=j, xt=xt: e.activation(out=sq, in_=xt[:, j, :], func=AF.Square, accum_out=ss[:, j:j + 1]), R=(xr,), W=("fsq", ("fss", j)))
                    ssr = tuple(("fss", j) for j in range(4))
                    S.op("act", lambda e: e.activation(out=ss, in_=ss, func=AF.Sqrt, scale=1.0 / D, bias=EPS), R=ssr, W=ssr)
                    S.op("dve", lambda e: e.reciprocal(out=ss, in_=ss), R=ssr, W=ssr)
                    for j in range(4):
                        S.op("dve", lambda e, j=j, xt=xt: e.scalar_tensor_tensor(out=xt[:, j, :], in0=xt[:, j, :], scalar=ss[:, j:j + 1], in1=gfin, op0=ALU.mult, op1=ALU.mult),
                             R=(xr, ("fss", j), "gfin"), W=(xr,))
                    S.dma(lambda e, xt=xt, s=s: e.dma_start(out=yv[s], in_=xt), R=(xr,), W=(("y", s),))
            S.flush()
    S.finish()
    return nc


def prep_inputs(inp, T, L, b):
    consts = make_consts(T)
    m = {}
    m["x"] = np.ascontiguousarray(np.asarray(inp["x"][b, :T], np.float32))
    m["mem"] = np.ascontiguousarray(np.asarray(inp["mem"][b], np.float32))
    m["pos"] = np.ascontiguousarray(np.asarray(inp["positions"][b, :T], np.int32).reshape(1, T))
    m["nfin"] = np.ascontiguousarray(np.asarray(inp["norm_final"], np.float32).reshape(1, D))
    for k, v in consts.items():
        m["c_" + k] = v
    return m, consts


T_FULL, L_FULL, B_FULL = 8192, 2, 4


def kernel(**inputs):
    inp = {k: np.asarray(v) for k, v in inputs.items()}
    T, L, B = T_FULL, L_FULL, B_FULL
    ws = [layer_weights(inp, l) for l in range(L)]
    in_maps = []
    consts = None
    for b in range(B):
        m, consts = prep_inputs(inp, T, L, b)
        for l in range(L):
            for k, v in ws[l].items():
                m["w%d_%s" % (l, k)] = v
        in_maps.append(m)
    wshapes = {k: v.shape for k, v in ws[0].items()}
    cshapes = {k: (v.shape, "bf16" if v.dtype == NBF else "f32") for k, v in consts.items()}
    nc = build(T, L, wshapes, cshapes)
    res = run_bass_kernel_spmd(nc, in_maps, core_ids=list(range(B)))
    out = np.stack([np.asarray(r["y"], dtype=np.float32) for r in res.results], axis=0)
    return out
```

```python
import numpy as np
import ml_dtypes
from contextlib import ExitStack
import concourse.bass as bass
import concourse.mybir as mybir
from concourse.bass_utils import run_bass_kernel_spmd
from concourse.alu_op_type import AluOpType as ALU

AF = mybir.ActivationFunctionType
F32, BF16, I32 = mybir.dt.float32, mybir.dt.bfloat16, mybir.dt.int32
NEG = -30000.0
D = 1024
DFF = 2816
EPS = 1e-6
NBF = ml_dtypes.bfloat16


class Sched:
    CENG = ("pe", "act", "dve", "pool", "sp")

    def __init__(self, nc, n_dma_sems=40):
        self.nc = nc
        self.eng = {"pe": nc.tensor, "act": nc.scalar, "dve": nc.vector,
                    "pool": nc.gpsimd, "sp": nc.sync}
        self.sem = {e: nc.alloc_semaphore("sem_" + e) for e in self.CENG}
        self.cnt = {e: 0 for e in self.CENG}
        self.dsem = [nc.alloc_semaphore("dsem%d" % i) for i in range(n_dma_sems)]
        self.dval = [0] * n_dma_sems
        self.dnext = 0
        self.ops = []
        self.all_tok = {}
        self.nops = 0
        self.last_w = {}
        self.readers = {}
        self.waited = {e: {} for e in self.CENG}
        self.sig_after = {e: [] for e in self.CENG}
        self.op_eng = {}
        self.op_isdma = {}

    def op(self, eng, fn, R=(), W=(), dma=False):
        if dma:
            eng = "pool"
        elif eng == "pool":
            eng = "dve"
        idx = self.nops
        self.nops += 1
        deps = set()
        for r in R:
            if r in self.last_w:
                deps.add(self.last_w[r])
        for w in W:
            if w in self.last_w:
                deps.add(self.last_w[w])
            for rd in self.readers.get(w, ()):
                deps.add(rd)
        deps.discard(idx)
        for r in R:
            self.readers.setdefault(r, []).append(idx)
        for w in W:
            self.last_w[w] = idx
            self.readers[w] = []
        self.op_eng[idx] = eng
        self.op_isdma[idx] = dma
        self.ops.append(dict(idx=idx, eng=eng, fn=fn, deps=deps, dma=dma, sig=False, barrier=False))
        return idx

    def dma(self, fn, R=(), W=(), q="sp"):
        return self.op(q, fn, R, W, dma=True)

    def barrier(self):
        self.ops.append(dict(idx=None, barrier=True))

    def _wait(self, eng, sem, val, key):
        w = self.waited[eng]
        if w.get(key, 0) >= val:
            return
        w[key] = val
        self.eng[eng].wait_ge(sem, val)

    def flush(self):
        ops = self.ops
        self.ops = []
        pend = {o["idx"]: o for o in ops if not o["barrier"]}
        last_on = {}
        for o in ops:
            if o["barrier"]:
                for e, lo in last_on.items():
                    lo["sig"] = True
                continue
            for d in o["deps"]:
                if d in pend and not pend[d]["dma"]:
                    if not (o["eng"] == "pe" and pend[d]["eng"] == "pe" and not o["dma"]):
                        pend[d]["sig"] = True
            if not o["dma"]:
                last_on[o["eng"]] = o
        for e, lo in last_on.items():
            lo["sig"] = True
        for o in ops:
            if o["barrier"]:
                for e in self.CENG:
                    for p in self.CENG:
                        if p != e and self.cnt[p] > 0:
                            self._wait(e, self.sem[p], self.cnt[p], p)
                    for i, v in enumerate(self.dval):
                        if v > 0:
                            self._wait(e, self.dsem[i], v, ("d", i))
                continue
            e = o["eng"]
            E = self.eng[e]
            need = {}
            for d in o["deps"]:
                if self.op_isdma[d]:
                    s_i, v = self.all_tok[d]
                    need[("d", s_i)] = max(need.get(("d", s_i), 0), v)
                else:
                    pe_ = self.op_eng[d]
                    if pe_ == "pe" and e == "pe" and not o["dma"]:
                        continue
                    tok = self.all_tok.get(d)
                    if tok is None or tok[1] is None:
                        v = None
                        for (i2, v2) in self.sig_after[pe_]:
                            if i2 >= d:
                                v = v2
                                break
                        assert v is not None, ("unsignalled dep", d, pe_)
                    else:
                        v = tok[1]
                    need[pe_] = max(need.get(pe_, 0), v)
            for k, v in need.items():
                if isinstance(k, tuple):
                    self._wait(e, self.dsem[k[1]], v, k)
                else:
                    self._wait(e, self.sem[k], v, k)
            if o["dma"]:
                si = self.dnext
                self.dnext = (self.dnext + 1) % len(self.dsem)
                if self.dval[si] > 0:
                    self._wait(e, self.dsem[si], self.dval[si], ("d", si))
                ins = o["fn"](E)
                self.dval[si] += 16
                ins.then_inc(self.dsem[si], 16)
                self.all_tok[o["idx"]] = (si, self.dval[si])
            else:
                ins = o["fn"](E)
                if o["sig"]:
                    self.cnt[e] += 1
                    ins.then_inc(self.sem[e], 1)
                    self.all_tok[o["idx"]] = (e, self.cnt[e])
                    self.sig_after[e].append((o["idx"], self.cnt[e]))
                else:
                    self.all_tok[o["idx"]] = (e, None)

    def finish(self):
        self.barrier()
        self.flush()


def make_consts(T):
    NT = T // 128
    NCT = (T // 16 - 1 + 127) // 128
    n_cmp = (T - 32) // 16 + 1
    c = {}
    c["ident_f"] = np.eye(128, dtype=np.float32)
    c["ident_b"] = np.eye(128, dtype=np.float32).astype(NBF)
    c["ones_f"] = np.ones((128, 128), np.float32)
    k = np.arange(128)[:, None]
    q = np.arange(128)[None, :]
    c["cb"] = np.where(k <= q, 0.0, NEG).astype(NBF)
    c["bb"] = np.where(k > q, 0.0, NEG).astype(NBF)
    mm = np.zeros((128, 4, 4, 128), np.float32)
    for i in range(4):
        for j in range(4):
            if j < i:
                mm[:, i, j, :] = NEG
            elif j == i:
                mm[:, i, j, :] = np.where(k <= q, 0.0, NEG)
    c["mm"] = mm.reshape(128, 4 * 512).astype(NBF)
    cm = np.zeros((128, 17, 128), np.float32)
    for dl in range(17):
        cm[:, dl, :] = np.where(16 * k + 31 - q <= 128 * dl, 0.0, NEG)
    c["cm"] = cm.reshape(128, 17 * 128).astype(NBF)
    ex = np.zeros((128, NT, 128), np.float32)
    for kt in range(NT):
        for half in range(2):
            j = 2 * kt + half
            if j < 128:
                ex[j, kt, half * 64:(half + 1) * 64] = -NEG
    c["ex"] = ex.reshape(128, NT * 128).astype(NBF)
    exh = np.zeros((64, NT, 128), np.float32)
    for kt in range(NT):
        for half in range(2):
            exh[(2 * kt + half) % 64, kt, half * 64:(half + 1) * 64] = -NEG
    c["exh"] = exh.reshape(64, NT * 128).astype(NBF)
    n_slc = T // 64
    cs = np.arange(n_cmp) * 16
    ss = np.arange(n_slc) * 64
    cover = np.minimum(cs[:, None] + 32, ss[None, :] + 64) - np.maximum(cs[:, None], ss[None, :])
    c2s = np.clip(cover, 0, None) / 16.0
    ms = np.zeros((NCT * 128, 129), np.float32)
    ms[:n_cmp, :n_slc] = c2s
    ms[:, 128] = 1.0
    c["msel"] = ms.reshape(NCT, 128, 129).transpose(1, 0, 2).reshape(128, NCT * 129).astype(NBF)
    fb = np.zeros((128, 256), np.float32)
    for qq in range(128):
        hi = 1 if qq >= 64 else 0
        for dl in (hi, hi - 1):
            fb[qq, 128 + dl] = 1e4
    c["fb"] = fb
    rp = np.zeros((128, 4), np.float32)
    p = np.arange(128)
    fa = (10000.0 ** (-(np.arange(32, dtype=np.float32)) / 32)).astype(np.float32)
    rp[:, 0] = fa[p % 32]
    rp[:, 1] = np.where((p % 64) < 32, -1.0, 1.0)
    fm = (10000.0 ** (-(np.arange(16, dtype=np.float32)) / 16)).astype(np.float32)
    rp[:, 2] = fm[p % 16]
    rp[:, 3] = np.where((p % 32) < 16, -1.0, 1.0)
    c["ropep"] = rp
    return c


IN_SPLITS = (512, 128, 128, 512, 128, 128, 128, 128, 128, 128, 24, 384, 256, 32, 3072)
OFF = np.concatenate([[0], np.cumsum(IN_SPLITS)]).tolist()
(O_AQ, O_AK, O_AV, O_BQ, O_BKC, O_BVC, O_BKS, O_BVS, O_BKW, O_BVW, O_BG, O_CQA, O_CKV, O_CKR, O_GBR) = OFF[:15]


def perm_half(cols, hd):
    cols = np.asarray(cols)
    n = len(cols) // hd
    out = []
    for h in range(n):
        blk = cols[h * hd:(h + 1) * hd]
        out.append(np.concatenate([blk[hd // 2:], blk[:hd // 2]]))
    return np.concatenate(out)


def fm_groups():
    g = []
    r = lambda a, n: list(range(a, a + n))
    for i in range(4):
        g.append(("aq%d" % i, r(O_AQ + 128 * i, 128)))
        g.append(("aq%dP" % i, perm_half(r(O_AQ + 128 * i, 128), 64)))
    g.append(("ak", r(O_AK, 128)))
    g.append(("akP", perm_half(r(O_AK, 128), 64)))
    for i in range(4):
        g.append(("bq%d" % i, r(O_BQ + 128 * i, 128)))
        g.append(("bq%dP" % i, perm_half(r(O_BQ + 128 * i, 128), 64)))
    g.append(("bks", r(O_BKS, 128)))
    g.append(("bksP", perm_half(r(O_BKS, 128), 64)))
    g.append(("bkw", r(O_BKW, 128)))
    g.append(("bkwP", perm_half(r(O_BKW, 128), 64)))
    g.append(("bkc", r(O_BKC, 128)))
    g.append(("bvc", r(O_BVC, 128)))
    for i in range(3):
        g.append(("cqa%d" % i, r(O_CQA + 128 * i, 128)))
    for i in range(2):
        g.append(("ckv%d" % i, r(O_CKV + 128 * i, 128)))
    kr = r(O_CKR, 32)
    g.append(("ckr", kr * 4))
    g.append(("ckrP", list(perm_half(kr, 32)) * 4))
    bg = r(O_BG, 24)
    g.append(("bg", bg + bg[:8] + r(O_BG, 24) * 4))
    return g


FMG = fm_groups()
FMI = {n: i for i, (n, _) in enumerate(FMG)}
NFM = len(FMG)


def layer_weights(inp, l):
    w = {}
    win = np.asarray(inp["w_in"][l], np.float32)
    cols = np.concatenate([np.asarray(c) for _, c in FMG])
    for _, c in FMG:
        assert len(c) == 128 or True
    w["wfm"] = np.ascontiguousarray(np.concatenate(
        [win[:, np.asarray(c)[:128]] for _, c in FMG], axis=1))
    w["wtm"] = np.ascontiguousarray(np.concatenate(
        [win[:, O_AV:O_AV + 128], win[:, O_BVS:O_BVS + 128], win[:, O_BVW:O_BVW + 128]], axis=1))
    w["wg"] = np.ascontiguousarray(win[:, O_GBR:O_GBR + 3072])
    w["nmix"] = np.ascontiguousarray(np.asarray(inp["norm_mix"][l], np.float32).reshape(8, 128).T)
    w["nx"] = np.ascontiguousarray(np.asarray(inp["norm_xattn"][l], np.float32).reshape(8, 128).T)
    w["nmem"] = np.ascontiguousarray(np.asarray(inp["norm_mem"][l], np.float32).reshape(8, 128).T)
    w["nffn"] = np.ascontiguousarray(np.asarray(inp["norm_ffn"][l], np.float32).reshape(8, 128).T)
    w["sinks"] = np.asarray(inp["swa_sinks"][l], np.float32).reshape(1, 8)
    w["pek"] = np.ascontiguousarray(np.asarray(inp["nsa_pe_k"][l], np.float32).T)
    w["pev"] = np.ascontiguousarray(np.asarray(inp["nsa_pe_v"][l], np.float32).T)
    w["wk1"] = np.ascontiguousarray(np.asarray(inp["nsa_wk1"][l], np.float32).reshape(32, 64, 64).transpose(1, 0, 2))
    w["wv1"] = np.ascontiguousarray(np.asarray(inp["nsa_wv1"][l], np.float32).reshape(32, 64, 64).transpose(1, 0, 2))
    w["wk2"] = np.asarray(inp["nsa_wk2"][l], np.float32)
    w["wv2"] = np.asarray(inp["nsa_wv2"][l], np.float32)
    w["qn"] = np.ascontiguousarray(np.asarray(inp["mla_q_norm"][l], np.float32).reshape(3, 128).T)
    w["kvn"] = np.ascontiguousarray(np.asarray(inp["mla_kv_norm"][l], np.float32).reshape(2, 128).T)
    wq = np.asarray(inp["mla_w_q_b"][l], np.float32).reshape(384, 8, 96)
    wqp = wq.copy()
    pi = perm_half(np.arange(64, 96), 32)
    wqp[:, :, 64:96] = wq[:, :, pi]
    w["wq"] = np.ascontiguousarray(wq.reshape(3, 128, 8 * 96).transpose(1, 0, 2))
    w["wqp"] = np.ascontiguousarray(wqp.reshape(3, 128, 8 * 96).transpose(1, 0, 2))
    wkv = np.asarray(inp["mla_w_kv_b"][l], np.float32).reshape(256, 8, 128)
    w["wkvk"] = np.ascontiguousarray(wkv[:, :, :64].reshape(2, 128, 512).transpose(1, 0, 2))
    w["wkvv"] = np.ascontiguousarray(wkv[:, :, 64:].reshape(2, 128, 512).transpose(1, 0, 2))
    for nm, key in (("wbra", "w_br_a"), ("wbrb", "w_br_b"), ("wbrc", "w_br_c")):
        w[nm] = np.ascontiguousarray(np.asarray(inp[key][l], np.float32).reshape(8, 64, 1024).transpose(1, 0, 2))
    w["wout"] = np.ascontiguousarray(np.asarray(inp["w_out"][l], np.float32).reshape(8, 128, 1024).transpose(1, 0, 2))
    w["wxq"] = np.ascontiguousarray(np.asarray(inp["w_xq"][l], np.float32).reshape(8, 128, 512).transpose(1, 0, 2))
    w["wxkv"] = np.ascontiguousarray(np.asarray(inp["w_xkv"][l], np.float32).reshape(8, 128, 1024).transpose(1, 0, 2))
    w["wxo"] = np.ascontiguousarray(np.asarray(inp["w_xo"][l], np.float32).reshape(4, 128, 1024).transpose(1, 0, 2))
    w["wgu"] = np.ascontiguousarray(np.asarray(inp["w_gate_up"][l], np.float32).reshape(8, 128, 2 * DFF).transpose(1, 0, 2))
    w["wdn"] = np.ascontiguousarray(np.asarray(inp["w_down"][l], np.float32).reshape(22, 128, 1024).transpose(1, 0, 2))
    return w


WSHAPES = None


def build(T, L, wshapes, cshapes, stop_after=None, debug=False):
    NT = T // 128
    NS = T // 512
    NCT = (T // 16 - 1 + 127) // 128
    n_cmp = (T - 32) // 16 + 1
    nc = bass.Bass("TRN2", target_bir_lowering=False)
    S = Sched(nc)

    def din(name, shape, dt=F32):
        return nc.dram_tensor(name, list(shape), dt, kind="ExternalInput").ap()

    def dscr(name, shape, dt):
        return nc.dram_tensor(name, list(shape), dt, kind=("ExternalOutput" if debug else "Internal")).ap()

    x_in = din("x", [T, D])
    mem_in = din("mem", [256, D])
    pos_in = din("pos", [1, T], I32)
    nfin_in = din("nfin", [1, D])
    C = {k: din("c_" + k, v[0], BF16 if v[1] == "bf16" else F32) for k, v in cshapes.items()}
    Wt = [{k: din("w%d_%s" % (l, k), shp) for k, shp in wshapes.items()} for l in range(L)]
    y_out = nc.dram_tensor("y", [T, D], F32, kind="ExternalOutput").ap()

    xres = dscr("xres", [T, D], F32)
    rcA, rsA, rcM, rsM = (dscr(n, [128, T], F32) for n in ("rcA", "rsA", "rcM", "rsM"))
    QA = dscr("QA", [64, 8, T], BF16)
    KA = dscr("KA", [64, 2, T], BF16)
    VA = dscr("VA", [T, 2, 128], BF16)
    QBU = dscr("QBU", [64, 8, T], BF16)
    QBR = dscr("QBR", [64, 8, T], BF16)
    KC = dscr("KC", [64, 2, T], BF16)
    VC = dscr("VC", [64, 2, T], BF16)
    KBS = dscr("KBS", [64, 2, T], BF16)
    VBS = dscr("VBS", [T, 2, 128], BF16)
    KBW = dscr("KBW", [64, 2, T], BF16)
    VBW = dscr("VBW", [T, 2, 128], BF16)
    GB = dscr("GB", [8, 3, T], F32)
    QLAT = dscr("QLAT", [128, 3, T], BF16)
    CKV = dscr("CKV", [128, 2, T], BF16)
    KPE = dscr("KPE", [32, T], BF16)
    OA = dscr("OA", [64, 8, T], BF16)
    OBP = dscr("OBP", [64, 8, T], F32)
    OB = dscr("OB", [64, 8, T], BF16)
    OC = dscr("OC", [64, 8, T], BF16)
    ACTD = dscr("ACTD", [128, 22, T], BF16)
    KCMP = dscr("KCMP", [64, 2, NCT * 128], BF16)
    VCMP = dscr("VCMP", [128, NCT, 2, 128], BF16)

    PS = [nc.alloc_psum_tensor("ps%d" % i, [128, 512], F32).ap() for i in range(8)]
    psr = [0]

    def ps_next():
        i = psr[0]
        psr[0] = (i + 1) % 8
        return i

    def PR(i):
        return ("ps", i)

    def sb(st, name, shape, dt):
        return st.enter_context(nc.sbuf_tensor(name, list(shape), dt)).ap() if False else nc_alloc(st, name, shape, dt)

    acnt = [0]

    def nc_alloc(st, name, shape, dt):
        acnt[0] += 1
        g = nc.sbuf_tensor("sb%d_%s" % (acnt[0], name), list(shape), dt)
        t = st.enter_context(g)
        return t.ap() if hasattr(t, "ap") and callable(t.ap) else t

    uid = [0]

    def u():
        uid[0] += 1
        return uid[0]

    def wload(dst, src, reg, pieces=1, axis=None):
        if pieces == 1:
            S.dma(lambda e: e.dma_start(out=dst, in_=src), R=(), W=(reg,), q="pool")
        else:
            n = dst.shape[1]
            step = (n + pieces - 1) // pieces
            for a in range(0, n, step):
                b = min(n, a + step)
                S.dma(lambda e, a=a, b=b: e.dma_start(out=dst[:, a:b], in_=src[:, a:b]), R=(), W=((reg, a),), q="pool")

    with ExitStack() as st:
        TWO_PI = 2.0 * np.pi
        c1 = float(np.float32(6.28125))
        c2 = float(np.float32(np.float32(TWO_PI - 6.28125).view(np.uint32) & np.uint32(0xFFFFF000)).view(np.float32)) if False else None
        r2 = TWO_PI - 6.28125
        c2 = float(np.array(np.array(r2, np.float32).view(np.uint32) & np.uint32(0xFFFFF000), np.uint32).view(np.float32))
        c3 = float(np.float32(r2 - c2))
        MAGIC = 12582912.0
        PIS = 3.1415925
        CH = min(T, 2048)
        ropep = nc_alloc(st, "ropep", [128, 4], F32)
        S.dma(lambda e: e.dma_start(out=ropep, in_=C["ropep"]), W=("ropep",))
        posi = nc_alloc(st, "posi", [128, CH], I32)
        posf = nc_alloc(st, "posf", [128, CH], F32)
        ang = nc_alloc(st, "ang", [128, CH], F32)
        kk = nc_alloc(st, "kk", [128, CH], F32)
        rr = nc_alloc(st, "rr", [128, CH], F32)
        sn = nc_alloc(st, "sn", [128, CH], F32)
        cs_ = nc_alloc(st, "cs", [128, CH], F32)
        xt = [nc_alloc(st, "xcp%d" % i, [128, 4, D], F32) for i in range(2)]
        xv = x_in.rearrange("(s j p) c -> s p j c", p=128, j=4)
        xrv = xres.rearrange("(s j p) c -> s p j c", p=128, j=4)
        for s in range(NS):
            b = xt[s % 2]
            S.dma(lambda e, b=b, s=s: e.dma_start(out=b, in_=xv[s]), W=(("xcp", s % 2),))
            S.dma(lambda e, b=b, s=s: e.dma_start(out=xrv[s], in_=b), R=(("xcp", s % 2),), W=(("xres", s),), q="pool")
        for c0 in range(0, T, CH):
            S.dma(lambda e, c0=c0: e.dma_start(out=posi, in_=pos_in[:, c0:c0 + CH].broadcast_to([128, CH])), W=("posi",))
            S.op("dve", lambda e: e.tensor_copy(out=posf, in_=posi), R=("posi",), W=("posf",))
            for (fc, sc, dc, ds) in ((0, 1, rcA, rsA), (2, 3, rcM, rsM)):
                S.op("dve", lambda e, fc=fc: e.tensor_scalar(out=ang, in0=posf, scalar1=ropep[:, fc:fc + 1], scalar2=None, op0=ALU.mult),
                     R=("posf", "ropep"), W=("ang",))
                S.op("dve", lambda e: e.tensor_scalar(out=kk, in0=ang, scalar1=float(1.0 / TWO_PI), scalar2=MAGIC, op0=ALU.mult, op1=ALU.add),
                     R=("ang",), W=("kk",))
                S.op("dve", lambda e: e.tensor_scalar(out=kk, in0=kk, scalar1=MAGIC, scalar2=None, op0=ALU.subtract),
                     R=("kk",), W=("kk",))
                S.op("dve", lambda e: e.scalar_tensor_tensor(out=rr, in0=kk, scalar=-c1, in1=ang, op0=ALU.mult, op1=ALU.add),
                     R=("kk", "ang"), W=("rr",))
                S.op("dve", lambda e: e.scalar_tensor_tensor(out=rr, in0=kk, scalar=-c2, in1=rr, op0=ALU.mult, op1=ALU.add),
                     R=("kk", "rr"), W=("rr",))
                S.op("dve", lambda e: e.scalar_tensor_tensor(out=rr, in0=kk, scalar=-c3, in1=rr, op0=ALU.mult, op1=ALU.add),
                     R=("kk", "rr"), W=("rr",))
                S.op("dve", lambda e: e.tensor_scalar(out=rr, in0=rr, scalar1=-PIS, scalar2=PIS, op0=ALU.max, op1=ALU.min),
                     R=("rr",), W=("rr",))
                S.op("act", lambda e: e.activation(out=sn, in_=rr, func=AF.Sin), R=("rr",), W=("sn",))
                S.op("dve", lambda e, sc=sc: e.tensor_scalar(out=sn, in0=sn, scalar1=ropep[:, sc:sc + 1], scalar2=None, op0=ALU.mult),
                     R=("sn", "ropep"), W=("sn",))
                S.dma(lambda e, ds=ds, c0=c0: e.dma_start(out=ds[:, c0:c0 + CH], in_=sn), R=("sn",), W=(("rope", u()),), q="pool")
                S.op("dve", lambda e: e.scalar_tensor_tensor(out=cs_, in0=rr, scalar=-1.0, in1=rr, op0=ALU.mult, op1=ALU.max), R=("rr",), W=("cs",))
                S.op("dve", lambda e: e.tensor_scalar(out=cs_, in0=cs_, scalar1=-1.0, scalar2=float(np.pi / 2), op0=ALU.mult, op1=ALU.add),
                     R=("cs",), W=("cs",))
                S.op("act", lambda e: e.activation(out=cs_, in_=cs_, func=AF.Sin), R=("cs",), W=("cs",))
                S.dma(lambda e, dc=dc, c0=c0: e.dma_start(out=dc[:, c0:c0 + CH], in_=cs_), R=("cs",), W=(("rope", u()),), q="pool")
        S.finish()

    def norm_A(xt, xreg, tmp, ntok_tiles=4):
        for j in range(ntok_tiles):
            S.op("act", lambda e, j=j: e.activation(out=tmp["sq"], in_=xt[:, j, :], func=AF.Square, accum_out=tmp["ss"][:, j:j + 1]),
                 R=(xreg,), W=("nt_sq", ("nt_ss", j)))
        ssr = tuple(("nt_ss", j) for j in range(ntok_tiles))
        S.op("act", lambda e: e.activation(out=tmp["ss"][:, 0:ntok_tiles], in_=tmp["ss"][:, 0:ntok_tiles], func=AF.Sqrt, scale=1.0 / D, bias=EPS),
             R=ssr, W=ssr)
        S.op("dve", lambda e: e.reciprocal(out=tmp["ss"][:, 0:ntok_tiles], in_=tmp["ss"][:, 0:ntok_tiles]), R=ssr, W=ssr)
        for j in range(ntok_tiles):
            S.op("dve", lambda e, j=j: e.tensor_scalar(out=tmp["hs"][:, j, :], in0=xt[:, j, :], scalar1=tmp["ss"][:, j:j + 1], scalar2=None, op0=ALU.mult),
                 R=(xreg, ("nt_ss", j)), W=(("nt_hs", j),))

    def norm_B(hT, hreg, gcol, gcreg, tmp, ntok_tiles=4):
        for k in range(8):
            b = ps_next()
            for j in range(ntok_tiles):
                S.op("pe", lambda e, k=k, j=j, b=b: e.transpose(out=PS[b][:, j * 128:(j + 1) * 128], in_=tmp["hs"][:, j, k * 128:(k + 1) * 128], identity=tmp["ident"]),
                     R=(("nt_hs", j), "ident_f"), W=(PR(b),))
            n = ntok_tiles * 128
            S.op("dve" if k % 2 == 0 else "act",
                 (lambda e, k=k, b=b, n=n: e.tensor_scalar(out=hT[:, k, 0:n], in0=PS[b][:, 0:n], scalar1=gcol[:, k:k + 1], scalar2=None, op0=ALU.mult))
                 if k % 2 == 0 else
                 (lambda e, k=k, b=b, n=n: e.activation(out=hT[:, k, 0:n], in_=PS[b][:, 0:n], func=AF.Copy, scale=gcol[:, k:k + 1])),
                 R=(PR(b), gcreg), W=((hreg, k),))

    def norm_T(xt, xreg, hT, hreg, gcol, gcreg, tmp, ntok_tiles=4):
        norm_A(xt, xreg, tmp, ntok_tiles)
        norm_B(hT, hreg, gcol, gcreg, tmp, ntok_tiles)

    def norm_tmp(st):
        t = {}
        t["sq"] = nc_alloc(st, "nt_sq", [128, D], F32)
        t["ss"] = nc_alloc(st, "nt_ss", [128, 4], F32)
        t["hs"] = nc_alloc(st, "nt_hs", [128, 4, D], F32)
        t["ident"] = nc_alloc(st, "ident_f", [128, 128], F32)
        S.dma(lambda e: e.dma_start(out=t["ident"], in_=C["ident_f"]), W=("ident_f",))
        return t

    xrv = xres.rearrange("(s j p) c -> s p j c", p=128, j=4)
    if stop_after == "P0":
        S.finish()
        return nc

    for l in range(L):
        W = Wt[l]
        with ExitStack() as st:
            S.barrier()
            tmp = norm_tmp(st)
            wfm = nc_alloc(st, "wfm", [128, 8, NFM * 128], BF16)
            wtm = nc_alloc(st, "wtm", [128, 8, 384], BF16)
            for k in range(8):
                S.dma(lambda e, k=k: e.dma_start(out=wfm[:, k, :], in_=W["wfm"][k * 128:(k + 1) * 128, :]), W=(("wfm", k),), q="pool")
                S.dma(lambda e, k=k: e.dma_start(out=wtm[:, k, :], in_=W["wtm"][k * 128:(k + 1) * 128, :]), W=(("wtm", k),), q="pool")
            wfr = tuple(("wfm", k) for k in range(8))
            wtr = tuple(("wtm", k) for k in range(8))
            gmix = nc_alloc(st, "gmix", [128, 8], F32)
            S.dma(lambda e: e.dma_start(out=gmix, in_=W["nmix"]), W=("gmix",))
            qn = nc_alloc(st, "qn", [128, 3], F32)
            kvn = nc_alloc(st, "kvn", [128, 2], F32)
            S.dma(lambda e: e.dma_start(out=qn, in_=W["qn"]), W=("qn",))
            S.dma(lambda e: e.dma_start(out=kvn, in_=W["kvn"]), W=("kvn",))
            ones_f = nc_alloc(st, "ones_f", [128, 128], F32)
            S.dma(lambda e: e.dma_start(out=ones_f, in_=C["ones_f"]), W=("ones_f",))
            xts = [nc_alloc(st, "p1x%d" % i, [128, 4, D], F32) for i in range(2)]
            hT = nc_alloc(st, "p1hT", [128, 8, 512], BF16)
            tab = [nc_alloc(st, "p1tab%d" % i, [128, 512], F32) for i in range(4)]
            t1 = [nc_alloc(st, "p1t1_%d" % i, [128, 512], F32) for i in range(2)]
            t2 = [nc_alloc(st, "p1t2_%d" % i, [128, 512], F32) for i in range(2)]
            ob = [nc_alloc(st, "p1ob%d" % i, [128, 512], BF16) for i in range(4)]
            lat = [nc_alloc(st, "p1lat%d" % i, [128, 512], F32) for i in range(3)]
            sqb = nc_alloc(st, "p1sqb", [128, 512], F32)
            rstd = nc_alloc(st, "p1rstd", [128, 512], F32)
            gsb = nc_alloc(st, "p1gsb", [128, 512], F32)
            vst = [nc_alloc(st, "p1vst%d" % i, [128, 6, 128], BF16) for i in range(2)]
            for i in range(2):
                S.op("pool", lambda e, i=i: e.memset(vst[i], 1.0), W=(("vst", i),))
            obc = [0]

            def next_ob():
                i = obc[0]
                obc[0] = (i + 1) % 4
                return i

            def proj_fm(gname):
                gi = FMI[gname]
                b = ps_next()
                for k in range(8):
                    S.op("pe", lambda e, k=k, b=b, gi=gi: e.matmul(PS[b], lhsT=wfm[:, k, gi * 128:(gi + 1) * 128], rhs=hT[:, k, :], start=(k == 0), stop=(k == 7)),
                         R=(("wfm", k), ("hT", k)), W=(PR(b),))
                return b

            tc = [0]
            def p1_load(s):
                S.dma(lambda e: e.dma_start(out=xts[s % 2], in_=xrv[s]), R=(("xres", s),), W=(("p1x", s % 2),))

            p1_load(0)
            norm_A(xts[0], ("p1x", 0), tmp)
            for s in range(NS):
                xt = xts[s % 2]
                xr = ("p1x", s % 2)
                if s + 1 < NS:
                    p1_load(s + 1)
                for i, tsrc in enumerate((rcA, rsA, rcM, rsM)):
                    S.dma(lambda e, i=i, tsrc=tsrc, s=s: e.dma_start(out=tab[i], in_=tsrc[:, s * 512:(s + 1) * 512]), W=(("tab", i),))
                norm_B(hT, "hT", gmix, "gmix", tmp)
                if s + 1 < NS:
                    norm_A(xts[(s + 1) % 2], ("p1x", (s + 1) % 2), tmp)
                tsl = slice(s * 512, (s + 1) * 512)

                def rope_group(nm, dsts, ci=0, si=1, rows=None):
                    ba = proj_fm(nm)
                    bp = proj_fm(nm + "P") if not nm.startswith("ckr") else proj_fm("ckrP")
                    i = tc[0] % 2
                    tc[0] += 1
                    o = next_ob()
                    S.op("dve", lambda e, ba=ba, i=i: e.tensor_tensor(out=t1[i], in0=PS[ba], in1=tab[ci], op=ALU.mult),
                         R=(PR(ba), ("tab", ci)), W=(("t1", i),))
                    S.op("dve", lambda e, bp=bp, i=i: e.tensor_tensor(out=t2[i], in0=PS[bp], in1=tab[si], op=ALU.mult),
                         R=(PR(bp), ("tab", si)), W=(("t2", i),))
                    S.op("pool", lambda e, i=i, o=o: e.tensor_tensor(out=ob[o], in0=t1[i], in1=t2[i], op=ALU.add),
                         R=(("t1", i), ("t2", i)), W=(("ob", o),))
                    emit_out(dsts, o)

                def emit_out(dsts, o):
                    if len(dsts) == 1 and dsts[0][1] is None:
                        dst = dsts[0][0]
                        S.dma(lambda e, dst=dst, o=o: e.dma_start(out=dst, in_=ob[o]), R=(("ob", o),), W=(("p1out", u()),))
                    else:
                        for (dst, psl) in dsts:
                            S.dma(lambda e, dst=dst, psl=psl, o=o: e.dma_start(out=dst, in_=ob[o][psl, :]), R=(("ob", o),), W=(("p1out", u()),))

                def plain_group(nm, dsts):
                    b = proj_fm(nm)
                    o = next_ob()
                    S.op("act", lambda e, b=b, o=o: e.copy(out=ob[o], in_=PS[b]), R=(PR(b),), W=(("ob", o),))
                    emit_out(dsts, o)

                lo, hi = slice(0, 64), slice(64, 128)

                def both(Dt, i):
                    return [(Dt[:, 2 * i, tsl], lo), (Dt[:, 2 * i + 1, tsl], hi)]

                for i in range(4):
                    rope_group("aq%d" % i, both(QA, i))
                rope_group("ak", both(KA, 0))
                for i in range(4):
                    rope_group("bq%d" % i, both(QBR, i))
                    plain_group("bq%d" % i, both(QBU, i))
                rope_group("bks", both(KBS, 0))
                rope_group("bkw", both(KBW, 0))
                plain_group("bkc", both(KC, 0))
                plain_group("bvc", both(VC, 0))
                rope_group("ckr", [(KPE[:, tsl], slice(64, 96))], ci=2, si=3)
                b = proj_fm("bg")
                S.op("act", lambda e, b=b: e.activation(out=gsb[0:24, :], in_=PS[b][0:24, :], func=AF.Sigmoid), R=(PR(b),), W=("gsb",))
                S.dma(lambda e, tsl=tsl: e.dma_start(out=GB.rearrange("h j t -> (h j) t")[:, tsl], in_=gsb[0:24, :]), R=("gsb",), W=(("p1out", u()),), q="pool")

                def latent(names, gcol, gcreg, dstT, nfeat):
                    n = len(names)
                    bsum = ps_next()
                    for i, nm in enumerate(names):
                        b = proj_fm(nm)
                        S.op("act", lambda e, b=b, i=i: e.copy(out=lat[i], in_=PS[b]), R=(PR(b),), W=(("lat", i),))
                        S.op("act", lambda e, b=b: e.activation(out=sqb, in_=PS[b], func=AF.Square), R=(PR(b),), W=("sqb",))
                        S.op("pe", lambda e, i=i, bsum=bsum: e.matmul(PS[bsum], lhsT=ones_f, rhs=sqb, start=(i == 0), stop=(i == n - 1)),
                             R=("ones_f", "sqb"), W=(PR(bsum),))
                    S.op("act", lambda e, bsum=bsum: e.activation(out=rstd, in_=PS[bsum], func=AF.Sqrt, scale=1.0 / nfeat, bias=EPS),
                         R=(PR(bsum),), W=("rstd",))
                    S.op("dve", lambda e: e.reciprocal(out=rstd, in_=rstd), R=("rstd",), W=("rstd",))
                    for i in range(n):
                        o = next_ob()
                        S.op("dve", lambda e, i=i, o=o: e.scalar_tensor_tensor(out=ob[o], in0=lat[i], scalar=gcol[:, i:i + 1], in1=rstd, op0=ALU.mult, op1=ALU.mult),
                             R=(("lat", i), gcreg, "rstd"), W=(("ob", o),))
                        S.dma(lambda e, i=i, o=o, tsl=tsl: e.dma_start(out=dstT[:, i, tsl], in_=ob[o]), R=(("ob", o),), W=(("p1out", u()),), q="pool")

                latent(["cqa0", "cqa1", "cqa2"], qn, "qn", QLAT, 384)
                latent(["ckv0", "ckv1"], kvn, "kvn", CKV, 256)
                vi = s % 2
                for j in range(4):
                    b = ps_next()
                    for k in range(8):
                        S.op("pe", lambda e, k=k, b=b, j=j: e.matmul(PS[b][:, 0:384], lhsT=hT[:, k, j * 128:(j + 1) * 128], rhs=wtm[:, k, :], start=(k == 0), stop=(k == 7)),
                             R=(("hT", k), ("wtm", k)), W=(PR(b),))
                    S.op("act", lambda e, b=b, vi=vi: e.copy(out=vst[vi][:, :, 0:64], in_=PS[b][:, 0:384].rearrange("p (a c) -> p a c", c=64)),
                         R=(PR(b),), W=(("vst", vi),))
                    tok = slice(s * 512 + j * 128, s * 512 + (j + 1) * 128)
                    for m, dst in enumerate((VA, VBS, VBW)):
                        S.dma(lambda e, m=m, dst=dst, tok=tok, vi=vi: e.dma_start(out=dst[tok, :, :], in_=vst[vi][:, 2 * m:2 * m + 2, :]),
                              R=(("vst", vi),), W=(("p1out", u()),), q="pool")
            S.flush()
        if stop_after == "P1":
            break

        def bc4(ap):
            return ap.unsqueeze(1).broadcast_to([ap.shape[0], 4, 128])

        def load_consts_att(st, names):
            d = {}
            for nm in names:
                shp = cshapes[nm][0]
                dt = BF16 if cshapes[nm][1] == "bf16" else F32
                d[nm] = nc_alloc(st, "c_" + nm, shp, dt)
                S.dma(lambda e, nm=nm: e.dma_start(out=d[nm], in_=C[nm]), W=("c_" + nm,))
            return d

        class Att:
            def __init__(self, st, sbanks, nE=None):
                self.sb = sbanks
                self.sc = 0
                self.ec = 0
                self.depth = max(1, len(sbanks) - 1)
                nE = nE or (self.depth + 3)
                self.E = [nc_alloc(st, "attE%d" % i, [128, 512], BF16) for i in range(nE)]
                self.pend = []

            def tile(self, kT, kreg, q, qreg, masks, v, vreg, ob, first, last, scale, extra=None):
                b = self.sb[self.sc % len(self.sb)]
                self.sc += 1
                n = len(masks)
                kr = list(kreg) if isinstance(kreg, list) else [kreg]
                qr_ = list(qreg) if isinstance(qreg, list) else [qreg]
                S.op("pe", lambda e: e.matmul(PS[b], lhsT=kT, rhs=q, start=True, stop=(n == 0)), R=tuple(kr + qr_), W=(PR(b),))
                for i, (ml, mr, mregs) in enumerate(masks):
                    S.op("pe", lambda e, ml=ml, mr=mr, i=i: e.matmul(PS[b], lhsT=ml, rhs=mr, start=False, stop=(i == n - 1)), R=tuple(mregs), W=(PR(b),))
                ei = self.ec % len(self.E)
                self.ec += 1
                Et = self.E[ei]
                S.op("act", lambda e: e.activation(out=Et, in_=PS[b], func=AF.Exp, scale=scale), R=(PR(b),), W=(("E", ei),))

                def pv():
                    S.op("pe", lambda e: e.matmul(PS[ob], lhsT=v, rhs=Et, start=first, stop=last), R=(vreg, ("E", ei)), W=(PR(ob),))
                    if extra is not None:
                        extra(Et, ("E", ei))
                self.pend.append(pv)
                while len(self.pend) > self.depth:
                    self.pend.pop(0)()

            def drain(self):
                while self.pend:
                    self.pend.pop(0)()

        def fin_den(ob, den, dreg, sink=None, gate=None, greg=None):
            S.op("dve", lambda e: e.tensor_scalar(out=den, in0=PS[ob][64:128, :], scalar1=1e-30, scalar2=None, op0=ALU.max), R=(PR(ob),), W=(dreg,))
            if sink is not None:
                S.op("dve", lambda e: e.tensor_tensor(out=den.rearrange("p (h q) -> p h q", h=4), in0=den.rearrange("p (h q) -> p h q", h=4),
                                                      in1=sink.unsqueeze(2).broadcast_to([64, 4, 128]), op=ALU.add), R=(dreg, "esink"), W=(dreg,))
            S.op("dve", lambda e: e.reciprocal(out=den, in_=den), R=(dreg,), W=(dreg,))
            if gate is not None:
                S.op("pool", lambda e: e.tensor_tensor(out=den, in0=den, in1=gate, op=ALU.mult), R=(dreg, greg), W=(dreg,))

        qv = lambda Q, qb: Q[:, :, qb * 128:(qb + 1) * 128]

        def window_pass(tag, Qd, Kd, Vd, wt, sink, gate_j, part_in, Od):
            with ExitStack() as st:
                S.barrier()
                cc = load_consts_att(st, ["ident_b", "cb", "bb"])
                att = Att(st, [0, 1, 2, 3])
                kA = nc_alloc(st, tag + "k", [64, 2, T], BF16)
                vA = nc_alloc(st, tag + "v", [128, NT, 2, 128], BF16)
                for g in range(2):
                    S.dma(lambda e, g=g: e.dma_start(out=kA[:, g, :], in_=Kd[:, g, :]), W=((tag + "k", g),))
                Vr = Vd.rearrange("(n p) g c -> p n g c", p=128)
                for n0 in range(0, NT, 8):
                    S.dma(lambda e, n0=n0: e.dma_start(out=vA[:, n0:min(NT, n0 + 8)], in_=Vr[:, n0:min(NT, n0 + 8)]), W=((tag + "v", n0),))
                if sink:
                    es = nc_alloc(st, "esink", [64, 8], F32)
                    S.dma(lambda e: e.dma_start(out=es, in_=W["sinks"].broadcast_to([64, 8])), W=("esink",))
                    S.op("act", lambda e: e.activation(out=es, in_=es, func=AF.Exp), R=("esink",), W=("esink",))
                qts = [nc_alloc(st, tag + "q%d" % i, [64, 8, 128], BF16) for i in range(3)]
                dens = [nc_alloc(st, tag + "den%d" % i, [64, 512], F32) for i in range(2)]
                outs = [nc_alloc(st, tag + "out%d" % i, [64, 512], BF16) for i in range(2)]
                if gate_j is not None:
                    gts = [nc_alloc(st, tag + "gt%d" % i, [64, 4, 128], F32) for i in range(2)]
                    pts = [nc_alloc(st, tag + "pt%d" % i, [64, 4, 128], F32) for i in range(2)]
                    tmps = [nc_alloc(st, tag + "tmp%d" % i, [64, 512], F32) for i in range(2)]
                units = [(qb, g) for qb in range(NT) for g in range(2)]

                def wp_q(qb):
                    S.dma(lambda e: e.dma_start(out=qts[qb % 3], in_=qv(Qd, qb)), W=((tag + "q", qb % 3),))

                def wp_loads(ui):
                    if gate_j is None:
                        return
                    qb, g = units[ui]
                    i2 = ui % 2
                    qsl = slice(qb * 128, (qb + 1) * 128)
                    S.dma(lambda e: e.dma_start(out=gts[i2], in_=GB[4 * g:4 * g + 4, gate_j, qsl].unsqueeze(0).broadcast_to([64, 4, 128])), W=((tag + "gt", i2),))
                    S.dma(lambda e: e.dma_start(out=pts[i2], in_=part_in[:, 4 * g:4 * g + 4, qsl]), W=((tag + "pt", i2),))

                def act_recip(den, dreg):
                    S.op("act", lambda e: e.activation(out=den, in_=den, func=AF.Ln), R=(dreg,), W=(dreg,))
                    S.op("act", lambda e: e.activation(out=den, in_=den, func=AF.Exp, scale=-1.0), R=(dreg,), W=(dreg,))

                wp_q(0)
                if NT > 1:
                    wp_q(1)
                wp_loads(0)
                for ui, (qb, g) in enumerate(units):
                    if g == 0 and qb + 2 < NT:
                        wp_q(qb + 2)
                    if ui + 1 < len(units):
                        wp_loads(ui + 1)
                    qt = qts[qb % 3]
                    qr = (tag + "q", qb % 3)
                    i2 = ui % 2
                    ob = 4 + i2
                    qsl = slice(qb * 128, (qb + 1) * 128)
                    kts = [kt for kt in range(qb - wt, qb + 1) if kt >= 0]
                    for ti, kt in enumerate(kts):
                        masks = []
                        if kt == qb - wt:
                            masks.append((cc["ident_b"], bc4(cc["bb"]), ("c_ident_b", "c_bb")))
                        if kt == qb:
                            masks.append((cc["ident_b"], bc4(cc["cb"]), ("c_ident_b", "c_cb")))
                        att.tile(kA[:, g, kt * 128:(kt + 1) * 128], (tag + "k", g), qt[:, 4 * g:4 * g + 4, :], qr, masks,
                                 vA[:, kt, g, :], (tag + "v", (kt // 8) * 8), ob, ti == 0, ti == len(kts) - 1, 0.125)
                    att.drain()
                    den, dreg = dens[i2], (tag + "den", i2)
                    out, oreg = outs[i2], (tag + "out", i2)
                    S.op("dve", lambda e, ob=ob, den=den: e.tensor_copy(out=den, in_=PS[ob][64:128, :]), R=(PR(ob),), W=(dreg,))
                    if sink:
                        S.op("dve", lambda e, den=den, g=g: e.tensor_tensor(out=den.rearrange("p (h q) -> p h q", h=4), in0=den.rearrange("p (h q) -> p h q", h=4),
                                                                         in1=es[:, 4 * g:4 * g + 4].unsqueeze(2).broadcast_to([64, 4, 128]), op=ALU.add), R=(dreg, "esink"), W=(dreg,))
                    act_recip(den, dreg)
                    if gate_j is None:
                        S.op("dve", lambda e, ob=ob, den=den, out=out: e.tensor_tensor(out=out, in0=PS[ob][0:64, :], in1=den, op=ALU.mult),
                             R=(PR(ob), dreg), W=(oreg,))
                    else:
                        gt, pt, tmp = gts[i2], pts[i2], tmps[i2]
                        S.op("dve", lambda e, den=den, gt=gt: e.tensor_tensor(out=den, in0=den, in1=gt.rearrange("p h q -> p (h q)"), op=ALU.mult), R=(dreg, (tag + "gt", i2)), W=(dreg,))
                        S.op("dve", lambda e, ob=ob, den=den, tmp=tmp: e.tensor_tensor(out=tmp, in0=PS[ob][0:64, :], in1=den, op=ALU.mult),
                             R=(PR(ob), dreg), W=((tag + "tmp", i2),))
                        S.op("dve", lambda e, tmp=tmp, pt=pt, out=out: e.tensor_tensor(out=out, in0=tmp, in1=pt.rearrange("p h q -> p (h q)"), op=ALU.add),
                             R=((tag + "tmp", i2), (tag + "pt", i2)), W=(oreg,))
                    S.dma(lambda e, out=out, g=g, qsl=qsl: e.dma_start(out=Od[:, 4 * g:4 * g + 4, qsl], in_=out.rearrange("p (h q) -> p h q", h=4)),
                          R=(oreg,), W=(("wout", u()),))
                S.flush()

        window_pass("pa", QA, KA, VA, 1, True, None, None, OA)
        if stop_after == "PA":
            break

        with ExitStack() as st:
            S.barrier()
            kc = nc_alloc(st, "kc", [64, 4, T], BF16)
            for g in range(2):
                S.dma(lambda e, g=g: e.dma_start(out=kc[:, g, :], in_=KC[:, g, :]), W=(("kc", g),))
                S.dma(lambda e, g=g: e.dma_start(out=kc[:, 2 + g, :], in_=VC[:, g, :]), W=(("kc", 2 + g),))
            w1 = nc_alloc(st, "w1", [64, 2, 32 * 64], BF16)
            S.dma(lambda e: e.dma_start(out=w1[:, 0, :], in_=W["wk1"].rearrange("d l o -> d (l o)")), W=("w1",), q="pool")
            S.dma(lambda e: e.dma_start(out=w1[:, 1, :], in_=W["wv1"].rearrange("d l o -> d (l o)")), W=("w1b",), q="pool")
            w2 = nc_alloc(st, "w2", [64, 2, 64], BF16)
            S.dma(lambda e: e.dma_start(out=w2[:, 0, :], in_=W["wk2"]), W=("w2",), q="pool")
            S.dma(lambda e: e.dma_start(out=w2[:, 1, :], in_=W["wv2"]), W=("w2b",), q="pool")
            pe = nc_alloc(st, "pe", [64, 2, 32], BF16)
            S.dma(lambda e: e.dma_start(out=pe[:, 0, :], in_=W["pek"]), W=("pe",), q="pool")
            S.dma(lambda e: e.dma_start(out=pe[:, 1, :], in_=W["pev"]), W=("peb",), q="pool")
            NC_ = NCT * 128
            hid = nc_alloc(st, "hid", [64, NC_], BF16)
            kcmp = nc_alloc(st, "kcmp", [64, 2, NC_], BF16)
            vcmp = nc_alloc(st, "vcmp", [128, NCT, 2, 128], BF16)
            S.op("pool", lambda e: e.memset(vcmp, 1.0), W=("vcmp",))
            S.op("pool", lambda e: e.memset(hid, 0.0), W=("hid",))
            for kv in range(2):
                for g in range(2):
                    b = 6
                    src = kc[:, 2 * kv + g, :].rearrange("p (c r) -> p c r", r=16)
                    for li in range(32):
                        rhs = src[:, (li // 16):(li // 16) + n_cmp, li % 16]
                        S.op("pe", lambda e, l=li, rhs=rhs, kv=kv: e.matmul(PS[b][0:64, 0:n_cmp], lhsT=w1[:, kv, l * 64:(l + 1) * 64], rhs=rhs, start=(l == 0), stop=False),
                             R=(("kc", 2 * kv + g), "w1", "w1b"), W=(PR(b),))
                    for li in range(32):
                        S.op("pe", lambda e, l=li, kv=kv: e.matmul(PS[b][0:64, 0:n_cmp], lhsT=w1[:, kv, l * 64:(l + 1) * 64], rhs=pe[:, kv, l:l + 1].broadcast_to([64, n_cmp]), start=False, stop=(l == 31)),
                             R=("pe", "peb", "w1", "w1b"), W=(PR(b),))
                    S.op("act", lambda e: e.activation(out=hid[:, 0:n_cmp], in_=PS[b][0:64, 0:n_cmp], func=AF.Silu), R=(PR(b),), W=("hid",))
                    if kv == 0:
                        b2 = 7
                        S.op("pe", lambda e: e.matmul(PS[b2][0:64, 0:NC_], lhsT=w2[:, 0, :], rhs=hid, start=True, stop=True), R=("hid", "w2"), W=(PR(b2),))
                        S.op("act", lambda e, g=g: e.copy(out=kcmp[:, g, :], in_=PS[b2][0:64, 0:NC_]), R=(PR(b2),), W=(("kcmp", g),))
                    else:
                        b2 = 7
                        for ct in range(NCT):
                            S.op("pe", lambda e, ct=ct: e.matmul(PS[b2][:, ct * 64:(ct + 1) * 64], lhsT=hid[:, ct * 128:(ct + 1) * 128], rhs=w2[:, 1, :], start=(ct == 0), stop=(ct == NCT - 1)),
                                 R=("hid", "w2b"), W=(PR(b2),))
                        S.op("act", lambda e, g=g: e.copy(out=vcmp[:, :, g, 0:64], in_=PS[b2][:, 0:NCT * 64].rearrange("p (c d) -> p c d", d=64)), R=(PR(b2),), W=("vcmp",))
            S.dma(lambda e: e.dma_start(out=KCMP, in_=kcmp), R=(("kcmp", 0), ("kcmp", 1)), W=("KCMPd",))
            S.dma(lambda e: e.dma_start(out=VCMP, in_=vcmp), R=("vcmp",), W=("VCMPd",))
            S.flush()
        with ExitStack() as st:
            S.barrier()
            cc = load_consts_att(st, ["ident_b", "ident_f", "cb", "cm", "msel", "fb"])
            att = Att(st, [0, 1, 2])
            NC_ = NCT * 128
            kcmp = nc_alloc(st, "kcmp2", [64, 2, NC_], BF16)
            vcmp = nc_alloc(st, "vcmp2", [128, NCT, 2, 128], BF16)
            S.dma(lambda e: e.dma_start(out=kcmp, in_=KCMP), R=("KCMPd",), W=(("kcmp", 0), ("kcmp", 1)))
            S.dma(lambda e: e.dma_start(out=vcmp, in_=VCMP), R=("VCMPd",), W=("vcmp",))
            kS = nc_alloc(st, "pbk", [128, 2, T], BF16)
            for g in range(2):
                S.dma(lambda e, g=g: e.dma_start(out=kS[64:128, g, :], in_=C["exh"]), W=(("exh", g),))
            QS = [[nc_alloc(st, "QS%d_%d" % (i, hf), [128, 512], BF16) for hf in range(2)] for i in range(2)]
            vS = nc_alloc(st, "pbv", [128, NT, 2, 128], BF16)
            for g in range(2):
                S.dma(lambda e, g=g: e.dma_start(out=kS[0:64, g, :], in_=KBS[:, g, :]), W=(("pbk", g),))
            Vr = VBS.rearrange("(n p) g c -> p n g c", p=128)
            for n0 in range(0, NT, 8):
                S.dma(lambda e, n0=n0: e.dma_start(out=vS[:, n0:min(NT, n0 + 8)], in_=Vr[:, n0:min(NT, n0 + 8)]), W=(("pbv", n0),))
            qus = [nc_alloc(st, "pbqu%d" % i, [64, 8, 128], BF16) for i in range(3)]
            qrs = [nc_alloc(st, "pbqr%d" % i, [64, 8, 128], BF16) for i in range(3)]
            gts = [nc_alloc(st, "pbgt%d" % i, [64, 2, 4, 128], F32) for i in range(2)]
            dens = [nc_alloc(st, "pbden%d" % i, [64, 512], F32) for i in range(2)]
            rcmp = [nc_alloc(st, "pbrc%d" % i, [64, 512], F32) for i in range(2)]
            tmps = [nc_alloc(st, "pbtmp%d" % i, [64, 512], F32) for i in range(2)]
            outs = [nc_alloc(st, "pbout%d" % i, [64, 512], F32) for i in range(2)]
            rd4 = nc_alloc(st, "rd4", [128, 4], F32)
            scr = nc_alloc(st, "scr", [128, 128], F32)
            scr2 = nc_alloc(st, "scr2", [128, 128], F32)
            m8 = nc_alloc(st, "m8", [128, 16], F32)
            selms = [nc_alloc(st, "selm%d" % i, [128, 128], F32) for i in range(2)]
            selT = [nc_alloc(st, "selT%d" % i, [128, 128], BF16) for i in range(2)]
            units = [(qb, g) for qb in range(NT) for g in range(2)]

            def loadq(qb):
                i = qb % 3
                S.dma(lambda e: e.dma_start(out=qus[i], in_=qv(QBU, qb)), W=(("pbqu", i),))
                S.dma(lambda e: e.dma_start(out=qrs[i], in_=qv(QBR, qb)), W=(("pbqr", i),))

            def cmp_job(ui):
                qb, g = units[ui]
                i2 = ui % 2
                qsl = slice(qb * 128, (qb + 1) * 128)
                gt = gts[i2]
                for jj in range(2):
                    S.dma(lambda e, jj=jj: e.dma_start(out=gt[:, jj, :, :], in_=GB[4 * g:4 * g + 4, jj, qsl].unsqueeze(0).broadcast_to([64, 4, 128])), W=(("pbgt", i2, jj),))
                nct = min(NCT, (8 * qb + 7 + 127) // 128)
                ob = 4
                for ct in range(nct):
                    dl = qb - 16 * ct
                    masks = []
                    if dl <= 16:
                        masks.append((cc["ident_b"], bc4(cc["cm"][:, dl * 128:(dl + 1) * 128]), ("c_ident_b", "c_cm")))

                    def extra(Et, ereg, ct=ct):
                        for h in range(4):
                            bi = 5 + h // 2
                            co = (h % 2) * 129
                            S.op("pe", lambda e, h=h, bi=bi, co=co: e.matmul(PS[bi][:, co:co + 129], lhsT=Et[:, h * 128:(h + 1) * 128], rhs=cc["msel"][:, ct * 129:(ct + 1) * 129],
                                                                          start=(ct == 0 and h % 2 == 0), stop=(ct == nct - 1), skip_group_check=True),
                                 R=(ereg, "c_msel"), W=(PR(bi),))
                    att.tile(kcmp[:, g, ct * 128:(ct + 1) * 128], ("kcmp", g), qus[qb % 3][:, 4 * g:4 * g + 4, :], ("pbqu", qb % 3), masks,
                             vcmp[:, ct, g, :], "vcmp", ob, ct == 0, ct == nct - 1, 0.125, extra=extra)
                att.drain()
                den, dreg = dens[i2], ("pbden", i2)
                fin_den(ob, den, dreg, gate=gt[:, 0, :, :].rearrange("p h q -> p (h q)"), greg=("pbgt", i2, 0))
                S.op("dve", lambda e: e.tensor_tensor(out=rcmp[i2], in0=PS[ob][0:64, :], in1=den, op=ALU.mult), R=(PR(ob), dreg), W=(("pbrc", i2),))
                for h in range(4):
                    bi = 5 + h // 2
                    co = (h % 2) * 129 + 128
                    S.op("dve", lambda e, h=h, bi=bi, co=co: e.tensor_scalar(out=rd4[:, h:h + 1], in0=PS[bi][:, co:co + 1], scalar1=1e-30, scalar2=None, op0=ALU.max),
                         R=(PR(bi),), W=("rd4",))
                S.op("dve", lambda e: e.reciprocal(out=rd4, in_=rd4), R=("rd4",), W=("rd4",))
                for h in range(4):
                    bi = 5 + h // 2
                    co = (h % 2) * 129
                    if h == 0:
                        S.op("dve", lambda e, bi=bi, co=co: e.tensor_scalar(out=scr, in0=PS[bi][:, co:co + 128], scalar1=rd4[:, 0:1], scalar2=None, op0=ALU.mult),
                             R=(PR(bi), "rd4"), W=("scr",))
                    else:
                        S.op("dve", lambda e, h=h, bi=bi, co=co: e.scalar_tensor_tensor(out=scr, in0=PS[bi][:, co:co + 128], scalar=rd4[:, h:h + 1], in1=scr, op0=ALU.mult, op1=ALU.add),
                             R=(PR(bi), "rd4", "scr"), W=("scr",))
                S.op("dve", lambda e: e.tensor_tensor(out=scr, in0=scr, in1=cc["fb"][:, 128 - 2 * qb:256 - 2 * qb], op=ALU.add), R=("scr", "c_fb"), W=("scr",))
                S.op("dve", lambda e: e.tensor_scalar(out=scr[:, 0:1], in0=scr[:, 0:1], scalar1=1e4, scalar2=None, op0=ALU.add), R=("scr",), W=("scr",))
                S.op("dve", lambda e: e.max(out=m8[:, 0:8], in_=scr), R=("scr",), W=("m8",))
                S.op("dve", lambda e: e.match_replace(out=scr2, in_to_replace=m8[:, 0:8], in_values=scr, imm_value=-1e30), R=("scr", "m8"), W=("scr2",))
                S.op("dve", lambda e: e.max(out=m8[:, 8:16], in_=scr2), R=("scr2",), W=("m8b",))
                S.op("dve", lambda e: e.tensor_scalar(out=selms[i2], in0=scr, scalar1=m8[:, 15:16], scalar2=1.0, op0=ALU.is_ge, op1=ALU.subtract), R=("scr", "m8b"), W=(("selm", i2),))

            def cmp_job2(ui):
                qb, g = units[ui]
                i2 = ui % 2
                S.op("pe", lambda e: e.transpose(out=PS[7][:, 0:128], in_=selms[i2], identity=cc["ident_f"]), R=(("selm", i2), "c_ident_f"), W=(PR(7),))
                nh = 2 if qb >= 32 else 1
                for hf in range(nh):
                    S.op("dve", lambda e, hf=hf: e.tensor_copy(out=QS[i2][hf][64:128, :].rearrange("p (h q) -> p h q", h=4),
                                                              in_=PS[7][hf * 64:(hf + 1) * 64, 0:128].unsqueeze(1).broadcast_to([64, 4, 128])),
                         R=(PR(7),), W=(("QS", i2, hf, "m"),))
                    S.op("dve", lambda e, hf=hf: e.tensor_copy(out=QS[i2][hf][0:64, :].rearrange("p (h q) -> p h q", h=4), in_=qrs[qb % 3][:, 4 * g:4 * g + 4, :]),
                         R=(("pbqr", qb % 3),), W=(("QS", i2, hf, "q"),))

            def sel_job(ui, nxt=None):
                qb, g = units[ui]
                i2 = ui % 2
                ob = 3
                qsl = slice(qb * 128, (qb + 1) * 128)
                for kt in range(qb + 1):
                    hf = kt // 32
                    masks = []
                    if kt == qb:
                        masks.append((cc["ident_b"], bc4(cc["cb"]), ("c_ident_b", "c_cb")))
                    att.tile(kS[:, g, kt * 128:(kt + 1) * 128], [("pbk", g), ("exh", g)], QS[i2][hf], [("QS", i2, hf, "m"), ("QS", i2, hf, "q")], masks,
                             vS[:, kt, g, :], ("pbv", (kt // 8) * 8), ob, kt == 0, kt == qb, 0.125)
                att.drain()
                if nxt is not None:
                    cmp_job2(nxt)
                den, dreg = dens[i2], ("pbden", i2)
                fin_den(ob, den, dreg, gate=gts[i2][:, 1, :, :].rearrange("p h q -> p (h q)"), greg=("pbgt", i2, 1))
                S.op("dve", lambda e: e.tensor_tensor(out=tmps[i2], in0=PS[ob][0:64, :], in1=den, op=ALU.mult), R=(PR(ob), dreg), W=(("pbtmp", i2),))
                S.op("dve", lambda e: e.tensor_tensor(out=outs[i2], in0=tmps[i2], in1=rcmp[i2], op=ALU.add), R=(("pbtmp", i2), ("pbrc", i2)), W=(("pbout", i2),))
                S.dma(lambda e: e.dma_start(out=OBP[:, 4 * g:4 * g + 4, qsl], in_=outs[i2].rearrange("p (h q) -> p h q", h=4)), R=(("pbout", i2),), W=(("wout", u()),))

            loadq(0)
            if NT > 1:
                loadq(1)
            cmp_job(0)
            cmp_job2(0)
            for ui in range(len(units)):
                qb, g = units[ui]
                if g == 0 and qb + 2 < NT:
                    loadq(qb + 2)
                if ui + 1 < len(units):
                    cmp_job(ui + 1)
                sel_job(ui, (ui + 1) if ui + 1 < len(units) else None)
            S.flush()
        if stop_after == "PB1":
            break

        window_pass("pw", QBR, KBW, VBW, 4, False, 2, OBP, OB)
        if stop_after == "PB2":
            break

        with ExitStack() as st:
            S.barrier()
            cc = load_consts_att(st, ["ident_b", "mm"])
            att = Att(st, [0, 1, 2, 3])
            qlat = nc_alloc(st, "qlat", [128, 3, T], BF16)
            ckv = nc_alloc(st, "ckv", [128, 2, T], BF16)
            for c in range(3):
                S.dma(lambda e, c=c: e.dma_start(out=qlat[:, c, :], in_=QLAT[:, c, :]), W=(("qlat", c),))
            for c in range(2):
                S.dma(lambda e, c=c: e.dma_start(out=ckv[:, c, :], in_=CKV[:, c, :]), W=(("ckv", c),))
            wq = nc_alloc(st, "wq", [128, 3, 768], BF16)
            wqp = nc_alloc(st, "wqp", [128, 3, 768], BF16)
            wkk = nc_alloc(st, "wkk", [128, 2, 512], BF16)
            wkv_ = nc_alloc(st, "wkv", [128, 2, 512], BF16)
            S.dma(lambda e: e.dma_start(out=wq, in_=W["wq"]), W=("wq",), q="pool")
            S.dma(lambda e: e.dma_start(out=wqp, in_=W["wqp"]), W=("wqp",), q="pool")
            S.dma(lambda e: e.dma_start(out=wkk, in_=W["wkvk"]), W=("wkk",), q="pool")
            S.dma(lambda e: e.dma_start(out=wkv_, in_=W["wkvv"]), W=("wkv",), q="pool")
            KH = nc_alloc(st, "KH", [96, T], BF16)
            VH = nc_alloc(st, "VH", [128, NT, 128], BF16)
            S.op("pool", lambda e: e.memset(VH, 1.0), W=("VH",))
            S.dma(lambda e: e.dma_start(out=KH[64:96, :], in_=KPE), W=("KHpe",))
            QH = [nc_alloc(st, "QH%d" % i, [96, 512], BF16) for i in range(2)]
            tabc = [nc_alloc(st, "mtc%d" % i, [96, 512], F32) for i in range(2)]
            tabs = [nc_alloc(st, "mts%d" % i, [96, 512], F32) for i in range(2)]
            t1 = nc_alloc(st, "mt1", [96, 512], F32)
            t2 = nc_alloc(st, "mt2", [96, 512], F32)
            dens = [nc_alloc(st, "mden%d" % i, [64, 512], F32) for i in range(2)]
            outs = [nc_alloc(st, "mout%d" % i, [64, 512], BF16) for i in range(2)]
            ucl = [0]

            def mla_head(h):
                for s in range(NS):
                    b = 6 + (s % 2)
                    for c in range(2):
                        S.op("pe", lambda e, c=c, b=b, s=s: e.matmul(PS[b][0:64, :], lhsT=wkk[:, c, h * 64:(h + 1) * 64], rhs=ckv[:, c, s * 512:(s + 1) * 512], start=(c == 0), stop=(c == 1)),
                             R=("wkk", ("ckv", c)), W=(PR(b),))
                    S.op("dve", lambda e, b=b, s=s: e.tensor_copy(out=KH[0:64, s * 512:(s + 1) * 512], in_=PS[b][0:64, :]), R=(PR(b),), W=("KHn",))
                for i4 in range(NT // 4):
                    b = 6 + (i4 % 2)
                    for t4 in range(4):
                        tt = i4 * 4 + t4
                        for c in range(2):
                            S.op("pe", lambda e, c=c, b=b, tt=tt, t4=t4: e.matmul(PS[b][:, t4 * 64:(t4 + 1) * 64], lhsT=ckv[:, c, tt * 128:(tt + 1) * 128], rhs=wkv_[:, c, h * 64:(h + 1) * 64],
                                                                             start=(t4 == 0 and c == 0), stop=(t4 == 3 and c == 1), skip_group_check=True),
                                 R=("wkv", ("ckv", c)), W=(PR(b),))
                    S.op("act", lambda e, b=b, i4=i4: e.copy(out=VH[:, i4 * 4:(i4 + 1) * 4, 0:64], in_=PS[b][:, 0:256].rearrange("p (a d) -> p a d", d=64)), R=(PR(b),), W=("VH",))
                def qproj(qs, i2):
                    tsl = slice(qs * 512, (qs + 1) * 512)
                    S.dma(lambda e: e.dma_start(out=tabc[i2][64:96, :], in_=rcM[64:96, tsl]), W=(("mtc", i2),))
                    S.dma(lambda e: e.dma_start(out=tabs[i2][64:96, :], in_=rsM[64:96, tsl]), W=(("mts", i2),))
                    ba, bb_ = 6, 7
                    for c in range(3):
                        S.op("pe", lambda e, c=c: e.matmul(PS[ba][0:96, :], lhsT=wq[:, c, h * 96:(h + 1) * 96], rhs=qlat[:, c, tsl], start=(c == 0), stop=(c == 2)),
                             R=("wq", ("qlat", c)), W=(PR(ba),))
                    for c in range(3):
                        S.op("pe", lambda e, c=c: e.matmul(PS[bb_][0:96, :], lhsT=wqp[:, c, h * 96:(h + 1) * 96], rhs=qlat[:, c, tsl], start=(c == 0), stop=(c == 2)),
                             R=("wqp", ("qlat", c)), W=(PR(bb_),))
                    qh, qreg = QH[i2], ("QH", i2)
                    S.op("act", lambda e: e.copy(out=qh[0:64, :], in_=PS[ba][0:64, :]), R=(PR(ba),), W=((qreg, "n"),))
                    S.op("dve", lambda e: e.tensor_tensor(out=t1[64:96, :], in0=PS[ba][64:96, :], in1=tabc[i2][64:96, :], op=ALU.mult), R=(PR(ba), ("mtc", i2)), W=("mt1",))
                    S.op("dve", lambda e: e.tensor_tensor(out=t2[64:96, :], in0=PS[bb_][64:96, :], in1=tabs[i2][64:96, :], op=ALU.mult), R=(PR(bb_), ("mts", i2)), W=("mt2",))
                    S.op("dve", lambda e: e.tensor_tensor(out=qh[64:96, :], in0=t1[64:96, :], in1=t2[64:96, :], op=ALU.add), R=("mt1", "mt2"), W=((qreg, "r"),))

                qproj(0, ucl[0] % 2)
                for qs in range(NS):
                    i2 = ucl[0] % 2
                    ucl[0] += 1
                    ob = 4 + i2
                    tsl = slice(qs * 512, (qs + 1) * 512)
                    if qs + 1 < NS:
                        qproj(qs + 1, ucl[0] % 2)
                    qh, qreg = QH[i2], ("QH", i2)
                    nk = 4 * qs + 4
                    for kt in range(nk):
                        masks = []
                        if kt >= 4 * qs:
                            i = kt - 4 * qs
                            masks.append((cc["ident_b"], cc["mm"][:, i * 512:(i + 1) * 512], ("c_ident_b", "c_mm")))
                        att.tile(KH[:, kt * 128:(kt + 1) * 128], ["KHn", "KHpe"], qh, [(qreg, "n"), (qreg, "r")], masks, VH[:, kt, :], "VH", ob, kt == 0, kt == nk - 1, float(96 ** -0.5))
                    att.drain()
                    den, dreg = dens[i2], ("mden", i2)
                    fin_den(ob, den, dreg)
                    out = outs[i2]
                    S.op("dve", lambda e, ob=ob, den=den, out=out: e.tensor_tensor(out=out, in0=PS[ob][0:64, :], in1=den, op=ALU.mult), R=(PR(ob), dreg), W=(("mout", i2),))
                    S.dma(lambda e, out=out, tsl=tsl: e.dma_start(out=OC[:, h, tsl], in_=out), R=(("mout", i2),), W=(("wout", u()),), q="pool")
            for h_ in range(8):
                mla_head(h_)
            S.flush()
        if stop_after == "PC":
            break

        with ExitStack() as st:
            S.barrier()
            tmp = norm_tmp(st)
            wg = nc_alloc(st, "wg", [128, 8, 3072], BF16)
            for k in range(8):
                S.dma(lambda e, k=k: e.dma_start(out=wg[:, k, :], in_=W["wg"][k * 128:(k + 1) * 128, :]), W=(("wg", k),))
            wbr = nc_alloc(st, "wbr", [64, 3, 8, 1024], BF16)
            for xi, nm in enumerate(("wbra", "wbrb", "wbrc")):
                S.dma(lambda e, xi=xi, nm=nm: e.dma_start(out=wbr[:, xi, :, :], in_=W[nm]), W=(("wbr", xi),))
            wout = nc_alloc(st, "wout", [128, 8, 1024], BF16)
            S.dma(lambda e: e.dma_start(out=wout, in_=W["wout"]), W=("wout",))
            gmix = nc_alloc(st, "gmix", [128, 8], F32)
            S.dma(lambda e: e.dma_start(out=gmix, in_=W["nmix"]), W=("gmix",))
            xt1 = nc_alloc(st, "pmx", [128, 4, D], F32)
            xts = [xt1, xt1]
            hT = nc_alloc(st, "pmhT", [128, 8, 512], BF16)
            oin1 = [nc_alloc(st, "pmo_%d" % xi, [64, 8, 512], BF16) for xi in range(3)]
            oin = [oin1, oin1]
            gsb = [nc_alloc(st, "pmg%d" % i, [128, 512], F32) for i in range(2)]
            tmpm = nc_alloc(st, "pmt", [128, 512], F32)
            macc = nc_alloc(st, "pmacc", [128, 512], F32)
            mT = nc_alloc(st, "pmmT", [128, 8, 512], BF16)
            gc = 0
            for s in range(NS):
                xt, xr = xts[s % 2], ("pmx", 0)
                tsl = slice(s * 512, (s + 1) * 512)
                S.dma(lambda e, xt=xt, s=s: e.dma_start(out=xt, in_=xrv[s]), R=(("xres", s),), W=(xr,))
                for xi, Od in enumerate((OA, OB, OC)):
                    S.dma(lambda e, xi=xi, Od=Od, tsl=tsl, s=s: e.dma_start(out=oin[s % 2][xi], in_=Od[:, :, tsl]), W=(("pmo", 0, xi),))
                norm_T(xt, xr, hT, "pmhT", gmix, "gmix", tmp)
                for cg in range(8):
                    for xi in range(3):
                        bp = ps_next()
                        for h in range(8):
                            S.op("pe", lambda e, bp=bp, xi=xi, cg=cg, s=s, h=h: e.matmul(PS[bp], lhsT=wbr[:, xi, h, cg * 128:(cg + 1) * 128],
                                                                                  rhs=oin[s % 2][xi][:, h, :], start=(h == 0), stop=(h == 7)),
                                 R=(("wbr", xi), ("pmo", 0, xi)), W=(PR(bp),))
                        bg = ps_next()
                        for k in range(8):
                            S.op("pe", lambda e, bg=bg, xi=xi, k=k, cg=cg: e.matmul(PS[bg], lhsT=wg[:, k, xi * 1024 + cg * 128: xi * 1024 + (cg + 1) * 128], rhs=hT[:, k, :], start=(k == 0), stop=(k == 7)),
                                 R=(("wg", k), ("pmhT", k)), W=(PR(bg),))
                        gi = gc % 2
                        gc += 1
                        S.op("act", lambda e, bg=bg, gi=gi: e.activation(out=gsb[gi], in_=PS[bg], func=AF.Sigmoid), R=(PR(bg),), W=(("pmg", gi),))
                        if xi == 0:
                            S.op("dve", lambda e, bp=bp, gi=gi: e.tensor_tensor(out=macc, in0=PS[bp], in1=gsb[gi], op=ALU.mult), R=(PR(bp), ("pmg", gi)), W=("pmacc",))
                        else:
                            S.op("dve", lambda e, bp=bp, gi=gi: e.tensor_tensor(out=tmpm, in0=PS[bp], in1=gsb[gi], op=ALU.mult), R=(PR(bp), ("pmg", gi)), W=("pmt",))
                            if xi == 1:
                                S.op("dve", lambda e: e.tensor_tensor(out=macc, in0=macc, in1=tmpm, op=ALU.add), R=("pmacc", "pmt"), W=("pmacc",))
                            else:
                                S.op("dve", lambda e, cg=cg: e.tensor_tensor(out=mT[:, cg, :], in0=macc, in1=tmpm, op=ALU.add), R=("pmacc", "pmt"), W=(("pmmT", cg),))
                for j in range(4):
                    for half in range(2):
                        b = ps_next()
                        for cg in range(8):
                            S.op("pe", lambda e, b=b, cg=cg, j=j, half=half: e.matmul(PS[b], lhsT=mT[:, cg, j * 128:(j + 1) * 128], rhs=wout[:, cg, half * 512:(half + 1) * 512], start=(cg == 0), stop=(cg == 7)),
                                 R=(("pmmT", cg), "wout"), W=(PR(b),))
                        S.op("dve", lambda e, b=b, j=j, half=half, xt=xt: e.tensor_tensor(out=xt[:, j, half * 512:(half + 1) * 512], in0=PS[b], in1=xt[:, j, half * 512:(half + 1) * 512], op=ALU.add),
                             R=(PR(b), xr), W=(xr,))
                S.dma(lambda e, xt=xt, s=s: e.dma_start(out=xrv[s], in_=xt), R=(xr,), W=(("xres", s),))
            S.flush()
        if stop_after == "PM":
            break

        with ExitStack() as st:
            S.barrier()
            tmp = norm_tmp(st)
            wxq = nc_alloc(st, "wxq", [128, 8, 512], BF16)
            wxkv = nc_alloc(st, "wxkv", [128, 8, 1024], BF16)
            wxo = nc_alloc(st, "wxo", [128, 4, 1024], BF16)
            S.dma(lambda e: e.dma_start(out=wxq, in_=W["wxq"]), W=("wxq",))
            S.dma(lambda e: e.dma_start(out=wxkv, in_=W["wxkv"]), W=("wxkv",))
            S.dma(lambda e: e.dma_start(out=wxo, in_=W["wxo"]), W=("wxo",))
            gx = nc_alloc(st, "gx", [128, 8], F32)
            gm = nc_alloc(st, "gm", [128, 8], F32)
            S.dma(lambda e: e.dma_start(out=gx, in_=W["nx"]), W=("gx",))
            S.dma(lambda e: e.dma_start(out=gm, in_=W["nmem"]), W=("gm",))
            ones_b = nc_alloc(st, "ones_b", [128, 128], BF16)
            S.dma(lambda e: e.dma_start(out=ones_b, in_=C["ones_f"]), W=("ones_b",))
            xts = [nc_alloc(st, "pxx%d" % i, [128, 4, D], F32) for i in range(2)]
            hT = nc_alloc(st, "pxhT", [128, 8, 512], BF16)
            KM = nc_alloc(st, "KM", [128, 4, 256], BF16)
            VM = nc_alloc(st, "VM", [128, 2, 512], BF16)
            memt = xts[1]
            S.dma(lambda e: e.dma_start(out=memt[:, 0:2, :], in_=mem_in.rearrange("(j p) c -> p j c", p=128)), W=(("pxx", 1),))
            norm_T(memt, ("pxx", 1), hT, "pxhT", gm, "gm", tmp, ntok_tiles=2)
            for h in range(4):
                b = ps_next()
                for k in range(8):
                    S.op("pe", lambda e, b=b, k=k, h=h: e.matmul(PS[b][:, 0:256], lhsT=wxkv[:, k, h * 128:(h + 1) * 128], rhs=hT[:, k, 0:256], start=(k == 0), stop=(k == 7)),
                         R=("wxkv", ("pxhT", k)), W=(PR(b),))
                S.op("act", lambda e, b=b, h=h: e.copy(out=KM[:, h, :], in_=PS[b][:, 0:256]), R=(PR(b),), W=("KM",))
            for mt in range(2):
                b = ps_next()
                for k in range(8):
                    S.op("pe", lambda e, b=b, k=k, mt=mt: e.matmul(PS[b], lhsT=hT[:, k, mt * 128:(mt + 1) * 128], rhs=wxkv[:, k, 512:1024], start=(k == 0), stop=(k == 7)),
                         R=("wxkv", ("pxhT", k)), W=(PR(b),))
                S.op("act", lambda e, b=b, mt=mt: e.copy(out=VM[:, mt, :], in_=PS[b]), R=(PR(b),), W=("VM",))
            qx = [nc_alloc(st, "qx%d" % i, [128, 512], BF16) for i in range(2)]
            Ex = [nc_alloc(st, "Ex%d" % i, [128, 512], BF16) for i in range(4)]
            denx = nc_alloc(st, "denx", [128, 512], F32)
            oxT = nc_alloc(st, "oxT", [128, 4, 512], BF16)
            ec = 0
            def px_load(s):
                S.dma(lambda e: e.dma_start(out=xts[s % 2], in_=xrv[s]), R=(("xres", s),), W=(("pxx", s % 2),))

            px_load(0)
            norm_A(xts[0], ("pxx", 0), tmp)
            for s in range(NS):
                xt, xr = xts[s % 2], ("pxx", s % 2)
                if s + 1 < NS:
                    px_load(s + 1)
                norm_B(hT, "pxhT", gx, "gx", tmp)
                if s + 1 < NS:
                    norm_A(xts[(s + 1) % 2], ("pxx", (s + 1) % 2), tmp)
                for h in range(4):
                    bq = ps_next()
                    for k in range(8):
                        S.op("pe", lambda e, bq=bq, k=k, h=h: e.matmul(PS[bq], lhsT=wxq[:, k, h * 128:(h + 1) * 128], rhs=hT[:, k, :], start=(k == 0), stop=(k == 7)),
                             R=("wxq", ("pxhT", k)), W=(PR(bq),))
                    qi = h % 2
                    S.op("act", lambda e, bq=bq, qi=qi: e.copy(out=qx[qi], in_=PS[bq]), R=(PR(bq),), W=(("qx", qi),))
                    bo = ps_next()
                    bd = ps_next()
                    for mt in range(2):
                        bs = ps_next()
                        S.op("pe", lambda e, bs=bs, h=h, mt=mt, qi=qi: e.matmul(PS[bs], lhsT=KM[:, h, mt * 128:(mt + 1) * 128], rhs=qx[qi], start=True, stop=True),
                             R=("KM", ("qx", qi)), W=(PR(bs),))
                        ei = ec % 4
                        ec += 1
                        S.op("act", lambda e, bs=bs, ei=ei: e.activation(out=Ex[ei], in_=PS[bs], func=AF.Exp, scale=float(128 ** -0.5)), R=(PR(bs),), W=(("Ex", ei),))
                        S.op("pe", lambda e, bo=bo, h=h, mt=mt, ei=ei: e.matmul(PS[bo], lhsT=VM[:, mt, h * 128:(h + 1) * 128], rhs=Ex[ei], start=(mt == 0), stop=(mt == 1)),
                             R=("VM", ("Ex", ei)), W=(PR(bo),))
                        S.op("pe", lambda e, bd=bd, mt=mt, ei=ei: e.matmul(PS[bd], lhsT=ones_b, rhs=Ex[ei], start=(mt == 0), stop=(mt == 1)),
                             R=("ones_b", ("Ex", ei)), W=(PR(bd),))
                    S.op("dve", lambda e, bd=bd: e.reciprocal(out=denx, in_=PS[bd]), R=(PR(bd),), W=("denx",))
                    S.op("dve", lambda e, bo=bo, h=h: e.tensor_tensor(out=oxT[:, h, :], in0=PS[bo], in1=denx, op=ALU.mult), R=(PR(bo), "denx"), W=(("oxT", h),))
                for j in range(4):
                    for half in range(2):
                        b = ps_next()
                        for h in range(4):
                            S.op("pe", lambda e, b=b, h=h, j=j, half=half: e.matmul(PS[b], lhsT=oxT[:, h, j * 128:(j + 1) * 128], rhs=wxo[:, h, half * 512:(half + 1) * 512], start=(h == 0), stop=(h == 3)),
                                 R=(("oxT", h), "wxo"), W=(PR(b),))
                        S.op("dve", lambda e, b=b, j=j, half=half, xt=xt: e.tensor_tensor(out=xt[:, j, half * 512:(half + 1) * 512], in0=PS[b], in1=xt[:, j, half * 512:(half + 1) * 512], op=ALU.add),
                             R=(PR(b), xr), W=(xr,))
                S.dma(lambda e, xt=xt, s=s: e.dma_start(out=xrv[s], in_=xt), R=(xr,), W=(("xres", s),))
            S.flush()
        if stop_after == "PX":
            break

        with ExitStack() as st:
            S.barrier()
            tmp = norm_tmp(st)
            wgu = nc_alloc(st, "wgu", [128, 8, 2 * DFF], BF16)
            for k in range(8):
                S.dma(lambda e, k=k: e.dma_start(out=wgu[:, k, :], in_=W["wgu"][:, k, :]), W=(("wgu", k),))
            gf = nc_alloc(st, "gf", [128, 8], F32)
            S.dma(lambda e: e.dma_start(out=gf, in_=W["nffn"]), W=("gf",))
            xts = [nc_alloc(st, "pfx%d" % i, [128, 4, D], F32) for i in range(2)]
            hT = nc_alloc(st, "pfhT", [128, 8, 512], BF16)
            sg = [nc_alloc(st, "pfsg%d" % i, [128, 512], F32) for i in range(2)]
            actT1 = nc_alloc(st, "pfact", [128, 22, 512], BF16)
            actT = [actT1, actT1]
            def pf_load(s):
                S.dma(lambda e: e.dma_start(out=xts[s % 2], in_=xrv[s]), R=(("xres", s),), W=(("pfx", s % 2),))

            pf_load(0)
            norm_A(xts[0], ("pfx", 0), tmp)
            for s in range(NS):
                xt, xr = xts[s % 2], ("pfx", s % 2)
                tsl = slice(s * 512, (s + 1) * 512)
                if s + 1 < NS:
                    pf_load(s + 1)
                norm_B(hT, "pfhT", gf, "gf", tmp)
                if s + 1 < NS:
                    norm_A(xts[(s + 1) % 2], ("pfx", (s + 1) % 2), tmp)
                at_, ar = actT[s % 2], ("pfact", 0)
                for f in range(22):
                    bg = ps_next()
                    bu = ps_next()
                    for k in range(8):
                        S.op("pe", lambda e, bg=bg, k=k, f=f: e.matmul(PS[bg], lhsT=wgu[:, k, f * 128:(f + 1) * 128], rhs=hT[:, k, :], start=(k == 0), stop=(k == 7)),
                             R=(("wgu", k), ("pfhT", k)), W=(PR(bg),))
                    for k in range(8):
                        S.op("pe", lambda e, bu=bu, k=k, f=f: e.matmul(PS[bu], lhsT=wgu[:, k, DFF + f * 128:DFF + (f + 1) * 128], rhs=hT[:, k, :], start=(k == 0), stop=(k == 7)),
                             R=(("wgu", k), ("pfhT", k)), W=(PR(bu),))
                    si = f % 2
                    S.op("act", lambda e, bg=bg, si=si: e.activation(out=sg[si], in_=PS[bg], func=AF.Silu), R=(PR(bg),), W=(("pfsg", si),))
                    S.op("dve", lambda e, bu=bu, si=si, f=f, at_=at_: e.tensor_tensor(out=at_[:, f, :], in0=PS[bu], in1=sg[si], op=ALU.mult), R=(PR(bu), ("pfsg", si)), W=((ar, f),))
                S.dma(lambda e, at_=at_, tsl=tsl: e.dma_start(out=ACTD[:, :, tsl], in_=at_), R=tuple((ar, f) for f in range(22)), W=(("actd", s),))
            S.flush()
        with ExitStack() as st:
            S.barrier()
            wdn = nc_alloc(st, "wdn", [128, 22, 1024], BF16)
            S.dma(lambda e: e.dma_start(out=wdn, in_=W["wdn"]), W=("wdn",))
            xts = [nc_alloc(st, "pgx%d" % i, [128, 4, D], F32) for i in range(2)]
            actT = [nc_alloc(st, "pgact%d" % i, [128, 22, 512], BF16) for i in range(2)]
            last = (l == L - 1)
            if last:
                gfin = nc_alloc(st, "gfin", [128, D], F32)
                S.dma(lambda e: e.dma_start(out=gfin, in_=nfin_in.broadcast_to([128, D])), W=("gfin",))
                sq = nc_alloc(st, "fsq", [128, D], F32)
                ss = nc_alloc(st, "fss", [128, 4], F32)
            yv = y_out.rearrange("(s j p) c -> s p j c", p=128, j=4)

            def pg_load(s):
                S.dma(lambda e: e.dma_start(out=xts[s % 2], in_=xrv[s]), R=(("xres", s),), W=(("pgx", s % 2),))
                S.dma(lambda e: e.dma_start(out=actT[s % 2], in_=ACTD[:, :, s * 512:(s + 1) * 512]), R=(("actd", s),), W=(("pgact", s % 2),))

            for s in range(NS):
                xt, xr = xts[s % 2], ("pgx", s % 2)
                at_, ar = actT[s % 2], ("pgact", s % 2)
                tsl = slice(s * 512, (s + 1) * 512)
                if s == 0:
                    pg_load(0)
                if s + 1 < NS:
                    pg_load(s + 1)
                for j in range(4):
                    for half in range(2):
                        b = ps_next()
                        for f in range(22):
                            S.op("pe", lambda e, b=b, f=f, j=j, half=half, at_=at_: e.matmul(PS[b], lhsT=at_[:, f, j * 128:(j + 1) * 128], rhs=wdn[:, f, half * 512:(half + 1) * 512], start=(f == 0), stop=(f == 21)),
                                 R=(ar, "wdn"), W=(PR(b),))
                        S.op("dve", lambda e, b=b, j=j, half=half, xt=xt: e.tensor_tensor(out=xt[:, j, half * 512:(half + 1) * 512], in0=PS[b], in1=xt[:, j, half * 512:(half + 1) * 512], op=ALU.add),
                             R=(PR(b), xr), W=(xr,))
                if not last:
                    S.dma(lambda e, xt=xt, s=s: e.dma_start(out=xrv[s], in_=xt), R=(xr,), W=(("xres", s),))
                else:
                    for j in range(4):
                        S.op("act", lambda e, j=j, xt=xt: e.activation(out=sq, in_=xt[:, j, :], func=AF.Square, accum_out=ss[:, j:j + 1]), R=(xr,), W=("fsq", ("fss", j)))
                    ssr = tuple(("fss", j) for j in range(4))
                    S.op("act", lambda e: e.activation(out=ss, in_=ss, func=AF.Sqrt, scale=1.0 / D, bias=EPS), R=ssr, W=ssr)
                    S.op("dve", lambda e: e.reciprocal(out=ss, in_=ss), R=ssr, W=ssr)
                    for j in range(4):
                        S.op("dve", lambda e, j=j, xt=xt: e.scalar_tensor_tensor(out=xt[:, j, :], in0=xt[:, j, :], scalar=ss[:, j:j + 1], in1=gfin, op0=ALU.mult, op1=ALU.mult),
                             R=(xr, ("fss", j), "gfin"), W=(xr,))
                    S.dma(lambda e, xt=xt, s=s: e.dma_start(out=yv[s], in_=xt), R=(xr,), W=(("y", s),))
            S.flush()
    S.finish()
    return nc


def prep_inputs(inp, T, L, b):
    consts = make_consts(T)
    m = {}
    m["x"] = np.ascontiguousarray(np.asarray(inp["x"][b, :T], np.float32))
    m["mem"] = np.ascontiguousarray(np.asarray(inp["mem"][b], np.float32))
    m["pos"] = np.ascontiguousarray(np.asarray(inp["positions"][b, :T], np.int32).reshape(1, T))
    m["nfin"] = np.ascontiguousarray(np.asarray(inp["norm_final"], np.float32).reshape(1, D))
    for k, v in consts.items():
        m["c_" + k] = v
    return m, consts


T_FULL, L_FULL, B_FULL = 8192, 2, 4


def kernel(**inputs):
    inp = {k: np.asarray(v) for k, v in inputs.items()}
    T, L, B = T_FULL, L_FULL, B_FULL
    ws = [layer_weights(inp, l) for l in range(L)]
    in_maps = []
    consts = None
    for b in range(B):
        m, consts = prep_inputs(inp, T, L, b)
        for l in range(L):
            for k, v in ws[l].items():
                m["w%d_%s" % (l, k)] = v
        in_maps.append(m)
    wshapes = {k: v.shape for k, v in ws[0].items()}
    cshapes = {k: (v.shape, "bf16" if v.dtype == NBF else "f32") for k, v in consts.items()}
    nc = build(T, L, wshapes, cshapes)
    res = run_bass_kernel_spmd(nc, in_maps, core_ids=list(range(B)))
    out = np.stack([np.asarray(r["y"], dtype=np.float32) for r in res.results], axis=0)
    return out
```

```python
import numpy as np
import ml_dtypes
from contextlib import ExitStack
import concourse.bass as bass
import concourse.mybir as mybir
from concourse.bass_utils import run_bass_kernel_spmd
from concourse.alu_op_type import AluOpType as ALU

AF = mybir.ActivationFunctionType
F32, BF16, I32 = mybir.dt.float32, mybir.dt.bfloat16, mybir.dt.int32
NEG = -30000.0
D = 1024
DFF = 2816
EPS = 1e-6
NBF = ml_dtypes.bfloat16


class Sched:
    CENG = ("pe", "act", "dve", "pool", "sp")

    def __init__(self, nc, n_dma_sems=40):
        self.nc = nc
        self.eng = {"pe": nc.tensor, "act": nc.scalar, "dve": nc.vector,
                    "pool": nc.gpsimd, "sp": nc.sync}
        self.sem = {e: nc.alloc_semaphore("sem_" + e) for e in self.CENG}
        self.cnt = {e: 0 for e in self.CENG}
        self.dsem = [nc.alloc_semaphore("dsem%d" % i) for i in range(n_dma_sems)]
        self.dval = [0] * n_dma_sems
        self.dnext = 0
        self.ops = []
        self.all_tok = {}
        self.nops = 0
        self.last_w = {}
        self.readers = {}
        self.waited = {e: {} for e in self.CENG}
        self.sig_after = {e: [] for e in self.CENG}
        self.op_eng = {}
        self.op_isdma = {}

    def op(self, eng, fn, R=(), W=(), dma=False):
        if dma:
            eng = "pool"
        elif eng == "pool":
            eng = "dve"
        idx = self.nops
        self.nops += 1
        deps = set()
        for r in R:
            if r in self.last_w:
                deps.add(self.last_w[r])
        for w in W:
            if w in self.last_w:
                deps.add(self.last_w[w])
            for rd in self.readers.get(w, ()):
                deps.add(rd)
        deps.discard(idx)
        for r in R:
            self.readers.setdefault(r, []).append(idx)
        for w in W:
            self.last_w[w] = idx
            self.readers[w] = []
        self.op_eng[idx] = eng
        self.op_isdma[idx] = dma
        self.ops.append(dict(idx=idx, eng=eng, fn=fn, deps=deps, dma=dma, sig=False, barrier=False))
        return idx

    def dma(self, fn, R=(), W=(), q="sp"):
        return self.op(q, fn, R, W, dma=True)

    def barrier(self):
        self.ops.append(dict(idx=None, barrier=True))

    def _wait(self, eng, sem, val, key):
        w = self.waited[eng]
        if w.get(key, 0) >= val:
            return
        w[key] = val
        self.eng[eng].wait_ge(sem, val)

    def flush(self):
        ops = self.ops
        self.ops = []
        pend = {o["idx"]: o for o in ops if not o["barrier"]}
        last_on = {}
        for o in ops:
            if o["barrier"]:
                for e, lo in last_on.items():
                    lo["sig"] = True
                continue
            for d in o["deps"]:
                if d in pend and not pend[d]["dma"]:
                    if not (o["eng"] == "pe" and pend[d]["eng"] == "pe" and not o["dma"]):
                        pend[d]["sig"] = True
            if not o["dma"]:
                last_on[o["eng"]] = o
        for e, lo in last_on.items():
            lo["sig"] = True
        for o in ops:
            if o["barrier"]:
                for e in self.CENG:
                    for p in self.CENG:
                        if p != e and self.cnt[p] > 0:
                            self._wait(e, self.sem[p], self.cnt[p], p)
                    for i, v in enumerate(self.dval):
                        if v > 0:
                            self._wait(e, self.dsem[i], v, ("d", i))
                continue
            e = o["eng"]
            E = self.eng[e]
            need = {}
            for d in o["deps"]:
                if self.op_isdma[d]:
                    s_i, v = self.all_tok[d]
                    need[("d", s_i)] = max(need.get(("d", s_i), 0), v)
                else:
                    pe_ = self.op_eng[d]
                    if pe_ == "pe" and e == "pe" and not o["dma"]:
                        continue
                    tok = self.all_tok.get(d)
                    if tok is None or tok[1] is None:
                        v = None
                        for (i2, v2) in self.sig_after[pe_]:
                            if i2 >= d:
                                v = v2
                                break
                        assert v is not None, ("unsignalled dep", d, pe_)
                    else:
                        v = tok[1]
                    need[pe_] = max(need.get(pe_, 0), v)
            for k, v in need.items():
                if isinstance(k, tuple):
                    self._wait(e, self.dsem[k[1]], v, k)
                else:
                    self._wait(e, self.sem[k], v, k)
            if o["dma"]:
                si = self.dnext
                self.dnext = (self.dnext + 1) % len(self.dsem)
                if self.dval[si] > 0:
                    self._wait(e, self.dsem[si], self.dval[si], ("d", si))
                ins = o["fn"](E)
                self.dval[si] += 16
                ins.then_inc(self.dsem[si], 16)
                self.all_tok[o["idx"]] = (si, self.dval[si])
            else:
                ins = o["fn"](E)
                if o["sig"]:
                    self.cnt[e] += 1
                    ins.then_inc(self.sem[e], 1)
                    self.all_tok[o["idx"]] = (e, self.cnt[e])
                    self.sig_after[e].append((o["idx"], self.cnt[e]))
                else:
                    self.all_tok[o["idx"]] = (e, None)

    def finish(self):
        self.barrier()
        self.flush()


def make_consts(T):
    NT = T // 128
    NCT = (T // 16 - 1 + 127) // 128
    n_cmp = (T - 32) // 16 + 1
    c = {}
    c["ident_f"] = np.eye(128, dtype=np.float32)
    c["ident_b"] = np.eye(128, dtype=np.float32).astype(NBF)
    c["ones_f"] = np.ones((128, 128), np.float32)
    k = np.arange(128)[:, None]
    q = np.arange(128)[None, :]
    c["cb"] = np.where(k <= q, 0.0, NEG).astype(NBF)
    c["bb"] = np.where(k > q, 0.0, NEG).astype(NBF)
    mm = np.zeros((128, 4, 4, 128), np.float32)
    for i in range(4):
        for j in range(4):
            if j < i:
                mm[:, i, j, :] = NEG
            elif j == i:
                mm[:, i, j, :] = np.where(k <= q, 0.0, NEG)
    c["mm"] = mm.reshape(128, 4 * 512).astype(NBF)
    cm = np.zeros((128, 17, 128), np.float32)
    for dl in range(17):
        cm[:, dl, :] = np.where(16 * k + 31 - q <= 128 * dl, 0.0, NEG)
    c["cm"] = cm.reshape(128, 17 * 128).astype(NBF)
    ex = np.zeros((128, NT, 128), np.float32)
    for kt in range(NT):
        for half in range(2):
            j = 2 * kt + half
            if j < 128:
                ex[j, kt, half * 64:(half + 1) * 64] = -NEG
    c["ex"] = ex.reshape(128, NT * 128).astype(NBF)
    exh = np.zeros((64, NT, 128), np.float32)
    for kt in range(NT):
        for half in range(2):
            exh[(2 * kt + half) % 64, kt, half * 64:(half + 1) * 64] = -NEG
    c["exh"] = exh.reshape(64, NT * 128).astype(NBF)
    n_slc = T // 64
    cs = np.arange(n_cmp) * 16
    ss = np.arange(n_slc) * 64
    cover = np.minimum(cs[:, None] + 32, ss[None, :] + 64) - np.maximum(cs[:, None], ss[None, :])
    c2s = np.clip(cover, 0, None) / 16.0
    ms = np.zeros((NCT * 128, 129), np.float32)
    ms[:n_cmp, :n_slc] = c2s
    ms[:, 128] = 1.0
    c["msel"] = ms.reshape(NCT, 128, 129).transpose(1, 0, 2).reshape(128, NCT * 129).astype(NBF)
    fb = np.zeros((128, 256), np.float32)
    for qq in range(128):
        hi = 1 if qq >= 64 else 0
        for dl in (hi, hi - 1):
            fb[qq, 128 + dl] = 1e4
    c["fb"] = fb
    rp = np.zeros((128, 4), np.float32)
    p = np.arange(128)
    fa = (10000.0 ** (-(np.arange(32, dtype=np.float32)) / 32)).astype(np.float32)
    rp[:, 0] = fa[p % 32]
    rp[:, 1] = np.where((p % 64) < 32, -1.0, 1.0)
    fm = (10000.0 ** (-(np.arange(16, dtype=np.float32)) / 16)).astype(np.float32)
    rp[:, 2] = fm[p % 16]
    rp[:, 3] = np.where((p % 32) < 16, -1.0, 1.0)
    c["ropep"] = rp
    return c


IN_SPLITS = (512, 128, 128, 512, 128, 128, 128, 128, 128, 128, 24, 384, 256, 32, 3072)
OFF = np.concatenate([[0], np.cumsum(IN_SPLITS)]).tolist()
(O_AQ, O_AK, O_AV, O_BQ, O_BKC, O_BVC, O_BKS, O_BVS, O_BKW, O_BVW, O_BG, O_CQA, O_CKV, O_CKR, O_GBR) = OFF[:15]


def perm_half(cols, hd):
    cols = np.asarray(cols)
    n = len(cols) // hd
    out = []
    for h in range(n):
        blk = cols[h * hd:(h + 1) * hd]
        out.append(np.concatenate([blk[hd // 2:], blk[:hd // 2]]))
    return np.concatenate(out)


def fm_groups():
    g = []
    r = lambda a, n: list(range(a, a + n))
    for i in range(4):
        g.append(("aq%d" % i, r(O_AQ + 128 * i, 128)))
        g.append(("aq%dP" % i, perm_half(r(O_AQ + 128 * i, 128), 64)))
    g.append(("ak", r(O_AK, 128)))
    g.append(("akP", perm_half(r(O_AK, 128), 64)))
    for i in range(4):
        g.append(("bq%d" % i, r(O_BQ + 128 * i, 128)))
        g.append(("bq%dP" % i, perm_half(r(O_BQ + 128 * i, 128), 64)))
    g.append(("bks", r(O_BKS, 128)))
    g.append(("bksP", perm_half(r(O_BKS, 128), 64)))
    g.append(("bkw", r(O_BKW, 128)))
    g.append(("bkwP", perm_half(r(O_BKW, 128), 64)))
    g.append(("bkc", r(O_BKC, 128)))
    g.append(("bvc", r(O_BVC, 128)))
    for i in range(3):
        g.append(("cqa%d" % i, r(O_CQA + 128 * i, 128)))
    for i in range(2):
        g.append(("ckv%d" % i, r(O_CKV + 128 * i, 128)))
    kr = r(O_CKR, 32)
    g.append(("ckr", kr * 4))
    g.append(("ckrP", list(perm_half(kr, 32)) * 4))
    bg = r(O_BG, 24)
    g.append(("bg", bg + bg[:8] + r(O_BG, 24) * 4))
    return g


FMG = fm_groups()
FMI = {n: i for i, (n, _) in enumerate(FMG)}
NFM = len(FMG)


def layer_weights(inp, l):
    w = {}
    win = np.asarray(inp["w_in"][l], np.float32)
    cols = np.concatenate([np.asarray(c) for _, c in FMG])
    for _, c in FMG:
        assert len(c) == 128 or True
    w["wfm"] = np.ascontiguousarray(np.concatenate(
        [win[:, np.asarray(c)[:128]] for _, c in FMG], axis=1))
    w["wtm"] = np.ascontiguousarray(np.concatenate(
        [win[:, O_AV:O_AV + 128], win[:, O_BVS:O_BVS + 128], win[:, O_BVW:O_BVW + 128]], axis=1))
    w["wg"] = np.ascontiguousarray(win[:, O_GBR:O_GBR + 3072])
    w["nmix"] = np.ascontiguousarray(np.asarray(inp["norm_mix"][l], np.float32).reshape(8, 128).T)
    w["nx"] = np.ascontiguousarray(np.asarray(inp["norm_xattn"][l], np.float32).reshape(8, 128).T)
    w["nmem"] = np.ascontiguousarray(np.asarray(inp["norm_mem"][l], np.float32).reshape(8, 128).T)
    w["nffn"] = np.ascontiguousarray(np.asarray(inp["norm_ffn"][l], np.float32).reshape(8, 128).T)
    w["sinks"] = np.asarray(inp["swa_sinks"][l], np.float32).reshape(1, 8)
    w["pek"] = np.ascontiguousarray(np.asarray(inp["nsa_pe_k"][l], np.float32).T)
    w["pev"] = np.ascontiguousarray(np.asarray(inp["nsa_pe_v"][l], np.float32).T)
    w["wk1"] = np.ascontiguousarray(np.asarray(inp["nsa_wk1"][l], np.float32).reshape(32, 64, 64).transpose(1, 0, 2))
    w["wv1"] = np.ascontiguousarray(np.asarray(inp["nsa_wv1"][l], np.float32).reshape(32, 64, 64).transpose(1, 0, 2))
    w["wk2"] = np.asarray(inp["nsa_wk2"][l], np.float32)
    w["wv2"] = np.asarray(inp["nsa_wv2"][l], np.float32)
    w["qn"] = np.ascontiguousarray(np.asarray(inp["mla_q_norm"][l], np.float32).reshape(3, 128).T)
    w["kvn"] = np.ascontiguousarray(np.asarray(inp["mla_kv_norm"][l], np.float32).reshape(2, 128).T)
    wq = np.asarray(inp["mla_w_q_b"][l], np.float32).reshape(384, 8, 96)
    wqp = wq.copy()
    pi = perm_half(np.arange(64, 96), 32)
    wqp[:, :, 64:96] = wq[:, :, pi]
    w["wq"] = np.ascontiguousarray(wq.reshape(3, 128, 8 * 96).transpose(1, 0, 2))
    w["wqp"] = np.ascontiguousarray(wqp.reshape(3, 128, 8 * 96).transpose(1, 0, 2))
    wkv = np.asarray(inp["mla_w_kv_b"][l], np.float32).reshape(256, 8, 128)
    w["wkvk"] = np.ascontiguousarray(wkv[:, :, :64].reshape(2, 128, 512).transpose(1, 0, 2))
    w["wkvv"] = np.ascontiguousarray(wkv[:, :, 64:].reshape(2, 128, 512).transpose(1, 0, 2))
    for nm, key in (("wbra", "w_br_a"), ("wbrb", "w_br_b"), ("wbrc", "w_br_c")):
        w[nm] = np.ascontiguousarray(np.asarray(inp[key][l], np.float32).reshape(8, 64, 1024).transpose(1, 0, 2))
    w["wout"] = np.ascontiguousarray(np.asarray(inp["w_out"][l], np.float32).reshape(8, 128, 1024).transpose(1, 0, 2))
    w["wxq"] = np.ascontiguousarray(np.asarray(inp["w_xq"][l], np.float32).reshape(8, 128, 512).transpose(1, 0, 2))
    w["wxkv"] = np.ascontiguousarray(np.asarray(inp["w_xkv"][l], np.float32).reshape(8, 128, 1024).transpose(1, 0, 2))
    w["wxo"] = np.ascontiguousarray(np.asarray(inp["w_xo"][l], np.float32).reshape(4, 128, 1024).transpose(1, 0, 2))
    w["wgu"] = np.ascontiguousarray(np.asarray(inp["w_gate_up"][l], np.float32).reshape(8, 128, 2 * DFF).transpose(1, 0, 2))
    w["wdn"] = np.ascontiguousarray(np.asarray(inp["w_down"][l], np.float32).reshape(22, 128, 1024).transpose(1, 0, 2))
    return w


WSHAPES = None


def build(T, L, wshapes, cshapes, stop_after=None, debug=False):
    NT = T // 128
    NS = T // 512
    NCT = (T // 16 - 1 + 127) // 128
    n_cmp = (T - 32) // 16 + 1
    nc = bass.Bass("TRN2", target_bir_lowering=False)
    S = Sched(nc)

    def din(name, shape, dt=F32):
        return nc.dram_tensor(name, list(shape), dt, kind="ExternalInput").ap()

    def dscr(name, shape, dt):
        return nc.dram_tensor(name, list(shape), dt, kind=("ExternalOutput" if debug else "Internal")).ap()

    x_in = din("x", [T, D])
    mem_in = din("mem", [256, D])
    pos_in = din("pos", [1, T], I32)
    nfin_in = din("nfin", [1, D])
    C = {k: din("c_" + k, v[0], BF16 if v[1] == "bf16" else F32) for k, v in cshapes.items()}
    Wt = [{k: din("w%d_%s" % (l, k), shp) for k, shp in wshapes.items()} for l in range(L)]
    y_out = nc.dram_tensor("y", [T, D], F32, kind="ExternalOutput").ap()

    xres = dscr("xres", [T, D], F32)
    rcA, rsA, rcM, rsM = (dscr(n, [128, T], F32) for n in ("rcA", "rsA", "rcM", "rsM"))
    QA = dscr("QA", [64, 8, T], BF16)
    KA = dscr("KA", [64, 2, T], BF16)
    VA = dscr("VA", [T, 2, 128], BF16)
    QBU = dscr("QBU", [64, 8, T], BF16)
    QBR = dscr("QBR", [64, 8, T], BF16)
    KC = dscr("KC", [64, 2, T], BF16)
    VC = dscr("VC", [64, 2, T], BF16)
    KBS = dscr("KBS", [64, 2, T], BF16)
    VBS = dscr("VBS", [T, 2, 128], BF16)
    KBW = dscr("KBW", [64, 2, T], BF16)
    VBW = dscr("VBW", [T, 2, 128], BF16)
    GB = dscr("GB", [8, 3, T], F32)
    QLAT = dscr("QLAT", [128, 3, T], BF16)
    CKV = dscr("CKV", [128, 2, T], BF16)
    KPE = dscr("KPE", [32, T], BF16)
    OA = dscr("OA", [64, 8, T], BF16)
    OBP = dscr("OBP", [64, 8, T], F32)
    OB = dscr("OB", [64, 8, T], BF16)
    OC = dscr("OC", [64, 8, T], BF16)
    ACTD = dscr("ACTD", [128, 22, T], BF16)
    KCMP = dscr("KCMP", [64, 2, NCT * 128], BF16)
    VCMP = dscr("VCMP", [128, NCT, 2, 128], BF16)

    PS = [nc.alloc_psum_tensor("ps%d" % i, [128, 512], F32).ap() for i in range(8)]
    psr = [0]

    def ps_next():
        i = psr[0]
        psr[0] = (i + 1) % 8
        return i

    def PR(i):
        return ("ps", i)

    def sb(st, name, shape, dt):
        return st.enter_context(nc.sbuf_tensor(name, list(shape), dt)).ap() if False else nc_alloc(st, name, shape, dt)

    acnt = [0]

    def nc_alloc(st, name, shape, dt):
        acnt[0] += 1
        g = nc.sbuf_tensor("sb%d_%s" % (acnt[0], name), list(shape), dt)
        t = st.enter_context(g)
        return t.ap() if hasattr(t, "ap") and callable(t.ap) else t

    uid = [0]

    def u():
        uid[0] += 1
        return uid[0]

    def wload(dst, src, reg, pieces=1, axis=None):
        if pieces == 1:
            S.dma(lambda e: e.dma_start(out=dst, in_=src), R=(), W=(reg,), q="pool")
        else:
            n = dst.shape[1]
            step = (n + pieces - 1) // pieces
            for a in range(0, n, step):
                b = min(n, a + step)
                S.dma(lambda e, a=a, b=b: e.dma_start(out=dst[:, a:b], in_=src[:, a:b]), R=(), W=((reg, a),), q="pool")

    with ExitStack() as st:
        TWO_PI = 2.0 * np.pi
        c1 = float(np.float32(6.28125))
        c2 = float(np.float32(np.float32(TWO_PI - 6.28125).view(np.uint32) & np.uint32(0xFFFFF000)).view(np.float32)) if False else None
        r2 = TWO_PI - 6.28125
        c2 = float(np.array(np.array(r2, np.float32).view(np.uint32) & np.uint32(0xFFFFF000), np.uint32).view(np.float32))
        c3 = float(np.float32(r2 - c2))
        MAGIC = 12582912.0
        PIS = 3.1415925
        CH = min(T, 2048)
        ropep = nc_alloc(st, "ropep", [128, 4], F32)
        S.dma(lambda e: e.dma_start(out=ropep, in_=C["ropep"]), W=("ropep",))
        posi = nc_alloc(st, "posi", [128, CH], I32)
        posf = nc_alloc(st, "posf", [128, CH], F32)
        ang = nc_alloc(st, "ang", [128, CH], F32)
        kk = nc_alloc(st, "kk", [128, CH], F32)
        rr = nc_alloc(st, "rr", [128, CH], F32)
        sn = nc_alloc(st, "sn", [128, CH], F32)
        cs_ = nc_alloc(st, "cs", [128, CH], F32)
        xt = [nc_alloc(st, "xcp%d" % i, [128, 4, D], F32) for i in range(2)]
        xv = x_in.rearrange("(s j p) c -> s p j c", p=128, j=4)
        xrv = xres.rearrange("(s j p) c -> s p j c", p=128, j=4)
        for s in range(NS):
            b = xt[s % 2]
            S.dma(lambda e, b=b, s=s: e.dma_start(out=b, in_=xv[s]), W=(("xcp", s % 2),))
            S.dma(lambda e, b=b, s=s: e.dma_start(out=xrv[s], in_=b), R=(("xcp", s % 2),), W=(("xres", s),), q="pool")
        for c0 in range(0, T, CH):
            S.dma(lambda e, c0=c0: e.dma_start(out=posi, in_=pos_in[:, c0:c0 + CH].broadcast_to([128, CH])), W=("posi",))
            S.op("dve", lambda e: e.tensor_copy(out=posf, in_=posi), R=("posi",), W=("posf",))
            for (fc, sc, dc, ds) in ((0, 1, rcA, rsA), (2, 3, rcM, rsM)):
                S.op("dve", lambda e, fc=fc: e.tensor_scalar(out=ang, in0=posf, scalar1=ropep[:, fc:fc + 1], scalar2=None, op0=ALU.mult),
                     R=("posf", "ropep"), W=("ang",))
                S.op("dve", lambda e: e.tensor_scalar(out=kk, in0=ang, scalar1=float(1.0 / TWO_PI), scalar2=MAGIC, op0=ALU.mult, op1=ALU.add),
                     R=("ang",), W=("kk",))
                S.op("dve", lambda e: e.tensor_scalar(out=kk, in0=kk, scalar1=MAGIC, scalar2=None, op0=ALU.subtract),
                     R=("kk",), W=("kk",))
                S.op("dve", lambda e: e.scalar_tensor_tensor(out=rr, in0=kk, scalar=-c1, in1=ang, op0=ALU.mult, op1=ALU.add),
                     R=("kk", "ang"), W=("rr",))
                S.op("dve", lambda e: e.scalar_tensor_tensor(out=rr, in0=kk, scalar=-c2, in1=rr, op0=ALU.mult, op1=ALU.add),
                     R=("kk", "rr"), W=("rr",))
                S.op("dve", lambda e: e.scalar_tensor_tensor(out=rr, in0=kk, scalar=-c3, in1=rr, op0=ALU.mult, op1=ALU.add),
                     R=("kk", "rr"), W=("rr",))
                S.op("dve", lambda e: e.tensor_scalar(out=rr, in0=rr, scalar1=-PIS, scalar2=PIS, op0=ALU.max, op1=ALU.min),
                     R=("rr",), W=("rr",))
                S.op("act", lambda e: e.activation(out=sn, in_=rr, func=AF.Sin), R=("rr",), W=("sn",))
                S.op("dve", lambda e, sc=sc: e.tensor_scalar(out=sn, in0=sn, scalar1=ropep[:, sc:sc + 1], scalar2=None, op0=ALU.mult),
                     R=("sn", "ropep"), W=("sn",))
                S.dma(lambda e, ds=ds, c0=c0: e.dma_start(out=ds[:, c0:c0 + CH], in_=sn), R=("sn",), W=(("rope", u()),), q="pool")
                S.op("dve", lambda e: e.scalar_tensor_tensor(out=cs_, in0=rr, scalar=-1.0, in1=rr, op0=ALU.mult, op1=ALU.max), R=("rr",), W=("cs",))
                S.op("dve", lambda e: e.tensor_scalar(out=cs_, in0=cs_, scalar1=-1.0, scalar2=float(np.pi / 2), op0=ALU.mult, op1=ALU.add),
                     R=("cs",), W=("cs",))
                S.op("act", lambda e: e.activation(out=cs_, in_=cs_, func=AF.Sin), R=("cs",), W=("cs",))
                S.dma(lambda e, dc=dc, c0=c0: e.dma_start(out=dc[:, c0:c0 + CH], in_=cs_), R=("cs",), W=(("rope", u()),), q="pool")
        S.finish()

    def norm_A(xt, xreg, tmp, ntok_tiles=4):
        for j in range(ntok_tiles):
            S.op("act", lambda e, j=j: e.activation(out=tmp["sq"], in_=xt[:, j, :], func=AF.Square, accum_out=tmp["ss"][:, j:j + 1]),
                 R=(xreg,), W=("nt_sq", ("nt_ss", j)))
        ssr = tuple(("nt_ss", j) for j in range(ntok_tiles))
        S.op("act", lambda e: e.activation(out=tmp["ss"][:, 0:ntok_tiles], in_=tmp["ss"][:, 0:ntok_tiles], func=AF.Sqrt, scale=1.0 / D, bias=EPS),
             R=ssr, W=ssr)
        S.op("dve", lambda e: e.reciprocal(out=tmp["ss"][:, 0:ntok_tiles], in_=tmp["ss"][:, 0:ntok_tiles]), R=ssr, W=ssr)
        for j in range(ntok_tiles):
            S.op("dve", lambda e, j=j: e.tensor_scalar(out=tmp["hs"][:, j, :], in0=xt[:, j, :], scalar1=tmp["ss"][:, j:j + 1], scalar2=None, op0=ALU.mult),
                 R=(xreg, ("nt_ss", j)), W=(("nt_hs", j),))

    def norm_B(hT, hreg, gcol, gcreg, tmp, ntok_tiles=4):
        for k in range(8):
            b = ps_next()
            for j in range(ntok_tiles):
                S.op("pe", lambda e, k=k, j=j, b=b: e.transpose(out=PS[b][:, j * 128:(j + 1) * 128], in_=tmp["hs"][:, j, k * 128:(k + 1) * 128], identity=tmp["ident"]),
                     R=(("nt_hs", j), "ident_f"), W=(PR(b),))
            n = ntok_tiles * 128
            S.op("dve" if k % 2 == 0 else "act",
                 (lambda e, k=k, b=b, n=n: e.tensor_scalar(out=hT[:, k, 0:n], in0=PS[b][:, 0:n], scalar1=gcol[:, k:k + 1], scalar2=None, op0=ALU.mult))
                 if k % 2 == 0 else
                 (lambda e, k=k, b=b, n=n: e.activation(out=hT[:, k, 0:n], in_=PS[b][:, 0:n], func=AF.Copy, scale=gcol[:, k:k + 1])),
                 R=(PR(b), gcreg), W=((hreg, k),))

    def norm_T(xt, xreg, hT, hreg, gcol, gcreg, tmp, ntok_tiles=4):
        norm_A(xt, xreg, tmp, ntok_tiles)
        norm_B(hT, hreg, gcol, gcreg, tmp, ntok_tiles)

    def norm_tmp(st):
        t = {}
        t["sq"] = nc_alloc(st, "nt_sq", [128, D], F32)
        t["ss"] = nc_alloc(st, "nt_ss", [128, 4], F32)
        t["hs"] = nc_alloc(st, "nt_hs", [128, 4, D], F32)
        t["ident"] = nc_alloc(st, "ident_f", [128, 128], F32)
        S.dma(lambda e: e.dma_start(out=t["ident"], in_=C["ident_f"]), W=("ident_f",))
        return t

    xrv = xres.rearrange("(s j p) c -> s p j c", p=128, j=4)
    if stop_after == "P0":
        S.finish()
        return nc

    for l in range(L):
        W = Wt[l]
        with ExitStack() as st:
            S.barrier()
            tmp = norm_tmp(st)
            wfm = nc_alloc(st, "wfm", [128, 8, NFM * 128], BF16)
            wtm = nc_alloc(st, "wtm", [128, 8, 384], BF16)
            for k in range(8):
                S.dma(lambda e, k=k: e.dma_start(out=wfm[:, k, :], in_=W["wfm"][k * 128:(k + 1) * 128, :]), W=(("wfm", k),), q="pool")
                S.dma(lambda e, k=k: e.dma_start(out=wtm[:, k, :], in_=W["wtm"][k * 128:(k + 1) * 128, :]), W=(("wtm", k),), q="pool")
            wfr = tuple(("wfm", k) for k in range(8))
            wtr = tuple(("wtm", k) for k in range(8))
            gmix = nc_alloc(st, "gmix", [128, 8], F32)
            S.dma(lambda e: e.dma_start(out=gmix, in_=W["nmix"]), W=("gmix",))
            qn = nc_alloc(st, "qn", [128, 3], F32)
            kvn = nc_alloc(st, "kvn", [128, 2], F32)
            S.dma(lambda e: e.dma_start(out=qn, in_=W["qn"]), W=("qn",))
            S.dma(lambda e: e.dma_start(out=kvn, in_=W["kvn"]), W=("kvn",))
            ones_f = nc_alloc(st, "ones_f", [128, 128], F32)
            S.dma(lambda e: e.dma_start(out=ones_f, in_=C["ones_f"]), W=("ones_f",))
            xts = [nc_alloc(st, "p1x%d" % i, [128, 4, D], F32) for i in range(2)]
            hT = nc_alloc(st, "p1hT", [128, 8, 512], BF16)
            tab = [nc_alloc(st, "p1tab%d" % i, [128, 512], F32) for i in range(4)]
            t1 = [nc_alloc(st, "p1t1_%d" % i, [128, 512], F32) for i in range(2)]
            t2 = [nc_alloc(st, "p1t2_%d" % i, [128, 512], F32) for i in range(2)]
            ob = [nc_alloc(st, "p1ob%d" % i, [128, 512], BF16) for i in range(4)]
            lat = [nc_alloc(st, "p1lat%d" % i, [128, 512], F32) for i in range(3)]
            sqb = nc_alloc(st, "p1sqb", [128, 512], F32)
            rstd = nc_alloc(st, "p1rstd", [128, 512], F32)
            gsb = nc_alloc(st, "p1gsb", [128, 512], F32)
            vst = [nc_alloc(st, "p1vst%d" % i, [128, 6, 128], BF16) for i in range(2)]
            for i in range(2):
                S.op("pool", lambda e, i=i: e.memset(vst[i], 1.0), W=(("vst", i),))
            obc = [0]

            def next_ob():
                i = obc[0]
                obc[0] = (i + 1) % 4
                return i

            def proj_fm(gname):
                gi = FMI[gname]
                b = ps_next()
                for k in range(8):
                    S.op("pe", lambda e, k=k, b=b, gi=gi: e.matmul(PS[b], lhsT=wfm[:, k, gi * 128:(gi + 1) * 128], rhs=hT[:, k, :], start=(k == 0), stop=(k == 7)),
                         R=(("wfm", k), ("hT", k)), W=(PR(b),))
                return b

            tc = [0]
            def p1_load(s):
                S.dma(lambda e: e.dma_start(out=xts[s % 2], in_=xrv[s]), R=(("xres", s),), W=(("p1x", s % 2),))

            p1_load(0)
            norm_A(xts[0], ("p1x", 0), tmp)
            for s in range(NS):
                xt = xts[s % 2]
                xr = ("p1x", s % 2)
                if s + 1 < NS:
                    p1_load(s + 1)
                for i, tsrc in enumerate((rcA, rsA, rcM, rsM)):
                    S.dma(lambda e, i=i, tsrc=tsrc, s=s: e.dma_start(out=tab[i], in_=tsrc[:, s * 512:(s + 1) * 512]), W=(("tab", i),))
                norm_B(hT, "hT", gmix, "gmix", tmp)
                if s + 1 < NS:
                    norm_A(xts[(s + 1) % 2], ("p1x", (s + 1) % 2), tmp)
                tsl = slice(s * 512, (s + 1) * 512)

                def rope_group(nm, dsts, ci=0, si=1, rows=None):
                    ba = proj_fm(nm)
                    bp = proj_fm(nm + "P") if not nm.startswith("ckr") else proj_fm("ckrP")
                    i = tc[0] % 2
                    tc[0] += 1
                    o = next_ob()
                    S.op("dve", lambda e, ba=ba, i=i: e.tensor_tensor(out=t1[i], in0=PS[ba], in1=tab[ci], op=ALU.mult),
                         R=(PR(ba), ("tab", ci)), W=(("t1", i),))
                    S.op("dve", lambda e, bp=bp, i=i: e.tensor_tensor(out=t2[i], in0=PS[bp], in1=tab[si], op=ALU.mult),
                         R=(PR(bp), ("tab", si)), W=(("t2", i),))
                    S.op("pool", lambda e, i=i, o=o: e.tensor_tensor(out=ob[o], in0=t1[i], in1=t2[i], op=ALU.add),
                         R=(("t1", i), ("t2", i)), W=(("ob", o),))
                    emit_out(dsts, o)

                def emit_out(dsts, o):
                    if len(dsts) == 1 and dsts[0][1] is None:
                        dst = dsts[0][0]
                        S.dma(lambda e, dst=dst, o=o: e.dma_start(out=dst, in_=ob[o]), R=(("ob", o),), W=(("p1out", u()),))
                    else:
                        for (dst, psl) in dsts:
                            S.dma(lambda e, dst=dst, psl=psl, o=o: e.dma_start(out=dst, in_=ob[o][psl, :]), R=(("ob", o),), W=(("p1out", u()),))

                def plain_group(nm, dsts):
                    b = proj_fm(nm)
                    o = next_ob()
                    S.op("act", lambda e, b=b, o=o: e.copy(out=ob[o], in_=PS[b]), R=(PR(b),), W=(("ob", o),))
                    emit_out(dsts, o)

                lo, hi = slice(0, 64), slice(64, 128)

                def both(Dt, i):
                    return [(Dt[:, 2 * i, tsl], lo), (Dt[:, 2 * i + 1, tsl], hi)]

                for i in range(4):
                    rope_group("aq%d" % i, both(QA, i))
                rope_group("ak", both(KA, 0))
                for i in range(4):
                    rope_group("bq%d" % i, both(QBR, i))
                    plain_group("bq%d" % i, both(QBU, i))
                rope_group("bks", both(KBS, 0))
                rope_group("bkw", both(KBW, 0))
                plain_group("bkc", both(KC, 0))
                plain_group("bvc", both(VC, 0))
                rope_group("ckr", [(KPE[:, tsl], slice(64, 96))], ci=2, si=3)
                b = proj_fm("bg")
                S.op("act", lambda e, b=b: e.activation(out=gsb[0:24, :], in_=PS[b][0:24, :], func=AF.Sigmoid), R=(PR(b),), W=("gsb",))
                S.dma(lambda e, tsl=tsl: e.dma_start(out=GB.rearrange("h j t -> (h j) t")[:, tsl], in_=gsb[0:24, :]), R=("gsb",), W=(("p1out", u()),), q="pool")

                def latent(names, gcol, gcreg, dstT, nfeat):
                    n = len(names)
                    bsum = ps_next()
                    for i, nm in enumerate(names):
                        b = proj_fm(nm)
                        S.op("act", lambda e, b=b, i=i: e.copy(out=lat[i], in_=PS[b]), R=(PR(b),), W=(("lat", i),))
                        S.op("act", lambda e, b=b: e.activation(out=sqb, in_=PS[b], func=AF.Square), R=(PR(b),), W=("sqb",))
                        S.op("pe", lambda e, i=i, bsum=bsum: e.matmul(PS[bsum], lhsT=ones_f, rhs=sqb, start=(i == 0), stop=(i == n - 1)),
                             R=("ones_f", "sqb"), W=(PR(bsum),))
                    S.op("act", lambda e, bsum=bsum: e.activation(out=rstd, in_=PS[bsum], func=AF.Sqrt, scale=1.0 / nfeat, bias=EPS),
                         R=(PR(bsum),), W=("rstd",))
                    S.op("dve", lambda e: e.reciprocal(out=rstd, in_=rstd), R=("rstd",), W=("rstd",))
                    for i in range(n):
                        o = next_ob()
                        S.op("dve", lambda e, i=i, o=o: e.scalar_tensor_tensor(out=ob[o], in0=lat[i], scalar=gcol[:, i:i + 1], in1=rstd, op0=ALU.mult, op1=ALU.mult),
                             R=(("lat", i), gcreg, "rstd"), W=(("ob", o),))
                        S.dma(lambda e, i=i, o=o, tsl=tsl: e.dma_start(out=dstT[:, i, tsl], in_=ob[o]), R=(("ob", o),), W=(("p1out", u()),), q="pool")

                latent(["cqa0", "cqa1", "cqa2"], qn, "qn", QLAT, 384)
                latent(["ckv0", "ckv1"], kvn, "kvn", CKV, 256)
                vi = s % 2
                for j in range(4):
                    b = ps_next()
                    for k in range(8):
                        S.op("pe", lambda e, k=k, b=b, j=j: e.matmul(PS[b][:, 0:384], lhsT=hT[:, k, j * 128:(j + 1) * 128], rhs=wtm[:, k, :], start=(k == 0), stop=(k == 7)),
                             R=(("hT", k), ("wtm", k)), W=(PR(b),))
                    S.op("act", lambda e, b=b, vi=vi: e.copy(out=vst[vi][:, :, 0:64], in_=PS[b][:, 0:384].rearrange("p (a c) -> p a c", c=64)),
                         R=(PR(b),), W=(("vst", vi),))
                    tok = slice(s * 512 + j * 128, s * 512 + (j + 1) * 128)
                    for m, dst in enumerate((VA, VBS, VBW)):
                        S.dma(lambda e, m=m, dst=dst, tok=tok, vi=vi: e.dma_start(out=dst[tok, :, :], in_=vst[vi][:, 2 * m:2 * m + 2, :]),
                              R=(("vst", vi),), W=(("p1out", u()),), q="pool")
            S.flush()
        if stop_after == "P1":
            break

        def bc4(ap):
            return ap.unsqueeze(1).broadcast_to([ap.shape[0], 4, 128])

        def load_consts_att(st, names):
            d = {}
            for nm in names:
                shp = cshapes[nm][0]
                dt = BF16 if cshapes[nm][1] == "bf16" else F32
                d[nm] = nc_alloc(st, "c_" + nm, shp, dt)
                S.dma(lambda e, nm=nm: e.dma_start(out=d[nm], in_=C[nm]), W=("c_" + nm,))
            return d

        class Att:
            def __init__(self, st, sbanks, nE=None):
                self.sb = sbanks
                self.sc = 0
                self.ec = 0
                self.depth = max(1, len(sbanks) - 1)
                nE = nE or (self.depth + 3)
                self.E = [nc_alloc(st, "attE%d" % i, [128, 512], BF16) for i in range(nE)]
                self.pend = []

            def tile(self, kT, kreg, q, qreg, masks, v, vreg, ob, first, last, scale, extra=None):
                b = self.sb[self.sc % len(self.sb)]
                self.sc += 1
                n = len(masks)
                kr = list(kreg) if isinstance(kreg, list) else [kreg]
                qr_ = list(qreg) if isinstance(qreg, list) else [qreg]
                S.op("pe", lambda e: e.matmul(PS[b], lhsT=kT, rhs=q, start=True, stop=(n == 0)), R=tuple(kr + qr_), W=(PR(b),))
                for i, (ml, mr, mregs) in enumerate(masks):
                    S.op("pe", lambda e, ml=ml, mr=mr, i=i: e.matmul(PS[b], lhsT=ml, rhs=mr, start=False, stop=(i == n - 1)), R=tuple(mregs), W=(PR(b),))
                ei = self.ec % len(self.E)
                self.ec += 1
                Et = self.E[ei]
                S.op("act", lambda e: e.activation(out=Et, in_=PS[b], func=AF.Exp, scale=scale), R=(PR(b),), W=(("E", ei),))

                def pv():
                    S.op("pe", lambda e: e.matmul(PS[ob], lhsT=v, rhs=Et, start=first, stop=last), R=(vreg, ("E", ei)), W=(PR(ob),))
                    if extra is not None:
                        extra(Et, ("E", ei))
                self.pend.append(pv)
                while len(self.pend) > self.depth:
                    self.pend.pop(0)()

            def drain(self):
                while self.pend:
                    self.pend.pop(0)()

        def fin_den(ob, den, dreg, sink=None, gate=None, greg=None):
            S.op("dve", lambda e: e.tensor_scalar(out=den, in0=PS[ob][64:128, :], scalar1=1e-30, scalar2=None, op0=ALU.max), R=(PR(ob),), W=(dreg,))
            if sink is not None:
                S.op("dve", lambda e: e.tensor_tensor(out=den.rearrange("p (h q) -> p h q", h=4), in0=den.rearrange("p (h q) -> p h q", h=4),
                                                      in1=sink.unsqueeze(2).broadcast_to([64, 4, 128]), op=ALU.add), R=(dreg, "esink"), W=(dreg,))
            S.op("dve", lambda e: e.reciprocal(out=den, in_=den), R=(dreg,), W=(dreg,))
            if gate is not None:
                S.op("pool", lambda e: e.tensor_tensor(out=den, in0=den, in1=gate, op=ALU.mult), R=(dreg, greg), W=(dreg,))

        qv = lambda Q, qb: Q[:, :, qb * 128:(qb + 1) * 128]

        def window_pass(tag, Qd, Kd, Vd, wt, sink, gate_j, part_in, Od):
            with ExitStack() as st:
                S.barrier()
                cc = load_consts_att(st, ["ident_b", "cb", "bb"])
                att = Att(st, [0, 1, 2, 3])
                kA = nc_alloc(st, tag + "k", [64, 2, T], BF16)
                vA = nc_alloc(st, tag + "v", [128, NT, 2, 128], BF16)
                for g in range(2):
                    S.dma(lambda e, g=g: e.dma_start(out=kA[:, g, :], in_=Kd[:, g, :]), W=((tag + "k", g),))
                Vr = Vd.rearrange("(n p) g c -> p n g c", p=128)
                for n0 in range(0, NT, 8):
                    S.dma(lambda e, n0=n0: e.dma_start(out=vA[:, n0:min(NT, n0 + 8)], in_=Vr[:, n0:min(NT, n0 + 8)]), W=((tag + "v", n0),))
                if sink:
                    es = nc_alloc(st, "esink", [64, 8], F32)
                    S.dma(lambda e: e.dma_start(out=es, in_=W["sinks"].broadcast_to([64, 8])), W=("esink",))
                    S.op("act", lambda e: e.activation(out=es, in_=es, func=AF.Exp), R=("esink",), W=("esink",))
                qts = [nc_alloc(st, tag + "q%d" % i, [64, 8, 128], BF16) for i in range(3)]
                dens = [nc_alloc(st, tag + "den%d" % i, [64, 512], F32) for i in range(2)]
                outs = [nc_alloc(st, tag + "out%d" % i, [64, 512], BF16) for i in range(2)]
                if gate_j is not None:
                    gts = [nc_alloc(st, tag + "gt%d" % i, [64, 4, 128], F32) for i in range(2)]
                    pts = [nc_alloc(st, tag + "pt%d" % i, [64, 4, 128], F32) for i in range(2)]
                    tmps = [nc_alloc(st, tag + "tmp%d" % i, [64, 512], F32) for i in range(2)]
                units = [(qb, g) for qb in range(NT) for g in range(2)]

                def wp_q(qb):
                    S.dma(lambda e: e.dma_start(out=qts[qb % 3], in_=qv(Qd, qb)), W=((tag + "q", qb % 3),))

                def wp_loads(ui):
                    if gate_j is None:
                        return
                    qb, g = units[ui]
                    i2 = ui % 2
                    qsl = slice(qb * 128, (qb + 1) * 128)
                    S.dma(lambda e: e.dma_start(out=gts[i2], in_=GB[4 * g:4 * g + 4, gate_j, qsl].unsqueeze(0).broadcast_to([64, 4, 128])), W=((tag + "gt", i2),))
                    S.dma(lambda e: e.dma_start(out=pts[i2], in_=part_in[:, 4 * g:4 * g + 4, qsl]), W=((tag + "pt", i2),))

                def act_recip(den, dreg):
                    S.op("act", lambda e: e.activation(out=den, in_=den, func=AF.Ln), R=(dreg,), W=(dreg,))
                    S.op("act", lambda e: e.activation(out=den, in_=den, func=AF.Exp, scale=-1.0), R=(dreg,), W=(dreg,))

                wp_q(0)
                if NT > 1:
                    wp_q(1)
                wp_loads(0)
                for ui, (qb, g) in enumerate(units):
                    if g == 0 and qb + 2 < NT:
                        wp_q(qb + 2)
                    if ui + 1 < len(units):
                        wp_loads(ui + 1)
                    qt = qts[qb % 3]
                    qr = (tag + "q", qb % 3)
                    i2 = ui % 2
                    ob = 4 + i2
                    qsl = slice(qb * 128, (qb + 1) * 128)
                    kts = [kt for kt in range(qb - wt, qb + 1) if kt >= 0]
                    for ti, kt in enumerate(kts):
                        masks = []
                        if kt == qb - wt:
                            masks.append((cc["ident_b"], bc4(cc["bb"]), ("c_ident_b", "c_bb")))
                        if kt == qb:
                            masks.append((cc["ident_b"], bc4(cc["cb"]), ("c_ident_b", "c_cb")))
                        att.tile(kA[:, g, kt * 128:(kt + 1) * 128], (tag + "k", g), qt[:, 4 * g:4 * g + 4, :], qr, masks,
                                 vA[:, kt, g, :], (tag + "v", (kt // 8) * 8), ob, ti == 0, ti == len(kts) - 1, 0.125)
                    att.drain()
                    den, dreg = dens[i2], (tag + "den", i2)
                    out, oreg = outs[i2], (tag + "out", i2)
                    S.op("dve", lambda e, ob=ob, den=den: e.tensor_copy(out=den, in_=PS[ob][64:128, :]), R=(PR(ob),), W=(dreg,))
                    if sink:
                        S.op("dve", lambda e, den=den, g=g: e.tensor_tensor(out=den.rearrange("p (h q) -> p h q", h=4), in0=den.rearrange("p (h q) -> p h q", h=4),
                                                                         in1=es[:, 4 * g:4 * g + 4].unsqueeze(2).broadcast_to([64, 4, 128]), op=ALU.add), R=(dreg, "esink"), W=(dreg,))
                    act_recip(den, dreg)
                    if gate_j is None:
                        S.op("dve", lambda e, ob=ob, den=den, out=out: e.tensor_tensor(out=out, in0=PS[ob][0:64, :], in1=den, op=ALU.mult),
                             R=(PR(ob), dreg), W=(oreg,))
                    else:
                        gt, pt, tmp = gts[i2], pts[i2], tmps[i2]
                        S.op("dve", lambda e, den=den, gt=gt: e.tensor_tensor(out=den, in0=den, in1=gt.rearrange("p h q -> p (h q)"), op=ALU.mult), R=(dreg, (tag + "gt", i2)), W=(dreg,))
                        S.op("dve", lambda e, ob=ob, den=den, tmp=tmp: e.tensor_tensor(out=tmp, in0=PS[ob][0:64, :], in1=den, op=ALU.mult),
                             R=(PR(ob), dreg), W=((tag + "tmp", i2),))
                        S.op("dve", lambda e, tmp=tmp, pt=pt, out=out: e.tensor_tensor(out=out, in0=tmp, in1=pt.rearrange("p h q -> p (h q)"), op=ALU.add),
                             R=((tag + "tmp", i2), (tag + "pt", i2)), W=(oreg,))
                    S.dma(lambda e, out=out, g=g, qsl=qsl: e.dma_start(out=Od[:, 4 * g:4 * g + 4, qsl], in_=out.rearrange("p (h q) -> p h q", h=4)),
                          R=(oreg,), W=(("wout", u()),))
                S.flush()

        window_pass("pa", QA, KA, VA, 1, True, None, None, OA)
        if stop_after == "PA":
            break

        with ExitStack() as st:
            S.barrier()
            kc = nc_alloc(st, "kc", [64, 4, T], BF16)
            for g in range(2):
                S.dma(lambda e, g=g: e.dma_start(out=kc[:, g, :], in_=KC[:, g, :]), W=(("kc", g),))
                S.dma(lambda e, g=g: e.dma_start(out=kc[:, 2 + g, :], in_=VC[:, g, :]), W=(("kc", 2 + g),))
            w1 = nc_alloc(st, "w1", [64, 2, 32 * 64], BF16)
            S.dma(lambda e: e.dma_start(out=w1[:, 0, :], in_=W["wk1"].rearrange("d l o -> d (l o)")), W=("w1",), q="pool")
            S.dma(lambda e: e.dma_start(out=w1[:, 1, :], in_=W["wv1"].rearrange("d l o -> d (l o)")), W=("w1b",), q="pool")
            w2 = nc_alloc(st, "w2", [64, 2, 64], BF16)
            S.dma(lambda e: e.dma_start(out=w2[:, 0, :], in_=W["wk2"]), W=("w2",), q="pool")
            S.dma(lambda e: e.dma_start(out=w2[:, 1, :], in_=W["wv2"]), W=("w2b",), q="pool")
            pe = nc_alloc(st, "pe", [64, 2, 32], BF16)
            S.dma(lambda e: e.dma_start(out=pe[:, 0, :], in_=W["pek"]), W=("pe",), q="pool")
            S.dma(lambda e: e.dma_start(out=pe[:, 1, :], in_=W["pev"]), W=("peb",), q="pool")
            NC_ = NCT * 128
            hid = nc_alloc(st, "hid", [64, NC_], BF16)
            kcmp = nc_alloc(st, "kcmp", [64, 2, NC_], BF16)
            vcmp = nc_alloc(st, "vcmp", [128, NCT, 2, 128], BF16)
            S.op("pool", lambda e: e.memset(vcmp, 1.0), W=("vcmp",))
            S.op("pool", lambda e: e.memset(hid, 0.0), W=("hid",))
            for kv in range(2):
                for g in range(2):
                    b = 6
                    src = kc[:, 2 * kv + g, :].rearrange("p (c r) -> p c r", r=16)
                    for li in range(32):
                        rhs = src[:, (li // 16):(li // 16) + n_cmp, li % 16]
                        S.op("pe", lambda e, l=li, rhs=rhs, kv=kv: e.matmul(PS[b][0:64, 0:n_cmp], lhsT=w1[:, kv, l * 64:(l + 1) * 64], rhs=rhs, start=(l == 0), stop=False),
                             R=(("kc", 2 * kv + g), "w1", "w1b"), W=(PR(b),))
                    for li in range(32):
                        S.op("pe", lambda e, l=li, kv=kv: e.matmul(PS[b][0:64, 0:n_cmp], lhsT=w1[:, kv, l * 64:(l + 1) * 64], rhs=pe[:, kv, l:l + 1].broadcast_to([64, n_cmp]), start=False, stop=(l == 31)),
                             R=("pe", "peb", "w1", "w1b"), W=(PR(b),))
                    S.op("act", lambda e: e.activation(out=hid[:, 0:n_cmp], in_=PS[b][0:64, 0:n_cmp], func=AF.Silu), R=(PR(b),), W=("hid",))
                    if kv == 0:
                        b2 = 7
                        S.op("pe", lambda e: e.matmul(PS[b2][0:64, 0:NC_], lhsT=w2[:, 0, :], rhs=hid, start=True, stop=True), R=("hid", "w2"), W=(PR(b2),))
                        S.op("act", lambda e, g=g: e.copy(out=kcmp[:, g, :], in_=PS[b2][0:64, 0:NC_]), R=(PR(b2),), W=(("kcmp", g),))
                    else:
                        b2 = 7
                        for ct in range(NCT):
                            S.op("pe", lambda e, ct=ct: e.matmul(PS[b2][:, ct * 64:(ct + 1) * 64], lhsT=hid[:, ct * 128:(ct + 1) * 128], rhs=w2[:, 1, :], start=(ct == 0), stop=(ct == NCT - 1)),
                                 R=("hid", "w2b"), W=(PR(b2),))
                        S.op("act", lambda e, g=g: e.copy(out=vcmp[:, :, g, 0:64], in_=PS[b2][:, 0:NCT * 64].rearrange("p (c d) -> p c d", d=64)), R=(PR(b2),), W=("vcmp",))
            S.dma(lambda e: e.dma_start(out=KCMP, in_=kcmp), R=(("kcmp", 0), ("kcmp", 1)), W=("KCMPd",))
            S.dma(lambda e: e.dma_start(out=VCMP, in_=vcmp), R=("vcmp",), W=("VCMPd",))
            S.flush()
        with ExitStack() as st:
            S.barrier()
            cc = load_consts_att(st, ["ident_b", "ident_f", "cb", "cm", "msel", "fb"])
            att = Att(st, [0, 1, 2])
            NC_ = NCT * 128
            kcmp = nc_alloc(st, "kcmp2", [64, 2, NC_], BF16)
            vcmp = nc_alloc(st, "vcmp2", [128, NCT, 2, 128], BF16)
            S.dma(lambda e: e.dma_start(out=kcmp, in_=KCMP), R=("KCMPd",), W=(("kcmp", 0), ("kcmp", 1)))
            S.dma(lambda e: e.dma_start(out=vcmp, in_=VCMP), R=("VCMPd",), W=("vcmp",))
            kS = nc_alloc(st, "pbk", [128, 2, T], BF16)
            for g in range(2):
                S.dma(lambda e, g=g: e.dma_start(out=kS[64:128, g, :], in_=C["exh"]), W=(("exh", g),))
            QS = [[nc_alloc(st, "QS%d_%d" % (i, hf), [128, 512], BF16) for hf in range(2)] for i in range(2)]
            vS = nc_alloc(st, "pbv", [128, NT, 2, 128], BF16)
            for g in range(2):
                S.dma(lambda e, g=g: e.dma_start(out=kS[0:64, g, :], in_=KBS[:, g, :]), W=(("pbk", g),))
            Vr = VBS.rearrange("(n p) g c -> p n g c", p=128)
            for n0 in range(0, NT, 8):
                S.dma(lambda e, n0=n0: e.dma_start(out=vS[:, n0:min(NT, n0 + 8)], in_=Vr[:, n0:min(NT, n0 + 8)]), W=(("pbv", n0),))
            qus = [nc_alloc(st, "pbqu%d" % i, [64, 8, 128], BF16) for i in range(3)]
            qrs = [nc_alloc(st, "pbqr%d" % i, [64, 8, 128], BF16) for i in range(3)]
            gts = [nc_alloc(st, "pbgt%d" % i, [64, 2, 4, 128], F32) for i in range(2)]
            dens = [nc_alloc(st, "pbden%d" % i, [64, 512], F32) for i in range(2)]
            rcmp = [nc_alloc(st, "pbrc%d" % i, [64, 512], F32) for i in range(2)]
            tmps = [nc_alloc(st, "pbtmp%d" % i, [64, 512], F32) for i in range(2)]
            outs = [nc_alloc(st, "pbout%d" % i, [64, 512], F32) for i in range(2)]
            rd4 = nc_alloc(st, "rd4", [128, 4], F32)
            scr = nc_alloc(st, "scr", [128, 128], F32)
            scr2 = nc_alloc(st, "scr2", [128, 128], F32)
            m8 = nc_alloc(st, "m8", [128, 16], F32)
            selms = [nc_alloc(st, "selm%d" % i, [128, 128], F32) for i in range(2)]
            selT = [nc_alloc(st, "selT%d" % i, [128, 128], BF16) for i in range(2)]
            units = [(qb, g) for qb in range(NT) for g in range(2)]

            def loadq(qb):
                i = qb % 3
                S.dma(lambda e: e.dma_start(out=qus[i], in_=qv(QBU, qb)), W=(("pbqu", i),))
                S.dma(lambda e: e.dma_start(out=qrs[i], in_=qv(QBR, qb)), W=(("pbqr", i),))

            def cmp_job(ui):
                qb, g = units[ui]
                i2 = ui % 2
                qsl = slice(qb * 128, (qb + 1) * 128)
                gt = gts[i2]
                for jj in range(2):
                    S.dma(lambda e, jj=jj: e.dma_start(out=gt[:, jj, :, :], in_=GB[4 * g:4 * g + 4, jj, qsl].unsqueeze(0).broadcast_to([64, 4, 128])), W=(("pbgt", i2, jj),))
                nct = min(NCT, (8 * qb + 7 + 127) // 128)
                ob = 4
                for ct in range(nct):
                    dl = qb - 16 * ct
                    masks = []
                    if dl <= 16:
                        masks.append((cc["ident_b"], bc4(cc["cm"][:, dl * 128:(dl + 1) * 128]), ("c_ident_b", "c_cm")))

                    def extra(Et, ereg, ct=ct):
                        for h in range(4):
                            bi = 5 + h // 2
                            co = (h % 2) * 129
                            S.op("pe", lambda e, h=h, bi=bi, co=co: e.matmul(PS[bi][:, co:co + 129], lhsT=Et[:, h * 128:(h + 1) * 128], rhs=cc["msel"][:, ct * 129:(ct + 1) * 129],
                                                                          start=(ct == 0 and h % 2 == 0), stop=(ct == nct - 1), skip_group_check=True),
                                 R=(ereg, "c_msel"), W=(PR(bi),))
                    att.tile(kcmp[:, g, ct * 128:(ct + 1) * 128], ("kcmp", g), qus[qb % 3][:, 4 * g:4 * g + 4, :], ("pbqu", qb % 3), masks,
                             vcmp[:, ct, g, :], "vcmp", ob, ct == 0, ct == nct - 1, 0.125, extra=extra)
                att.drain()
                den, dreg = dens[i2], ("pbden", i2)
                fin_den(ob, den, dreg, gate=gt[:, 0, :, :].rearrange("p h q -> p (h q)"), greg=("pbgt", i2, 0))
                S.op("dve", lambda e: e.tensor_tensor(out=rcmp[i2], in0=PS[ob][0:64, :], in1=den, op=ALU.mult), R=(PR(ob), dreg), W=(("pbrc", i2),))
                for h in range(4):
                    bi = 5 + h // 2
                    co = (h % 2) * 129 + 128
                    S.op("dve", lambda e, h=h, bi=bi, co=co: e.tensor_scalar(out=rd4[:, h:h + 1], in0=PS[bi][:, co:co + 1], scalar1=1e-30, scalar2=None, op0=ALU.max),
                         R=(PR(bi),), W=("rd4",))
                S.op("dve", lambda e: e.reciprocal(out=rd4, in_=rd4), R=("rd4",), W=("rd4",))
                for h in range(4):
                    bi = 5 + h // 2
                    co = (h % 2) * 129
                    if h == 0:
                        S.op("dve", lambda e, bi=bi, co=co: e.tensor_scalar(out=scr, in0=PS[bi][:, co:co + 128], scalar1=rd4[:, 0:1], scalar2=None, op0=ALU.mult),
                             R=(PR(bi), "rd4"), W=("scr",))
                    else:
                        S.op("dve", lambda e, h=h, bi=bi, co=co: e.scalar_tensor_tensor(out=scr, in0=PS[bi][:, co:co + 128], scalar=rd4[:, h:h + 1], in1=scr, op0=ALU.mult, op1=ALU.add),
                             R=(PR(bi), "rd4", "scr"), W=("scr",))
                S.op("dve", lambda e: e.tensor_tensor(out=scr, in0=scr, in1=cc["fb"][:, 128 - 2 * qb:256 - 2 * qb], op=ALU.add), R=("scr", "c_fb"), W=("scr",))
                S.op("dve", lambda e: e.tensor_scalar(out=scr[:, 0:1], in0=scr[:, 0:1], scalar1=1e4, scalar2=None, op0=ALU.add), R=("scr",), W=("scr",))
                S.op("dve", lambda e: e.max(out=m8[:, 0:8], in_=scr), R=("scr",), W=("m8",))
                S.op("dve", lambda e: e.match_replace(out=scr2, in_to_replace=m8[:, 0:8], in_values=scr, imm_value=-1e30), R=("scr", "m8"), W=("scr2",))
                S.op("dve", lambda e: e.max(out=m8[:, 8:16], in_=scr2), R=("scr2",), W=("m8b",))
                S.op("dve", lambda e: e.tensor_scalar(out=selms[i2], in0=scr, scalar1=m8[:, 15:16], scalar2=1.0, op0=ALU.is_ge, op1=ALU.subtract), R=("scr", "m8b"), W=(("selm", i2),))

            def cmp_job2(ui):
                qb, g = units[ui]
                i2 = ui % 2
                S.op("pe", lambda e: e.transpose(out=PS[7][:, 0:128], in_=selms[i2], identity=cc["ident_f"]), R=(("selm", i2), "c_ident_f"), W=(PR(7),))
                nh = 2 if qb >= 32 else 1
                for hf in range(nh):
                    S.op("dve", lambda e, hf=hf: e.tensor_copy(out=QS[i2][hf][64:128, :].rearrange("p (h q) -> p h q", h=4),
                                                              in_=PS[7][hf * 64:(hf + 1) * 64, 0:128].unsqueeze(1).broadcast_to([64, 4, 128])),
                         R=(PR(7),), W=(("QS", i2, hf, "m"),))
                    S.op("dve", lambda e, hf=hf: e.tensor_copy(out=QS[i2][hf][0:64, :].rearrange("p (h q) -> p h q", h=4), in_=qrs[qb % 3][:, 4 * g:4 * g + 4, :]),
                         R=(("pbqr", qb % 3),), W=(("QS", i2, hf, "q"),))

            def sel_job(ui, nxt=None):
                qb, g = units[ui]
                i2 = ui % 2
                ob = 3
                qsl = slice(qb * 128, (qb + 1) * 128)
                for kt in range(qb + 1):
                    hf = kt // 32
                    masks = []
                    if kt == qb:
                        masks.append((cc["ident_b"], bc4(cc["cb"]), ("c_ident_b", "c_cb")))
                    att.tile(kS[:, g, kt * 128:(kt + 1) * 128], [("pbk", g), ("exh", g)], QS[i2][hf], [("QS", i2, hf, "m"), ("QS", i2, hf, "q")], masks,
                             vS[:, kt, g, :], ("pbv", (kt // 8) * 8), ob, kt == 0, kt == qb, 0.125)
                att.drain()
                if nxt is not None:
                    cmp_job2(nxt)
                den, dreg = dens[i2], ("pbden", i2)
                fin_den(ob, den, dreg, gate=gts[i2][:, 1, :, :].rearrange("p h q -> p (h q)"), greg=("pbgt", i2, 1))
                S.op("dve", lambda e: e.tensor_tensor(out=tmps[i2], in0=PS[ob][0:64, :], in1=den, op=ALU.mult), R=(PR(ob), dreg), W=(("pbtmp", i2),))
                S.op("dve", lambda e: e.tensor_tensor(out=outs[i2], in0=tmps[i2], in1=rcmp[i2], op=ALU.add), R=(("pbtmp", i2), ("pbrc", i2)), W=(("pbout", i2),))
                S.dma(lambda e: e.dma_start(out=OBP[:, 4 * g:4 * g + 4, qsl], in_=outs[i2].rearrange("p (h q) -> p h q", h=4)), R=(("pbout", i2),), W=(("wout", u()),))

            loadq(0)
            if NT > 1:
                loadq(1)
            cmp_job(0)
            cmp_job2(0)
            for ui in range(len(units)):
                qb, g = units[ui]
                if g == 0 and qb + 2 < NT:
                    loadq(qb + 2)
                if ui + 1 < len(units):
                    cmp_job(ui + 1)
                sel_job(ui, (ui + 1) if ui + 1 < len(units) else None)
            S.flush()
        if stop_after == "PB1":
            break

        window_pass("pw", QBR, KBW, VBW, 4, False, 2, OBP, OB)
        if stop_after == "PB2":
            break

        with ExitStack() as st:
            S.barrier()
            cc = load_consts_att(st, ["ident_b", "mm"])
            att = Att(st, [0, 1, 2, 3])
            qlat = nc_alloc(st, "qlat", [128, 3, T], BF16)
            ckv = nc_alloc(st, "ckv", [128, 2, T], BF16)
            for c in range(3):
                S.dma(lambda e, c=c: e.dma_start(out=qlat[:, c, :], in_=QLAT[:, c, :]), W=(("qlat", c),))
            for c in range(2):
                S.dma(lambda e, c=c: e.dma_start(out=ckv[:, c, :], in_=CKV[:, c, :]), W=(("ckv", c),))
            wq = nc_alloc(st, "wq", [128, 3, 768], BF16)
            wqp = nc_alloc(st, "wqp", [128, 3, 768], BF16)
            wkk = nc_alloc(st, "wkk", [128, 2, 512], BF16)
            wkv_ = nc_alloc(st, "wkv", [128, 2, 512], BF16)
            S.dma(lambda e: e.dma_start(out=wq, in_=W["wq"]), W=("wq",), q="pool")
            S.dma(lambda e: e.dma_start(out=wqp, in_=W["wqp"]), W=("wqp",), q="pool")
            S.dma(lambda e: e.dma_start(out=wkk, in_=W["wkvk"]), W=("wkk",), q="pool")
            S.dma(lambda e: e.dma_start(out=wkv_, in_=W["wkvv"]), W=("wkv",), q="pool")
            KH = nc_alloc(st, "KH", [96, T], BF16)
            VH = nc_alloc(st, "VH", [128, NT, 128], BF16)
            S.op("pool", lambda e: e.memset(VH, 1.0), W=("VH",))
            S.dma(lambda e: e.dma_start(out=KH[64:96, :], in_=KPE), W=("KHpe",))
            QH = [nc_alloc(st, "QH%d" % i, [96, 512], BF16) for i in range(2)]
            tabc = [nc_alloc(st, "mtc%d" % i, [96, 512], F32) for i in range(2)]
            tabs = [nc_alloc(st, "mts%d" % i, [96, 512], F32) for i in range(2)]
            t1 = nc_alloc(st, "mt1", [96, 512], F32)
            t2 = nc_alloc(st, "mt2", [96, 512], F32)
            dens = [nc_alloc(st, "mden%d" % i, [64, 512], F32) for i in range(2)]
            outs = [nc_alloc(st, "mout%d" % i, [64, 512], BF16) for i in range(2)]
            ucl = [0]

            def mla_head(h):
                for s in range(NS):
                    b = 6 + (s % 2)
                    for c in range(2):
                        S.op("pe", lambda e, c=c, b=b, s=s: e.matmul(PS[b][0:64, :], lhsT=wkk[:, c, h * 64:(h + 1) * 64], rhs=ckv[:, c, s * 512:(s + 1) * 512], start=(c == 0), stop=(c == 1)),
                             R=("wkk", ("ckv", c)), W=(PR(b),))
                    S.op("dve", lambda e, b=b, s=s: e.tensor_copy(out=KH[0:64, s * 512:(s + 1) * 512], in_=PS[b][0:64, :]), R=(PR(b),), W=("KHn",))
                for i4 in range(NT // 4):
                    b = 6 + (i4 % 2)
                    for t4 in range(4):
                        tt = i4 * 4 + t4
                        for c in range(2):
                            S.op("pe", lambda e, c=c, b=b, tt=tt, t4=t4: e.matmul(PS[b][:, t4 * 64:(t4 + 1) * 64], lhsT=ckv[:, c, tt * 128:(tt + 1) * 128], rhs=wkv_[:, c, h * 64:(h + 1) * 64],
                                                                             start=(t4 == 0 and c == 0), stop=(t4 == 3 and c == 1), skip_group_check=True),
                                 R=("wkv", ("ckv", c)), W=(PR(b),))
                    S.op("act", lambda e, b=b, i4=i4: e.copy(out=VH[:, i4 * 4:(i4 + 1) * 4, 0:64], in_=PS[b][:, 0:256].rearrange("p (a d) -> p a d", d=64)), R=(PR(b),), W=("VH",))
                def qproj(qs, i2):
                    tsl = slice(qs * 512, (qs + 1) * 512)
                    S.dma(lambda e: e.dma_start(out=tabc[i2][64:96, :], in_=rcM[64:96, tsl]), W=(("mtc", i2),))
                    S.dma(lambda e: e.dma_start(out=tabs[i2][64:96, :], in_=rsM[64:96, tsl]), W=(("mts", i2),))
                    ba, bb_ = 6, 7
                    for c in range(3):
                        S.op("pe", lambda e, c=c: e.matmul(PS[ba][0:96, :], lhsT=wq[:, c, h * 96:(h + 1) * 96], rhs=qlat[:, c, tsl], start=(c == 0), stop=(c == 2)),
                             R=("wq", ("qlat", c)), W=(PR(ba),))
                    for c in range(3):
                        S.op("pe", lambda e, c=c: e.matmul(PS[bb_][0:96, :], lhsT=wqp[:, c, h * 96:(h + 1) * 96], rhs=qlat[:, c, tsl], start=(c == 0), stop=(c == 2)),
                             R=("wqp", ("qlat", c)), W=(PR(bb_),))
                    qh, qreg = QH[i2], ("QH", i2)
                    S.op("act", lambda e: e.copy(out=qh[0:64, :], in_=PS[ba][0:64, :]), R=(PR(ba),), W=((qreg, "n"),))
                    S.op("dve", lambda e: e.tensor_tensor(out=t1[64:96, :], in0=PS[ba][64:96, :], in1=tabc[i2][64:96, :], op=ALU.mult), R=(PR(ba), ("mtc", i2)), W=("mt1",))
                    S.op("dve", lambda e: e.tensor_tensor(out=t2[64:96, :], in0=PS[bb_][64:96, :], in1=tabs[i2][64:96, :], op=ALU.mult), R=(PR(bb_), ("mts", i2)), W=("mt2",))
                    S.op("dve", lambda e: e.tensor_tensor(out=qh[64:96, :], in0=t1[64:96, :], in1=t2[64:96, :], op=ALU.add), R=("mt1", "mt2"), W=((qreg, "r"),))

                qproj(0, ucl[0] % 2)
                for qs in range(NS):
                    i2 = ucl[0] % 2
                    ucl[0] += 1
                    ob = 4 + i2
                    tsl = slice(qs * 512, (qs + 1) * 512)
                    if qs + 1 < NS:
                        qproj(qs + 1, ucl[0] % 2)
                    qh, qreg = QH[i2], ("QH", i2)
                    nk = 4 * qs + 4
                    for kt in range(nk):
                        masks = []
                        if kt >= 4 * qs:
                            i = kt - 4 * qs
                            masks.append((cc["ident_b"], cc["mm"][:, i * 512:(i + 1) * 512], ("c_ident_b", "c_mm")))
                        att.tile(KH[:, kt * 128:(kt + 1) * 128], ["KHn", "KHpe"], qh, [(qreg, "n"), (qreg, "r")], masks, VH[:, kt, :], "VH", ob, kt == 0, kt == nk - 1, float(96 ** -0.5))
                    att.drain()
                    den, dreg = dens[i2], ("mden", i2)
                    fin_den(ob, den, dreg)
                    out = outs[i2]
                    S.op("dve", lambda e, ob=ob, den=den, out=out: e.tensor_tensor(out=out, in0=PS[ob][0:64, :], in1=den, op=ALU.mult), R=(PR(ob), dreg), W=(("mout", i2),))
                    S.dma(lambda e, out=out, tsl=tsl: e.dma_start(out=OC[:, h, tsl], in_=out), R=(("mout", i2),), W=(("wout", u()),), q="pool")
            for h_ in range(8):
                mla_head(h_)
            S.flush()
        if stop_after == "PC":
            break

        with ExitStack() as st:
            S.barrier()
            tmp = norm_tmp(st)
            wg = nc_alloc(st, "wg", [128, 8, 3072], BF16)
            for k in range(8):
                S.dma(lambda e, k=k: e.dma_start(out=wg[:, k, :], in_=W["wg"][k * 128:(k + 1) * 128, :]), W=(("wg", k),))
            wbr = nc_alloc(st, "wbr", [128, 3, 4, 1024], BF16)
            for xi, nm in enumerate(("wbra", "wbrb", "wbrc")):
                for par in range(2):
                    S.dma(lambda e, xi=xi, nm=nm, par=par: e.dma_start(out=wbr[par * 64:(par + 1) * 64, xi, :, :], in_=W[nm].rearrange("d (p t) c -> d p t c", t=2)[:, :, par, :]),
                          W=(("wbr", xi, par),))
            wout = nc_alloc(st, "wout", [128, 8, 1024], BF16)
            S.dma(lambda e: e.dma_start(out=wout, in_=W["wout"]), W=("wout",))
            gmix = nc_alloc(st, "gmix", [128, 8], F32)
            S.dma(lambda e: e.dma_start(out=gmix, in_=W["nmix"]), W=("gmix",))
            xt1 = nc_alloc(st, "pmx", [128, 4, D], F32)
            xts = [xt1, xt1]
            hT = nc_alloc(st, "pmhT", [128, 8, 512], BF16)
            oin1 = [nc_alloc(st, "pmo_%d" % xi, [128, 4, 512], BF16) for xi in range(3)]
            oin = [oin1, oin1]
            gsb = [nc_alloc(st, "pmg%d" % i, [128, 512], F32) for i in range(2)]
            tmpm = nc_alloc(st, "pmt", [128, 512], F32)
            macc = nc_alloc(st, "pmacc", [128, 512], F32)
            mT = nc_alloc(st, "pmmT", [128, 8, 512], BF16)
            gc = 0
            for s in range(NS):
                xt, xr = xts[s % 2], ("pmx", 0)
                tsl = slice(s * 512, (s + 1) * 512)
                S.dma(lambda e, xt=xt, s=s: e.dma_start(out=xt, in_=xrv[s]), R=(("xres", s),), W=(xr,))
                for xi, Od in enumerate((OA, OB, OC)):
                    for par in range(2):
                        S.dma(lambda e, xi=xi, Od=Od, tsl=tsl, s=s, par=par: e.dma_start(out=oin[s % 2][xi][par * 64:(par + 1) * 64, :, :],
                                                                                      in_=Od[:, :, tsl].rearrange("d (p t) q -> d p t q", t=2)[:, :, par, :]), W=(("pmo", 0, xi, par),))
                norm_T(xt, xr, hT, "pmhT", gmix, "gmix", tmp)
                for cg in range(8):
                    for xi in range(3):
                        bp = ps_next()
                        for h in range(4):
                            S.op("pe", lambda e, bp=bp, xi=xi, cg=cg, s=s, h=h: e.matmul(PS[bp], lhsT=wbr[:, xi, h, cg * 128:(cg + 1) * 128],
                                                                                  rhs=oin[s % 2][xi][:, h, :], start=(h == 0), stop=(h == 3)),
                                 R=(("wbr", xi, 0), ("wbr", xi, 1), ("pmo", 0, xi, 0), ("pmo", 0, xi, 1)), W=(PR(bp),))
                        bg = ps_next()
                        for k in range(8):
                            S.op("pe", lambda e, bg=bg, xi=xi, k=k, cg=cg: e.matmul(PS[bg], lhsT=wg[:, k, xi * 1024 + cg * 128: xi * 1024 + (cg + 1) * 128], rhs=hT[:, k, :], start=(k == 0), stop=(k == 7)),
                                 R=(("wg", k), ("pmhT", k)), W=(PR(bg),))
                        gi = gc % 2
                        gc += 1
                        S.op("act", lambda e, bg=bg, gi=gi: e.activation(out=gsb[gi], in_=PS[bg], func=AF.Sigmoid), R=(PR(bg),), W=(("pmg", gi),))
                        if xi == 0:
                            S.op("dve", lambda e, bp=bp, gi=gi: e.tensor_tensor(out=macc, in0=PS[bp], in1=gsb[gi], op=ALU.mult), R=(PR(bp), ("pmg", gi)), W=("pmacc",))
                        else:
                            S.op("dve", lambda e, bp=bp, gi=gi: e.tensor_tensor(out=tmpm, in0=PS[bp], in1=gsb[gi], op=ALU.mult), R=(PR(bp), ("pmg", gi)), W=("pmt",))
                            if xi == 1:
                                S.op("dve", lambda e: e.tensor_tensor(out=macc, in0=macc, in1=tmpm, op=ALU.add), R=("pmacc", "pmt"), W=("pmacc",))
                            else:
                                S.op("dve", lambda e, cg=cg: e.tensor_tensor(out=mT[:, cg, :], in0=macc, in1=tmpm, op=ALU.add), R=("pmacc", "pmt"), W=(("pmmT", cg),))
                for j in range(4):
                    for half in range(2):
                        b = ps_next()
                        for cg in range(8):
                            S.op("pe", lambda e, b=b, cg=cg, j=j, half=half: e.matmul(PS[b], lhsT=mT[:, cg, j * 128:(j + 1) * 128], rhs=wout[:, cg, half * 512:(half + 1) * 512], start=(cg == 0), stop=(cg == 7)),
                                 R=(("pmmT", cg), "wout"), W=(PR(b),))
                        S.op("dve", lambda e, b=b, j=j, half=half, xt=xt: e.tensor_tensor(out=xt[:, j, half * 512:(half + 1) * 512], in0=PS[b], in1=xt[:, j, half * 512:(half + 1) * 512], op=ALU.add),
                             R=(PR(b), xr), W=(xr,))
                S.dma(lambda e, xt=xt, s=s: e.dma_start(out=xrv[s], in_=xt), R=(xr,), W=(("xres", s),))
            S.flush()
        if stop_after == "PM":
            break

        with ExitStack() as st:
            S.barrier()
            tmp = norm_tmp(st)
            wxq = nc_alloc(st, "wxq", [128, 8, 512], BF16)
            wxkv = nc_alloc(st, "wxkv", [128, 8, 1024], BF16)
            wxo = nc_alloc(st, "wxo", [128, 4, 1024], BF16)
            S.dma(lambda e: e.dma_start(out=wxq, in_=W["wxq"]), W=("wxq",))
            S.dma(lambda e: e.dma_start(out=wxkv, in_=W["wxkv"]), W=("wxkv",))
            S.dma(lambda e: e.dma_start(out=wxo, in_=W["wxo"]), W=("wxo",))
            gx = nc_alloc(st, "gx", [128, 8], F32)
            gm = nc_alloc(st, "gm", [128, 8], F32)
            S.dma(lambda e: e.dma_start(out=gx, in_=W["nx"]), W=("gx",))
            S.dma(lambda e: e.dma_start(out=gm, in_=W["nmem"]), W=("gm",))
            ones_b = nc_alloc(st, "ones_b", [128, 128], BF16)
            S.dma(lambda e: e.dma_start(out=ones_b, in_=C["ones_f"]), W=("ones_b",))
            xts = [nc_alloc(st, "pxx%d" % i, [128, 4, D], F32) for i in range(2)]
            hT = nc_alloc(st, "pxhT", [128, 8, 512], BF16)
            KM = nc_alloc(st, "KM", [128, 4, 256], BF16)
            VM = nc_alloc(st, "VM", [128, 2, 512], BF16)
            memt = xts[1]
            S.dma(lambda e: e.dma_start(out=memt[:, 0:2, :], in_=mem_in.rearrange("(j p) c -> p j c", p=128)), W=(("pxx", 1),))
            norm_T(memt, ("pxx", 1), hT, "pxhT", gm, "gm", tmp, ntok_tiles=2)
            for h in range(4):
                b = ps_next()
                for k in range(8):
                    S.op("pe", lambda e, b=b, k=k, h=h: e.matmul(PS[b][:, 0:256], lhsT=wxkv[:, k, h * 128:(h + 1) * 128], rhs=hT[:, k, 0:256], start=(k == 0), stop=(k == 7)),
                         R=("wxkv", ("pxhT", k)), W=(PR(b),))
                S.op("act", lambda e, b=b, h=h: e.copy(out=KM[:, h, :], in_=PS[b][:, 0:256]), R=(PR(b),), W=("KM",))
            for mt in range(2):
                b = ps_next()
                for k in range(8):
                    S.op("pe", lambda e, b=b, k=k, mt=mt: e.matmul(PS[b], lhsT=hT[:, k, mt * 128:(mt + 1) * 128], rhs=wxkv[:, k, 512:1024], start=(k == 0), stop=(k == 7)),
                         R=("wxkv", ("pxhT", k)), W=(PR(b),))
                S.op("act", lambda e, b=b, mt=mt: e.copy(out=VM[:, mt, :], in_=PS[b]), R=(PR(b),), W=("VM",))
            qx = [nc_alloc(st, "qx%d" % i, [128, 512], BF16) for i in range(2)]
            Ex = [nc_alloc(st, "Ex%d" % i, [128, 512], BF16) for i in range(4)]
            denx = nc_alloc(st, "denx", [128, 512], F32)
            oxT = nc_alloc(st, "oxT", [128, 4, 512], BF16)
            ec = 0
            def px_load(s):
                S.dma(lambda e: e.dma_start(out=xts[s % 2], in_=xrv[s]), R=(("xres", s),), W=(("pxx", s % 2),))

            px_load(0)
            norm_A(xts[0], ("pxx", 0), tmp)
            for s in range(NS):
                xt, xr = xts[s % 2], ("pxx", s % 2)
                if s + 1 < NS:
                    px_load(s + 1)
                norm_B(hT, "pxhT", gx, "gx", tmp)
                if s + 1 < NS:
                    norm_A(xts[(s + 1) % 2], ("pxx", (s + 1) % 2), tmp)
                for h in range(4):
                    bq = ps_next()
                    for k in range(8):
                        S.op("pe", lambda e, bq=bq, k=k, h=h: e.matmul(PS[bq], lhsT=wxq[:, k, h * 128:(h + 1) * 128], rhs=hT[:, k, :], start=(k == 0), stop=(k == 7)),
                             R=("wxq", ("pxhT", k)), W=(PR(bq),))
                    qi = h % 2
                    S.op("act", lambda e, bq=bq, qi=qi: e.copy(out=qx[qi], in_=PS[bq]), R=(PR(bq),), W=(("qx", qi),))
                    bo = ps_next()
                    bd = ps_next()
                    for mt in range(2):
                        bs = ps_next()
                        S.op("pe", lambda e, bs=bs, h=h, mt=mt, qi=qi: e.matmul(PS[bs], lhsT=KM[:, h, mt * 128:(mt + 1) * 128], rhs=qx[qi], start=True, stop=True),
                             R=("KM", ("qx", qi)), W=(PR(bs),))
                        ei = ec % 4
                        ec += 1
                        S.op("act", lambda e, bs=bs, ei=ei: e.activation(out=Ex[ei], in_=PS[bs], func=AF.Exp, scale=float(128 ** -0.5)), R=(PR(bs),), W=(("Ex", ei),))
                        S.op("pe", lambda e, bo=bo, h=h, mt=mt, ei=ei: e.matmul(PS[bo], lhsT=VM[:, mt, h * 128:(h + 1) * 128], rhs=Ex[ei], start=(mt == 0), stop=(mt == 1)),
                             R=("VM", ("Ex", ei)), W=(PR(bo),))
                        S.op("pe", lambda e, bd=bd, mt=mt, ei=ei: e.matmul(PS[bd], lhsT=ones_b, rhs=Ex[ei], start=(mt == 0), stop=(mt == 1)),
                             R=("ones_b", ("Ex", ei)), W=(PR(bd),))
                    S.op("dve", lambda e, bd=bd: e.reciprocal(out=denx, in_=PS[bd]), R=(PR(bd),), W=("denx",))
                    S.op("dve", lambda e, bo=bo, h=h: e.tensor_tensor(out=oxT[:, h, :], in0=PS[bo], in1=denx, op=ALU.mult), R=(PR(bo), "denx"), W=(("oxT", h),))
                for j in range(4):
                    for half in range(2):
                        b = ps_next()
                        for h in range(4):
                            S.op("pe", lambda e, b=b, h=h, j=j, half=half: e.matmul(PS[b], lhsT=oxT[:, h, j * 128:(j + 1) * 128], rhs=wxo[:, h, half * 512:(half + 1) * 512], start=(h == 0), stop=(h == 3)),
                                 R=(("oxT", h), "wxo"), W=(PR(b),))
                        S.op("dve", lambda e, b=b, j=j, half=half, xt=xt: e.tensor_tensor(out=xt[:, j, half * 512:(half + 1) * 512], in0=PS[b], in1=xt[:, j, half * 512:(half + 1) * 512], op=ALU.add),
                             R=(PR(b), xr), W=(xr,))
                S.dma(lambda e, xt=xt, s=s: e.dma_start(out=xrv[s], in_=xt), R=(xr,), W=(("xres", s),))
            S.flush()
        if stop_after == "PX":
            break

        with ExitStack() as st:
            S.barrier()
            tmp = norm_tmp(st)
            wgu = nc_alloc(st, "wgu", [128, 8, 2 * DFF], BF16)
            for k in range(8):
                S.dma(lambda e, k=k: e.dma_start(out=wgu[:, k, :], in_=W["wgu"][:, k, :]), W=(("wgu", k),))
            gf = nc_alloc(st, "gf", [128, 8], F32)
            S.dma(lambda e: e.dma_start(out=gf, in_=W["nffn"]), W=("gf",))
            xts = [nc_alloc(st, "pfx%d" % i, [128, 4, D], F32) for i in range(2)]
            hT = nc_alloc(st, "pfhT", [128, 8, 512], BF16)
            sg = [nc_alloc(st, "pfsg%d" % i, [128, 512], F32) for i in range(2)]
            actT1 = nc_alloc(st, "pfact", [128, 22, 512], BF16)
            actT = [actT1, actT1]
            def pf_load(s):
                S.dma(lambda e: e.dma_start(out=xts[s % 2], in_=xrv[s]), R=(("xres", s),), W=(("pfx", s % 2),))

            pf_load(0)
            norm_A(xts[0], ("pfx", 0), tmp)
            for s in range(NS):
                xt, xr = xts[s % 2], ("pfx", s % 2)
                tsl = slice(s * 512, (s + 1) * 512)
                if s + 1 < NS:
                    pf_load(s + 1)
                norm_B(hT, "pfhT", gf, "gf", tmp)
                if s + 1 < NS:
                    norm_A(xts[(s + 1) % 2], ("pfx", (s + 1) % 2), tmp)
                at_, ar = actT[s % 2], ("pfact", 0)
                for f in range(22):
                    bg = ps_next()
                    bu = ps_next()
                    for k in range(8):
                        S.op("pe", lambda e, bg=bg, k=k, f=f: e.matmul(PS[bg], lhsT=wgu[:, k, f * 128:(f + 1) * 128], rhs=hT[:, k, :], start=(k == 0), stop=(k == 7)),
                             R=(("wgu", k), ("pfhT", k)), W=(PR(bg),))
                    for k in range(8):
                        S.op("pe", lambda e, bu=bu, k=k, f=f: e.matmul(PS[bu], lhsT=wgu[:, k, DFF + f * 128:DFF + (f + 1) * 128], rhs=hT[:, k, :], start=(k == 0), stop=(k == 7)),
                             R=(("wgu", k), ("pfhT", k)), W=(PR(bu),))
                    si = f % 2
                    S.op("act", lambda e, bg=bg, si=si: e.activation(out=sg[si], in_=PS[bg], func=AF.Silu), R=(PR(bg),), W=(("pfsg", si),))
                    S.op("dve", lambda e, bu=bu, si=si, f=f, at_=at_: e.tensor_tensor(out=at_[:, f, :], in0=PS[bu], in1=sg[si], op=ALU.mult), R=(PR(bu), ("pfsg", si)), W=((ar, f),))
                S.dma(lambda e, at_=at_, tsl=tsl: e.dma_start(out=ACTD[:, :, tsl], in_=at_), R=tuple((ar, f) for f in range(22)), W=(("actd", s),))
            S.flush()
        with ExitStack() as st:
            S.barrier()
            wdn = nc_alloc(st, "wdn", [128, 22, 1024], BF16)
            S.dma(lambda e: e.dma_start(out=wdn, in_=W["wdn"]), W=("wdn",))
            xts = [nc_alloc(st, "pgx%d" % i, [128, 4, D], F32) for i in range(2)]
            actT = [nc_alloc(st, "pgact%d" % i, [128, 22, 512], BF16) for i in range(2)]
            last = (l == L - 1)
            if last:
                gfin = nc_alloc(st, "gfin", [128, D], F32)
                S.dma(lambda e: e.dma_start(out=gfin, in_=nfin_in.broadcast_to([128, D])), W=("gfin",))
                sq = nc_alloc(st, "fsq", [128, D], F32)
                ss = nc_alloc(st, "fss", [128, 4], F32)
            yv = y_out.rearrange("(s j p) c -> s p j c", p=128, j=4)

            def pg_load(s):
                S.dma(lambda e: e.dma_start(out=xts[s % 2], in_=xrv[s]), R=(("xres", s),), W=(("pgx", s % 2),))
                S.dma(lambda e: e.dma_start(out=actT[s % 2], in_=ACTD[:, :, s * 512:(s + 1) * 512]), R=(("actd", s),), W=(("pgact", s % 2),))

            for s in range(NS):
                xt, xr = xts[s % 2], ("pgx", s % 2)
                at_, ar = actT[s % 2], ("pgact", s % 2)
                tsl = slice(s * 512, (s + 1) * 512)
                if s == 0:
                    pg_load(0)
                if s + 1 < NS:
                    pg_load(s + 1)
                for j in range(4):
                    for half in range(2):
                        b = ps_next()
                        for f in range(22):
                            S.op("pe", lambda e, b=b, f=f, j=j, half=half, at_=at_: e.matmul(PS[b], lhsT=at_[:, f, j * 128:(j + 1) * 128], rhs=wdn[:, f, half * 512:(half + 1) * 512], start=(f == 0), stop=(f == 21)),
                                 R=(ar, "wdn"), W=(PR(b),))
                        S.op("dve", lambda e, b=b, j=j, half=half, xt=xt: e.tensor_tensor(out=xt[:, j, half * 512:(half + 1) * 512], in0=PS[b], in1=xt[:, j, half * 512:(half + 1) * 512], op=ALU.add),
                             R=(PR(b), xr), W=(xr,))
                if not last:
                    S.dma(lambda e, xt=xt, s=s: e.dma_start(out=xrv[s], in_=xt), R=(xr,), W=(("xres", s),))
                else:
                    for j in range(4):
                        S.op("act", lambda e, j=j, xt=xt: e.activation(out=sq, in_=xt[:, j, :], func=AF.Square, accum_out=ss[:, j:j + 1]), R=(xr,), W=("fsq", ("fss", j)))
                    ssr = tuple(("fss", j) for j in range(4))
                    S.op("act", lambda e: e.activation(out=ss, in_=ss, func=AF.Sqrt, scale=1.0 / D, bias=EPS), R=ssr, W=ssr)
                    S.op("dve", lambda e: e.reciprocal(out=ss, in_=ss), R=ssr, W=ssr)
                    for j in range(4):
                        S.op("dve", lambda e, j=j, xt=xt: e.scalar_tensor_tensor(out=xt[:, j, :], in0=xt[:, j, :], scalar=ss[:, j:j + 1], in1=gfin, op0=ALU.mult, op1=ALU.mult),
                             R=(xr, ("fss", j), "gfin"), W=(xr,))
                    S.dma(lambda e, xt=xt, s=s: e.dma_start(out=yv[s], in_=xt), R=(xr,), W=(("y", s),))
            S.flush()
    S.finish()
    return nc


def prep_inputs(inp, T, L, b):
    consts = make_consts(T)
    m = {}
    m["x"] = np.ascontiguousarray(np.asarray(inp["x"][b, :T], np.float32))
    m["mem"] = np.ascontiguousarray(np.asarray(inp["mem"][b], np.float32))
    m["pos"] = np.ascontiguousarray(np.asarray(inp["positions"][b, :T], np.int32).reshape(1, T))
    m["nfin"] = np.ascontiguousarray(np.asarray(inp["norm_final"], np.float32).reshape(1, D))
    for k, v in consts.items():
        m["c_" + k] = v
    return m, consts


T_FULL, L_FULL, B_FULL = 8192, 2, 4


def kernel(**inputs):
    inp = {k: np.asarray(v) for k, v in inputs.items()}
    T, L, B = T_FULL, L_FULL, B_FULL
    ws = [layer_weights(inp, l) for l in range(L)]
    in_maps = []
    consts = None
    for b in range(B):
        m, consts = prep_inputs(inp, T, L, b)
        for l in range(L):
            for k, v in ws[l].items():
                m["w%d_%s" % (l, k)] = v
        in_maps.append(m)
    wshapes = {k: v.shape for k, v in ws[0].items()}
    cshapes = {k: (v.shape, "bf16" if v.dtype == NBF else "f32") for k, v in consts.items()}
    nc = build(T, L, wshapes, cshapes)
    res = run_bass_kernel_spmd(nc, in_maps, core_ids=list(range(B)))
    out = np.stack([np.asarray(r["y"], dtype=np.float32) for r in res.results], axis=0)
    return out
```
